# Optimizing a Trainium2 kernel written in Bass

```python
import math
import numpy as np
import jax
import jax.numpy as jnp
from jax import lax

D_MODEL = 1024
BATCH = 32
SEQ = 256
DEPTH = 2
DEC_BATCH = 4
DEC_SEQ = 2048
PAST_LEN = 512

GRID_W = 64
RET_H = 4
RET_DK = 64
RET_DV = 64
RET_W = RET_H * RET_DV
DIFF_H = 4
DIFF_DH = 64
DIFF_DV = 2 * DIFF_DH
DIFF_W = DIFF_H * DIFF_DV
HG_H = 4
HG_DK = 64
HG_DV = 64
HG_W = HG_H * HG_DV
MIX_W = RET_W + DIFF_W + HG_W
SPLIT_SIZES = (RET_H * RET_DK, RET_H * RET_DK, RET_W, RET_W,
               DIFF_H * 2 * DIFF_DH, DIFF_H * 2 * DIFF_DH, DIFF_W, DIFF_W,
               HG_H * HG_DK, HG_H * HG_DK, HG_H * HG_DK, HG_W, HG_W)
IN_W = sum(SPLIT_SIZES)
RET_CHUNK = 64
HG_CHUNK = 32
QBLOCK = 128
ROPE_BASE = 10000.0
EPS = 1e-6

kernel_name = 'hybrid_diffusion_parallel_heads_step'


def rms_f32(x):
    xf = x.astype(jnp.float32)
    return xf * lax.rsqrt(jnp.mean(xf * xf, axis=-1, keepdims=True) + EPS)


def rmsnorm(x, g):
    return (rms_f32(x) * g.astype(jnp.float32)).astype(x.dtype)


def head_rms(x, dtype):
    return rms_f32(x).astype(dtype)


def rope_1d(x, pos):
    half = x.shape[-1] // 2
    inv = ROPE_BASE ** (-jnp.arange(half, dtype=jnp.float32) / half)
    ang = pos.astype(jnp.float32)[:, None] * inv[None, :]
    cos = jnp.cos(ang)[:, None, :].astype(x.dtype)
    sin = jnp.sin(ang)[:, None, :].astype(x.dtype)
    x1, x2 = x[..., :half], x[..., half:]
    return jnp.concatenate([x1 * cos - x2 * sin, x1 * sin + x2 * cos], axis=-1)


def axial_rope(x):
    T = x.shape[1]
    rows = T // GRID_W
    row = jnp.repeat(jnp.arange(rows), GRID_W)
    col = jnp.tile(jnp.arange(GRID_W), rows)
    half = x.shape[-1] // 2
    return jnp.concatenate([rope_1d(x[..., :half], row), rope_1d(x[..., half:], col)], axis=-1)


def to_chunks(x, c):
    B, T, H, D = x.shape
    return x.reshape(B, T // c, c, H, D).transpose(1, 0, 3, 2, 4)


def from_chunks(y):
    nC, B, H, C, D = y.shape
    return y.transpose(1, 0, 3, 2, 4).reshape(B, nC * C, H, D)


def retention_scan(q, k, v, log_g, s0):
    f32 = jnp.float32
    idx = jnp.arange(RET_CHUNK, dtype=f32)
    b = (idx + 1.0)[None, :] * log_g[:, None]
    tril = jnp.tril(jnp.ones((RET_CHUNK, RET_CHUNK), dtype=bool))
    diff = jnp.where(tril, b[:, :, None] - b[:, None, :], 0.0)
    dmat = jnp.where(tril, jnp.exp(diff), 0.0)
    q_dec = jnp.exp(b)[:, :, None]
    k_dec = jnp.exp(b[:, -1:] - b)[:, :, None]
    s_dec = jnp.exp(b[:, -1])[:, None, None]

    def step(s, xs):
        qc, kc, vc = xs
        a = jnp.einsum('bhtd,bhsd->bhts', qc, kc) * dmat
        o = jnp.einsum('bhts,bhse->bhte', a, vc) + jnp.einsum('bhtd,bhde->bhte', qc * q_dec, s)
        s = s * s_dec + jnp.einsum('bhsd,bhse->bhde', kc * k_dec, vc)
        return s, o

    xs = tuple(to_chunks(a.astype(f32), RET_CHUNK) for a in (q, k, v))
    s, o = lax.scan(step, s0.astype(f32), xs)
    return from_chunks(o), s


def hgrn_scan(q, k, v, log_f, s0):
    f32 = jnp.float32
    tril = jnp.tril(jnp.ones((HG_CHUNK, HG_CHUNK), dtype=bool))[:, :, None]

    def step(s, xs):
        qc, kc, vc, lfc = xs
        b = jnp.cumsum(lfc, axis=2)
        diff = jnp.where(tril, b[:, :, :, None, :] - b[:, :, None, :, :], 0.0)
        dec = jnp.where(tril, jnp.exp(diff), 0.0)
        a = jnp.einsum('bhtd,bhsd,bhtsd->bhts', qc, kc, dec)
        b_last = b[:, :, -1:, :]
        o = jnp.einsum('bhts,bhse->bhte', a, vc) + jnp.einsum('bhtd,bhde->bhte', qc * jnp.exp(b), s)
        s = s * jnp.exp(b_last)[:, :, 0, :, None] + jnp.einsum('bhsd,bhse->bhde', kc * jnp.exp(b_last - b), vc)
        return s, o

    xs = tuple(to_chunks(a.astype(f32), HG_CHUNK) for a in (q, k, v, log_f))
    s, o = lax.scan(step, s0.astype(f32), xs)
    return from_chunks(o), s


def diff_attention(q, k, v, lam):
    B, Tq = q.shape[0], q.shape[1]
    nb = Tq // QBLOCK
    qb = q.reshape(B, nb, QBLOCK, DIFF_H, 2, DIFF_DH).swapaxes(0, 1)
    scale = DIFF_DH ** -0.5

    def block(qi):
        s = jnp.einsum('bqhcd,bkhcd->bhcqk', qi, k, preferred_element_type=jnp.float32) * scale
        p = jax.nn.softmax(s, axis=-1)
        w = p[:, :, 0] - lam * p[:, :, 1]
        return jnp.einsum('bhqk,bkhe->bqhe', w.astype(v.dtype), v)

    o = lax.map(block, qb)
    return o.swapaxes(0, 1).reshape(B, Tq, DIFF_H, DIFF_DV)


def trunk_layer(x, mod_vec, l, norm_g, w_ada, b_ada, w_in, w_out, ret_decay_logit,
                diff_qn_g, diff_kn_g, diff_lambda, hgrn_lb_logit, ctx):
    f32 = jnp.float32
    B, T, _ = x.shape
    dt = x.dtype
    latent = ctx is not None
    flip = lambda a: jnp.flip(a, axis=1)

    m = (jnp.dot(jax.nn.silu(mod_vec), w_ada[l]) + b_ada[l]).reshape(-1, 1, 3 * D_MODEL)
    shift, scale, gate = jnp.split(m, 3, axis=-1)
    h = rmsnorm(x, norm_g[l]) * (1 + scale) + shift
    z = jnp.dot(h, w_in[l])
    offsets = np.cumsum(SPLIT_SIZES)[:-1].tolist()
    rq, rk, rv, rg, dq, dk, dv, dg, hq, hff, hfb, hi, hg = jnp.split(z, offsets, axis=-1)

    if latent:
        k_ctx, v_ctx, s_ret_f0, s_ret_b0, s_hg_f0, s_hg_b0 = ctx
    else:
        s_ret_f0 = jnp.zeros((B, RET_H, RET_DK, RET_DV), f32)
        s_ret_b0 = jnp.zeros((B, RET_H, RET_DK, RET_DV), f32)
        s_hg_f0 = jnp.zeros((B, HG_H, HG_DK, HG_DV), f32)
        s_hg_b0 = jnp.zeros((B, HG_H, HG_DK, HG_DV), f32)

    rq = rq.reshape(B, T, RET_H, RET_DK)
    rk = rk.reshape(B, T, RET_H, RET_DK) * (RET_DK ** -0.5)
    rv = rv.reshape(B, T, RET_H, RET_DV)
    if latent:
        rq, rk = axial_rope(rq), axial_rope(rk)
    log_g = jax.nn.log_sigmoid(ret_decay_logit[l].astype(f32))
    o_f, s_ret_f = retention_scan(rq, rk, rv, log_g[0], s_ret_f0)
    o_b, s_ret_b = retention_scan(flip(rq), flip(rk), flip(rv), log_g[1], s_ret_b0)
    ret_out = head_rms(o_f + flip(o_b), dt).reshape(B, T, RET_W) * jax.nn.silu(rg)

    dq = rmsnorm(dq.reshape(B, T, DIFF_H, 2, DIFF_DH), diff_qn_g[l])
    dk = rmsnorm(dk.reshape(B, T, DIFF_H, 2, DIFF_DH), diff_kn_g[l])
    dv = dv.reshape(B, T, DIFF_H, DIFF_DV)
    if latent:
        dq_r = axial_rope(dq.reshape(B, T, 2 * DIFF_H, DIFF_DH)).reshape(B, T, DIFF_H, 2, DIFF_DH)
        dk_r = axial_rope(dk.reshape(B, T, 2 * DIFF_H, DIFF_DH)).reshape(B, T, DIFF_H, 2, DIFF_DH)
        keys = jnp.concatenate([dk_r, k_ctx.astype(dt)], axis=1)
        vals = jnp.concatenate([dv, v_ctx.astype(dt)], axis=1)
    else:
        dq_r, keys, vals = dq, dk, dv
    lam_init = 0.8 - 0.6 * math.exp(-0.3 * l)
    lp = diff_lambda[l].astype(f32)
    lam = jnp.exp(jnp.sum(lp[0] * lp[1])) - jnp.exp(jnp.sum(lp[2] * lp[3])) + lam_init
    d_o = diff_attention(dq_r, keys, vals, lam)
    diff_out = (head_rms(d_o, f32) * (1.0 - lam_init)).astype(dt).reshape(B, T, DIFF_W) * jax.nn.silu(dg)

    lb_all = jax.nn.softmax(hgrn_lb_logit.astype(f32), axis=0)
    lb_all = jnp.cumsum(lb_all, axis=0) - lb_all[0]
    lb = lb_all[l].reshape(HG_H, HG_DK)

    def forget(fz):
        fz = fz.reshape(B, T, HG_H, HG_DK).astype(f32)
        f = lb + (1.0 - lb) * jax.nn.sigmoid(fz)
        return jnp.log(f), 1.0 - f

    lf_f, kf = forget(hff)
    lf_b, kb = forget(hfb)
    hq = hq.reshape(B, T, HG_H, HG_DK) * (HG_DK ** -0.5)
    hi = hi.reshape(B, T, HG_H, HG_DV)
    o_f, s_hg_f = hgrn_scan(hq, kf, hi, lf_f, s_hg_f0)
    o_b, s_hg_b = hgrn_scan(flip(hq), flip(kb), flip(hi), flip(lf_b), s_hg_b0)
    hg_out = head_rms(o_f + flip(o_b), dt).reshape(B, T, HG_W) * jax.nn.silu(hg)

    mixed = jnp.concatenate([ret_out, diff_out, hg_out], axis=-1).astype(dt)
    x_new = x + gate * jnp.dot(mixed, w_out[l])
    if latent:
        return x_new, None
    return x_new, (dk, dv, s_ret_f, s_ret_b, s_hg_f, s_hg_b)


def setup_inputs(seed: int = 0) -> dict:
    key = jax.random.key(seed)
    ks = jax.random.split(key, 20)
    f32 = jnp.float32
    nrm = lambda k, shape: jax.random.normal(k, shape, f32)
    ret_base = jnp.log(2.0 ** (5.0 + jnp.arange(RET_H, dtype=f32)) - 1.0)
    return {
        'x_prompt': nrm(ks[0], (BATCH, SEQ, D_MODEL)),
        'x_sample': nrm(ks[1], (DEC_BATCH, DEC_SEQ, D_MODEL)),
        'c': nrm(ks[2], (DEC_BATCH, D_MODEL)),
        'c_ctx': nrm(ks[3], (D_MODEL,)),
        'cache_diff_k': nrm(ks[4], (DEC_BATCH, DEPTH, PAST_LEN, DIFF_H, 2, DIFF_DH)),
        'cache_diff_v': nrm(ks[5], (DEC_BATCH, DEPTH, PAST_LEN, DIFF_H, DIFF_DV)),
        'state_ret_fwd': 0.3 * nrm(ks[6], (DEC_BATCH, DEPTH, RET_H, RET_DK, RET_DV)),
        'state_ret_bwd': 0.3 * nrm(ks[7], (DEC_BATCH, DEPTH, RET_H, RET_DK, RET_DV)),
        'state_hgrn_fwd': 0.3 * nrm(ks[8], (DEC_BATCH, DEPTH, HG_H, HG_DK, HG_DV)),
        'state_hgrn_bwd': 0.3 * nrm(ks[9], (DEC_BATCH, DEPTH, HG_H, HG_DK, HG_DV)),
        'norm_g': 1.0 + 0.02 * nrm(ks[10], (DEPTH, D_MODEL)),
        'w_ada': 0.5 * D_MODEL ** -0.5 * nrm(ks[11], (DEPTH, D_MODEL, 3 * D_MODEL)),
        'b_ada': 0.02 * nrm(ks[12], (DEPTH, 3 * D_MODEL)),
        'w_in': D_MODEL ** -0.5 * nrm(ks[13], (DEPTH, D_MODEL, IN_W)),
        'w_out': MIX_W ** -0.5 * nrm(ks[14], (DEPTH, MIX_W, D_MODEL)),
        'ret_decay_logit': ret_base[None, None, :] + 0.1 * nrm(ks[15], (DEPTH, 2, RET_H)),
        'diff_qn_g': 1.0 + 0.02 * nrm(ks[16], (DEPTH, DIFF_DH)),
        'diff_kn_g': 1.0 + 0.02 * nrm(ks[17], (DEPTH, DIFF_DH)),
        'diff_lambda': 0.1 * nrm(ks[18], (DEPTH, 4, DIFF_DH)),
        'hgrn_lb_logit': 1.0 + 0.1 * nrm(ks[19], (DEPTH, HG_W)),
    }


def reference(x_prompt, x_sample, c, c_ctx, cache_diff_k, cache_diff_v, state_ret_fwd,
              state_ret_bwd, state_hgrn_fwd, state_hgrn_bwd, norm_g, w_ada, b_ada, w_in,
              w_out, ret_decay_logit, diff_qn_g, diff_kn_g, diff_lambda, hgrn_lb_logit):
    def run(x, mod_vec, l, ctx):
        return trunk_layer(x, mod_vec, l, norm_g, w_ada, b_ada, w_in, w_out, ret_decay_logit,
                           diff_qn_g, diff_kn_g, diff_lambda, hgrn_lb_logit, ctx)

    y_prompt = x_prompt
    per_layer = []
    for l in range(DEPTH):
        y_prompt, tensors = run(y_prompt, c_ctx, l, None)
        per_layer.append(tensors)
    new_cache_diff_k = jnp.stack([t[0] for t in per_layer], axis=1)
    new_cache_diff_v = jnp.stack([t[1] for t in per_layer], axis=1)
    new_state_ret_fwd = jnp.stack([t[2] for t in per_layer], axis=1)
    new_state_ret_bwd = jnp.stack([t[3] for t in per_layer], axis=1)
    new_state_hgrn_fwd = jnp.stack([t[4] for t in per_layer], axis=1)
    new_state_hgrn_bwd = jnp.stack([t[5] for t in per_layer], axis=1)

    y_sample = x_sample
    for l in range(DEPTH):
        ctx = (cache_diff_k[:, l], cache_diff_v[:, l], state_ret_fwd[:, l], state_ret_bwd[:, l],
               state_hgrn_fwd[:, l], state_hgrn_bwd[:, l])
        y_sample, _ = run(y_sample, c, l, ctx)

    return (y_prompt, y_sample, new_cache_diff_k, new_cache_diff_v, new_state_ret_fwd,
            new_state_ret_bwd, new_state_hgrn_fwd, new_state_hgrn_bwd)
```

```python
import math
import types
from contextlib import ExitStack

import numpy as np
import ml_dtypes

import concourse.bass as bass
import concourse.mybir as mybir
from concourse.bass_utils import run_bass_kernel_spmd

F32 = mybir.dt.float32
BF16 = mybir.dt.bfloat16
ALU = mybir.AluOpType
AF = mybir.ActivationFunctionType
AX = mybir.AxisListType

ENGS = ["pe", "act", "dve", "pool", "sp"]
NDSEM = 8
SAME_ENG_SYNC = True
import os as _os0
REORDER = _os0.environ.get('REORDER', '1') == '1'
PSUM_EXCL = _os0.environ.get('PSUM_EXCL', '1') == '1'
REORDER_ENGS = _os0.environ.get('REORDER_ENGS', 'pe,act,dve,pool,sp').split(',')

D_MODEL = 1024
NT = 16
TOK = 2048
NKT = 20
EPS = 1e-6
UNIT_W = [512, 512, 640, 640, 512, 512, 512, 512]
UNIT_OFF = [0, 512, 1024, 1664, 2304, 2816, 3328, 3840]
UNIT_CHUNK = [0, 1, 6, 7, 2, 3, 4, 5]
BIGNEG = -30000.0


class Buf:
    __slots__ = ("w", "r", "name", "excl")

    def __init__(self, name="", excl=False):
        self.w = None
        self.r = []
        self.name = name
        self.excl = excl


class Op:
    __slots__ = ("eng", "fn", "waits", "marked", "semval", "is_dma", "dsem", "dval", "cost", "lat", "idx", "prio",
                 "pos", "succs", "nrem", "ready", "fin", "is_bar", "per_eng", "per_dsem")


class _Probe:
    def __init__(self):
        self.rec = None

    def __getattr__(self, name):
        def f(*a, **k):
            self.rec = (name, a, k)
            return self
        return f


def _nfree(ap):
    n = 1
    for d in ap.shape[1:]:
        n *= int(d)
    return n


def _estimate(eng, fn, is_dma):
    pr = _Probe()
    try:
        fn(pr)
        name, a, k = pr.rec
    except Exception:
        name, a, k = "?", (), {}
    out = k.get("out", a[0] if a else None)
    try:
        if is_dma:
            nbytes = _nfree(out) * int(out.shape[0]) * mybir.dt.size(out.dtype)
            return 120.0, 2200.0 + nbytes / 120.0
        if eng == "pe":
            if name == "transpose":
                return 80.0, 80.0
            rhs = k.get("rhs", a[2] if len(a) > 2 else None)
            lhsT = k.get("lhsT", a[1] if len(a) > 1 else None)
            n = _nfree(rhs)
            c = (max(64, n) / 2.4 + 25.0) * 1.25
            if lhsT.dtype == F32:
                c *= 4.0
            return c, c
        n = _nfree(out)
        if eng == "act":
            c = 190.0 + n / 1.2 + (90.0 if k.get("accum_out") is not None else 0.0)
        elif eng == "dve":
            c = 130.0 + n / 0.7
        else:
            c = 700.0 + n / 0.4
        return c, c
    except Exception:
        return 300.0, 300.0


def _freeze(fn):
    if fn.__closure__ is None:
        return fn
    cells = []
    for c in fn.__closure__:
        try:
            cells.append(types.CellType(c.cell_contents))
        except ValueError:
            cells.append(c)
    return types.FunctionType(fn.__code__, fn.__globals__, fn.__name__, fn.__defaults__, tuple(cells))


LAT_X = 100.0
LAT_S = 120.0


class Prog:
    def __init__(self, nc):
        self.nc = nc
        self.all = []
        self.cur_bar = None
        self.since = []
        self.load = {e: 0.0 for e in ENGS}

    def _new(self, eng):
        op = Op()
        op.eng = eng
        op.fn = None
        op.marked = False
        op.semval = None
        op.is_dma = False
        op.dsem = None
        op.dval = None
        op.cost = 0.0
        op.lat = 0.0
        op.is_bar = False
        op.idx = len(self.all)
        op.waits = []
        self.all.append(op)
        return op

    def barrier(self):
        b = self._new("virt")
        b.is_bar = True
        b.waits = list(self.since)
        self.since = []
        self.cur_bar = b
        self.load = {e: 0.0 for e in ENGS}

    def emit(self, eng, fn, reads=(), writes=(), extra=(), is_dma=False):
        op = self._new(eng)
        op.fn = _freeze(fn)
        op.is_dma = is_dma
        op.cost, op.lat = _estimate(eng, op.fn, is_dma)
        self.load[eng] += op.cost
        waits = set()
        if PSUM_EXCL:
            for b in reads:
                if b.excl:
                    for r in b.r:
                        if r.eng != eng:
                            waits.add(r)
        for b in reads:
            if b.w is not None:
                waits.add(b.w)
        for b in writes:
            if b.w is not None:
                waits.add(b.w)
            for r in b.r:
                waits.add(r)
        for w in extra:
            if w is not None:
                waits.add(w)
        if self.cur_bar is not None:
            waits.add(self.cur_bar)
        waits.discard(op)
        op.waits = list(waits)
        for b in reads:
            b.r.append(op)
        for b in writes:
            b.w = op
            b.r = []
        self.since.append(op)
        return op

    def pe(self, fn, reads=(), writes=(), extra=()):
        return self.emit("pe", fn, reads, writes, extra)

    def act(self, fn, reads=(), writes=(), extra=()):
        return self.emit("act", fn, reads, writes, extra)

    def dve(self, fn, reads=(), writes=(), extra=()):
        return self.emit("dve", fn, reads, writes, extra)

    def pool(self, fn, reads=(), writes=(), extra=()):
        return self.emit("pool", fn, reads, writes, extra)

    def on(self, eng, fn, reads=(), writes=(), extra=()):
        if eng == "any":
            return self.any(fn, reads, writes, extra)
        return self.emit(eng, fn, reads, writes, extra)

    def any(self, fn, reads=(), writes=(), extra=()):
        f = _freeze(fn)
        best = None
        for e in ("dve", "pool"):
            c, _ = _estimate(e, f, False)
            tot = self.load[e] + c
            if best is None or tot < best[0]:
                best = (tot, e)
        return self.emit(best[1], fn, reads, writes, extra)

    def dma(self, out, in_, reads=(), writes=(), extra=()):
        return self.emit("sp", lambda e: e.dma_start(out=out, in_=in_), reads, writes, extra, is_dma=True)

    def schedule(self, reorder=True):
        import heapq
        ops = self.all
        for op in ops:
            op.succs = []
        for op in ops:
            for w in op.waits:
                w.succs.append(op)
        for op in reversed(ops):
            m = 0.0
            for s_ in op.succs:
                l_ = s_.prio + (0.0 if op.is_bar else (LAT_S if s_.eng == op.eng else LAT_X))
                if l_ > m:
                    m = l_
            op.prio = m + op.lat
        order = {e: [] for e in ENGS}
        if not reorder:
            for op in ops:
                if not op.is_bar:
                    order[op.eng].append(op)
            return order
        for op in ops:
            op.nrem = len(op.waits)
            op.ready = 0.0
            op.fin = None
        fixed = [e for e in ENGS if e not in REORDER_ENGS]
        lastop = {}
        for op in ops:
            if op.is_bar or op.eng not in fixed:
                continue
            p_ = lastop.get(op.eng)
            if p_ is not None and p_ not in op.waits:
                p_.succs.append(op)
                op.nrem += 1
            lastop[op.eng] = op
        future = {e: [] for e in ENGS}
        now = {e: [] for e in ENGS}
        free = {e: 0.0 for e in ENGS}

        def release(op):
            for s_ in op.succs:
                if op.is_bar:
                    t = op.fin
                elif s_.is_bar:
                    t = op.fin
                elif s_.eng == op.eng:
                    t = op.fin + (0.0 if op.eng == "pe" else LAT_S)
                else:
                    t = op.fin + LAT_X
                if t > s_.ready:
                    s_.ready = t
                s_.nrem -= 1
                if s_.nrem == 0:
                    if s_.is_bar:
                        s_.fin = s_.ready
                        release(s_)
                    else:
                        heapq.heappush(future[s_.eng], (s_.ready, s_.idx, s_))

        import sys
        sys.setrecursionlimit(100000)
        roots = [op for op in ops if op.nrem == 0]
        for op in roots:
            if op.is_bar:
                op.fin = 0.0
                release(op)
            else:
                heapq.heappush(future[op.eng], (0.0, op.idx, op))
        nleft = sum(1 for op in ops if not op.is_bar)
        while nleft > 0:
            best = None
            for e in ENGS:
                f = future[e]
                nw = now[e]
                while f and f[0][0] <= free[e]:
                    r_, i_, o_ = heapq.heappop(f)
                    heapq.heappush(nw, (-o_.prio, o_.idx, o_))
                if nw:
                    st = free[e]
                elif f:
                    st = f[0][0]
                else:
                    continue
                if best is None or st < best[0]:
                    best = (st, e)
            st, e = best
            if now[e]:
                _, _, op = heapq.heappop(now[e])
            else:
                _, _, op = heapq.heappop(future[e])
            if op.is_dma:
                free[e] = st + op.cost
                op.fin = st + op.lat
            else:
                free[e] = st + op.cost
                op.fin = st + op.cost
            order[e].append(op)
            nleft -= 1
            release(op)
        self.est_ns = max(free.values())
        return order

    def build(self, sems, dsems, reorder=True):
        order = self.schedule(reorder)
        if reorder:
            print('[sched] est_us=%.1f' % (self.est_ns / 1e3), {e: len(order[e]) for e in ENGS})
        for e in ENGS:
            for i, op in enumerate(order[e]):
                op.pos = i

        def skip_same(w_eng, eng):
            return w_eng == eng and (eng == "pe" or not SAME_ENG_SYNC)

        dcnt = [0] * NDSEM
        prev_on_sem = [None] * NDSEM
        dma_prev = {}
        nd = 0
        for op in order["sp"]:
            k = nd % NDSEM
            nd += 1
            op.dsem = k
            dcnt[k] += 16
            op.dval = dcnt[k]
            dma_prev[id(op)] = prev_on_sem[k]
            prev_on_sem[k] = op
        final_dvals = list(dcnt)
        for b in self.all:
            if b.is_bar:
                pe_ = {}
                pd_ = {}
                for w in b.waits:
                    if w.is_bar:
                        continue
                    if w.is_dma:
                        if pd_.get(w.dsem, 0) < w.dval:
                            pd_[w.dsem] = w.dval
                    else:
                        c = pe_.get(w.eng)
                        if c is None or c.pos < w.pos:
                            pe_[w.eng] = w
                b.per_eng = pe_
                b.per_dsem = pd_
        for op in self.all:
            if op.is_bar:
                for w in op.per_eng.values():
                    w.marked = True
                continue
            for w in op.waits:
                if w.is_bar or w.is_dma:
                    continue
                if not skip_same(w.eng, op.eng):
                    w.marked = True
        for e in ENGS:
            cnt = 0
            for op in order[e]:
                if not op.is_dma and op.marked:
                    cnt += 1
                    op.semval = cnt

        def run_engine(ename, eng):
            waited = {}

            def need(semkey, sem, val):
                if waited.get(semkey, 0) >= val:
                    return
                eng.wait_ge(sem, val)
                waited[semkey] = val

            for op in order[ename]:
                for w in op.waits:
                    if w.is_bar:
                        for we, wo in w.per_eng.items():
                            if not (we == ename and ename == "pe"):
                                need(("e", we), sems[we], wo.semval)
                        for k, v in w.per_dsem.items():
                            need(("d", k), dsems[k], v)
                    elif w.is_dma:
                        need(("d", w.dsem), dsems[w.dsem], w.dval)
                    elif not skip_same(w.eng, ename):
                        need(("e", w.eng), sems[w.eng], w.semval)
                if op.is_dma:
                    p = dma_prev[id(op)]
                    if p is not None:
                        need(("d", p.dsem), dsems[p.dsem], p.dval)
                ins = op.fn(eng)
                if op.is_dma:
                    ins.then_inc(dsems[op.dsem], 16)
                elif op.marked:
                    ins.then_inc(sems[ename], 1)
            if ename == "sp":
                for k in range(NDSEM):
                    if final_dvals[k] > 0:
                        need(("d", k), dsems[k], final_dvals[k])

        return run_engine


class Ring:
    def __init__(self, tiles):
        self.tiles = tiles
        self.bufs = [Buf() for _ in tiles]
        self.i = 0

    def next(self):
        k = self.i % len(self.tiles)
        self.i += 1
        return self.tiles[k], self.bufs[k]


def build_program(L=2, dbg=False, units_enabled=None):
    nc = bass.Bass("TRN2", target_bir_lowering=False)

    def din(name, shape, dt=F32):
        return nc.dram_tensor(name, list(shape), dt, kind="ExternalInput").ap()

    def dout(name, shape, dt=F32):
        return nc.dram_tensor(name, list(shape), dt, kind="ExternalOutput").ap()

    x_in = din("x", [TOK, D_MODEL])
    modv = din("modv", [128, 8])
    ck_in = din("ck", [2, 512, 512])
    cv_in = din("cv", [2, 512, 512])
    srf_in = din("srf", [2, 4, 64, 64])
    srb_in = din("srb", [2, 4, 64, 64])
    shf_in = din("shf", [2, 4, 64, 64])
    shb_in = din("shb", [2, 4, 64, 64])
    normg_in = din("normg", [2, 128, 8])
    wada_in = din("wada", [2, 128, 8, 3072])
    bada_in = din("bada", [2, 3072])
    win_in = din("win", [2, 128, 8, 4352])
    wout_in = din("wout", [2, 128, 8, 1024])
    rdl_in = din("rdl", [2, 8])
    qng_in = din("qng", [2, 64])
    kng_in = din("kng", [2, 64])
    dlam_in = din("dlam", [2, 256])
    hlb_in = din("hlb", [512])
    ropec_in = din("ropec", [128, 16, 64])
    ropes_in = din("ropes", [128, 16, 64])
    qmask_in = din("qmask", [8, 2048], BF16)
    kmask_in = din("kmask", [8, 2560], BF16)
    keep_in = din("keep", [128, 32])
    identb_in = din("identb", [128, 128], BF16)
    identf_in = din("identf", [128, 128])
    cst_in = din("cst", [128, 11, 128])
    ind_in = din("ind", [128, 4])

    y_out = dout("y", [TOK, D_MODEL])
    nk_out = dout("nk", [2, TOK, 512])
    nv_out = dout("nv", [2, TOK, 512])
    nsrf_out = dout("nsrf", [2, 8, 4, 64, 64])
    nsrb_out = dout("nsrb", [2, 8, 4, 64, 64])
    nshf_out = dout("nshf", [2, 8, 4, 64, 64])
    nshb_out = dout("nshb", [2, 8, 4, 64, 64])
    xs_scr = nc.dram_tensor("xs_scr", [TOK, D_MODEL], F32, kind="Internal").ap()
    dbg_out = dout("dbgmix", [2, 128, 8, TOK], BF16) if dbg else None

    es = ExitStack()
    with es:
        def sb(name, shape, dt):
            return es.enter_context(nc.sbuf_tensor("sb_" + name, list(shape), dt))

        hT = sb("hT", [128, 8, TOK], BF16)
        mixT = sb("mixT", [128, 8, TOK], BF16)
        wstage = sb("wstage", [128, 8 * 512], F32)
        wbf = sb("wbf", [128, 8 * 640], BF16)
        arena = sb("arena", [128, 61440], mybir.dt.uint8)
        cst = sb("cst", [128, 11, 128], F32)
        ind = sb("ind", [128, 4], F32)
        ropec = sb("ropec", [128, 16, 64], F32)
        ropes = sb("ropes", [128, 16, 64], F32)
        identb = sb("identb", [128, 128], BF16)
        identf = sb("identf", [128, 128], F32)
        keep = sb("keep", [128, 32], F32)
        gate_b = sb("gate_b", [128, 1024], F32)
        modt = sb("modt", [128, 8], F32)
        smod = sb("smod", [128, 8], F32)
        normg = sb("normg", [128, 8], F32)
        modT = sb("modT", [128, 2, 8], F32)
        modA = sb("modA", [128, 8], F32)
        small = sb("small", [128, 64], F32)
        rdl = sb("rdl", [128, 8], F32)
        lg = sb("lg", [128, 8], F32)
        g4 = sb("g4", [128, 256], F32)
        lball = sb("lball", [128, 2, 256], F32)
        lb2 = sb("lb2", [128, 256], F32)
        omlb2 = sb("omlb2", [128, 256], F32)
        cneg = sb("cneg", [128, 8], F32)
        zerosb = sb("zerosb", [128, 512], BF16)
        lgrow = sb("lgrow", [128, 4], F32)
        lgrow1 = sb("lgrow1", [128, 4], F32)
        dtm = sb("dtm", [128, 2, 128], F32)
        qd = sb("qd", [128, 2, 128], F32)
        kd = sb("kd", [128, 2, 128], F32)
        e12 = sb("e12", [128, 2, 128], F32)
        stS = sb("stS", [128, 2, 128], F32)

        n_f32_512 = 3
        r512 = Ring([sb("r512_%d" % i, [128, 512], F32) for i in range(n_f32_512)])
        r256 = Ring([sb("r256_%d" % i, [128, 256], F32) for i in range(8)])
        r128 = Ring([sb("r128_%d" % i, [128, 128], F32) for i in range(8)])
        rb256 = Ring([sb("rb256_%d" % i, [128, 256], BF16) for i in range(3)])
        rb128 = Ring([sb("rb128_%d" % i, [128, 128], BF16) for i in range(8)])
        rpt = Ring([sb("rpt_%d" % i, [128, 512], BF16) for i in range(3)])
        rs = Ring([sb("rs_%d" % i, [128, 8], F32) for i in range(16)])

        banks = [es.enter_context(nc.psum_tensor("bank%d" % i, [128, 512], F32)) for i in range(8)]
        banksb = [b.bitcast(BF16) for b in banks]
        bankB = [Buf("bank%d" % i, excl=True) for i in range(8)]

        sems = {e: es.enter_context(nc.semaphore("s_" + e)) for e in ENGS}
        dsems = [es.enter_context(nc.semaphore("d%d" % k)) for k in range(NDSEM)]

        P = Prog(nc)

        B_hT = [Buf() for _ in range(NT)]
        B_mixT = [[Buf() for _ in range(NT)] for _ in range(8)]
        B_wstage = Buf()
        B_wbf = Buf()
        B_cst = Buf()
        B_misc = Buf()
        B_gate = Buf()
        B_modAB = Buf()
        B_pair = Buf()
        B_stS = [Buf(), Buf()]

        def arena_view(off_bytes, shape, dt):
            n = 1
            for s in shape[1:]:
                n *= s
            esz = 2 if dt == BF16 else 4
            a = arena[:, off_bytes:off_bytes + n * esz].bitcast(dt)
            if len(shape) == 2:
                return a
            if len(shape) == 3:
                return a.rearrange("p (a b) -> p a b", a=shape[1], b=shape[2])
            if len(shape) == 4:
                return a.rearrange("p (a b c) -> p a b c", a=shape[1], b=shape[2], c=shape[3])
            raise ValueError

        xring = Ring([arena_view(36864 + i * 4096, [128, 1024], F32) for i in range(3)])
        xhring = Ring([arena_view(49152 + i * 2048, [128, 1024], BF16) for i in range(2)])
        B_xs = [Buf() for _ in range(NT)]

        M1, L1, M2, L2, IOTA1, IOTA2, COLA, COLB, TRIF, TRIB, BM = [cst[:, i, :] for i in range(11)]

        P.dma(cst[:], cst_in, writes=[B_cst])
        P.dma(ind[:], ind_in, writes=[B_cst])
        P.dma(ropec[:], ropec_in, writes=[B_cst])
        P.dma(ropes[:], ropes_in, writes=[B_cst])
        P.dma(identb[:], identb_in, writes=[B_cst])
        P.dma(identf[:], identf_in, writes=[B_cst])
        P.dma(keep[:], keep_in, writes=[B_cst])
        P.dma(modt[:], modv, writes=[B_cst])
        hlb, hlbb = r512.next()
        P.dma(hlb[:], hlb_in.partition_broadcast(128), writes=[hlbb])
        P.pool(lambda e: e.memset(cneg[:], -0.5), writes=[B_cst])
        P.pool(lambda e: e.memset(zerosb[:], 0.0), writes=[B_cst])
        t_, tb_ = rs.next()
        P.act(lambda e, t_=t_: e.activation(out=t_[:, 0:8], in_=modt[:], func=AF.Tanh, scale=0.5), reads=[B_cst], writes=[tb_])
        P.dve(lambda e, t_=t_: e.scalar_tensor_tensor(out=smod[:], in0=t_[:, 0:8], scalar=1.0, in1=modt[:], op0=ALU.add, op1=ALU.mult),
              reads=[tb_, B_cst], writes=[B_cst])
        P.dve(lambda e: e.tensor_scalar(out=smod[:], in0=smod[:], scalar1=0.5, scalar2=None, op0=ALU.mult), reads=[B_cst], writes=[B_cst])
        P.act(lambda e: e.activation(out=hlb[:], in_=hlb[:], func=AF.Exp), reads=[hlbb], writes=[hlbb])
        den_, denb_ = r256.next()
        P.dve(lambda e: e.tensor_tensor(out=den_[:], in0=hlb[:, 0:256], in1=hlb[:, 256:512], op=ALU.add), reads=[hlbb], writes=[denb_])
        P.dve(lambda e: e.reciprocal(out=den_[:], in_=den_[:]), reads=[denb_], writes=[denb_])
        P.dve(lambda e: e.tensor_tensor(out=hlb[:, 0:256], in0=hlb[:, 0:256], in1=den_[:], op=ALU.mult), reads=[hlbb, denb_], writes=[hlbb])
        P.dve(lambda e: e.tensor_tensor(out=hlb[:, 256:512], in0=hlb[:, 256:512], in1=den_[:], op=ALU.mult), reads=[hlbb, denb_], writes=[hlbb])
        P.dve(lambda e: e.tensor_tensor(out=lball[:, 0, :], in0=hlb[:, 0:256], in1=hlb[:, 0:256], op=ALU.subtract), reads=[hlbb], writes=[B_cst])
        P.dve(lambda e: e.tensor_tensor(out=lball[:, 1, :], in0=hlb[:, 0:256], in1=hlb[:, 256:512], op=ALU.add), reads=[hlbb], writes=[B_cst])
        P.dve(lambda e: e.tensor_tensor(out=lball[:, 1, :], in0=lball[:, 1, :], in1=hlb[:, 0:256], op=ALU.subtract), reads=[hlbb, B_cst], writes=[B_cst])

        def rstd_from_ss(ss_ap, ssb, n, mult, add):
            t1, b1 = rs.next()
            P.dve(lambda e: e.tensor_scalar(out=t1[:, 0:n], in0=ss_ap, scalar1=mult, scalar2=add, op0=ALU.mult, op1=ALU.add),
                  reads=[ssb], writes=[b1])
            t2, b2 = rs.next()
            P.pool(lambda e: e.tensor_tensor(out=t2[:, 0:n], in0=t1[:, 0:n], in1=cneg[:, 0:n], op=ALU.pow), reads=[b1, B_cst], writes=[b2])
            return t2, b2

        def rope(src, srcbufs, dst, dstbufs, T, G, eng_a, eng_b):
            W = G * 64
            t1, b1 = r256.next()
            t2, b2 = r256.next()
            cv_ = ropec[:, T, :]
            sv_ = ropes[:, T, :].rearrange("p (h j i) -> p h j i", h=2, j=2, i=16)
            src3 = src.rearrange("p (g d) -> p g d", g=G, d=64)
            src5 = src.rearrange("p (g h j i) -> p g h j i", g=G, h=2, j=2, i=16)
            t13 = t1[:, 0:W].rearrange("p (g d) -> p g d", g=G, d=64)
            t25 = t2[:, 0:W].rearrange("p (g h j i) -> p g h j i", g=G, h=2, j=2, i=16)
            P.on(eng_a, lambda e: e.tensor_tensor(out=t13, in0=src3, in1=cv_.unsqueeze(1).to_broadcast([128, G, 64]), op=ALU.mult),
                 reads=list(srcbufs) + [B_cst], writes=[b1])
            P.on(eng_b, lambda e: e.tensor_tensor(out=t25[:, :, :, 0, :], in0=src5[:, :, :, 1, :],
                                                  in1=sv_[:, :, 0, :].unsqueeze(1).to_broadcast([128, G, 2, 16]), op=ALU.mult),
                 reads=list(srcbufs) + [B_cst], writes=[b2])
            P.on(eng_b, lambda e: e.tensor_tensor(out=t25[:, :, :, 1, :], in0=src5[:, :, :, 0, :],
                                                  in1=sv_[:, :, 1, :].unsqueeze(1).to_broadcast([128, G, 2, 16]), op=ALU.mult),
                 reads=list(srcbufs) + [B_cst], writes=[b2])
            P.on(eng_a, lambda e: e.tensor_tensor(out=dst, in0=t1[:, 0:W], in1=t2[:, 0:W], op=ALU.add), reads=[b1, b2], writes=list(dstbufs))

        def tslice(T):
            return slice(T * 128, (T + 1) * 128)

        def setup_layer(l):
            stg = [wstage[:, 0:4096].rearrange("p (k n) -> p k n", k=8, n=512), arena_view(16384, [128, 8, 512], F32)]
            stgB = [B_wstage, Buf()]
            smb = arena_view(32768, [128, 8, 128], F32)
            B_smb = Buf()
            P.dve(lambda e: e.tensor_copy(out=smb, in_=smod[:].unsqueeze(2).to_broadcast([128, 8, 128])), reads=[B_cst], writes=[B_smb])
            P.dma(normg[:], normg_in[l], writes=[B_misc])
            P.dma(rdl[:], rdl_in[l].partition_broadcast(128), writes=[B_misc])
            dlam, dlamb = r256.next()
            P.dma(dlam[:], dlam_in[l].partition_broadcast(128), writes=[dlamb])
            P.dma(g4[:, 0:64], qng_in[l].partition_broadcast(128), writes=[B_misc])
            P.dma(g4[:, 64:128], qng_in[l].partition_broadcast(128), writes=[B_misc])
            P.dma(g4[:, 128:192], kng_in[l].partition_broadcast(128), writes=[B_misc])
            P.dma(g4[:, 192:256], kng_in[l].partition_broadcast(128), writes=[B_misc])
            for cb in range(6):
                st_, stb_ = stg[cb % 2], stgB[cb % 2]
                P.dma(st_, wada_in[l][:, :, cb * 512:(cb + 1) * 512], writes=[stb_])
                bt, btb = r512.next()
                P.dma(bt[:], bada_in[l][cb * 512:(cb + 1) * 512].partition_broadcast(128), writes=[btb])
                bk = cb % 4
                for kc in range(8):
                    P.pe(lambda e, kc=kc, st_=st_, bk=bk: e.matmul(banks[bk][:, 0:512], lhsT=smb[:, kc, :], rhs=st_[:, kc, :],
                                                                     start=(kc == 0), stop=(kc == 7)),
                         reads=[B_smb, stb_], writes=[bankB[bk]])
                if cb >= 4:
                    P.dve(lambda e, bk=bk, bt=bt, cb=cb: e.tensor_tensor(out=gate_b[:, (cb - 4) * 512:(cb - 3) * 512], in0=banks[bk][:, 0:512],
                                                                          in1=bt[:], op=ALU.add),
                          reads=[bankB[bk], btb], writes=[B_gate])
                else:
                    P.dve(lambda e, bk=bk, bt=bt: e.tensor_tensor(out=bt[:], in0=banks[bk][:, 0:512], in1=bt[:], op=ALU.add),
                          reads=[bankB[bk], btb], writes=[btb])
                    which = cb // 2
                    tb = 4 + (cb % 2)
                    for jj in range(4):
                        kc = (cb % 2) * 4 + jj
                        P.pe(lambda e, jj=jj, bt=bt, tb=tb: e.transpose(banks[tb][:, jj * 128:(jj + 1) * 128], bt[:, jj * 128:(jj + 1) * 128], identf[:]),
                             reads=[btb, B_cst], writes=[bankB[tb]])
                        P.act(lambda e, jj=jj, tb=tb, which=which, kc=kc: e.copy(out=modT[:, which, kc:kc + 1], in_=banks[tb][:, jj * 128:jj * 128 + 1]),
                              reads=[bankB[tb]], writes=[B_modAB])
            P.dve(lambda e: e.scalar_tensor_tensor(out=modA[:], in0=modT[:, 1, :], scalar=1.0, in1=normg[:], op0=ALU.add, op1=ALU.mult),
                  reads=[B_modAB, B_misc], writes=[B_modAB])
            pr, prb = r256.next()
            P.dve(lambda e: e.tensor_tensor(out=pr[:, 0:64], in0=dlam[:, 0:64], in1=dlam[:, 64:128], op=ALU.mult), reads=[dlamb], writes=[prb])
            P.dve(lambda e: e.tensor_tensor(out=pr[:, 64:128], in0=dlam[:, 128:192], in1=dlam[:, 192:256], op=ALU.mult), reads=[dlamb], writes=[prb])
            s12, s12b = rs.next()
            P.dve(lambda e: e.tensor_reduce(out=s12[:, 0:2], in_=pr[:, 0:128].rearrange("p (a d) -> p a d", a=2, d=64), axis=AX.X, op=ALU.add),
                  reads=[prb], writes=[s12b])
            P.act(lambda e: e.activation(out=s12[:, 0:2], in_=s12[:, 0:2], func=AF.Exp), reads=[s12b], writes=[s12b])
            lam_init = 0.8 - 0.6 * math.exp(-0.3 * l)
            P.dve(lambda e: e.tensor_tensor(out=small[:, 0:1], in0=s12[:, 1:2], in1=s12[:, 0:1], op=ALU.subtract), reads=[s12b], writes=[B_misc])
            P.dve(lambda e: e.tensor_scalar(out=small[:, 0:1], in0=small[:, 0:1], scalar1=-lam_init, scalar2=None, op0=ALU.add),
                  reads=[B_misc], writes=[B_misc])
            P.act(lambda e: e.activation(out=lg[:], in_=rdl[:], func=AF.Exp, scale=-1.0), reads=[B_misc], writes=[B_misc])
            P.act(lambda e: e.activation(out=lg[:], in_=lg[:], func=AF.Ln, bias=1.0), reads=[B_misc], writes=[B_misc])
            P.dve(lambda e: e.tensor_scalar(out=lg[:], in0=lg[:], scalar1=-1.0, scalar2=None, op0=ALU.mult), reads=[B_misc], writes=[B_misc])

        def norm_phase(l):
            src = x_in if l == 0 else xs_scr
            for T in range(NT):
                xt, xb_ = xring.next()
                P.dma(xt[:], src[T * 128:(T + 1) * 128, :], reads=([B_xs[T]] if l > 0 else []), writes=[xb_])
                xh, xhb = xhring.next()
                ss, ssb = rs.next()
                P.act(lambda e, xt=xt, xh=xh, ss=ss: e.activation(out=xh[:], in_=xt[:], func=AF.Square, accum_out=ss[:, 0:1]),
                      reads=[xb_], writes=[xhb, ssb])
                rstd, rb_ = rstd_from_ss(ss[:, 0:1], ssb, 1, 1.0 / D_MODEL, EPS)
                P.dve(lambda e, xt=xt, xh=xh, rstd=rstd: e.tensor_scalar(out=xh[:], in0=xt[:], scalar1=rstd[:, 0:1], scalar2=None, op0=ALU.mult),
                      reads=[xb_, rb_], writes=[xhb])
                bk = 6 + (T % 2)
                for kc in range(8):
                    P.pe(lambda e, kc=kc, xh=xh, bk=bk: e.transpose(banksb[bk][:, kc * 128:(kc + 1) * 128], xh[:, kc * 128:(kc + 1) * 128], identb[:]),
                         reads=[xhb, B_cst], writes=[bankB[bk]])
                for kc in range(8):
                    if kc % 2 == 0:
                        P.dve(lambda e, kc=kc, bk=bk, T=T: e.tensor_scalar(out=hT[:, kc, tslice(T)], in0=banksb[bk][:, kc * 128:(kc + 1) * 128],
                                                                            scalar1=modA[:, kc:kc + 1], scalar2=modT[:, 0, kc:kc + 1],
                                                                            op0=ALU.mult, op1=ALU.add),
                              reads=[bankB[bk], B_modAB], writes=[B_hT[T]])
                    else:
                        P.act(lambda e, kc=kc, bk=bk, T=T: e.activation(out=hT[:, kc, tslice(T)], in_=banksb[bk][:, kc * 128:(kc + 1) * 128],
                                                                         func=AF.Identity, scale=modA[:, kc:kc + 1], bias=modT[:, 0, kc:kc + 1]),
                              reads=[bankB[bk], B_modAB], writes=[B_hT[T]])

        def load_unit_weights(l, u):
            W = UNIT_W[u]
            wb = wbf[:, 0:8 * W].rearrange("p (k n) -> p k n", k=8, n=W)
            engs = ["dve", "pool", "act", "dve", "pool", "act", "dve", "pool"]
            for (a, b) in [(0, 512)] + ([(512, W)] if W > 512 else []):
                wd = b - a
                ws = wstage[:, 0:8 * wd].rearrange("p (k n) -> p k n", k=8, n=wd)
                P.dma(ws, win_in[l][:, :, UNIT_OFF[u] + a:UNIT_OFF[u] + b], writes=[B_wstage])
                for kc in range(8):
                    if engs[kc] == "act":
                        P.act(lambda e, kc=kc, ws=ws, a=a, b=b: e.copy(out=wb[:, kc, a:b], in_=ws[:, kc, :]), reads=[B_wstage], writes=[B_wbf])
                    else:
                        P.on(engs[kc], lambda e, kc=kc, ws=ws, a=a, b=b: e.tensor_copy(out=wb[:, kc, a:b], in_=ws[:, kc, :]), reads=[B_wstage], writes=[B_wbf])
            return wb

        def project(wb, T, bk, c0, c1):
            for kc in range(8):
                P.pe(lambda e, kc=kc: e.matmul(banks[bk][:, 0:c1 - c0], lhsT=hT[:, kc, tslice(T)], rhs=wb[:, kc, c0:c1],
                                               start=(kc == 0), stop=(kc == 7)),
                     reads=[B_hT[T], B_wbf], writes=[bankB[bk]])

        def mixed_out(mt, mtb, chunk, T, tbank, eng="act"):
            P.pe(lambda e: e.transpose(banksb[tbank][:, 0:128], mt, identb[:]), reads=[mtb, B_cst], writes=[bankB[tbank]])
            if eng == "act":
                P.act(lambda e: e.copy(out=mixT[:, chunk, tslice(T)], in_=banksb[tbank][:, 0:128]), reads=[bankB[tbank]], writes=[B_mixT[chunk][T]])
            else:
                P.dve(lambda e: e.tensor_copy(out=mixT[:, chunk, tslice(T)], in_=banksb[tbank][:, 0:128]), reads=[bankB[tbank]], writes=[B_mixT[chunk][T]])

        def diff_unit(l, h, DB):
            u = 4 + h
            chunk = UNIT_CHUNK[u]
            if h == 0:
                P.barrier()
            par = h % 2
            base = par * 27728
            QT = arena_view(base + 0, [128, 2, TOK], BF16)
            KT = arena_view(base + 8192, [128, 2, 2560], BF16)
            V = arena_view(base + 18432, [128, NKT, 130], BF16)
            sg = arena_view(base + 23632, [128, NT, 128], BF16)
            ckst = arena_view(55456, [128, 4, 128], F32)
            cvst = arena_view(57504, [128, 4, 128], F32)
            ckb = arena_view(59552, [128, 4, 128], BF16)
            B_QT, B_KT, B_V, B_sg, B_qm = DB["set"][par]
            B_ck = DB["ck"]
            wb = load_unit_weights(l, u)
            import os as _os
            DD0 = _os.environ.get('DIFFDBG', '')
            if 'nomask' not in DD0:
                for c in range(2):
                    P.dma(QT[64:72, c, :], qmask_in, writes=[B_qm])
                    P.dma(KT[64:72, c, :], kmask_in, writes=[B_qm])
            if 'noctx' not in DD0:
                P.dma(ckst, ck_in[l].rearrange("(t p) n -> p t n", p=128)[:, :, h * 128:(h + 1) * 128], writes=[B_ck])
                P.dma(cvst, cv_in[l].rearrange("(t p) n -> p t n", p=128)[:, :, h * 128:(h + 1) * 128], writes=[B_ck])
                P.pool(lambda e: e.memset(V[:, :, 128:130], 1.0), writes=B_V)
                P.dve(lambda e: e.tensor_copy(out=ckb, in_=ckst), reads=[B_ck], writes=[B_ck])
                for pt in range(4):
                    bk = 5
                    for c in range(2):
                        P.pe(lambda e, pt=pt, c=c, bk=bk: e.transpose(banksb[bk][0:64, c * 128:(c + 1) * 128], ckb[:, pt, c * 64:(c + 1) * 64], identb[:]),
                             reads=[B_ck, B_cst], writes=[bankB[bk]])
                    P.act(lambda e, pt=pt, bk=bk: e.copy(out=KT[0:64, :, 2048 + pt * 128:2048 + (pt + 1) * 128],
                                                         in_=banksb[bk][0:64, 0:256].rearrange("p (c t) -> p c t", c=2, t=128)),
                          reads=[bankB[bk]], writes=[B_KT[16 + pt]])
                    P.any(lambda e, pt=pt: e.tensor_copy(out=V[:, 16 + pt, 0:128], in_=cvst[:, pt, :]), reads=[B_ck], writes=[B_V[16 + pt]])

            if 'noA' in DD0:
                return
            for T in range(NT):
                zb = 6 + (T % 2)
                project(wb, T, zb, 0, 512)
                z = banks[zb]
                sq, sqb = r256.next()
                P.act(lambda e, z=z, sq=sq: e.activation(out=sq[:], in_=z[:, 0:256], func=AF.Square), reads=[bankB[zb]], writes=[sqb])
                ss, ssb = rs.next()
                P.dve(lambda e, sq=sq, ss=ss: e.tensor_reduce(out=ss[:, 0:4], in_=sq[:].rearrange("p (g d) -> p g d", g=4, d=64), axis=AX.X, op=ALU.add),
                      reads=[sqb], writes=[ssb])
                rstd, rb_ = rstd_from_ss(ss[:, 0:4], ssb, 4, 1.0 / 64, EPS)
                nq, nqb = r256.next()
                P.dve(lambda e, z=z, nq=nq, rstd=rstd: e.tensor_tensor(out=nq[:].rearrange("p (g d) -> p g d", g=4, d=64),
                                                                      in0=z[:, 0:256].rearrange("p (g d) -> p g d", g=4, d=64),
                                                                      in1=rstd[:, 0:4].unsqueeze(2).to_broadcast([128, 4, 64]), op=ALU.mult),
                      reads=[bankB[zb], rb_], writes=[nqb])
                P.any(lambda e, nq=nq: e.tensor_tensor(out=nq[:], in0=nq[:], in1=g4[:], op=ALU.mult), reads=[nqb, B_misc], writes=[nqb])
                if 'nonk' not in DD0:
                    P.dma(nk_out[l][T * 128:(T + 1) * 128, h * 128:(h + 1) * 128], nq[:, 128:256], reads=[nqb])
                rt, rtb = rb256.next()
                import os as _os
                rope(nq[:], [nqb], rt[:], [rtb], T, 4, "any", "any")
                if 'noT' not in DD0:
                    tb = 5
                    for g in range(4):
                        P.pe(lambda e, g=g, rt=rt, tb=tb: e.transpose(banksb[tb][0:64, g * 128:(g + 1) * 128], rt[:, g * 64:(g + 1) * 64], identb[:]),
                             reads=[rtb, B_cst], writes=[bankB[tb]])
                    if 'noTq' not in DD0:
                      P.act(lambda e, tb=tb, T=T: e.copy(out=QT[0:64, :, tslice(T)], in_=banksb[tb][0:64, 0:256].rearrange("p (c t) -> p c t", c=2, t=128)),
                          reads=[bankB[tb]], writes=[B_QT[T]])
                    if 'noTk' not in DD0:
                      P.act(lambda e, tb=tb, T=T: e.copy(out=KT[0:64, :, tslice(T)], in_=banksb[tb][0:64, 256:512].rearrange("p (c t) -> p c t", c=2, t=128)),
                          reads=[bankB[tb]], writes=[B_KT[T]])
                vst, vstb = r128.next()
                P.dve(lambda e, z=z, vst=vst: e.tensor_copy(out=vst[:], in_=z[:, 256:384]), reads=[bankB[zb]], writes=[vstb])
                if 'nonk' not in DD0:
                    P.dma(nv_out[l][T * 128:(T + 1) * 128, h * 128:(h + 1) * 128], vst[:], reads=[vstb])
                P.any(lambda e, vst=vst, T=T: e.tensor_copy(out=V[:, T, 0:128], in_=vst[:]), reads=[vstb], writes=[B_V[T]])
                th, thb = r128.next()
                P.act(lambda e, z=z, th=th: e.activation(out=th[:], in_=z[:, 384:512], func=AF.Tanh, scale=0.5), reads=[bankB[zb]], writes=[thb])
                P.dve(lambda e, z=z, th=th, T=T: e.scalar_tensor_tensor(out=sg[:, T, :], in0=th[:], scalar=1.0, in1=z[:, 384:512], op0=ALU.add, op1=ALU.mult),
                      reads=[thb, bankB[zb]], writes=[B_sg[T]])
            import os as _os
            DD = _os.environ.get('DIFFDBG', '')
            if 'noB' in DD:
                return
            KR = 64 if 'k64' in DD else 72
            lam_init = 0.8 - 0.6 * math.exp(-0.3 * l)
            c0 = 0.5 * (1.0 - lam_init)
            OB = [2, 3, 4]

            def acc(c, qi):
                a = c * 4 + qi
                return OB[a // 3], (a % 3) * 160

            for qb in range(4):
                for k in OB:
                    P.pe(lambda e, k=k: e.matmul(banks[k][:, 0:512], lhsT=zerosb[:, 0:128], rhs=zerosb[:, 0:512], start=True, stop=False, skip_group_check=True),
                         reads=[B_cst], writes=[bankB[k]])
                steps = [(c, kt) for c in range(2) for kt in range(NKT)]
                pts = {}

                def emit_st(i):
                    c, kt = steps[i]
                    sbk = i % 2
                    P.pe(lambda e, c=c, kt=kt, sbk=sbk: e.matmul(banks[sbk][:, 0:512], lhsT=KT[0:KR, c, kt * 128:(kt + 1) * 128],
                                                                 rhs=QT[0:KR, c, qb * 512:(qb + 1) * 512], start=True, stop=True),
                         reads=[B_KT[kt], B_qm] + B_QT[qb * 4:qb * 4 + 4], writes=[bankB[sbk]])
                    pt_, ptb = rpt.next()
                    P.act(lambda e, sbk=sbk, pt_=pt_: e.activation(out=pt_[:], in_=banks[sbk][:, 0:512], func=AF.Exp, scale=0.125),
                          reads=[bankB[sbk]], writes=[ptb])
                    pts[i] = (pt_, ptb)

                def emit_pv(i):
                    c, kt = steps[i]
                    pt_, ptb = pts.pop(i)
                    for qi in range(4):
                        bk, off = acc(c, qi)
                        P.pe(lambda e, qi=qi, bk=bk, off=off, pt_=pt_, kt=kt: e.matmul(banks[bk][:, off:off + 129], lhsT=pt_[:, qi * 128:(qi + 1) * 128],
                                                                                       rhs=V[:, kt, 0:129], start=False, stop=(kt == NKT - 1), skip_group_check=True),
                             reads=[ptb, B_V[kt]], writes=[bankB[bk]])

                emit_st(0)
                for i in range(len(steps)):
                    if i + 1 < len(steps):
                        emit_st(i + 1)
                    emit_pv(i)
                for qi in range(4):
                    T = qb * 4 + qi
                    b0, o0 = acc(0, qi)
                    b1, o1 = acc(1, qi)
                    r01, r01b = rs.next()
                    P.dve(lambda e, r01=r01: e.reciprocal(out=r01[:, 0:1], in_=banks[b0][:, o0 + 128:o0 + 129]), reads=[bankB[b0]], writes=[r01b])
                    P.dve(lambda e, r01=r01: e.reciprocal(out=r01[:, 1:2], in_=banks[b1][:, o1 + 128:o1 + 129]), reads=[bankB[b1]], writes=[r01b])
                    P.dve(lambda e, r01=r01: e.tensor_tensor(out=r01[:, 1:2], in0=r01[:, 1:2], in1=small[:, 0:1], op=ALU.mult), reads=[r01b, B_misc], writes=[r01b])
                    d, db = r128.next()
                    P.dve(lambda e, d=d, r01=r01: e.tensor_scalar(out=d[:], in0=banks[b0][:, o0:o0 + 128], scalar1=r01[:, 0:1], scalar2=None, op0=ALU.mult),
                          reads=[bankB[b0], r01b], writes=[db])
                    P.dve(lambda e, d=d, r01=r01: e.scalar_tensor_tensor(out=d[:], in0=banks[b1][:, o1:o1 + 128], scalar=r01[:, 1:2], in1=d[:],
                                                                        op0=ALU.mult, op1=ALU.add),
                          reads=[bankB[b1], r01b, db], writes=[db])
                    jk, jkb = r128.next()
                    ss, ssb = rs.next()
                    P.act(lambda e, d=d, jk=jk, ss=ss: e.activation(out=jk[:], in_=d[:], func=AF.Square, accum_out=ss[:, 0:1]), reads=[db], writes=[jkb, ssb])
                    rstd, rb_ = rstd_from_ss(ss[:, 0:1], ssb, 1, 1.0 / (128 * c0 * c0), EPS / (c0 * c0))
                    mt, mtb = rb128.next()
                    P.dve(lambda e, d=d, rstd=rstd, mt=mt, T=T: e.scalar_tensor_tensor(out=mt[:], in0=d[:], scalar=rstd[:, 0:1], in1=sg[:, T, :],
                                                                                      op0=ALU.mult, op1=ALU.mult),
                          reads=[db, rb_, B_sg[T]], writes=[mtb])
                    mixed_out(mt[:], mtb, chunk, T, 5, eng="dve")

        def ret_unit(l, p, RB):
            u = p
            chunk = UNIT_CHUNK[u]
            if p == 0:
                P.barrier()
            base = p * 28672
            qT = arena_view(base + 0, [128, TOK], BF16)
            kT = arena_view(base + 4096, [128, TOK], BF16)
            ktok = arena_view(base + 8192, [128, NT, 128], BF16)
            v = arena_view(base + 12288, [128, NT, 128], BF16)
            sg = arena_view(base + 16384, [128, NT, 128], BF16)
            Sbf = arena_view(base + 20480, [128, 2, NT, 128], BF16)
            if p == 0:
                dtm_, qd_, kd_, stS_, lgrow_ = dtm, qd, kd, stS, lgrow
            else:
                dtm_ = arena_view(57344, [128, 2, 128], F32)
                qd_ = arena_view(58368, [128, 2, 128], F32)
                kd_ = arena_view(59392, [128, 2, 128], F32)
                stS_ = arena_view(60416, [128, 2, 128], F32)
                lgrow_ = lgrow1
            B_pair_, B_stS_ = RB[p]
            B_q = [Buf() for _ in range(NT)]
            B_k = [Buf() for _ in range(NT)]
            B_kt = [Buf() for _ in range(NT)]
            B_v = [Buf() for _ in range(NT)]
            B_sg = [Buf() for _ in range(NT)]
            B_S = [[Buf() for _ in range(NT)] for _ in range(2)]
            wb = load_unit_weights(l, u)
            for d_ in range(2):
                for hh in range(2):
                    col = d_ * 4 + 2 * p + hh
                    P.dve(lambda e, d_=d_, hh=hh, col=col: e.tensor_copy(out=lgrow_[hh * 64:(hh + 1) * 64, d_:d_ + 1], in_=lg[hh * 64:(hh + 1) * 64, col:col + 1]),
                          reads=[B_misc], writes=[B_pair_])
            for hh in range(2):
                e12t, e12b = r256.next()
                cf = 2 * p + hh
                cb_ = 4 + 2 * p + hh
                P.act(lambda e, cf=cf: e.activation(out=e12t[:, 0:128], in_=M1, func=AF.Exp, scale=lg[:, cf:cf + 1]), reads=[e12b, B_cst, B_misc], writes=[e12b, B_pair_])
                P.act(lambda e, cb_=cb_: e.activation(out=e12t[:, 128:256], in_=M2, func=AF.Exp, scale=lg[:, cb_:cb_ + 1]), reads=[e12b, B_cst, B_misc], writes=[e12b, B_pair_])
                P.dve(lambda e: e.tensor_tensor(out=e12t[:, 0:128], in0=e12t[:, 0:128], in1=L1, op=ALU.mult), reads=[e12b, B_pair_, B_cst], writes=[e12b, B_pair_])
                P.dve(lambda e: e.tensor_tensor(out=e12t[:, 128:256], in0=e12t[:, 128:256], in1=L2, op=ALU.mult), reads=[e12b, B_pair_, B_cst], writes=[e12b, B_pair_])
                P.dve(lambda e, hh=hh: e.tensor_tensor(out=dtm_[:, hh, :], in0=e12t[:, 0:128], in1=e12t[:, 128:256], op=ALU.add), reads=[e12b, B_pair_], writes=[e12b, B_pair_])
                P.act(lambda e, hh=hh, cf=cf: e.activation(out=kd_[:, 0, hh * 64:(hh + 1) * 64], in_=COLA[:, 0:64], func=AF.Exp, scale=lg[:, cf:cf + 1]),
                      reads=[B_cst, B_misc], writes=[B_pair_])
                P.act(lambda e, hh=hh, cb_=cb_: e.activation(out=kd_[:, 1, hh * 64:(hh + 1) * 64], in_=COLB[:, 0:64], func=AF.Exp, scale=lg[:, cb_:cb_ + 1]),
                      reads=[B_cst, B_misc], writes=[B_pair_])
            P.dve(lambda e: e.tensor_scalar(out=kd_[:], in0=kd_[:], scalar1=0.125, scalar2=None, op0=ALU.mult), reads=[B_pair_], writes=[B_pair_])
            P.act(lambda e: e.activation(out=qd_[:, 0, :], in_=IOTA1, func=AF.Exp, scale=lgrow_[:, 0:1]), reads=[B_cst, B_pair_], writes=[B_pair_])
            P.act(lambda e: e.activation(out=qd_[:, 1, :], in_=IOTA2, func=AF.Exp, scale=lgrow_[:, 1:2]), reads=[B_cst, B_pair_], writes=[B_pair_])
            P.act(lambda e: e.activation(out=lgrow_[:, 2:4], in_=lgrow_[:, 0:2], func=AF.Exp, scale=128.0), reads=[B_pair_], writes=[B_pair_])
            for T in range(NT):
                zb = T % 4
                project(wb, T, zb, 0, 512)
                z = banks[zb]
                rq, rqb = rb128.next()
                t1, b1 = r256.next()
                t2, b2 = r256.next()
                cv_ = ropec[:, T, :]
                sv_ = ropes[:, T, :].rearrange("p (h j i) -> p h j i", h=2, j=2, i=16)
                src3 = z[:, 0:256].rearrange("p (g d) -> p g d", g=4, d=64)
                src5 = z[:, 0:256].rearrange("p (g h j i) -> p g h j i", g=4, h=2, j=2, i=16)
                t13 = t1[:].rearrange("p (g d) -> p g d", g=4, d=64)
                t25 = t2[:].rearrange("p (g h j i) -> p g h j i", g=4, h=2, j=2, i=16)
                P.dve(lambda e, t13=t13, src3=src3, cv_=cv_: e.tensor_tensor(out=t13, in0=src3, in1=cv_.unsqueeze(1).to_broadcast([128, 4, 64]), op=ALU.mult),
                      reads=[bankB[zb], B_cst], writes=[b1])
                P.dve(lambda e, t25=t25, src5=src5, sv_=sv_: e.tensor_tensor(out=t25[:, :, :, 0, :], in0=src5[:, :, :, 1, :],
                                                                             in1=sv_[:, :, 0, :].unsqueeze(1).to_broadcast([128, 4, 2, 16]), op=ALU.mult),
                      reads=[bankB[zb], B_cst], writes=[b2])
                P.dve(lambda e, t25=t25, src5=src5, sv_=sv_: e.tensor_tensor(out=t25[:, :, :, 1, :], in0=src5[:, :, :, 0, :],
                                                                             in1=sv_[:, :, 1, :].unsqueeze(1).to_broadcast([128, 4, 2, 16]), op=ALU.mult),
                      reads=[bankB[zb], B_cst], writes=[b2])
                P.any(lambda e, t1=t1, t2=t2, rq=rq: e.tensor_tensor(out=rq[:], in0=t1[:, 0:128], in1=t2[:, 0:128], op=ALU.add), reads=[b1, b2], writes=[rqb])
                P.any(lambda e, t1=t1, t2=t2, T=T: e.tensor_tensor(out=ktok[:, T, :], in0=t1[:, 128:256], in1=t2[:, 128:256], op=ALU.add),
                       reads=[b1, b2], writes=[B_kt[T]])
                tb = 4 + (T % 2)
                P.pe(lambda e, rq=rq, tb=tb: e.transpose(banksb[tb][:, 0:128], rq[:], identb[:]), reads=[rqb, B_cst], writes=[bankB[tb]])
                P.pe(lambda e, T=T, tb=tb: e.transpose(banksb[tb][:, 128:256], ktok[:, T, :], identb[:]), reads=[B_kt[T], B_cst], writes=[bankB[tb]])
                P.act(lambda e, T=T, tb=tb: e.copy(out=qT[:, tslice(T)], in_=banksb[tb][:, 0:128]), reads=[bankB[tb]], writes=[B_q[T]])
                P.act(lambda e, T=T, tb=tb: e.activation(out=kT[:, tslice(T)], in_=banksb[tb][:, 128:256], func=AF.Copy, scale=0.125),
                      reads=[bankB[tb]], writes=[B_k[T]])
                P.act(lambda e, z=z, T=T: e.copy(out=v[:, T, :], in_=z[:, 256:384]), reads=[bankB[zb]], writes=[B_v[T]])
                th, thb = r128.next()
                P.act(lambda e, z=z, th=th: e.activation(out=th[:], in_=z[:, 384:512], func=AF.Tanh, scale=0.5), reads=[bankB[zb]], writes=[thb])
                P.dve(lambda e, z=z, th=th, T=T: e.scalar_tensor_tensor(out=sg[:, T, :], in0=th[:], scalar=1.0, in1=z[:, 384:512], op0=ALU.add, op1=ALU.mult),
                      reads=[thb, bankB[zb]], writes=[B_sg[T]])
            st_in = [srf_in, srb_in]
            st_out = [nsrf_out, nsrb_out]
            for d_ in range(2):
                P.pool(lambda e, d_=d_: e.memset(stS_[:, d_, :], 0.0), writes=[B_stS_[d_]])
                for hh in range(2):
                    P.dma(stS_[hh * 64:(hh + 1) * 64, d_, hh * 64:(hh + 1) * 64], st_in[d_][l, 2 * p + hh], writes=[B_stS_[d_]])
            for step in range(NT):
                for d_ in range(2):
                    T = step if d_ == 0 else NT - 1 - step
                    S = stS_[:, d_, :]
                    kcol = d_ * 16 + T
                    P.dve(lambda e, S=S, kcol=kcol: e.tensor_scalar(out=S, in0=S, scalar1=keep[:, kcol:kcol + 1], scalar2=None, op0=ALU.mult),
                          reads=[B_stS_[d_], B_cst], writes=[B_stS_[d_]])
                    P.act(lambda e, S=S, d_=d_, T=T: e.copy(out=Sbf[:, d_, T, :], in_=S), reads=[B_stS_[d_]], writes=[B_S[d_][T]])
                    kt_, ktb_ = rb128.next()
                    P.any(lambda e, kt_=kt_, T=T, d_=d_: e.tensor_tensor(out=kt_[:], in0=ktok[:, T, :], in1=kd_[:, d_, :], op=ALU.mult),
                           reads=[B_kt[T], B_pair_], writes=[ktb_])
                    ub = d_
                    P.pe(lambda e, kt_=kt_, T=T, ub=ub: e.matmul(banks[ub][:, 0:128], lhsT=kt_[:], rhs=v[:, T, :], start=True, stop=True),
                         reads=[ktb_, B_v[T]], writes=[bankB[ub]])
                    tmp, tmpb = r128.next()
                    P.dve(lambda e, tmp=tmp, ub=ub: e.tensor_tensor(out=tmp[:], in0=banks[ub][:, 0:128], in1=BM, op=ALU.mult),
                          reads=[bankB[ub], B_cst], writes=[tmpb])
                    P.dve(lambda e, S=S, tmp=tmp, d_=d_: e.scalar_tensor_tensor(out=S, in0=S, scalar=lgrow_[:, 2 + d_:3 + d_], in1=tmp[:], op0=ALU.mult, op1=ALU.add),
                          reads=[B_stS_[d_], B_pair_, tmpb], writes=[B_stS_[d_]])
                    is_out = (T % 2 == 1) if d_ == 0 else (T % 2 == 0)
                    if is_out:
                        so, sob = r128.next()
                        P.act(lambda e, so=so, S=S: e.copy(out=so[:], in_=S), reads=[B_stS_[d_]], writes=[sob])
                        for hh in range(2):
                            P.dma(st_out[d_][l, T // 2, 2 * p + hh], so[hh * 64:(hh + 1) * 64, hh * 64:(hh + 1) * 64], reads=[sob])
            for T in range(NT):
                qf, qfb = rb128.next()
                qb_, qbb = rb128.next()
                P.dve(lambda e, qf=qf, T=T: e.tensor_tensor(out=qf[:], in0=qT[:, tslice(T)], in1=qd_[:, 0, :], op=ALU.mult), reads=[B_q[T], B_pair_], writes=[qfb])
                P.any(lambda e, qb_=qb_, T=T: e.tensor_tensor(out=qb_[:], in0=qT[:, tslice(T)], in1=qd_[:, 1, :], op=ALU.mult), reads=[B_q[T], B_pair_], writes=[qbb])
                ob = 6 + (T % 2)
                ab = 2 + (T % 2)
                P.pe(lambda e, qf=qf, T=T, ob=ob: e.matmul(banks[ob][:, 0:128], lhsT=qf[:], rhs=Sbf[:, 0, T, :], start=True, stop=False, skip_group_check=True),
                     reads=[qfb, B_S[0][T]], writes=[bankB[ob]])
                P.pe(lambda e, qb_=qb_, T=T, ob=ob: e.matmul(banks[ob][:, 0:128], lhsT=qb_[:], rhs=Sbf[:, 1, T, :], start=False, stop=False, skip_group_check=True),
                     reads=[qbb, B_S[1][T]], writes=[bankB[ob]])
                am, amb = rb256.next()
                for hh in range(2):
                    P.pe(lambda e, hh=hh, T=T: e.matmul(banks[2 + hh][:, 0:128], lhsT=kT[hh * 64:(hh + 1) * 64, tslice(T)],
                                                        rhs=qT[hh * 64:(hh + 1) * 64, tslice(T)], start=True, stop=True),
                         reads=[B_k[T], B_q[T]], writes=[bankB[2 + hh]])
                    P.dve(lambda e, am=am, hh=hh: e.tensor_tensor(out=am[:, hh * 128:(hh + 1) * 128], in0=banks[2 + hh][:, 0:128], in1=dtm_[:, hh, :], op=ALU.mult),
                          reads=[bankB[2 + hh], B_pair_], writes=[amb])
                for hh in range(2):
                    P.pe(lambda e, hh=hh, am=am, T=T, ob=ob: e.matmul(banks[ob][:, hh * 64:(hh + 1) * 64], lhsT=am[:, hh * 128:(hh + 1) * 128],
                                                                      rhs=v[:, T, hh * 64:(hh + 1) * 64], start=False, stop=(hh == 1), skip_group_check=True),
                         reads=[amb, B_v[T]], writes=[bankB[ob]])
                finish_pair(banks[ob][:, 0:128], [bankB[ob]], sg, B_sg, chunk, T, 0.5, 4 + (T % 2))

        def finish_pair(o_ap, obufs, sg, B_sg, chunk, T, c0, tbank):
            ss, ssb = rs.next()
            jk, jkb = r128.next()
            for hh in range(2):
                P.act(lambda e, hh=hh: e.activation(out=jk[:, hh * 64:(hh + 1) * 64], in_=o_ap[:, hh * 64:(hh + 1) * 64], func=AF.Square,
                                                    accum_out=ss[:, hh:hh + 1]),
                      reads=obufs, writes=[jkb, ssb])
            rstd, rb_ = rstd_from_ss(ss[:, 0:2], ssb, 2, 1.0 / (64 * c0 * c0), EPS / (c0 * c0))
            mt, mtb = rb128.next()
            for hh in range(2):
                P.dve(lambda e, hh=hh: e.scalar_tensor_tensor(out=mt[:, hh * 64:(hh + 1) * 64], in0=o_ap[:, hh * 64:(hh + 1) * 64], scalar=rstd[:, hh:hh + 1],
                                                              in1=sg[:, T, hh * 64:(hh + 1) * 64], op0=ALU.mult, op1=ALU.mult),
                      reads=list(obufs) + [rb_, B_sg[T]], writes=[mtb])
            mixed_out(mt[:], mtb, chunk, T, tbank)

        def hgrn_unit(l, p):
            u = 2 + p
            chunk = UNIT_CHUNK[u]
            P.barrier()
            q = arena_view(0, [128, NT, 128], BF16)
            kk = arena_view(4096, [128, NT, 256], BF16)
            lf = arena_view(12288, [128, NT, 256], F32)
            v = arena_view(28672, [128, NT, 128], BF16)
            sg = arena_view(32768, [128, NT, 128], BF16)
            oacc = arena_view(36864, [128, NT, 128], F32)
            vmall = arena_view(45056, [128, NT, 512], BF16)
            B_vm = [Buf() for _ in range(NT)]
            B_q = [Buf() for _ in range(NT)]
            B_kk = [Buf() for _ in range(NT)]
            B_lf = [Buf() for _ in range(NT)]
            B_v = [Buf() for _ in range(NT)]
            B_sg = [Buf() for _ in range(NT)]
            B_oa = [Buf() for _ in range(NT)]
            wb = load_unit_weights(l, u)
            for half in range(2):
                P.dve(lambda e, half=half: e.tensor_copy(out=lb2[:, half * 128:(half + 1) * 128], in_=lball[:, l, p * 128:(p + 1) * 128]),
                      reads=[B_cst], writes=[B_pair])
            P.dve(lambda e: e.tensor_scalar(out=omlb2[:], in0=lb2[:], scalar1=-1.0, scalar2=1.0, op0=ALU.mult, op1=ALU.add), reads=[B_pair], writes=[B_pair])
            for T in range(NT):
                zb = 2 * (T % 2)
                project(wb, T, zb, 0, 512)
                project(wb, T, zb + 1, 512, 640)
                z = banks[zb]
                z2 = banks[zb + 1]
                u_, ub_ = r512.next()
                P.act(lambda e, z=z, u_=u_: e.activation(out=u_[:, 0:384], in_=z[:, 0:384], func=AF.Exp, scale=-1.0), reads=[bankB[zb]], writes=[ub_])
                P.any(lambda e, u_=u_: e.tensor_scalar(out=u_[:, 0:384], in0=u_[:, 0:384], scalar1=1.0, scalar2=None, op0=ALU.add), reads=[ub_], writes=[ub_])
                P.dve(lambda e, u_=u_: e.reciprocal(out=u_[:, 0:384], in_=u_[:, 0:384]), reads=[ub_], writes=[ub_])
                P.dve(lambda e, u_=u_, z=z, T=T: e.tensor_tensor(out=sg[:, T, :], in0=u_[:, 256:384], in1=z[:, 256:384], op=ALU.mult),
                      reads=[ub_, bankB[zb]], writes=[B_sg[T]])
                f_, fb_ = r256.next()
                P.any(lambda e, u_=u_, f_=f_: e.tensor_tensor(out=f_[:], in0=u_[:, 0:256], in1=omlb2[:], op=ALU.mult), reads=[ub_, B_pair], writes=[fb_])
                P.any(lambda e, f_=f_: e.tensor_tensor(out=f_[:], in0=f_[:], in1=lb2[:], op=ALU.add), reads=[fb_, B_pair], writes=[fb_])
                P.act(lambda e, f_=f_, T=T: e.activation(out=lf[:, T, :], in_=f_[:], func=AF.Ln), reads=[fb_], writes=[B_lf[T]])
                P.dve(lambda e, f_=f_, T=T: e.tensor_scalar(out=kk[:, T, :], in0=f_[:], scalar1=-1.0, scalar2=1.0, op0=ALU.mult, op1=ALU.add),
                      reads=[fb_], writes=[B_kk[T]])
                P.act(lambda e, z=z, T=T: e.activation(out=q[:, T, :], in_=z[:, 384:512], func=AF.Copy, scale=0.125), reads=[bankB[zb]], writes=[B_q[T]])
                P.act(lambda e, z2=z2, T=T: e.copy(out=v[:, T, :], in_=z2[:, 0:128]), reads=[bankB[zb + 1]], writes=[B_v[T]])
            st_in = [shf_in, shb_in]
            st_out = [nshf_out, nshb_out]
            TRI = [TRIF, TRIB]
            for d_ in range(2):
                P.pool(lambda e, d_=d_: e.memset(stS[:, d_, :], 0.0), writes=[B_stS[d_]])
                for hh in range(2):
                    P.dma(stS[hh * 64:(hh + 1) * 64, d_, hh * 64:(hh + 1) * 64], st_in[d_][l, 2 * p + hh], writes=[B_stS[d_]])
            done_first = [False] * NT
            for step in range(NT):
                for d_ in range(2):
                    T = step if d_ == 0 else NT - 1 - step
                    S = stS[:, d_, :]
                    lfd = lf[:, T, d_ * 128:(d_ + 1) * 128]
                    kcol = d_ * 16 + T
                    P.dve(lambda e, S=S, kcol=kcol: e.tensor_scalar(out=S, in0=S, scalar1=keep[:, kcol:kcol + 1], scalar2=None, op0=ALU.mult),
                          reads=[B_stS[d_], B_cst], writes=[B_stS[d_]])
                    sbf, sbfb = rb128.next()
                    P.act(lambda e, S=S, sbf=sbf: e.copy(out=sbf[:], in_=S), reads=[B_stS[d_]], writes=[sbfb])
                    P.pe(lambda e, lfd=lfd, d_=d_: e.matmul(banks[0][:, 0:128], lhsT=TRI[d_], rhs=lfd, start=True, stop=True),
                         reads=[B_cst, B_lf[T]], writes=[bankB[0]])
                    P.pe(lambda e, lfd=lfd: e.matmul(banks[0][:, 128:132], lhsT=lfd, rhs=ind[:], start=True, stop=True),
                         reads=[B_cst, B_lf[T]], writes=[bankB[0]])
                    G_, Gb_ = rs.next()
                    P.act(lambda e, G_=G_: e.activation(out=G_[:, 0:4], in_=banks[0][:, 128:132], func=AF.Exp), reads=[bankB[0]], writes=[Gb_])
                    eq, eqb = r128.next()
                    ek, ekb = r128.next()
                    P.act(lambda e, eq=eq: e.activation(out=eq[:], in_=banks[0][:, 0:128], func=AF.Exp), reads=[bankB[0]], writes=[eqb])
                    P.act(lambda e, ek=ek: e.activation(out=ek[:], in_=banks[0][:, 0:128], func=AF.Exp, scale=-1.0), reads=[bankB[0]], writes=[ekb])
                    qt_, qtb = rb128.next()
                    kt_, ktb = rb128.next()
                    P.dve(lambda e, qt_=qt_, eq=eq, T=T: e.tensor_tensor(out=qt_[:], in0=q[:, T, :], in1=eq[:], op=ALU.mult), reads=[B_q[T], eqb], writes=[qtb])
                    P.any(lambda e, kt_=kt_, ek=ek, T=T, d_=d_: e.tensor_tensor(out=kt_[:], in0=kk[:, T, d_ * 128:(d_ + 1) * 128], in1=ek[:], op=ALU.mult),
                           reads=[B_kk[T], ekb], writes=[ktb])
                    P.pe(lambda e, qt_=qt_: e.transpose(banksb[1][:, 0:128], qt_[:], identb[:]), reads=[qtb, B_cst], writes=[bankB[1]])
                    P.pe(lambda e, kt_=kt_: e.transpose(banksb[1][:, 128:256], kt_[:], identb[:]), reads=[ktb, B_cst], writes=[bankB[1]])
                    qkT, qkTb = rb256.next()
                    P.act(lambda e, qkT=qkT: e.copy(out=qkT[:], in_=banksb[1][:, 0:256]), reads=[bankB[1]], writes=[qkTb])
                    vm, vmb = vmall[:, T, :], B_vm[T]
                    if not done_first[T]:
                        for j in range(4):
                            P.any(lambda e, j=j, vm=vm, T=T: e.tensor_scalar(out=vm[:, j * 128:(j + 1) * 128], in0=v[:, T, :], scalar1=ind[:, j:j + 1],
                                                                             scalar2=None, op0=ALU.mult),
                                  reads=[B_v[T], B_cst], writes=[vmb])
                    P.pe(lambda e, kt_=kt_, vm=vm: e.matmul(banks[2][:, 0:512], lhsT=kt_[:], rhs=vm, start=True, stop=True),
                         reads=[ktb, vmb], writes=[bankB[2]])
                    am, amb = rb256.next()
                    for hh in range(2):
                        abk = 3 + hh
                        P.pe(lambda e, hh=hh, qkT=qkT, abk=abk: e.matmul(banks[abk][:, 0:128], lhsT=qkT[hh * 64:(hh + 1) * 64, 128:256],
                                                                          rhs=qkT[hh * 64:(hh + 1) * 64, 0:128], start=True, stop=True),
                             reads=[qkTb], writes=[bankB[abk]])
                        P.dve(lambda e, am=am, d_=d_, hh=hh, abk=abk: e.tensor_tensor(out=am[:, hh * 128:(hh + 1) * 128], in0=banks[abk][:, 0:128], in1=TRI[d_], op=ALU.mult),
                              reads=[bankB[abk], B_cst], writes=[amb])
                    ob = 5 + d_
                    jorder = [0, 1, 2, 3] if d_ == 0 else [3, 2, 1, 0]
                    cur, curb = sbf, sbfb
                    for ji, j in enumerate(jorder):
                        P.pe(lambda e, j=j, qkT=qkT, cur=cur: e.matmul(banks[ob][32 * j:32 * j + 32, 0:128], lhsT=qkT[:, 32 * j:32 * j + 32], rhs=cur[:],
                                                                       start=True, stop=False, tile_position=(0, 32 * j), skip_group_check=True),
                             reads=[qkTb, curb], writes=[bankB[ob]])
                        tg, tgb = r128.next()
                        P.dve(lambda e, tg=tg, j=j, G_=G_: e.scalar_tensor_tensor(out=tg[:], in0=banks[2][:, j * 128:(j + 1) * 128], scalar=G_[:, j:j + 1], in1=BM,
                                                                                  op0=ALU.mult, op1=ALU.mult),
                              reads=[bankB[2], Gb_, B_cst], writes=[tgb])
                        P.dve(lambda e, S=S, tg=tg, j=j, G_=G_: e.scalar_tensor_tensor(out=S, in0=S, scalar=G_[:, j:j + 1], in1=tg[:], op0=ALU.mult, op1=ALU.add),
                              reads=[B_stS[d_], Gb_, tgb], writes=[B_stS[d_]])
                        if ji < 3:
                            cur, curb = rb128.next()
                            P.act(lambda e, S=S, cur=cur: e.copy(out=cur[:], in_=S), reads=[B_stS[d_]], writes=[curb])
                    for hh in range(2):
                        P.pe(lambda e, hh=hh, am=am, T=T: e.matmul(banks[ob][:, hh * 64:(hh + 1) * 64], lhsT=am[:, hh * 128:(hh + 1) * 128],
                                                                   rhs=v[:, T, hh * 64:(hh + 1) * 64], start=False, stop=(hh == 1), skip_group_check=True),
                             reads=[amb, B_v[T]], writes=[bankB[ob]])
                    is_out = (T % 2 == 1) if d_ == 0 else (T % 2 == 0)
                    if is_out:
                        so, sob = r128.next()
                        P.act(lambda e, so=so, S=S: e.copy(out=so[:], in_=S), reads=[B_stS[d_]], writes=[sob])
                        for hh in range(2):
                            P.dma(st_out[d_][l, T // 2, 2 * p + hh], so[hh * 64:(hh + 1) * 64, hh * 64:(hh + 1) * 64], reads=[sob])
                    if not done_first[T]:
                        done_first[T] = True
                        P.act(lambda e, T=T: e.copy(out=oacc[:, T, :], in_=banks[ob][:, 0:128]), reads=[bankB[ob]], writes=[B_oa[T]])
                    else:
                        ot, otb = r128.next()
                        P.dve(lambda e, ot=ot, T=T: e.tensor_tensor(out=ot[:], in0=banks[ob][:, 0:128], in1=oacc[:, T, :], op=ALU.add),
                              reads=[bankB[ob], B_oa[T]], writes=[otb])
                        finish_pair(ot[:], [otb], sg, B_sg, chunk, T, 1.0, 7)

        def out_phase(l, last):
            P.barrier()
            wo = arena_view(0, [128, 8, 1024], BF16)
            B_wo = Buf()
            for half in range(2):
                ws = wstage[:, 0:4096].rearrange("p (k n) -> p k n", k=8, n=512)
                P.dma(ws, wout_in[l][:, :, half * 512:(half + 1) * 512], writes=[B_wstage])
                for kc in range(8):
                    eng = "any"
                    P.on(eng, lambda e, kc=kc, half=half: e.tensor_copy(out=wo[:, kc, half * 512:(half + 1) * 512], in_=ws[:, kc, :]),
                         reads=[B_wstage], writes=[B_wo])
            src = x_in if l == 0 else xs_scr
            dst = y_out if last else xs_scr
            for T in range(NT):
                xt, xb_ = xring.next()
                P.dma(xt[:], src[T * 128:(T + 1) * 128, :], reads=([B_xs[T]] if l > 0 else []), writes=[xb_])
                for nb in range(2):
                    bk = 2 * (T % 2) + nb
                    for c in range(8):
                        P.pe(lambda e, c=c, nb=nb, bk=bk, T=T: e.matmul(banks[bk][:, 0:512], lhsT=mixT[:, c, tslice(T)], rhs=wo[:, c, nb * 512:(nb + 1) * 512],
                                                                        start=(c == 0), stop=(c == 7)),
                             reads=[B_mixT[c][T], B_wo], writes=[bankB[bk]])
                    tmp, tmpb = r512.next()
                    P.dve(lambda e, tmp=tmp, bk=bk, nb=nb: e.tensor_tensor(out=tmp[:], in0=banks[bk][:, 0:512], in1=gate_b[:, nb * 512:(nb + 1) * 512], op=ALU.mult),
                          reads=[bankB[bk], B_gate], writes=[tmpb])
                    P.any(lambda e, tmp=tmp, xt=xt, nb=nb: e.tensor_tensor(out=xt[:, nb * 512:(nb + 1) * 512], in0=tmp[:], in1=xt[:, nb * 512:(nb + 1) * 512], op=ALU.add),
                           reads=[tmpb, xb_], writes=[xb_])
                P.dma(dst[T * 128:(T + 1) * 128, :], xt[:], reads=[xb_], writes=([] if last else [B_xs[T]]))

        for l in range(L):
            setup_layer(l)
            norm_phase(l)
            RB = [(Buf(), [Buf(), Buf()]) for _ in range(2)]
            for p in range(2):
                if units_enabled is None or ("r%d" % p) in units_enabled:
                    ret_unit(l, p, RB)
            for p in range(2):
                if units_enabled is None or ("g%d" % p) in units_enabled:
                    hgrn_unit(l, p)
            DB = {"set": [([Buf() for _ in range(NT)], [Buf() for _ in range(NKT)], [Buf() for _ in range(NKT)], [Buf() for _ in range(NT)], Buf()) for _ in range(2)], "ck": Buf()}
            for h in range(4):
                if units_enabled is None or ("d%d" % h) in units_enabled:
                    diff_unit(l, h, DB)
            if dbg:
                P.barrier()
                P.dma(dbg_out[l], mixT[:], reads=[b for row in B_mixT for b in row])
            out_phase(l, last=(l == L - 1))

        with nc.Block() as block:
            run = P.build(sems, dsems, reorder=REORDER)
            block.sync(lambda e: run("sp", e))
            block.tensor(lambda e: run("pe", e))
            block.scalar(lambda e: run("act", e))
            block.vector(lambda e: run("dve", e))
            block.gpsimd(lambda e: run("pool", e))
    return nc


def _unit_perm():
    off = dict(rq=0, rk=256, rv=512, rg=768, dq=1024, dk=1536, dv=2048, dg=2560, hq=3072, hff=3328, hfb=3584, hi=3840, hg=4096)
    cols = []
    for p in range(2):
        for n in ("rq", "rk", "rv", "rg"):
            cols += list(range(off[n] + 128 * p, off[n] + 128 * p + 128))
    for p in range(2):
        for n in ("hff", "hfb", "hg", "hq", "hi"):
            cols += list(range(off[n] + 128 * p, off[n] + 128 * p + 128))
    for h in range(4):
        for n in ("dq", "dk", "dv", "dg"):
            cols += list(range(off[n] + 128 * h, off[n] + 128 * h + 128))
    return np.array(cols, dtype=np.int64)


def _constants():
    s = np.arange(128, dtype=np.float32)[:, None]
    t = np.arange(128, dtype=np.float32)[None, :]
    M1 = np.maximum(t - s, 0)
    L1 = (s <= t).astype(np.float32)
    M2 = np.maximum(s - t, 0)
    L2 = (s >= t).astype(np.float32)
    IOTA1 = np.broadcast_to(t + 1, (128, 128))
    IOTA2 = np.broadcast_to(128 - t, (128, 128))
    COLA = np.broadcast_to(127 - s, (128, 128))
    COLB = np.broadcast_to(s, (128, 128))
    same = (np.floor(s / 32) == np.floor(t / 32))
    TRIF = (same & (s <= t)).astype(np.float32)
    TRIB = (same & (s >= t)).astype(np.float32)
    BM = (np.floor(s / 64) == np.floor(t / 64)).astype(np.float32)
    cst = np.stack([M1, L1, M2, L2, IOTA1, IOTA2, COLA, COLB, TRIF, TRIB, BM], axis=1).astype(np.float32)
    ind = (np.floor(np.arange(128)[:, None] / 32) == np.arange(4)[None, :]).astype(np.float32)
    return np.ascontiguousarray(cst), np.ascontiguousarray(ind)


def _rope_tables(sample):
    ropec = np.ones((128, 16, 64), np.float32)
    ropes = np.zeros((128, 16, 64), np.float32)
    if sample:
        tt = np.arange(TOK)
        row = (tt // 64).astype(np.float32)
        col = (tt % 64).astype(np.float32)
        inv = (np.float32(10000.0) ** (-np.arange(16, dtype=np.float32) / np.float32(16))).astype(np.float32)
        ar = (row[:, None] * inv[None, :]).astype(np.float32)
        ac = (col[:, None] * inv[None, :]).astype(np.float32)
        c = np.concatenate([np.cos(ar), np.cos(ar), np.cos(ac), np.cos(ac)], axis=1).astype(np.float32)
        s_ = np.concatenate([-np.sin(ar), np.sin(ar), -np.sin(ac), np.sin(ac)], axis=1).astype(np.float32)
        ropec = np.ascontiguousarray(c.reshape(16, 128, 64).transpose(1, 0, 2))
        ropes = np.ascontiguousarray(s_.reshape(16, 128, 64).transpose(1, 0, 2))
    return ropec, ropes


_NC_CACHE = {}


def kernel(x_prompt, x_sample, c, c_ctx, cache_diff_k, cache_diff_v, state_ret_fwd, state_ret_bwd,
           state_hgrn_fwd, state_hgrn_bwd, norm_g, w_ada, b_ada, w_in, w_out, ret_decay_logit,
           diff_qn_g, diff_kn_g, diff_lambda, hgrn_lb_logit, _dbg=False, _units=None, _L=2):
    f32 = np.float32
    bf = ml_dtypes.bfloat16
    A = lambda a: np.ascontiguousarray(np.asarray(a, dtype=f32))
    x_prompt, x_sample, c, c_ctx = A(x_prompt), A(x_sample), A(c), A(c_ctx)
    perm = _unit_perm()
    w_in_p = A(w_in)[:, :, perm]
    win = np.ascontiguousarray(w_in_p.reshape(2, 8, 128, 4352).transpose(0, 2, 1, 3))
    wada = np.ascontiguousarray(A(w_ada).reshape(2, 8, 128, 3072).transpose(0, 2, 1, 3))
    wout = np.ascontiguousarray(A(w_out).reshape(2, 8, 128, 1024).transpose(0, 2, 1, 3))
    normg = np.ascontiguousarray(A(norm_g).reshape(2, 8, 128).transpose(0, 2, 1))
    cst, ind = _constants()
    shared = dict(
        normg=normg, wada=wada, bada=A(b_ada), win=win, wout=wout, rdl=A(ret_decay_logit).reshape(2, 8),
        qng=A(diff_qn_g), kng=A(diff_kn_g), dlam=A(diff_lambda).reshape(2, 256), hlb=A(hgrn_lb_logit).reshape(512),
        identb=np.eye(128, dtype=f32).astype(bf), identf=np.eye(128, dtype=f32), cst=cst, ind=ind,
    )
    ropec_s, ropes_s = _rope_tables(True)
    ropec_p, ropes_p = _rope_tables(False)
    z64 = np.zeros((2, 4, 64, 64), f32)
    zc = np.zeros((2, 512, 512), f32)
    in_maps = []
    for core in range(8):
        m = dict(shared)
        if core < 4:
            b = core
            m["x"] = x_sample[b]
            m["modv"] = np.ascontiguousarray(c[b].reshape(8, 128).T)
            m["ck"] = np.ascontiguousarray(A(cache_diff_k)[b].reshape(2, 512, 512))
            m["cv"] = np.ascontiguousarray(A(cache_diff_v)[b].reshape(2, 512, 512))
            m["srf"], m["srb"] = A(state_ret_fwd)[b], A(state_ret_bwd)[b]
            m["shf"], m["shb"] = A(state_hgrn_fwd)[b], A(state_hgrn_bwd)[b]
            m["ropec"], m["ropes"] = ropec_s, ropes_s
            m["qmask"] = np.zeros((8, 2048), f32).astype(bf)
            m["kmask"] = np.zeros((8, 2560), f32).astype(bf)
            m["keep"] = np.ones((128, 32), f32)
        else:
            j = core - 4
            m["x"] = np.ascontiguousarray(x_prompt[8 * j:8 * j + 8].reshape(2048, 1024))
            m["modv"] = np.ascontiguousarray(c_ctx.reshape(8, 128).T)
            m["ck"], m["cv"] = zc, zc
            m["srf"], m["srb"], m["shf"], m["shb"] = z64, z64, z64, z64
            m["ropec"], m["ropes"] = ropec_p, ropes_p
            seq = np.arange(2048) // 256
            qm = (seq[None, :] == np.arange(8)[:, None]).astype(f32)
            km = np.full((8, 2560), BIGNEG, f32)
            km[:, :2048] = np.where(seq[None, :] == np.arange(8)[:, None], 0.0, BIGNEG)
            m["qmask"] = qm.astype(bf)
            m["kmask"] = km.astype(bf)
            kf = np.array([0.0 if T % 2 == 0 else 1.0 for T in range(16)], f32)
            kb = np.array([0.0 if T % 2 == 1 else 1.0 for T in range(16)], f32)
            m["keep"] = np.ascontiguousarray(np.broadcast_to(np.concatenate([kf, kb])[None, :], (128, 32)))
        in_maps.append(m)

    key = (_L, _dbg, None if _units is None else tuple(sorted(_units)))
    if key not in _NC_CACHE:
        _NC_CACHE[key] = build_program(L=_L, dbg=_dbg, units_enabled=_units)
    nc = _NC_CACHE[key]
    res = run_bass_kernel_spmd(nc, in_maps, core_ids=list(range(8)))
    R = res.results

    y_sample = np.stack([R[b]["y"] for b in range(4)], axis=0)
    y_prompt = np.concatenate([R[4 + j]["y"].reshape(8, 256, 1024) for j in range(4)], axis=0)
    nk = np.concatenate([R[4 + j]["nk"].reshape(2, 8, 256, 4, 2, 64).transpose(1, 0, 2, 3, 4, 5) for j in range(4)], axis=0)
    nv = np.concatenate([R[4 + j]["nv"].reshape(2, 8, 256, 4, 128).transpose(1, 0, 2, 3, 4) for j in range(4)], axis=0)
    st = []
    for name in ("nsrf", "nsrb", "nshf", "nshb"):
        st.append(np.concatenate([R[4 + j][name].transpose(1, 0, 2, 3, 4) for j in range(4)], axis=0))
    outs = (y_prompt, y_sample, np.ascontiguousarray(nk), np.ascontiguousarray(nv), *[np.ascontiguousarray(s) for s in st])
    if _dbg:
        return outs, [R[i]["dbgmix"] for i in range(8)]
    return outs
```

```python
import math
import types
from contextlib import ExitStack

import numpy as np
import ml_dtypes

import concourse.bass as bass
import concourse.mybir as mybir
from concourse.bass_utils import run_bass_kernel_spmd

F32 = mybir.dt.float32
BF16 = mybir.dt.bfloat16
ALU = mybir.AluOpType
AF = mybir.ActivationFunctionType
AX = mybir.AxisListType

ENGS = ["pe", "act", "dve", "pool", "sp"]
NDSEM = 8
SAME_ENG_SYNC = True
import os as _os0
REORDER = _os0.environ.get('REORDER', '1') == '1'
PSUM_EXCL = _os0.environ.get('PSUM_EXCL', '1') == '1'
REORDER_ENGS = _os0.environ.get('REORDER_ENGS', 'pe,act,dve,pool,sp').split(',')

D_MODEL = 1024
NT = 16
TOK = 2048
NKT = 20
EPS = 1e-6
UNIT_W = [512, 512, 640, 640, 512, 512, 512, 512]
UNIT_OFF = [0, 512, 1024, 1664, 2304, 2816, 3328, 3840]
UNIT_CHUNK = [0, 1, 6, 7, 2, 3, 4, 5]
BIGNEG = -30000.0


class Buf:
    __slots__ = ("w", "r", "name", "excl")

    def __init__(self, name="", excl=False):
        self.w = None
        self.r = []
        self.name = name
        self.excl = excl


class Op:
    __slots__ = ("eng", "fn", "waits", "marked", "semval", "is_dma", "dsem", "dval", "cost", "lat", "idx", "prio",
                 "pos", "succs", "nrem", "ready", "fin", "is_bar", "per_eng", "per_dsem")


class _Probe:
    def __init__(self):
        self.rec = None

    def __getattr__(self, name):
        def f(*a, **k):
            self.rec = (name, a, k)
            return self
        return f


def _nfree(ap):
    n = 1
    for d in ap.shape[1:]:
        n *= int(d)
    return n


def _estimate(eng, fn, is_dma):
    pr = _Probe()
    try:
        fn(pr)
        name, a, k = pr.rec
    except Exception:
        name, a, k = "?", (), {}
    out = k.get("out", a[0] if a else None)
    try:
        if is_dma:
            nbytes = _nfree(out) * int(out.shape[0]) * mybir.dt.size(out.dtype)
            return 120.0, 2200.0 + nbytes / 120.0
        if eng == "pe":
            if name == "transpose":
                return 80.0, 80.0
            rhs = k.get("rhs", a[2] if len(a) > 2 else None)
            lhsT = k.get("lhsT", a[1] if len(a) > 1 else None)
            n = _nfree(rhs)
            c = (max(64, n) / 2.4 + 25.0) * 1.25
            if lhsT.dtype == F32:
                c *= 4.0
            return c, c
        n = _nfree(out)
        if eng == "act":
            c = 190.0 + n / 1.2 + (90.0 if k.get("accum_out") is not None else 0.0)
        elif eng == "dve":
            c = 130.0 + n / 0.7
        else:
            c = 700.0 + n / 0.4
        return c, c
    except Exception:
        return 300.0, 300.0


def _freeze(fn):
    if fn.__closure__ is None:
        return fn
    cells = []
    for c in fn.__closure__:
        try:
            cells.append(types.CellType(c.cell_contents))
        except ValueError:
            cells.append(c)
    return types.FunctionType(fn.__code__, fn.__globals__, fn.__name__, fn.__defaults__, tuple(cells))


LAT_X = 200.0
LAT_S = 250.0


class Prog:
    def __init__(self, nc):
        self.nc = nc
        self.all = []
        self.cur_bar = None
        self.since = []
        self.load = {e: 0.0 for e in ENGS}

    def _new(self, eng):
        op = Op()
        op.eng = eng
        op.fn = None
        op.marked = False
        op.semval = None
        op.is_dma = False
        op.dsem = None
        op.dval = None
        op.cost = 0.0
        op.lat = 0.0
        op.is_bar = False
        op.idx = len(self.all)
        op.waits = []
        self.all.append(op)
        return op

    def barrier(self):
        b = self._new("virt")
        b.is_bar = True
        b.waits = list(self.since)
        self.since = []
        self.cur_bar = b
        self.load = {e: 0.0 for e in ENGS}

    def emit(self, eng, fn, reads=(), writes=(), extra=(), is_dma=False):
        op = self._new(eng)
        op.fn = _freeze(fn)
        op.is_dma = is_dma
        op.cost, op.lat = _estimate(eng, op.fn, is_dma)
        self.load[eng] += op.cost
        waits = set()
        if PSUM_EXCL:
            for b in reads:
                if b.excl:
                    for r in b.r:
                        if r.eng != eng:
                            waits.add(r)
        for b in reads:
            if b.w is not None:
                waits.add(b.w)
        for b in writes:
            if b.w is not None:
                waits.add(b.w)
            for r in b.r:
                waits.add(r)
        for w in extra:
            if w is not None:
                waits.add(w)
        if self.cur_bar is not None:
            waits.add(self.cur_bar)
        waits.discard(op)
        op.waits = list(waits)
        for b in reads:
            b.r.append(op)
        for b in writes:
            b.w = op
            b.r = []
        self.since.append(op)
        return op

    def pe(self, fn, reads=(), writes=(), extra=()):
        return self.emit("pe", fn, reads, writes, extra)

    def act(self, fn, reads=(), writes=(), extra=()):
        return self.emit("act", fn, reads, writes, extra)

    def dve(self, fn, reads=(), writes=(), extra=()):
        return self.emit("dve", fn, reads, writes, extra)

    def pool(self, fn, reads=(), writes=(), extra=()):
        return self.emit("pool", fn, reads, writes, extra)

    def on(self, eng, fn, reads=(), writes=(), extra=()):
        if eng == "any":
            return self.any(fn, reads, writes, extra)
        return self.emit(eng, fn, reads, writes, extra)

    def any(self, fn, reads=(), writes=(), extra=()):
        f = _freeze(fn)
        best = None
        for e in ("dve", "pool"):
            c, _ = _estimate(e, f, False)
            tot = self.load[e] + c
            if best is None or tot < best[0]:
                best = (tot, e)
        return self.emit(best[1], fn, reads, writes, extra)

    def dma(self, out, in_, reads=(), writes=(), extra=()):
        return self.emit("sp", lambda e: e.dma_start(out=out, in_=in_), reads, writes, extra, is_dma=True)

    def schedule(self, reorder=True):
        import heapq
        ops = self.all
        for op in ops:
            op.succs = []
        for op in ops:
            for w in op.waits:
                w.succs.append(op)
        for op in reversed(ops):
            m = 0.0
            for s_ in op.succs:
                l_ = s_.prio + (0.0 if op.is_bar else (LAT_S if s_.eng == op.eng else LAT_X))
                if l_ > m:
                    m = l_
            op.prio = m + op.lat
        order = {e: [] for e in ENGS}
        if not reorder:
            for op in ops:
                if not op.is_bar:
                    order[op.eng].append(op)
            return order
        for op in ops:
            op.nrem = len(op.waits)
            op.ready = 0.0
            op.fin = None
        fixed = [e for e in ENGS if e not in REORDER_ENGS]
        lastop = {}
        for op in ops:
            if op.is_bar or op.eng not in fixed:
                continue
            p_ = lastop.get(op.eng)
            if p_ is not None and p_ not in op.waits:
                p_.succs.append(op)
                op.nrem += 1
            lastop[op.eng] = op
        future = {e: [] for e in ENGS}
        now = {e: [] for e in ENGS}
        free = {e: 0.0 for e in ENGS}

        def release(op):
            for s_ in op.succs:
                if op.is_bar:
                    t = op.fin
                elif s_.is_bar:
                    t = op.fin
                elif s_.eng == op.eng:
                    t = op.fin + (0.0 if op.eng == "pe" else LAT_S)
                else:
                    t = op.fin + LAT_X
                if t > s_.ready:
                    s_.ready = t
                s_.nrem -= 1
                if s_.nrem == 0:
                    if s_.is_bar:
                        s_.fin = s_.ready
                        release(s_)
                    else:
                        heapq.heappush(future[s_.eng], (s_.ready, s_.idx, s_))

        import sys
        sys.setrecursionlimit(100000)
        roots = [op for op in ops if op.nrem == 0]
        for op in roots:
            if op.is_bar:
                op.fin = 0.0
                release(op)
            else:
                heapq.heappush(future[op.eng], (0.0, op.idx, op))
        nleft = sum(1 for op in ops if not op.is_bar)
        while nleft > 0:
            best = None
            for e in ENGS:
                f = future[e]
                nw = now[e]
                while f and f[0][0] <= free[e]:
                    r_, i_, o_ = heapq.heappop(f)
                    heapq.heappush(nw, (-o_.prio, o_.idx, o_))
                if nw:
                    st = free[e]
                elif f:
                    st = f[0][0]
                else:
                    continue
                if best is None or st < best[0]:
                    best = (st, e)
            st, e = best
            if now[e]:
                _, _, op = heapq.heappop(now[e])
            else:
                _, _, op = heapq.heappop(future[e])
            if op.is_dma:
                free[e] = st + op.cost
                op.fin = st + op.lat
            else:
                free[e] = st + op.cost
                op.fin = st + op.cost
            order[e].append(op)
            nleft -= 1
            release(op)
        self.est_ns = max(free.values())
        return order

    def build(self, sems, dsems, reorder=True):
        order = self.schedule(reorder)
        if reorder:
            print('[sched] est_us=%.1f' % (self.est_ns / 1e3), {e: len(order[e]) for e in ENGS})
        for e in ENGS:
            for i, op in enumerate(order[e]):
                op.pos = i

        def skip_same(w_eng, eng):
            return w_eng == eng and (eng == "pe" or not SAME_ENG_SYNC)

        dcnt = [0] * NDSEM
        prev_on_sem = [None] * NDSEM
        dma_prev = {}
        nd = 0
        for op in order["sp"]:
            k = nd % NDSEM
            nd += 1
            op.dsem = k
            dcnt[k] += 16
            op.dval = dcnt[k]
            dma_prev[id(op)] = prev_on_sem[k]
            prev_on_sem[k] = op
        final_dvals = list(dcnt)
        for b in self.all:
            if b.is_bar:
                pe_ = {}
                pd_ = {}
                for w in b.waits:
                    if w.is_bar:
                        continue
                    if w.is_dma:
                        if pd_.get(w.dsem, 0) < w.dval:
                            pd_[w.dsem] = w.dval
                    else:
                        c = pe_.get(w.eng)
                        if c is None or c.pos < w.pos:
                            pe_[w.eng] = w
                b.per_eng = pe_
                b.per_dsem = pd_
        for op in self.all:
            if op.is_bar:
                for w in op.per_eng.values():
                    w.marked = True
                continue
            for w in op.waits:
                if w.is_bar or w.is_dma:
                    continue
                if not skip_same(w.eng, op.eng):
                    w.marked = True
        for e in ENGS:
            cnt = 0
            for op in order[e]:
                if not op.is_dma and op.marked:
                    cnt += 1
                    op.semval = cnt

        def run_engine(ename, eng):
            waited = {}

            def need(semkey, sem, val):
                if waited.get(semkey, 0) >= val:
                    return
                eng.wait_ge(sem, val)
                waited[semkey] = val

            for op in order[ename]:
                for w in op.waits:
                    if w.is_bar:
                        for we, wo in w.per_eng.items():
                            if not (we == ename and ename == "pe"):
                                need(("e", we), sems[we], wo.semval)
                        for k, v in w.per_dsem.items():
                            need(("d", k), dsems[k], v)
                    elif w.is_dma:
                        need(("d", w.dsem), dsems[w.dsem], w.dval)
                    elif not skip_same(w.eng, ename):
                        need(("e", w.eng), sems[w.eng], w.semval)
                if op.is_dma:
                    p = dma_prev[id(op)]
                    if p is not None:
                        need(("d", p.dsem), dsems[p.dsem], p.dval)
                ins = op.fn(eng)
                if op.is_dma:
                    ins.then_inc(dsems[op.dsem], 16)
                elif op.marked:
                    ins.then_inc(sems[ename], 1)
            if ename == "sp":
                for k in range(NDSEM):
                    if final_dvals[k] > 0:
                        need(("d", k), dsems[k], final_dvals[k])

        return run_engine


class Ring:
    def __init__(self, tiles):
        self.tiles = tiles
        self.bufs = [Buf() for _ in tiles]
        self.i = 0

    def next(self):
        k = self.i % len(self.tiles)
        self.i += 1
        return self.tiles[k], self.bufs[k]


def build_program(L=2, dbg=False, units_enabled=None):
    nc = bass.Bass("TRN2", target_bir_lowering=False)

    def din(name, shape, dt=F32):
        return nc.dram_tensor(name, list(shape), dt, kind="ExternalInput").ap()

    def dout(name, shape, dt=F32):
        return nc.dram_tensor(name, list(shape), dt, kind="ExternalOutput").ap()

    x_in = din("x", [TOK, D_MODEL])
    modv = din("modv", [128, 8])
    ck_in = din("ck", [2, 512, 512])
    cv_in = din("cv", [2, 512, 512])
    srf_in = din("srf", [2, 4, 64, 64])
    srb_in = din("srb", [2, 4, 64, 64])
    shf_in = din("shf", [2, 4, 64, 64])
    shb_in = din("shb", [2, 4, 64, 64])
    normg_in = din("normg", [2, 128, 8])
    wada_in = din("wada", [2, 128, 8, 3072])
    bada_in = din("bada", [2, 3072])
    win_in = din("win", [2, 128, 8, 4352])
    wout_in = din("wout", [2, 128, 8, 1024])
    rdl_in = din("rdl", [2, 8])
    qng_in = din("qng", [2, 64])
    kng_in = din("kng", [2, 64])
    dlam_in = din("dlam", [2, 256])
    hlb_in = din("hlb", [512])
    ropec_in = din("ropec", [128, 16, 64])
    ropes_in = din("ropes", [128, 16, 64])
    qmask_in = din("qmask", [8, 2048], BF16)
    kmask_in = din("kmask", [8, 2560], BF16)
    keep_in = din("keep", [128, 32])
    identb_in = din("identb", [128, 128], BF16)
    identf_in = din("identf", [128, 128])
    cst_in = din("cst", [128, 11, 128])
    ind_in = din("ind", [128, 4])

    y_out = dout("y", [TOK, D_MODEL])
    nk_out = dout("nk", [2, TOK, 512])
    nv_out = dout("nv", [2, TOK, 512])
    nsrf_out = dout("nsrf", [2, 8, 4, 64, 64])
    nsrb_out = dout("nsrb", [2, 8, 4, 64, 64])
    nshf_out = dout("nshf", [2, 8, 4, 64, 64])
    nshb_out = dout("nshb", [2, 8, 4, 64, 64])
    xs_scr = nc.dram_tensor("xs_scr", [TOK, D_MODEL], F32, kind="Internal").ap()
    dbg_out = dout("dbgmix", [2, 128, 8, TOK], BF16) if dbg else None

    es = ExitStack()
    with es:
        def sb(name, shape, dt):
            return es.enter_context(nc.sbuf_tensor("sb_" + name, list(shape), dt))

        hT = sb("hT", [128, 8, TOK], BF16)
        mixT = sb("mixT", [128, 8, TOK], BF16)
        wstage = sb("wstage", [128, 8 * 512], F32)
        wbf = sb("wbf", [128, 8 * 640], BF16)
        arena = sb("arena", [128, 61440], mybir.dt.uint8)
        cst = sb("cst", [128, 11, 128], F32)
        ind = sb("ind", [128, 4], F32)
        ropec = sb("ropec", [128, 16, 64], F32)
        ropes = sb("ropes", [128, 16, 64], F32)
        identb = sb("identb", [128, 128], BF16)
        identf = sb("identf", [128, 128], F32)
        keep = sb("keep", [128, 32], F32)
        gate_b = sb("gate_b", [128, 1024], F32)
        modt = sb("modt", [128, 8], F32)
        smod = sb("smod", [128, 8], F32)
        normg = sb("normg", [128, 8], F32)
        modT = sb("modT", [128, 2, 8], F32)
        modA = sb("modA", [128, 8], F32)
        small = sb("small", [128, 64], F32)
        rdl = sb("rdl", [128, 8], F32)
        lg = sb("lg", [128, 8], F32)
        g4 = sb("g4", [128, 256], F32)
        lball = sb("lball", [128, 2, 256], F32)
        lb2 = sb("lb2", [128, 256], F32)
        omlb2 = sb("omlb2", [128, 256], F32)
        cneg = sb("cneg", [128, 8], F32)
        zerosb = sb("zerosb", [128, 512], BF16)
        lgrow = sb("lgrow", [128, 4], F32)
        lgrow1 = sb("lgrow1", [128, 4], F32)
        dtm = sb("dtm", [128, 2, 128], F32)
        qd = sb("qd", [128, 2, 128], F32)
        kd = sb("kd", [128, 2, 128], F32)
        e12 = sb("e12", [128, 2, 128], F32)
        stS = sb("stS", [128, 2, 128], F32)

        n_f32_512 = 3
        r512 = Ring([sb("r512_%d" % i, [128, 512], F32) for i in range(n_f32_512)])
        r256 = Ring([sb("r256_%d" % i, [128, 256], F32) for i in range(8)])
        r128 = Ring([sb("r128_%d" % i, [128, 128], F32) for i in range(8)])
        rb256 = Ring([sb("rb256_%d" % i, [128, 256], BF16) for i in range(3)])
        rb128 = Ring([sb("rb128_%d" % i, [128, 128], BF16) for i in range(8)])
        rpt = Ring([sb("rpt_%d" % i, [128, 512], BF16) for i in range(3)])
        rs = Ring([sb("rs_%d" % i, [128, 8], F32) for i in range(16)])

        banks = [es.enter_context(nc.psum_tensor("bank%d" % i, [128, 512], F32)) for i in range(8)]
        banksb = [b.bitcast(BF16) for b in banks]
        bankB = [Buf("bank%d" % i, excl=True) for i in range(8)]

        sems = {e: es.enter_context(nc.semaphore("s_" + e)) for e in ENGS}
        dsems = [es.enter_context(nc.semaphore("d%d" % k)) for k in range(NDSEM)]

        P = Prog(nc)

        B_hT = [Buf() for _ in range(NT)]
        B_mixT = [[Buf() for _ in range(NT)] for _ in range(8)]
        B_wstage = Buf()
        B_wbf = Buf()
        B_cst = Buf()
        B_misc = Buf()
        B_gate = Buf()
        B_modAB = Buf()
        B_pair = Buf()
        B_stS = [Buf(), Buf()]

        def arena_view(off_bytes, shape, dt):
            n = 1
            for s in shape[1:]:
                n *= s
            esz = 2 if dt == BF16 else 4
            a = arena[:, off_bytes:off_bytes + n * esz].bitcast(dt)
            if len(shape) == 2:
                return a
            if len(shape) == 3:
                return a.rearrange("p (a b) -> p a b", a=shape[1], b=shape[2])
            if len(shape) == 4:
                return a.rearrange("p (a b c) -> p a b c", a=shape[1], b=shape[2], c=shape[3])
            raise ValueError

        xring = Ring([arena_view(36864 + i * 4096, [128, 1024], F32) for i in range(3)])
        xhring = Ring([arena_view(49152 + i * 2048, [128, 1024], BF16) for i in range(2)])
        B_xs = [Buf() for _ in range(NT)]

        M1, L1, M2, L2, IOTA1, IOTA2, COLA, COLB, TRIF, TRIB, BM = [cst[:, i, :] for i in range(11)]

        P.dma(cst[:], cst_in, writes=[B_cst])
        P.dma(ind[:], ind_in, writes=[B_cst])
        P.dma(ropec[:], ropec_in, writes=[B_cst])
        P.dma(ropes[:], ropes_in, writes=[B_cst])
        P.dma(identb[:], identb_in, writes=[B_cst])
        P.dma(identf[:], identf_in, writes=[B_cst])
        P.dma(keep[:], keep_in, writes=[B_cst])
        P.dma(modt[:], modv, writes=[B_cst])
        hlb, hlbb = r512.next()
        P.dma(hlb[:], hlb_in.partition_broadcast(128), writes=[hlbb])
        P.pool(lambda e: e.memset(cneg[:], -0.5), writes=[B_cst])
        P.pool(lambda e: e.memset(zerosb[:], 0.0), writes=[B_cst])
        t_, tb_ = rs.next()
        P.act(lambda e, t_=t_: e.activation(out=t_[:, 0:8], in_=modt[:], func=AF.Tanh, scale=0.5), reads=[B_cst], writes=[tb_])
        P.dve(lambda e, t_=t_: e.scalar_tensor_tensor(out=smod[:], in0=t_[:, 0:8], scalar=1.0, in1=modt[:], op0=ALU.add, op1=ALU.mult),
              reads=[tb_, B_cst], writes=[B_cst])
        P.dve(lambda e: e.tensor_scalar(out=smod[:], in0=smod[:], scalar1=0.5, scalar2=None, op0=ALU.mult), reads=[B_cst], writes=[B_cst])
        P.act(lambda e: e.activation(out=hlb[:], in_=hlb[:], func=AF.Exp), reads=[hlbb], writes=[hlbb])
        den_, denb_ = r256.next()
        P.dve(lambda e: e.tensor_tensor(out=den_[:], in0=hlb[:, 0:256], in1=hlb[:, 256:512], op=ALU.add), reads=[hlbb], writes=[denb_])
        P.dve(lambda e: e.reciprocal(out=den_[:], in_=den_[:]), reads=[denb_], writes=[denb_])
        P.dve(lambda e: e.tensor_tensor(out=hlb[:, 0:256], in0=hlb[:, 0:256], in1=den_[:], op=ALU.mult), reads=[hlbb, denb_], writes=[hlbb])
        P.dve(lambda e: e.tensor_tensor(out=hlb[:, 256:512], in0=hlb[:, 256:512], in1=den_[:], op=ALU.mult), reads=[hlbb, denb_], writes=[hlbb])
        P.dve(lambda e: e.tensor_tensor(out=lball[:, 0, :], in0=hlb[:, 0:256], in1=hlb[:, 0:256], op=ALU.subtract), reads=[hlbb], writes=[B_cst])
        P.dve(lambda e: e.tensor_tensor(out=lball[:, 1, :], in0=hlb[:, 0:256], in1=hlb[:, 256:512], op=ALU.add), reads=[hlbb], writes=[B_cst])
        P.dve(lambda e: e.tensor_tensor(out=lball[:, 1, :], in0=lball[:, 1, :], in1=hlb[:, 0:256], op=ALU.subtract), reads=[hlbb, B_cst], writes=[B_cst])

        def rstd_from_ss(ss_ap, ssb, n, mult, add):
            t1, b1 = rs.next()
            P.dve(lambda e: e.tensor_scalar(out=t1[:, 0:n], in0=ss_ap, scalar1=mult, scalar2=add, op0=ALU.mult, op1=ALU.add),
                  reads=[ssb], writes=[b1])
            t2, b2 = rs.next()
            P.pool(lambda e: e.tensor_tensor(out=t2[:, 0:n], in0=t1[:, 0:n], in1=cneg[:, 0:n], op=ALU.pow), reads=[b1, B_cst], writes=[b2])
            return t2, b2

        def rope(src, srcbufs, dst, dstbufs, T, G, eng_a, eng_b):
            W = G * 64
            t1, b1 = r256.next()
            t2, b2 = r256.next()
            cv_ = ropec[:, T, :]
            sv_ = ropes[:, T, :].rearrange("p (h j i) -> p h j i", h=2, j=2, i=16)
            src3 = src.rearrange("p (g d) -> p g d", g=G, d=64)
            src5 = src.rearrange("p (g h j i) -> p g h j i", g=G, h=2, j=2, i=16)
            t13 = t1[:, 0:W].rearrange("p (g d) -> p g d", g=G, d=64)
            t25 = t2[:, 0:W].rearrange("p (g h j i) -> p g h j i", g=G, h=2, j=2, i=16)
            P.on(eng_a, lambda e: e.tensor_tensor(out=t13, in0=src3, in1=cv_.unsqueeze(1).to_broadcast([128, G, 64]), op=ALU.mult),
                 reads=list(srcbufs) + [B_cst], writes=[b1])
            P.on(eng_b, lambda e: e.tensor_tensor(out=t25[:, :, :, 0, :], in0=src5[:, :, :, 1, :],
                                                  in1=sv_[:, :, 0, :].unsqueeze(1).to_broadcast([128, G, 2, 16]), op=ALU.mult),
                 reads=list(srcbufs) + [B_cst], writes=[b2])
            P.on(eng_b, lambda e: e.tensor_tensor(out=t25[:, :, :, 1, :], in0=src5[:, :, :, 0, :],
                                                  in1=sv_[:, :, 1, :].unsqueeze(1).to_broadcast([128, G, 2, 16]), op=ALU.mult),
                 reads=list(srcbufs) + [B_cst], writes=[b2])
            P.on(eng_a, lambda e: e.tensor_tensor(out=dst, in0=t1[:, 0:W], in1=t2[:, 0:W], op=ALU.add), reads=[b1, b2], writes=list(dstbufs))

        def tslice(T):
            return slice(T * 128, (T + 1) * 128)

        def setup_layer(l):
            stg = [wstage[:, 0:4096].rearrange("p (k n) -> p k n", k=8, n=512), arena_view(16384, [128, 8, 512], F32)]
            stgB = [B_wstage, Buf()]
            smb = arena_view(32768, [128, 8, 128], F32)
            B_smb = Buf()
            P.dve(lambda e: e.tensor_copy(out=smb, in_=smod[:].unsqueeze(2).to_broadcast([128, 8, 128])), reads=[B_cst], writes=[B_smb])
            P.dma(normg[:], normg_in[l], writes=[B_misc])
            P.dma(rdl[:], rdl_in[l].partition_broadcast(128), writes=[B_misc])
            dlam, dlamb = r256.next()
            P.dma(dlam[:], dlam_in[l].partition_broadcast(128), writes=[dlamb])
            P.dma(g4[:, 0:64], qng_in[l].partition_broadcast(128), writes=[B_misc])
            P.dma(g4[:, 64:128], qng_in[l].partition_broadcast(128), writes=[B_misc])
            P.dma(g4[:, 128:192], kng_in[l].partition_broadcast(128), writes=[B_misc])
            P.dma(g4[:, 192:256], kng_in[l].partition_broadcast(128), writes=[B_misc])
            for cb in range(6):
                st_, stb_ = stg[cb % 2], stgB[cb % 2]
                P.dma(st_, wada_in[l][:, :, cb * 512:(cb + 1) * 512], writes=[stb_])
                bt, btb = r512.next()
                P.dma(bt[:], bada_in[l][cb * 512:(cb + 1) * 512].partition_broadcast(128), writes=[btb])
                bk = cb % 4
                for kc in range(8):
                    P.pe(lambda e, kc=kc, st_=st_, bk=bk: e.matmul(banks[bk][:, 0:512], lhsT=smb[:, kc, :], rhs=st_[:, kc, :],
                                                                     start=(kc == 0), stop=(kc == 7)),
                         reads=[B_smb, stb_], writes=[bankB[bk]])
                if cb >= 4:
                    P.dve(lambda e, bk=bk, bt=bt, cb=cb: e.tensor_tensor(out=gate_b[:, (cb - 4) * 512:(cb - 3) * 512], in0=banks[bk][:, 0:512],
                                                                          in1=bt[:], op=ALU.add),
                          reads=[bankB[bk], btb], writes=[B_gate])
                else:
                    P.dve(lambda e, bk=bk, bt=bt: e.tensor_tensor(out=bt[:], in0=banks[bk][:, 0:512], in1=bt[:], op=ALU.add),
                          reads=[bankB[bk], btb], writes=[btb])
                    which = cb // 2
                    tb = 4 + (cb % 2)
                    for jj in range(4):
                        kc = (cb % 2) * 4 + jj
                        P.pe(lambda e, jj=jj, bt=bt, tb=tb: e.transpose(banks[tb][:, jj * 128:(jj + 1) * 128], bt[:, jj * 128:(jj + 1) * 128], identf[:]),
                             reads=[btb, B_cst], writes=[bankB[tb]])
                        P.act(lambda e, jj=jj, tb=tb, which=which, kc=kc: e.copy(out=modT[:, which, kc:kc + 1], in_=banks[tb][:, jj * 128:jj * 128 + 1]),
                              reads=[bankB[tb]], writes=[B_modAB])
            P.dve(lambda e: e.scalar_tensor_tensor(out=modA[:], in0=modT[:, 1, :], scalar=1.0, in1=normg[:], op0=ALU.add, op1=ALU.mult),
                  reads=[B_modAB, B_misc], writes=[B_modAB])
            pr, prb = r256.next()
            P.dve(lambda e: e.tensor_tensor(out=pr[:, 0:64], in0=dlam[:, 0:64], in1=dlam[:, 64:128], op=ALU.mult), reads=[dlamb], writes=[prb])
            P.dve(lambda e: e.tensor_tensor(out=pr[:, 64:128], in0=dlam[:, 128:192], in1=dlam[:, 192:256], op=ALU.mult), reads=[dlamb], writes=[prb])
            s12, s12b = rs.next()
            P.dve(lambda e: e.tensor_reduce(out=s12[:, 0:2], in_=pr[:, 0:128].rearrange("p (a d) -> p a d", a=2, d=64), axis=AX.X, op=ALU.add),
                  reads=[prb], writes=[s12b])
            P.act(lambda e: e.activation(out=s12[:, 0:2], in_=s12[:, 0:2], func=AF.Exp), reads=[s12b], writes=[s12b])
            lam_init = 0.8 - 0.6 * math.exp(-0.3 * l)
            P.dve(lambda e: e.tensor_tensor(out=small[:, 0:1], in0=s12[:, 1:2], in1=s12[:, 0:1], op=ALU.subtract), reads=[s12b], writes=[B_misc])
            P.dve(lambda e: e.tensor_scalar(out=small[:, 0:1], in0=small[:, 0:1], scalar1=-lam_init, scalar2=None, op0=ALU.add),
                  reads=[B_misc], writes=[B_misc])
            P.act(lambda e: e.activation(out=lg[:], in_=rdl[:], func=AF.Exp, scale=-1.0), reads=[B_misc], writes=[B_misc])
            P.act(lambda e: e.activation(out=lg[:], in_=lg[:], func=AF.Ln, bias=1.0), reads=[B_misc], writes=[B_misc])
            P.dve(lambda e: e.tensor_scalar(out=lg[:], in0=lg[:], scalar1=-1.0, scalar2=None, op0=ALU.mult), reads=[B_misc], writes=[B_misc])

        def norm_phase(l):
            src = x_in if l == 0 else xs_scr
            for T in range(NT):
                xt, xb_ = xring.next()
                P.dma(xt[:], src[T * 128:(T + 1) * 128, :], reads=([B_xs[T]] if l > 0 else []), writes=[xb_])
                xh, xhb = xhring.next()
                ss, ssb = rs.next()
                P.act(lambda e, xt=xt, xh=xh, ss=ss: e.activation(out=xh[:], in_=xt[:], func=AF.Square, accum_out=ss[:, 0:1]),
                      reads=[xb_], writes=[xhb, ssb])
                rstd, rb_ = rstd_from_ss(ss[:, 0:1], ssb, 1, 1.0 / D_MODEL, EPS)
                P.dve(lambda e, xt=xt, xh=xh, rstd=rstd: e.tensor_scalar(out=xh[:], in0=xt[:], scalar1=rstd[:, 0:1], scalar2=None, op0=ALU.mult),
                      reads=[xb_, rb_], writes=[xhb])
                bk = 6 + (T % 2)
                for kc in range(8):
                    P.pe(lambda e, kc=kc, xh=xh, bk=bk: e.transpose(banksb[bk][:, kc * 128:(kc + 1) * 128], xh[:, kc * 128:(kc + 1) * 128], identb[:]),
                         reads=[xhb, B_cst], writes=[bankB[bk]])
                for kc in range(8):
                    if kc % 2 == 0:
                        P.dve(lambda e, kc=kc, bk=bk, T=T: e.tensor_scalar(out=hT[:, kc, tslice(T)], in0=banksb[bk][:, kc * 128:(kc + 1) * 128],
                                                                            scalar1=modA[:, kc:kc + 1], scalar2=modT[:, 0, kc:kc + 1],
                                                                            op0=ALU.mult, op1=ALU.add),
                              reads=[bankB[bk], B_modAB], writes=[B_hT[T]])
                    else:
                        P.act(lambda e, kc=kc, bk=bk, T=T: e.activation(out=hT[:, kc, tslice(T)], in_=banksb[bk][:, kc * 128:(kc + 1) * 128],
                                                                         func=AF.Identity, scale=modA[:, kc:kc + 1], bias=modT[:, 0, kc:kc + 1]),
                              reads=[bankB[bk], B_modAB], writes=[B_hT[T]])

        def load_unit_weights(l, u):
            W = UNIT_W[u]
            wb = wbf[:, 0:8 * W].rearrange("p (k n) -> p k n", k=8, n=W)
            engs = ["dve", "pool", "act", "dve", "pool", "act", "dve", "pool"]
            for (a, b) in [(0, 512)] + ([(512, W)] if W > 512 else []):
                wd = b - a
                ws = wstage[:, 0:8 * wd].rearrange("p (k n) -> p k n", k=8, n=wd)
                P.dma(ws, win_in[l][:, :, UNIT_OFF[u] + a:UNIT_OFF[u] + b], writes=[B_wstage])
                for kc in range(8):
                    if engs[kc] == "act":
                        P.act(lambda e, kc=kc, ws=ws, a=a, b=b: e.copy(out=wb[:, kc, a:b], in_=ws[:, kc, :]), reads=[B_wstage], writes=[B_wbf])
                    else:
                        P.on(engs[kc], lambda e, kc=kc, ws=ws, a=a, b=b: e.tensor_copy(out=wb[:, kc, a:b], in_=ws[:, kc, :]), reads=[B_wstage], writes=[B_wbf])
            return wb

        def project(wb, T, bk, c0, c1):
            for kc in range(8):
                P.pe(lambda e, kc=kc: e.matmul(banks[bk][:, 0:c1 - c0], lhsT=hT[:, kc, tslice(T)], rhs=wb[:, kc, c0:c1],
                                               start=(kc == 0), stop=(kc == 7)),
                     reads=[B_hT[T], B_wbf], writes=[bankB[bk]])

        def mixed_out(mt, mtb, chunk, T, tbank, eng="act"):
            P.pe(lambda e: e.transpose(banksb[tbank][:, 0:128], mt, identb[:]), reads=[mtb, B_cst], writes=[bankB[tbank]])
            if eng == "act":
                P.act(lambda e: e.copy(out=mixT[:, chunk, tslice(T)], in_=banksb[tbank][:, 0:128]), reads=[bankB[tbank]], writes=[B_mixT[chunk][T]])
            else:
                P.dve(lambda e: e.tensor_copy(out=mixT[:, chunk, tslice(T)], in_=banksb[tbank][:, 0:128]), reads=[bankB[tbank]], writes=[B_mixT[chunk][T]])

        def diff_unit(l, h, DB):
            u = 4 + h
            chunk = UNIT_CHUNK[u]
            if h == 0:
                P.barrier()
            par = h % 2
            base = par * 27728
            QT = arena_view(base + 0, [128, 2, TOK], BF16)
            KT = arena_view(base + 8192, [128, 2, 2560], BF16)
            V = arena_view(base + 18432, [128, NKT, 130], BF16)
            sg = arena_view(base + 23632, [128, NT, 128], BF16)
            ckst = arena_view(55456, [128, 4, 128], F32)
            cvst = arena_view(57504, [128, 4, 128], F32)
            ckb = arena_view(59552, [128, 4, 128], BF16)
            B_QT, B_KT, B_V, B_sg, B_qm = DB["set"][par]
            B_ck = DB["ck"]
            wb = load_unit_weights(l, u)
            import os as _os
            DD0 = _os.environ.get('DIFFDBG', '')
            if 'nomask' not in DD0:
                for c in range(2):
                    P.dma(QT[64:72, c, :], qmask_in, writes=[B_qm])
                    P.dma(KT[64:72, c, :], kmask_in, writes=[B_qm])
            if 'noctx' not in DD0:
                P.dma(ckst, ck_in[l].rearrange("(t p) n -> p t n", p=128)[:, :, h * 128:(h + 1) * 128], writes=[B_ck])
                P.dma(cvst, cv_in[l].rearrange("(t p) n -> p t n", p=128)[:, :, h * 128:(h + 1) * 128], writes=[B_ck])
                P.pool(lambda e: e.memset(V[:, :, 128:130], 1.0), writes=B_V)
                P.dve(lambda e: e.tensor_copy(out=ckb, in_=ckst), reads=[B_ck], writes=[B_ck])
                for pt in range(4):
                    bk = 5
                    for c in range(2):
                        P.pe(lambda e, pt=pt, c=c, bk=bk: e.transpose(banksb[bk][0:64, c * 128:(c + 1) * 128], ckb[:, pt, c * 64:(c + 1) * 64], identb[:]),
                             reads=[B_ck, B_cst], writes=[bankB[bk]])
                    P.act(lambda e, pt=pt, bk=bk: e.copy(out=KT[0:64, :, 2048 + pt * 128:2048 + (pt + 1) * 128],
                                                         in_=banksb[bk][0:64, 0:256].rearrange("p (c t) -> p c t", c=2, t=128)),
                          reads=[bankB[bk]], writes=[B_KT[16 + pt]])
                    P.any(lambda e, pt=pt: e.tensor_copy(out=V[:, 16 + pt, 0:128], in_=cvst[:, pt, :]), reads=[B_ck], writes=[B_V[16 + pt]])

            if 'noA' in DD0:
                return
            for T in range(NT):
                zb = 6 + (T % 2)
                project(wb, T, zb, 0, 512)
                z = banks[zb]
                sq, sqb = r256.next()
                P.act(lambda e, z=z, sq=sq: e.activation(out=sq[:], in_=z[:, 0:256], func=AF.Square), reads=[bankB[zb]], writes=[sqb])
                ss, ssb = rs.next()
                P.dve(lambda e, sq=sq, ss=ss: e.tensor_reduce(out=ss[:, 0:4], in_=sq[:].rearrange("p (g d) -> p g d", g=4, d=64), axis=AX.X, op=ALU.add),
                      reads=[sqb], writes=[ssb])
                rstd, rb_ = rstd_from_ss(ss[:, 0:4], ssb, 4, 1.0 / 64, EPS)
                nq, nqb = r256.next()
                P.dve(lambda e, z=z, nq=nq, rstd=rstd: e.tensor_tensor(out=nq[:].rearrange("p (g d) -> p g d", g=4, d=64),
                                                                      in0=z[:, 0:256].rearrange("p (g d) -> p g d", g=4, d=64),
                                                                      in1=rstd[:, 0:4].unsqueeze(2).to_broadcast([128, 4, 64]), op=ALU.mult),
                      reads=[bankB[zb], rb_], writes=[nqb])
                P.any(lambda e, nq=nq: e.tensor_tensor(out=nq[:], in0=nq[:], in1=g4[:], op=ALU.mult), reads=[nqb, B_misc], writes=[nqb])
                if 'nonk' not in DD0:
                    P.dma(nk_out[l][T * 128:(T + 1) * 128, h * 128:(h + 1) * 128], nq[:, 128:256], reads=[nqb])
                rt, rtb = rb256.next()
                import os as _os
                rope(nq[:], [nqb], rt[:], [rtb], T, 4, "any", "any")
                if 'noT' not in DD0:
                    tb = 5
                    for g in range(4):
                        P.pe(lambda e, g=g, rt=rt, tb=tb: e.transpose(banksb[tb][0:64, g * 128:(g + 1) * 128], rt[:, g * 64:(g + 1) * 64], identb[:]),
                             reads=[rtb, B_cst], writes=[bankB[tb]])
                    if 'noTq' not in DD0:
                      P.act(lambda e, tb=tb, T=T: e.copy(out=QT[0:64, :, tslice(T)], in_=banksb[tb][0:64, 0:256].rearrange("p (c t) -> p c t", c=2, t=128)),
                          reads=[bankB[tb]], writes=[B_QT[T]])
                    if 'noTk' not in DD0:
                      P.act(lambda e, tb=tb, T=T: e.copy(out=KT[0:64, :, tslice(T)], in_=banksb[tb][0:64, 256:512].rearrange("p (c t) -> p c t", c=2, t=128)),
                          reads=[bankB[tb]], writes=[B_KT[T]])
                vst, vstb = r128.next()
                P.dve(lambda e, z=z, vst=vst: e.tensor_copy(out=vst[:], in_=z[:, 256:384]), reads=[bankB[zb]], writes=[vstb])
                if 'nonk' not in DD0:
                    P.dma(nv_out[l][T * 128:(T + 1) * 128, h * 128:(h + 1) * 128], vst[:], reads=[vstb])
                P.any(lambda e, vst=vst, T=T: e.tensor_copy(out=V[:, T, 0:128], in_=vst[:]), reads=[vstb], writes=[B_V[T]])
                th, thb = r128.next()
                P.act(lambda e, z=z, th=th: e.activation(out=th[:], in_=z[:, 384:512], func=AF.Tanh, scale=0.5), reads=[bankB[zb]], writes=[thb])
                P.dve(lambda e, z=z, th=th, T=T: e.scalar_tensor_tensor(out=sg[:, T, :], in0=th[:], scalar=1.0, in1=z[:, 384:512], op0=ALU.add, op1=ALU.mult),
                      reads=[thb, bankB[zb]], writes=[B_sg[T]])
            import os as _os
            DD = _os.environ.get('DIFFDBG', '')
            if 'noB' in DD:
                return
            KR = 64 if 'k64' in DD else 72
            lam_init = 0.8 - 0.6 * math.exp(-0.3 * l)
            c0 = 0.5 * (1.0 - lam_init)
            OB = [2, 3, 4]

            def acc(c, qi):
                a = c * 4 + qi
                return OB[a // 3], (a % 3) * 160

            for qb in range(4):
                for k in OB:
                    P.pe(lambda e, k=k: e.matmul(banks[k][:, 0:512], lhsT=zerosb[:, 0:128], rhs=zerosb[:, 0:512], start=True, stop=False, skip_group_check=True),
                         reads=[B_cst], writes=[bankB[k]])
                steps = [(c, kt) for c in range(2) for kt in range(NKT)]
                pts = {}

                def emit_st(i):
                    c, kt = steps[i]
                    sbk = i % 2
                    P.pe(lambda e, c=c, kt=kt, sbk=sbk: e.matmul(banks[sbk][:, 0:512], lhsT=KT[0:KR, c, kt * 128:(kt + 1) * 128],
                                                                 rhs=QT[0:KR, c, qb * 512:(qb + 1) * 512], start=True, stop=True),
                         reads=[B_KT[kt], B_qm] + B_QT[qb * 4:qb * 4 + 4], writes=[bankB[sbk]])
                    pt_, ptb = rpt.next()
                    P.act(lambda e, sbk=sbk, pt_=pt_: e.activation(out=pt_[:], in_=banks[sbk][:, 0:512], func=AF.Exp, scale=0.125),
                          reads=[bankB[sbk]], writes=[ptb])
                    pts[i] = (pt_, ptb)

                def emit_pv(i):
                    c, kt = steps[i]
                    pt_, ptb = pts.pop(i)
                    for qi in range(4):
                        bk, off = acc(c, qi)
                        P.pe(lambda e, qi=qi, bk=bk, off=off, pt_=pt_, kt=kt: e.matmul(banks[bk][:, off:off + 129], lhsT=pt_[:, qi * 128:(qi + 1) * 128],
                                                                                       rhs=V[:, kt, 0:129], start=False, stop=(kt == NKT - 1), skip_group_check=True),
                             reads=[ptb, B_V[kt]], writes=[bankB[bk]])

                emit_st(0)
                for i in range(len(steps)):
                    if i + 1 < len(steps):
                        emit_st(i + 1)
                    emit_pv(i)
                for qi in range(4):
                    T = qb * 4 + qi
                    b0, o0 = acc(0, qi)
                    b1, o1 = acc(1, qi)
                    r01, r01b = rs.next()
                    P.dve(lambda e, r01=r01: e.reciprocal(out=r01[:, 0:1], in_=banks[b0][:, o0 + 128:o0 + 129]), reads=[bankB[b0]], writes=[r01b])
                    P.dve(lambda e, r01=r01: e.reciprocal(out=r01[:, 1:2], in_=banks[b1][:, o1 + 128:o1 + 129]), reads=[bankB[b1]], writes=[r01b])
                    P.dve(lambda e, r01=r01: e.tensor_tensor(out=r01[:, 1:2], in0=r01[:, 1:2], in1=small[:, 0:1], op=ALU.mult), reads=[r01b, B_misc], writes=[r01b])
                    d, db = r128.next()
                    P.dve(lambda e, d=d, r01=r01: e.tensor_scalar(out=d[:], in0=banks[b0][:, o0:o0 + 128], scalar1=r01[:, 0:1], scalar2=None, op0=ALU.mult),
                          reads=[bankB[b0], r01b], writes=[db])
                    P.dve(lambda e, d=d, r01=r01: e.scalar_tensor_tensor(out=d[:], in0=banks[b1][:, o1:o1 + 128], scalar=r01[:, 1:2], in1=d[:],
                                                                        op0=ALU.mult, op1=ALU.add),
                          reads=[bankB[b1], r01b, db], writes=[db])
                    jk, jkb = r128.next()
                    ss, ssb = rs.next()
                    P.act(lambda e, d=d, jk=jk, ss=ss: e.activation(out=jk[:], in_=d[:], func=AF.Square, accum_out=ss[:, 0:1]), reads=[db], writes=[jkb, ssb])
                    rstd, rb_ = rstd_from_ss(ss[:, 0:1], ssb, 1, 1.0 / (128 * c0 * c0), EPS / (c0 * c0))
                    mt, mtb = rb128.next()
                    P.dve(lambda e, d=d, rstd=rstd, mt=mt, T=T: e.scalar_tensor_tensor(out=mt[:], in0=d[:], scalar=rstd[:, 0:1], in1=sg[:, T, :],
                                                                                      op0=ALU.mult, op1=ALU.mult),
                          reads=[db, rb_, B_sg[T]], writes=[mtb])
                    mixed_out(mt[:], mtb, chunk, T, 5, eng="dve")

        def ret_unit(l, p, RB):
            u = p
            chunk = UNIT_CHUNK[u]
            if p == 0:
                P.barrier()
            base = p * 28672
            qT = arena_view(base + 0, [128, TOK], BF16)
            kT = arena_view(base + 4096, [128, TOK], BF16)
            ktok = arena_view(base + 8192, [128, NT, 128], BF16)
            v = arena_view(base + 12288, [128, NT, 128], BF16)
            sg = arena_view(base + 16384, [128, NT, 128], BF16)
            Sbf = arena_view(base + 20480, [128, 2, NT, 128], BF16)
            if p == 0:
                dtm_, qd_, kd_, stS_, lgrow_ = dtm, qd, kd, stS, lgrow
            else:
                dtm_ = arena_view(57344, [128, 2, 128], F32)
                qd_ = arena_view(58368, [128, 2, 128], F32)
                kd_ = arena_view(59392, [128, 2, 128], F32)
                stS_ = arena_view(60416, [128, 2, 128], F32)
                lgrow_ = lgrow1
            B_pair_, B_stS_ = RB[p]
            B_q = [Buf() for _ in range(NT)]
            B_k = [Buf() for _ in range(NT)]
            B_kt = [Buf() for _ in range(NT)]
            B_v = [Buf() for _ in range(NT)]
            B_sg = [Buf() for _ in range(NT)]
            B_S = [[Buf() for _ in range(NT)] for _ in range(2)]
            wb = load_unit_weights(l, u)
            for d_ in range(2):
                for hh in range(2):
                    col = d_ * 4 + 2 * p + hh
                    P.dve(lambda e, d_=d_, hh=hh, col=col: e.tensor_copy(out=lgrow_[hh * 64:(hh + 1) * 64, d_:d_ + 1], in_=lg[hh * 64:(hh + 1) * 64, col:col + 1]),
                          reads=[B_misc], writes=[B_pair_])
            for hh in range(2):
                e12t, e12b = r256.next()
                cf = 2 * p + hh
                cb_ = 4 + 2 * p + hh
                P.act(lambda e, cf=cf: e.activation(out=e12t[:, 0:128], in_=M1, func=AF.Exp, scale=lg[:, cf:cf + 1]), reads=[e12b, B_cst, B_misc], writes=[e12b, B_pair_])
                P.act(lambda e, cb_=cb_: e.activation(out=e12t[:, 128:256], in_=M2, func=AF.Exp, scale=lg[:, cb_:cb_ + 1]), reads=[e12b, B_cst, B_misc], writes=[e12b, B_pair_])
                P.dve(lambda e: e.tensor_tensor(out=e12t[:, 0:128], in0=e12t[:, 0:128], in1=L1, op=ALU.mult), reads=[e12b, B_pair_, B_cst], writes=[e12b, B_pair_])
                P.dve(lambda e: e.tensor_tensor(out=e12t[:, 128:256], in0=e12t[:, 128:256], in1=L2, op=ALU.mult), reads=[e12b, B_pair_, B_cst], writes=[e12b, B_pair_])
                P.dve(lambda e, hh=hh: e.tensor_tensor(out=dtm_[:, hh, :], in0=e12t[:, 0:128], in1=e12t[:, 128:256], op=ALU.add), reads=[e12b, B_pair_], writes=[e12b, B_pair_])
                P.act(lambda e, hh=hh, cf=cf: e.activation(out=kd_[:, 0, hh * 64:(hh + 1) * 64], in_=COLA[:, 0:64], func=AF.Exp, scale=lg[:, cf:cf + 1]),
                      reads=[B_cst, B_misc], writes=[B_pair_])
                P.act(lambda e, hh=hh, cb_=cb_: e.activation(out=kd_[:, 1, hh * 64:(hh + 1) * 64], in_=COLB[:, 0:64], func=AF.Exp, scale=lg[:, cb_:cb_ + 1]),
                      reads=[B_cst, B_misc], writes=[B_pair_])
            P.dve(lambda e: e.tensor_scalar(out=kd_[:], in0=kd_[:], scalar1=0.125, scalar2=None, op0=ALU.mult), reads=[B_pair_], writes=[B_pair_])
            P.act(lambda e: e.activation(out=qd_[:, 0, :], in_=IOTA1, func=AF.Exp, scale=lgrow_[:, 0:1]), reads=[B_cst, B_pair_], writes=[B_pair_])
            P.act(lambda e: e.activation(out=qd_[:, 1, :], in_=IOTA2, func=AF.Exp, scale=lgrow_[:, 1:2]), reads=[B_cst, B_pair_], writes=[B_pair_])
            P.act(lambda e: e.activation(out=lgrow_[:, 2:4], in_=lgrow_[:, 0:2], func=AF.Exp, scale=128.0), reads=[B_pair_], writes=[B_pair_])
            for T in range(NT):
                zb = T % 4
                project(wb, T, zb, 0, 512)
                z = banks[zb]
                rq, rqb = rb128.next()
                t1, b1 = r256.next()
                t2, b2 = r256.next()
                cv_ = ropec[:, T, :]
                sv_ = ropes[:, T, :].rearrange("p (h j i) -> p h j i", h=2, j=2, i=16)
                src3 = z[:, 0:256].rearrange("p (g d) -> p g d", g=4, d=64)
                src5 = z[:, 0:256].rearrange("p (g h j i) -> p g h j i", g=4, h=2, j=2, i=16)
                t13 = t1[:].rearrange("p (g d) -> p g d", g=4, d=64)
                t25 = t2[:].rearrange("p (g h j i) -> p g h j i", g=4, h=2, j=2, i=16)
                P.dve(lambda e, t13=t13, src3=src3, cv_=cv_: e.tensor_tensor(out=t13, in0=src3, in1=cv_.unsqueeze(1).to_broadcast([128, 4, 64]), op=ALU.mult),
                      reads=[bankB[zb], B_cst], writes=[b1])
                P.dve(lambda e, t25=t25, src5=src5, sv_=sv_: e.tensor_tensor(out=t25[:, :, :, 0, :], in0=src5[:, :, :, 1, :],
                                                                             in1=sv_[:, :, 0, :].unsqueeze(1).to_broadcast([128, 4, 2, 16]), op=ALU.mult),
                      reads=[bankB[zb], B_cst], writes=[b2])
                P.dve(lambda e, t25=t25, src5=src5, sv_=sv_: e.tensor_tensor(out=t25[:, :, :, 1, :], in0=src5[:, :, :, 0, :],
                                                                             in1=sv_[:, :, 1, :].unsqueeze(1).to_broadcast([128, 4, 2, 16]), op=ALU.mult),
                      reads=[bankB[zb], B_cst], writes=[b2])
                P.any(lambda e, t1=t1, t2=t2, rq=rq: e.tensor_tensor(out=rq[:], in0=t1[:, 0:128], in1=t2[:, 0:128], op=ALU.add), reads=[b1, b2], writes=[rqb])
                P.any(lambda e, t1=t1, t2=t2, T=T: e.tensor_tensor(out=ktok[:, T, :], in0=t1[:, 128:256], in1=t2[:, 128:256], op=ALU.add),
                       reads=[b1, b2], writes=[B_kt[T]])
                tb = 4 + (T % 2)
                P.pe(lambda e, rq=rq, tb=tb: e.transpose(banksb[tb][:, 0:128], rq[:], identb[:]), reads=[rqb, B_cst], writes=[bankB[tb]])
                P.pe(lambda e, T=T, tb=tb: e.transpose(banksb[tb][:, 128:256], ktok[:, T, :], identb[:]), reads=[B_kt[T], B_cst], writes=[bankB[tb]])
                P.act(lambda e, T=T, tb=tb: e.copy(out=qT[:, tslice(T)], in_=banksb[tb][:, 0:128]), reads=[bankB[tb]], writes=[B_q[T]])
                P.act(lambda e, T=T, tb=tb: e.activation(out=kT[:, tslice(T)], in_=banksb[tb][:, 128:256], func=AF.Copy, scale=0.125),
                      reads=[bankB[tb]], writes=[B_k[T]])
                P.act(lambda e, z=z, T=T: e.copy(out=v[:, T, :], in_=z[:, 256:384]), reads=[bankB[zb]], writes=[B_v[T]])
                th, thb = r128.next()
                P.act(lambda e, z=z, th=th: e.activation(out=th[:], in_=z[:, 384:512], func=AF.Tanh, scale=0.5), reads=[bankB[zb]], writes=[thb])
                P.dve(lambda e, z=z, th=th, T=T: e.scalar_tensor_tensor(out=sg[:, T, :], in0=th[:], scalar=1.0, in1=z[:, 384:512], op0=ALU.add, op1=ALU.mult),
                      reads=[thb, bankB[zb]], writes=[B_sg[T]])
            st_in = [srf_in, srb_in]
            st_out = [nsrf_out, nsrb_out]
            for d_ in range(2):
                P.pool(lambda e, d_=d_: e.memset(stS_[:, d_, :], 0.0), writes=[B_stS_[d_]])
                for hh in range(2):
                    P.dma(stS_[hh * 64:(hh + 1) * 64, d_, hh * 64:(hh + 1) * 64], st_in[d_][l, 2 * p + hh], writes=[B_stS_[d_]])
            for step in range(NT):
                for d_ in range(2):
                    T = step if d_ == 0 else NT - 1 - step
                    S = stS_[:, d_, :]
                    kcol = d_ * 16 + T
                    P.dve(lambda e, S=S, kcol=kcol: e.tensor_scalar(out=S, in0=S, scalar1=keep[:, kcol:kcol + 1], scalar2=None, op0=ALU.mult),
                          reads=[B_stS_[d_], B_cst], writes=[B_stS_[d_]])
                    P.act(lambda e, S=S, d_=d_, T=T: e.copy(out=Sbf[:, d_, T, :], in_=S), reads=[B_stS_[d_]], writes=[B_S[d_][T]])
                    kt_, ktb_ = rb128.next()
                    P.any(lambda e, kt_=kt_, T=T, d_=d_: e.tensor_tensor(out=kt_[:], in0=ktok[:, T, :], in1=kd_[:, d_, :], op=ALU.mult),
                           reads=[B_kt[T], B_pair_], writes=[ktb_])
                    ub = d_
                    P.pe(lambda e, kt_=kt_, T=T, ub=ub: e.matmul(banks[ub][:, 0:128], lhsT=kt_[:], rhs=v[:, T, :], start=True, stop=True),
                         reads=[ktb_, B_v[T]], writes=[bankB[ub]])
                    tmp, tmpb = r128.next()
                    P.dve(lambda e, tmp=tmp, ub=ub: e.tensor_tensor(out=tmp[:], in0=banks[ub][:, 0:128], in1=BM, op=ALU.mult),
                          reads=[bankB[ub], B_cst], writes=[tmpb])
                    P.dve(lambda e, S=S, tmp=tmp, d_=d_: e.scalar_tensor_tensor(out=S, in0=S, scalar=lgrow_[:, 2 + d_:3 + d_], in1=tmp[:], op0=ALU.mult, op1=ALU.add),
                          reads=[B_stS_[d_], B_pair_, tmpb], writes=[B_stS_[d_]])
                    is_out = (T % 2 == 1) if d_ == 0 else (T % 2 == 0)
                    if is_out:
                        so, sob = r128.next()
                        P.act(lambda e, so=so, S=S: e.copy(out=so[:], in_=S), reads=[B_stS_[d_]], writes=[sob])
                        for hh in range(2):
                            P.dma(st_out[d_][l, T // 2, 2 * p + hh], so[hh * 64:(hh + 1) * 64, hh * 64:(hh + 1) * 64], reads=[sob])
            for T in range(NT):
                qf, qfb = rb128.next()
                qb_, qbb = rb128.next()
                P.dve(lambda e, qf=qf, T=T: e.tensor_tensor(out=qf[:], in0=qT[:, tslice(T)], in1=qd_[:, 0, :], op=ALU.mult), reads=[B_q[T], B_pair_], writes=[qfb])
                P.any(lambda e, qb_=qb_, T=T: e.tensor_tensor(out=qb_[:], in0=qT[:, tslice(T)], in1=qd_[:, 1, :], op=ALU.mult), reads=[B_q[T], B_pair_], writes=[qbb])
                ob = 6 + (T % 2)
                ab = 2 + (T % 2)
                P.pe(lambda e, qf=qf, T=T, ob=ob: e.matmul(banks[ob][:, 0:128], lhsT=qf[:], rhs=Sbf[:, 0, T, :], start=True, stop=False, skip_group_check=True),
                     reads=[qfb, B_S[0][T]], writes=[bankB[ob]])
                P.pe(lambda e, qb_=qb_, T=T, ob=ob: e.matmul(banks[ob][:, 0:128], lhsT=qb_[:], rhs=Sbf[:, 1, T, :], start=False, stop=False, skip_group_check=True),
                     reads=[qbb, B_S[1][T]], writes=[bankB[ob]])
                am, amb = rb256.next()
                for hh in range(2):
                    P.pe(lambda e, hh=hh, T=T: e.matmul(banks[2 + hh][:, 0:128], lhsT=kT[hh * 64:(hh + 1) * 64, tslice(T)],
                                                        rhs=qT[hh * 64:(hh + 1) * 64, tslice(T)], start=True, stop=True),
                         reads=[B_k[T], B_q[T]], writes=[bankB[2 + hh]])
                    P.dve(lambda e, am=am, hh=hh: e.tensor_tensor(out=am[:, hh * 128:(hh + 1) * 128], in0=banks[2 + hh][:, 0:128], in1=dtm_[:, hh, :], op=ALU.mult),
                          reads=[bankB[2 + hh], B_pair_], writes=[amb])
                for hh in range(2):
                    P.pe(lambda e, hh=hh, am=am, T=T, ob=ob: e.matmul(banks[ob][:, hh * 64:(hh + 1) * 64], lhsT=am[:, hh * 128:(hh + 1) * 128],
                                                                      rhs=v[:, T, hh * 64:(hh + 1) * 64], start=False, stop=(hh == 1), skip_group_check=True),
                         reads=[amb, B_v[T]], writes=[bankB[ob]])
                finish_pair(banks[ob][:, 0:128], [bankB[ob]], sg, B_sg, chunk, T, 0.5, 4 + (T % 2))

        def finish_pair(o_ap, obufs, sg, B_sg, chunk, T, c0, tbank):
            ss, ssb = rs.next()
            jk, jkb = r128.next()
            for hh in range(2):
                P.act(lambda e, hh=hh: e.activation(out=jk[:, hh * 64:(hh + 1) * 64], in_=o_ap[:, hh * 64:(hh + 1) * 64], func=AF.Square,
                                                    accum_out=ss[:, hh:hh + 1]),
                      reads=obufs, writes=[jkb, ssb])
            rstd, rb_ = rstd_from_ss(ss[:, 0:2], ssb, 2, 1.0 / (64 * c0 * c0), EPS / (c0 * c0))
            mt, mtb = rb128.next()
            for hh in range(2):
                P.dve(lambda e, hh=hh: e.scalar_tensor_tensor(out=mt[:, hh * 64:(hh + 1) * 64], in0=o_ap[:, hh * 64:(hh + 1) * 64], scalar=rstd[:, hh:hh + 1],
                                                              in1=sg[:, T, hh * 64:(hh + 1) * 64], op0=ALU.mult, op1=ALU.mult),
                      reads=list(obufs) + [rb_, B_sg[T]], writes=[mtb])
            mixed_out(mt[:], mtb, chunk, T, tbank)

        def hgrn_unit(l, p):
            u = 2 + p
            chunk = UNIT_CHUNK[u]
            P.barrier()
            q = arena_view(0, [128, NT, 128], BF16)
            kk = arena_view(4096, [128, NT, 256], BF16)
            lf = arena_view(12288, [128, NT, 256], F32)
            v = arena_view(28672, [128, NT, 128], BF16)
            sg = arena_view(32768, [128, NT, 128], BF16)
            oacc = arena_view(36864, [128, NT, 128], F32)
            vmall = arena_view(45056, [128, NT, 512], BF16)
            B_vm = [Buf() for _ in range(NT)]
            B_q = [Buf() for _ in range(NT)]
            B_kk = [Buf() for _ in range(NT)]
            B_lf = [Buf() for _ in range(NT)]
            B_v = [Buf() for _ in range(NT)]
            B_sg = [Buf() for _ in range(NT)]
            B_oa = [Buf() for _ in range(NT)]
            wb = load_unit_weights(l, u)
            for half in range(2):
                P.dve(lambda e, half=half: e.tensor_copy(out=lb2[:, half * 128:(half + 1) * 128], in_=lball[:, l, p * 128:(p + 1) * 128]),
                      reads=[B_cst], writes=[B_pair])
            P.dve(lambda e: e.tensor_scalar(out=omlb2[:], in0=lb2[:], scalar1=-1.0, scalar2=1.0, op0=ALU.mult, op1=ALU.add), reads=[B_pair], writes=[B_pair])
            for T in range(NT):
                zb = 2 * (T % 2)
                project(wb, T, zb, 0, 512)
                project(wb, T, zb + 1, 512, 640)
                z = banks[zb]
                z2 = banks[zb + 1]
                u_, ub_ = r512.next()
                P.act(lambda e, z=z, u_=u_: e.activation(out=u_[:, 0:384], in_=z[:, 0:384], func=AF.Exp, scale=-1.0), reads=[bankB[zb]], writes=[ub_])
                P.any(lambda e, u_=u_: e.tensor_scalar(out=u_[:, 0:384], in0=u_[:, 0:384], scalar1=1.0, scalar2=None, op0=ALU.add), reads=[ub_], writes=[ub_])
                P.dve(lambda e, u_=u_: e.reciprocal(out=u_[:, 0:384], in_=u_[:, 0:384]), reads=[ub_], writes=[ub_])
                P.dve(lambda e, u_=u_, z=z, T=T: e.tensor_tensor(out=sg[:, T, :], in0=u_[:, 256:384], in1=z[:, 256:384], op=ALU.mult),
                      reads=[ub_, bankB[zb]], writes=[B_sg[T]])
                f_, fb_ = r256.next()
                P.any(lambda e, u_=u_, f_=f_: e.tensor_tensor(out=f_[:], in0=u_[:, 0:256], in1=omlb2[:], op=ALU.mult), reads=[ub_, B_pair], writes=[fb_])
                P.any(lambda e, f_=f_: e.tensor_tensor(out=f_[:], in0=f_[:], in1=lb2[:], op=ALU.add), reads=[fb_, B_pair], writes=[fb_])
                P.act(lambda e, f_=f_, T=T: e.activation(out=lf[:, T, :], in_=f_[:], func=AF.Ln), reads=[fb_], writes=[B_lf[T]])
                P.dve(lambda e, f_=f_, T=T: e.tensor_scalar(out=kk[:, T, :], in0=f_[:], scalar1=-1.0, scalar2=1.0, op0=ALU.mult, op1=ALU.add),
                      reads=[fb_], writes=[B_kk[T]])
                P.act(lambda e, z=z, T=T: e.activation(out=q[:, T, :], in_=z[:, 384:512], func=AF.Copy, scale=0.125), reads=[bankB[zb]], writes=[B_q[T]])
                P.act(lambda e, z2=z2, T=T: e.copy(out=v[:, T, :], in_=z2[:, 0:128]), reads=[bankB[zb + 1]], writes=[B_v[T]])
            st_in = [shf_in, shb_in]
            st_out = [nshf_out, nshb_out]
            TRI = [TRIF, TRIB]
            for d_ in range(2):
                P.pool(lambda e, d_=d_: e.memset(stS[:, d_, :], 0.0), writes=[B_stS[d_]])
                for hh in range(2):
                    P.dma(stS[hh * 64:(hh + 1) * 64, d_, hh * 64:(hh + 1) * 64], st_in[d_][l, 2 * p + hh], writes=[B_stS[d_]])
            done_first = [False] * NT
            for step in range(NT):
                for d_ in range(2):
                    T = step if d_ == 0 else NT - 1 - step
                    S = stS[:, d_, :]
                    lfd = lf[:, T, d_ * 128:(d_ + 1) * 128]
                    kcol = d_ * 16 + T
                    P.dve(lambda e, S=S, kcol=kcol: e.tensor_scalar(out=S, in0=S, scalar1=keep[:, kcol:kcol + 1], scalar2=None, op0=ALU.mult),
                          reads=[B_stS[d_], B_cst], writes=[B_stS[d_]])
                    sbf, sbfb = rb128.next()
                    P.act(lambda e, S=S, sbf=sbf: e.copy(out=sbf[:], in_=S), reads=[B_stS[d_]], writes=[sbfb])
                    P.pe(lambda e, lfd=lfd, d_=d_: e.matmul(banks[0][:, 0:128], lhsT=TRI[d_], rhs=lfd, start=True, stop=True),
                         reads=[B_cst, B_lf[T]], writes=[bankB[0]])
                    P.pe(lambda e, lfd=lfd: e.matmul(banks[0][:, 128:132], lhsT=lfd, rhs=ind[:], start=True, stop=True),
                         reads=[B_cst, B_lf[T]], writes=[bankB[0]])
                    G_, Gb_ = rs.next()
                    P.act(lambda e, G_=G_: e.activation(out=G_[:, 0:4], in_=banks[0][:, 128:132], func=AF.Exp), reads=[bankB[0]], writes=[Gb_])
                    eq, eqb = r128.next()
                    ek, ekb = r128.next()
                    P.act(lambda e, eq=eq: e.activation(out=eq[:], in_=banks[0][:, 0:128], func=AF.Exp), reads=[bankB[0]], writes=[eqb])
                    P.act(lambda e, ek=ek: e.activation(out=ek[:], in_=banks[0][:, 0:128], func=AF.Exp, scale=-1.0), reads=[bankB[0]], writes=[ekb])
                    qt_, qtb = rb128.next()
                    kt_, ktb = rb128.next()
                    P.dve(lambda e, qt_=qt_, eq=eq, T=T: e.tensor_tensor(out=qt_[:], in0=q[:, T, :], in1=eq[:], op=ALU.mult), reads=[B_q[T], eqb], writes=[qtb])
                    P.any(lambda e, kt_=kt_, ek=ek, T=T, d_=d_: e.tensor_tensor(out=kt_[:], in0=kk[:, T, d_ * 128:(d_ + 1) * 128], in1=ek[:], op=ALU.mult),
                           reads=[B_kk[T], ekb], writes=[ktb])
                    P.pe(lambda e, qt_=qt_: e.transpose(banksb[1][:, 0:128], qt_[:], identb[:]), reads=[qtb, B_cst], writes=[bankB[1]])
                    P.pe(lambda e, kt_=kt_: e.transpose(banksb[1][:, 128:256], kt_[:], identb[:]), reads=[ktb, B_cst], writes=[bankB[1]])
                    qkT, qkTb = rb256.next()
                    P.act(lambda e, qkT=qkT: e.copy(out=qkT[:], in_=banksb[1][:, 0:256]), reads=[bankB[1]], writes=[qkTb])
                    vm, vmb = vmall[:, T, :], B_vm[T]
                    if not done_first[T]:
                        for j in range(4):
                            P.any(lambda e, j=j, vm=vm, T=T: e.tensor_scalar(out=vm[:, j * 128:(j + 1) * 128], in0=v[:, T, :], scalar1=ind[:, j:j + 1],
                                                                             scalar2=None, op0=ALU.mult),
                                  reads=[B_v[T], B_cst], writes=[vmb])
                    P.pe(lambda e, kt_=kt_, vm=vm: e.matmul(banks[2][:, 0:512], lhsT=kt_[:], rhs=vm, start=True, stop=True),
                         reads=[ktb, vmb], writes=[bankB[2]])
                    am, amb = rb256.next()
                    for hh in range(2):
                        abk = 3 + hh
                        P.pe(lambda e, hh=hh, qkT=qkT, abk=abk: e.matmul(banks[abk][:, 0:128], lhsT=qkT[hh * 64:(hh + 1) * 64, 128:256],
                                                                          rhs=qkT[hh * 64:(hh + 1) * 64, 0:128], start=True, stop=True),
                             reads=[qkTb], writes=[bankB[abk]])
                        P.dve(lambda e, am=am, d_=d_, hh=hh, abk=abk: e.tensor_tensor(out=am[:, hh * 128:(hh + 1) * 128], in0=banks[abk][:, 0:128], in1=TRI[d_], op=ALU.mult),
                              reads=[bankB[abk], B_cst], writes=[amb])
                    ob = 5 + d_
                    jorder = [0, 1, 2, 3] if d_ == 0 else [3, 2, 1, 0]
                    cur, curb = sbf, sbfb
                    for ji, j in enumerate(jorder):
                        P.pe(lambda e, j=j, qkT=qkT, cur=cur: e.matmul(banks[ob][32 * j:32 * j + 32, 0:128], lhsT=qkT[:, 32 * j:32 * j + 32], rhs=cur[:],
                                                                       start=True, stop=False, tile_position=(0, 32 * j), skip_group_check=True),
                             reads=[qkTb, curb], writes=[bankB[ob]])
                        tg, tgb = r128.next()
                        P.dve(lambda e, tg=tg, j=j, G_=G_: e.scalar_tensor_tensor(out=tg[:], in0=banks[2][:, j * 128:(j + 1) * 128], scalar=G_[:, j:j + 1], in1=BM,
                                                                                  op0=ALU.mult, op1=ALU.mult),
                              reads=[bankB[2], Gb_, B_cst], writes=[tgb])
                        P.dve(lambda e, S=S, tg=tg, j=j, G_=G_: e.scalar_tensor_tensor(out=S, in0=S, scalar=G_[:, j:j + 1], in1=tg[:], op0=ALU.mult, op1=ALU.add),
                              reads=[B_stS[d_], Gb_, tgb], writes=[B_stS[d_]])
                        if ji < 3:
                            cur, curb = rb128.next()
                            P.act(lambda e, S=S, cur=cur: e.copy(out=cur[:], in_=S), reads=[B_stS[d_]], writes=[curb])
                    for hh in range(2):
                        P.pe(lambda e, hh=hh, am=am, T=T: e.matmul(banks[ob][:, hh * 64:(hh + 1) * 64], lhsT=am[:, hh * 128:(hh + 1) * 128],
                                                                   rhs=v[:, T, hh * 64:(hh + 1) * 64], start=False, stop=(hh == 1), skip_group_check=True),
                             reads=[amb, B_v[T]], writes=[bankB[ob]])
                    is_out = (T % 2 == 1) if d_ == 0 else (T % 2 == 0)
                    if is_out:
                        so, sob = r128.next()
                        P.act(lambda e, so=so, S=S: e.copy(out=so[:], in_=S), reads=[B_stS[d_]], writes=[sob])
                        for hh in range(2):
                            P.dma(st_out[d_][l, T // 2, 2 * p + hh], so[hh * 64:(hh + 1) * 64, hh * 64:(hh + 1) * 64], reads=[sob])
                    if not done_first[T]:
                        done_first[T] = True
                        P.act(lambda e, T=T: e.copy(out=oacc[:, T, :], in_=banks[ob][:, 0:128]), reads=[bankB[ob]], writes=[B_oa[T]])
                    else:
                        ot, otb = r128.next()
                        P.dve(lambda e, ot=ot, T=T: e.tensor_tensor(out=ot[:], in0=banks[ob][:, 0:128], in1=oacc[:, T, :], op=ALU.add),
                              reads=[bankB[ob], B_oa[T]], writes=[otb])
                        finish_pair(ot[:], [otb], sg, B_sg, chunk, T, 1.0, 7)

        def out_phase(l, last):
            P.barrier()
            wo = arena_view(0, [128, 8, 1024], BF16)
            B_wo = Buf()
            for half in range(2):
                ws = wstage[:, 0:4096].rearrange("p (k n) -> p k n", k=8, n=512)
                P.dma(ws, wout_in[l][:, :, half * 512:(half + 1) * 512], writes=[B_wstage])
                for kc in range(8):
                    eng = "any"
                    P.on(eng, lambda e, kc=kc, half=half: e.tensor_copy(out=wo[:, kc, half * 512:(half + 1) * 512], in_=ws[:, kc, :]),
                         reads=[B_wstage], writes=[B_wo])
            src = x_in if l == 0 else xs_scr
            dst = y_out if last else xs_scr
            for T in range(NT):
                xt, xb_ = xring.next()
                P.dma(xt[:], src[T * 128:(T + 1) * 128, :], reads=([B_xs[T]] if l > 0 else []), writes=[xb_])
                for nb in range(2):
                    bk = 2 * (T % 2) + nb
                    for c in range(8):
                        P.pe(lambda e, c=c, nb=nb, bk=bk, T=T: e.matmul(banks[bk][:, 0:512], lhsT=mixT[:, c, tslice(T)], rhs=wo[:, c, nb * 512:(nb + 1) * 512],
                                                                        start=(c == 0), stop=(c == 7)),
                             reads=[B_mixT[c][T], B_wo], writes=[bankB[bk]])
                    tmp, tmpb = r512.next()
                    P.dve(lambda e, tmp=tmp, bk=bk, nb=nb: e.tensor_tensor(out=tmp[:], in0=banks[bk][:, 0:512], in1=gate_b[:, nb * 512:(nb + 1) * 512], op=ALU.mult),
                          reads=[bankB[bk], B_gate], writes=[tmpb])
                    P.any(lambda e, tmp=tmp, xt=xt, nb=nb: e.tensor_tensor(out=xt[:, nb * 512:(nb + 1) * 512], in0=tmp[:], in1=xt[:, nb * 512:(nb + 1) * 512], op=ALU.add),
                           reads=[tmpb, xb_], writes=[xb_])
                P.dma(dst[T * 128:(T + 1) * 128, :], xt[:], reads=[xb_], writes=([] if last else [B_xs[T]]))

        for l in range(L):
            setup_layer(l)
            norm_phase(l)
            RB = [(Buf(), [Buf(), Buf()]) for _ in range(2)]
            for p in range(2):
                if units_enabled is None or ("r%d" % p) in units_enabled:
                    ret_unit(l, p, RB)
            for p in range(2):
                if units_enabled is None or ("g%d" % p) in units_enabled:
                    hgrn_unit(l, p)
            DB = {"set": [([Buf() for _ in range(NT)], [Buf() for _ in range(NKT)], [Buf() for _ in range(NKT)], [Buf() for _ in range(NT)], Buf()) for _ in range(2)], "ck": Buf()}
            for h in range(4):
                if units_enabled is None or ("d%d" % h) in units_enabled:
                    diff_unit(l, h, DB)
            if dbg:
                P.barrier()
                P.dma(dbg_out[l], mixT[:], reads=[b for row in B_mixT for b in row])
            out_phase(l, last=(l == L - 1))

        with nc.Block() as block:
            run = P.build(sems, dsems, reorder=REORDER)
            block.sync(lambda e: run("sp", e))
            block.tensor(lambda e: run("pe", e))
            block.scalar(lambda e: run("act", e))
            block.vector(lambda e: run("dve", e))
            block.gpsimd(lambda e: run("pool", e))
    return nc


def _unit_perm():
    off = dict(rq=0, rk=256, rv=512, rg=768, dq=1024, dk=1536, dv=2048, dg=2560, hq=3072, hff=3328, hfb=3584, hi=3840, hg=4096)
    cols = []
    for p in range(2):
        for n in ("rq", "rk", "rv", "rg"):
            cols += list(range(off[n] + 128 * p, off[n] + 128 * p + 128))
    for p in range(2):
        for n in ("hff", "hfb", "hg", "hq", "hi"):
            cols += list(range(off[n] + 128 * p, off[n] + 128 * p + 128))
    for h in range(4):
        for n in ("dq", "dk", "dv", "dg"):
            cols += list(range(off[n] + 128 * h, off[n] + 128 * h + 128))
    return np.array(cols, dtype=np.int64)


def _constants():
    s = np.arange(128, dtype=np.float32)[:, None]
    t = np.arange(128, dtype=np.float32)[None, :]
    M1 = np.maximum(t - s, 0)
    L1 = (s <= t).astype(np.float32)
    M2 = np.maximum(s - t, 0)
    L2 = (s >= t).astype(np.float32)
    IOTA1 = np.broadcast_to(t + 1, (128, 128))
    IOTA2 = np.broadcast_to(128 - t, (128, 128))
    COLA = np.broadcast_to(127 - s, (128, 128))
    COLB = np.broadcast_to(s, (128, 128))
    same = (np.floor(s / 32) == np.floor(t / 32))
    TRIF = (same & (s <= t)).astype(np.float32)
    TRIB = (same & (s >= t)).astype(np.float32)
    BM = (np.floor(s / 64) == np.floor(t / 64)).astype(np.float32)
    cst = np.stack([M1, L1, M2, L2, IOTA1, IOTA2, COLA, COLB, TRIF, TRIB, BM], axis=1).astype(np.float32)
    ind = (np.floor(np.arange(128)[:, None] / 32) == np.arange(4)[None, :]).astype(np.float32)
    return np.ascontiguousarray(cst), np.ascontiguousarray(ind)


def _rope_tables(sample):
    ropec = np.ones((128, 16, 64), np.float32)
    ropes = np.zeros((128, 16, 64), np.float32)
    if sample:
        tt = np.arange(TOK)
        row = (tt // 64).astype(np.float32)
        col = (tt % 64).astype(np.float32)
        inv = (np.float32(10000.0) ** (-np.arange(16, dtype=np.float32) / np.float32(16))).astype(np.float32)
        ar = (row[:, None] * inv[None, :]).astype(np.float32)
        ac = (col[:, None] * inv[None, :]).astype(np.float32)
        c = np.concatenate([np.cos(ar), np.cos(ar), np.cos(ac), np.cos(ac)], axis=1).astype(np.float32)
        s_ = np.concatenate([-np.sin(ar), np.sin(ar), -np.sin(ac), np.sin(ac)], axis=1).astype(np.float32)
        ropec = np.ascontiguousarray(c.reshape(16, 128, 64).transpose(1, 0, 2))
        ropes = np.ascontiguousarray(s_.reshape(16, 128, 64).transpose(1, 0, 2))
    return ropec, ropes


_NC_CACHE = {}


def kernel(x_prompt, x_sample, c, c_ctx, cache_diff_k, cache_diff_v, state_ret_fwd, state_ret_bwd,
           state_hgrn_fwd, state_hgrn_bwd, norm_g, w_ada, b_ada, w_in, w_out, ret_decay_logit,
           diff_qn_g, diff_kn_g, diff_lambda, hgrn_lb_logit, _dbg=False, _units=None, _L=2):
    f32 = np.float32
    bf = ml_dtypes.bfloat16
    A = lambda a: np.ascontiguousarray(np.asarray(a, dtype=f32))
    x_prompt, x_sample, c, c_ctx = A(x_prompt), A(x_sample), A(c), A(c_ctx)
    perm = _unit_perm()
    w_in_p = A(w_in)[:, :, perm]
    win = np.ascontiguousarray(w_in_p.reshape(2, 8, 128, 4352).transpose(0, 2, 1, 3))
    wada = np.ascontiguousarray(A(w_ada).reshape(2, 8, 128, 3072).transpose(0, 2, 1, 3))
    wout = np.ascontiguousarray(A(w_out).reshape(2, 8, 128, 1024).transpose(0, 2, 1, 3))
    normg = np.ascontiguousarray(A(norm_g).reshape(2, 8, 128).transpose(0, 2, 1))
    cst, ind = _constants()
    shared = dict(
        normg=normg, wada=wada, bada=A(b_ada), win=win, wout=wout, rdl=A(ret_decay_logit).reshape(2, 8),
        qng=A(diff_qn_g), kng=A(diff_kn_g), dlam=A(diff_lambda).reshape(2, 256), hlb=A(hgrn_lb_logit).reshape(512),
        identb=np.eye(128, dtype=f32).astype(bf), identf=np.eye(128, dtype=f32), cst=cst, ind=ind,
    )
    ropec_s, ropes_s = _rope_tables(True)
    ropec_p, ropes_p = _rope_tables(False)
    z64 = np.zeros((2, 4, 64, 64), f32)
    zc = np.zeros((2, 512, 512), f32)
    in_maps = []
    for core in range(8):
        m = dict(shared)
        if core < 4:
            b = core
            m["x"] = x_sample[b]
            m["modv"] = np.ascontiguousarray(c[b].reshape(8, 128).T)
            m["ck"] = np.ascontiguousarray(A(cache_diff_k)[b].reshape(2, 512, 512))
            m["cv"] = np.ascontiguousarray(A(cache_diff_v)[b].reshape(2, 512, 512))
            m["srf"], m["srb"] = A(state_ret_fwd)[b], A(state_ret_bwd)[b]
            m["shf"], m["shb"] = A(state_hgrn_fwd)[b], A(state_hgrn_bwd)[b]
            m["ropec"], m["ropes"] = ropec_s, ropes_s
            m["qmask"] = np.zeros((8, 2048), f32).astype(bf)
            m["kmask"] = np.zeros((8, 2560), f32).astype(bf)
            m["keep"] = np.ones((128, 32), f32)
        else:
            j = core - 4
            m["x"] = np.ascontiguousarray(x_prompt[8 * j:8 * j + 8].reshape(2048, 1024))
            m["modv"] = np.ascontiguousarray(c_ctx.reshape(8, 128).T)
            m["ck"], m["cv"] = zc, zc
            m["srf"], m["srb"], m["shf"], m["shb"] = z64, z64, z64, z64
            m["ropec"], m["ropes"] = ropec_p, ropes_p
            seq = np.arange(2048) // 256
            qm = (seq[None, :] == np.arange(8)[:, None]).astype(f32)
            km = np.full((8, 2560), BIGNEG, f32)
            km[:, :2048] = np.where(seq[None, :] == np.arange(8)[:, None], 0.0, BIGNEG)
            m["qmask"] = qm.astype(bf)
            m["kmask"] = km.astype(bf)
            kf = np.array([0.0 if T % 2 == 0 else 1.0 for T in range(16)], f32)
            kb = np.array([0.0 if T % 2 == 1 else 1.0 for T in range(16)], f32)
            m["keep"] = np.ascontiguousarray(np.broadcast_to(np.concatenate([kf, kb])[None, :], (128, 32)))
        in_maps.append(m)

    key = (_L, _dbg, None if _units is None else tuple(sorted(_units)))
    if key not in _NC_CACHE:
        _NC_CACHE[key] = build_program(L=_L, dbg=_dbg, units_enabled=_units)
    nc = _NC_CACHE[key]
    res = run_bass_kernel_spmd(nc, in_maps, core_ids=list(range(8)))
    R = res.results

    y_sample = np.stack([R[b]["y"] for b in range(4)], axis=0)
    y_prompt = np.concatenate([R[4 + j]["y"].reshape(8, 256, 1024) for j in range(4)], axis=0)
    nk = np.concatenate([R[4 + j]["nk"].reshape(2, 8, 256, 4, 2, 64).transpose(1, 0, 2, 3, 4, 5) for j in range(4)], axis=0)
    nv = np.concatenate([R[4 + j]["nv"].reshape(2, 8, 256, 4, 128).transpose(1, 0, 2, 3, 4) for j in range(4)], axis=0)
    st = []
    for name in ("nsrf", "nsrb", "nshf", "nshb"):
        st.append(np.concatenate([R[4 + j][name].transpose(1, 0, 2, 3, 4) for j in range(4)], axis=0))
    outs = (y_prompt, y_sample, np.ascontiguousarray(nk), np.ascontiguousarray(nv), *[np.ascontiguousarray(s) for s in st])
    if _dbg:
        return outs, [R[i]["dbgmix"] for i in range(8)]
    return outs
```

```python
import math
import types
from contextlib import ExitStack

import numpy as np
import ml_dtypes

import concourse.bass as bass
import concourse.mybir as mybir
from concourse.bass_utils import run_bass_kernel_spmd

F32 = mybir.dt.float32
BF16 = mybir.dt.bfloat16
ALU = mybir.AluOpType
AF = mybir.ActivationFunctionType
AX = mybir.AxisListType

ENGS = ["pe", "act", "dve", "pool", "sp"]
NDSEM = 8
SAME_ENG_SYNC = True
import os as _os0
REORDER = _os0.environ.get('REORDER', '1') == '1'
PSUM_EXCL = _os0.environ.get('PSUM_EXCL', '1') == '1'
REORDER_ENGS = _os0.environ.get('REORDER_ENGS', 'pe,act,dve,pool,sp').split(',')

D_MODEL = 1024
NT = 16
TOK = 2048
NKT = 20
EPS = 1e-6
UNIT_W = [512, 512, 640, 640, 512, 512, 512, 512]
UNIT_OFF = [0, 512, 1024, 1664, 2304, 2816, 3328, 3840]
UNIT_CHUNK = [0, 1, 6, 7, 2, 3, 4, 5]
BIGNEG = -30000.0


class Buf:
    __slots__ = ("w", "r", "name", "excl")

    def __init__(self, name="", excl=False):
        self.w = None
        self.r = []
        self.name = name
        self.excl = excl


class Op:
    __slots__ = ("eng", "fn", "waits", "marked", "semval", "is_dma", "dsem", "dval", "cost", "lat", "idx", "prio",
                 "pos", "succs", "nrem", "ready", "fin", "is_bar", "per_eng", "per_dsem")


class _Probe:
    def __init__(self):
        self.rec = None

    def __getattr__(self, name):
        def f(*a, **k):
            self.rec = (name, a, k)
            return self
        return f


def _nfree(ap):
    n = 1
    for d in ap.shape[1:]:
        n *= int(d)
    return n


def _estimate(eng, fn, is_dma):
    pr = _Probe()
    try:
        fn(pr)
        name, a, k = pr.rec
    except Exception:
        name, a, k = "?", (), {}
    out = k.get("out", a[0] if a else None)
    try:
        if is_dma:
            nbytes = _nfree(out) * int(out.shape[0]) * mybir.dt.size(out.dtype)
            return 120.0, 2200.0 + nbytes / 120.0
        if eng == "pe":
            if name == "transpose":
                return 80.0, 80.0
            rhs = k.get("rhs", a[2] if len(a) > 2 else None)
            lhsT = k.get("lhsT", a[1] if len(a) > 1 else None)
            n = _nfree(rhs)
            c = (max(64, n) / 2.4 + 25.0) * 1.25
            if lhsT.dtype == F32:
                c *= 4.0
            return c, c
        n = _nfree(out)
        if eng == "act":
            c = 190.0 + n / 1.2 + (90.0 if k.get("accum_out") is not None else 0.0)
        elif eng == "dve":
            c = 130.0 + n / 0.7
        else:
            c = 700.0 + n / 0.4
        return c, c
    except Exception:
        return 300.0, 300.0


def _freeze(fn):
    if fn.__closure__ is None:
        return fn
    cells = []
    for c in fn.__closure__:
        try:
            cells.append(types.CellType(c.cell_contents))
        except ValueError:
            cells.append(c)
    return types.FunctionType(fn.__code__, fn.__globals__, fn.__name__, fn.__defaults__, tuple(cells))


LAT_X = 200.0
LAT_S = 50.0


class Prog:
    def __init__(self, nc):
        self.nc = nc
        self.all = []
        self.cur_bar = None
        self.since = []
        self.load = {e: 0.0 for e in ENGS}

    def _new(self, eng):
        op = Op()
        op.eng = eng
        op.fn = None
        op.marked = False
        op.semval = None
        op.is_dma = False
        op.dsem = None
        op.dval = None
        op.cost = 0.0
        op.lat = 0.0
        op.is_bar = False
        op.idx = len(self.all)
        op.waits = []
        self.all.append(op)
        return op

    def barrier(self):
        b = self._new("virt")
        b.is_bar = True
        b.waits = list(self.since)
        self.since = []
        self.cur_bar = b
        self.load = {e: 0.0 for e in ENGS}

    def emit(self, eng, fn, reads=(), writes=(), extra=(), is_dma=False):
        op = self._new(eng)
        op.fn = _freeze(fn)
        op.is_dma = is_dma
        op.cost, op.lat = _estimate(eng, op.fn, is_dma)
        self.load[eng] += op.cost
        waits = set()
        if PSUM_EXCL:
            for b in reads:
                if b.excl:
                    for r in b.r:
                        if r.eng != eng:
                            waits.add(r)
        for b in reads:
            if b.w is not None:
                waits.add(b.w)
        for b in writes:
            if b.w is not None:
                waits.add(b.w)
            for r in b.r:
                waits.add(r)
        for w in extra:
            if w is not None:
                waits.add(w)
        if self.cur_bar is not None:
            waits.add(self.cur_bar)
        waits.discard(op)
        op.waits = list(waits)
        for b in reads:
            b.r.append(op)
        for b in writes:
            b.w = op
            b.r = []
        self.since.append(op)
        return op

    def pe(self, fn, reads=(), writes=(), extra=()):
        return self.emit("pe", fn, reads, writes, extra)

    def act(self, fn, reads=(), writes=(), extra=()):
        return self.emit("act", fn, reads, writes, extra)

    def dve(self, fn, reads=(), writes=(), extra=()):
        return self.emit("dve", fn, reads, writes, extra)

    def pool(self, fn, reads=(), writes=(), extra=()):
        return self.emit("pool", fn, reads, writes, extra)

    def on(self, eng, fn, reads=(), writes=(), extra=()):
        if eng == "any":
            return self.any(fn, reads, writes, extra)
        return self.emit(eng, fn, reads, writes, extra)

    def any(self, fn, reads=(), writes=(), extra=()):
        f = _freeze(fn)
        best = None
        for e in ("dve", "pool"):
            c, _ = _estimate(e, f, False)
            tot = self.load[e] + c
            if best is None or tot < best[0]:
                best = (tot, e)
        return self.emit(best[1], fn, reads, writes, extra)

    def dma(self, out, in_, reads=(), writes=(), extra=()):
        return self.emit("sp", lambda e: e.dma_start(out=out, in_=in_), reads, writes, extra, is_dma=True)

    def schedule(self, reorder=True):
        import heapq
        ops = self.all
        for op in ops:
            op.succs = []
        for op in ops:
            for w in op.waits:
                w.succs.append(op)
        for op in reversed(ops):
            m = 0.0
            for s_ in op.succs:
                l_ = s_.prio + (0.0 if op.is_bar else (LAT_S if s_.eng == op.eng else LAT_X))
                if l_ > m:
                    m = l_
            op.prio = m + op.lat
        order = {e: [] for e in ENGS}
        if not reorder:
            for op in ops:
                if not op.is_bar:
                    order[op.eng].append(op)
            return order
        for op in ops:
            op.nrem = len(op.waits)
            op.ready = 0.0
            op.fin = None
        fixed = [e for e in ENGS if e not in REORDER_ENGS]
        lastop = {}
        for op in ops:
            if op.is_bar or op.eng not in fixed:
                continue
            p_ = lastop.get(op.eng)
            if p_ is not None and p_ not in op.waits:
                p_.succs.append(op)
                op.nrem += 1
            lastop[op.eng] = op
        future = {e: [] for e in ENGS}
        now = {e: [] for e in ENGS}
        free = {e: 0.0 for e in ENGS}

        def release(op):
            for s_ in op.succs:
                if op.is_bar:
                    t = op.fin
                elif s_.is_bar:
                    t = op.fin
                elif s_.eng == op.eng:
                    t = op.fin + (0.0 if op.eng == "pe" else LAT_S)
                else:
                    t = op.fin + LAT_X
                if t > s_.ready:
                    s_.ready = t
                s_.nrem -= 1
                if s_.nrem == 0:
                    if s_.is_bar:
                        s_.fin = s_.ready
                        release(s_)
                    else:
                        heapq.heappush(future[s_.eng], (s_.ready, s_.idx, s_))

        import sys
        sys.setrecursionlimit(100000)
        roots = [op for op in ops if op.nrem == 0]
        for op in roots:
            if op.is_bar:
                op.fin = 0.0
                release(op)
            else:
                heapq.heappush(future[op.eng], (0.0, op.idx, op))
        nleft = sum(1 for op in ops if not op.is_bar)
        while nleft > 0:
            best = None
            for e in ENGS:
                f = future[e]
                nw = now[e]
                while f and f[0][0] <= free[e]:
                    r_, i_, o_ = heapq.heappop(f)
                    heapq.heappush(nw, (-o_.prio, o_.idx, o_))
                if nw:
                    st = free[e]
                elif f:
                    st = f[0][0]
                else:
                    continue
                if best is None or st < best[0]:
                    best = (st, e)
            st, e = best
            if now[e]:
                _, _, op = heapq.heappop(now[e])
            else:
                _, _, op = heapq.heappop(future[e])
            if op.is_dma:
                free[e] = st + op.cost
                op.fin = st + op.lat
            else:
                free[e] = st + op.cost
                op.fin = st + op.cost
            order[e].append(op)
            nleft -= 1
            release(op)
        self.est_ns = max(free.values())
        return order

    def build(self, sems, dsems, reorder=True):
        order = self.schedule(reorder)
        if reorder:
            print('[sched] est_us=%.1f' % (self.est_ns / 1e3), {e: len(order[e]) for e in ENGS})
        for e in ENGS:
            for i, op in enumerate(order[e]):
                op.pos = i

        def skip_same(w_eng, eng):
            return w_eng == eng and (eng == "pe" or not SAME_ENG_SYNC)

        dcnt = [0] * NDSEM
        prev_on_sem = [None] * NDSEM
        dma_prev = {}
        nd = 0
        for op in order["sp"]:
            k = nd % NDSEM
            nd += 1
            op.dsem = k
            dcnt[k] += 16
            op.dval = dcnt[k]
            dma_prev[id(op)] = prev_on_sem[k]
            prev_on_sem[k] = op
        final_dvals = list(dcnt)
        for b in self.all:
            if b.is_bar:
                pe_ = {}
                pd_ = {}
                for w in b.waits:
                    if w.is_bar:
                        continue
                    if w.is_dma:
                        if pd_.get(w.dsem, 0) < w.dval:
                            pd_[w.dsem] = w.dval
                    else:
                        c = pe_.get(w.eng)
                        if c is None or c.pos < w.pos:
                            pe_[w.eng] = w
                b.per_eng = pe_
                b.per_dsem = pd_
        for op in self.all:
            if op.is_bar:
                for w in op.per_eng.values():
                    w.marked = True
                continue
            for w in op.waits:
                if w.is_bar or w.is_dma:
                    continue
                if not skip_same(w.eng, op.eng):
                    w.marked = True
        for e in ENGS:
            cnt = 0
            for op in order[e]:
                if not op.is_dma and op.marked:
                    cnt += 1
                    op.semval = cnt

        def run_engine(ename, eng):
            waited = {}

            def need(semkey, sem, val):
                if waited.get(semkey, 0) >= val:
                    return
                eng.wait_ge(sem, val)
                waited[semkey] = val

            for op in order[ename]:
                for w in op.waits:
                    if w.is_bar:
                        for we, wo in w.per_eng.items():
                            if not (we == ename and ename == "pe"):
                                need(("e", we), sems[we], wo.semval)
                        for k, v in w.per_dsem.items():
                            need(("d", k), dsems[k], v)
                    elif w.is_dma:
                        need(("d", w.dsem), dsems[w.dsem], w.dval)
                    elif not skip_same(w.eng, ename):
                        need(("e", w.eng), sems[w.eng], w.semval)
                if op.is_dma:
                    p = dma_prev[id(op)]
                    if p is not None:
                        need(("d", p.dsem), dsems[p.dsem], p.dval)
                ins = op.fn(eng)
                if op.is_dma:
                    ins.then_inc(dsems[op.dsem], 16)
                elif op.marked:
                    ins.then_inc(sems[ename], 1)
            if ename == "sp":
                for k in range(NDSEM):
                    if final_dvals[k] > 0:
                        need(("d", k), dsems[k], final_dvals[k])

        return run_engine


class Ring:
    def __init__(self, tiles):
        self.tiles = tiles
        self.bufs = [Buf() for _ in tiles]
        self.i = 0

    def next(self):
        k = self.i % len(self.tiles)
        self.i += 1
        return self.tiles[k], self.bufs[k]


def build_program(L=2, dbg=False, units_enabled=None):
    nc = bass.Bass("TRN2", target_bir_lowering=False)

    def din(name, shape, dt=F32):
        return nc.dram_tensor(name, list(shape), dt, kind="ExternalInput").ap()

    def dout(name, shape, dt=F32):
        return nc.dram_tensor(name, list(shape), dt, kind="ExternalOutput").ap()

    x_in = din("x", [TOK, D_MODEL])
    modv = din("modv", [128, 8])
    ck_in = din("ck", [2, 512, 512])
    cv_in = din("cv", [2, 512, 512])
    srf_in = din("srf", [2, 4, 64, 64])
    srb_in = din("srb", [2, 4, 64, 64])
    shf_in = din("shf", [2, 4, 64, 64])
    shb_in = din("shb", [2, 4, 64, 64])
    normg_in = din("normg", [2, 128, 8])
    wada_in = din("wada", [2, 128, 8, 3072])
    bada_in = din("bada", [2, 3072])
    win_in = din("win", [2, 128, 8, 4352])
    wout_in = din("wout", [2, 128, 8, 1024])
    rdl_in = din("rdl", [2, 8])
    qng_in = din("qng", [2, 64])
    kng_in = din("kng", [2, 64])
    dlam_in = din("dlam", [2, 256])
    hlb_in = din("hlb", [512])
    ropec_in = din("ropec", [128, 16, 64])
    ropes_in = din("ropes", [128, 16, 64])
    qmask_in = din("qmask", [8, 2048], BF16)
    kmask_in = din("kmask", [8, 2560], BF16)
    keep_in = din("keep", [128, 32])
    identb_in = din("identb", [128, 128], BF16)
    identf_in = din("identf", [128, 128])
    cst_in = din("cst", [128, 11, 128])
    ind_in = din("ind", [128, 4])

    y_out = dout("y", [TOK, D_MODEL])
    nk_out = dout("nk", [2, TOK, 512])
    nv_out = dout("nv", [2, TOK, 512])
    nsrf_out = dout("nsrf", [2, 8, 4, 64, 64])
    nsrb_out = dout("nsrb", [2, 8, 4, 64, 64])
    nshf_out = dout("nshf", [2, 8, 4, 64, 64])
    nshb_out = dout("nshb", [2, 8, 4, 64, 64])
    xs_scr = nc.dram_tensor("xs_scr", [TOK, D_MODEL], F32, kind="Internal").ap()
    dbg_out = dout("dbgmix", [2, 128, 8, TOK], BF16) if dbg else None

    es = ExitStack()
    with es:
        def sb(name, shape, dt):
            return es.enter_context(nc.sbuf_tensor("sb_" + name, list(shape), dt))

        hT = sb("hT", [128, 8, TOK], BF16)
        mixT = sb("mixT", [128, 8, TOK], BF16)
        wstage = sb("wstage", [128, 8 * 512], F32)
        wbf = sb("wbf", [128, 8 * 640], BF16)
        arena = sb("arena", [128, 61440], mybir.dt.uint8)
        cst = sb("cst", [128, 11, 128], F32)
        ind = sb("ind", [128, 4], F32)
        ropec = sb("ropec", [128, 16, 64], F32)
        ropes = sb("ropes", [128, 16, 64], F32)
        identb = sb("identb", [128, 128], BF16)
        identf = sb("identf", [128, 128], F32)
        keep = sb("keep", [128, 32], F32)
        gate_b = sb("gate_b", [128, 1024], F32)
        modt = sb("modt", [128, 8], F32)
        smod = sb("smod", [128, 8], F32)
        normg = sb("normg", [128, 8], F32)
        modT = sb("modT", [128, 2, 8], F32)
        modA = sb("modA", [128, 8], F32)
        small = sb("small", [128, 64], F32)
        rdl = sb("rdl", [128, 8], F32)
        lg = sb("lg", [128, 8], F32)
        g4 = sb("g4", [128, 256], F32)
        lball = sb("lball", [128, 2, 256], F32)
        lb2 = sb("lb2", [128, 256], F32)
        omlb2 = sb("omlb2", [128, 256], F32)
        cneg = sb("cneg", [128, 8], F32)
        zerosb = sb("zerosb", [128, 512], BF16)
        lgrow = sb("lgrow", [128, 4], F32)
        lgrow1 = sb("lgrow1", [128, 4], F32)
        dtm = sb("dtm", [128, 2, 128], F32)
        qd = sb("qd", [128, 2, 128], F32)
        kd = sb("kd", [128, 2, 128], F32)
        e12 = sb("e12", [128, 2, 128], F32)
        stS = sb("stS", [128, 2, 128], F32)

        n_f32_512 = 3
        r512 = Ring([sb("r512_%d" % i, [128, 512], F32) for i in range(n_f32_512)])
        r256 = Ring([sb("r256_%d" % i, [128, 256], F32) for i in range(8)])
        r128 = Ring([sb("r128_%d" % i, [128, 128], F32) for i in range(8)])
        rb256 = Ring([sb("rb256_%d" % i, [128, 256], BF16) for i in range(3)])
        rb128 = Ring([sb("rb128_%d" % i, [128, 128], BF16) for i in range(8)])
        rpt = Ring([sb("rpt_%d" % i, [128, 512], BF16) for i in range(3)])
        rs = Ring([sb("rs_%d" % i, [128, 8], F32) for i in range(16)])

        banks = [es.enter_context(nc.psum_tensor("bank%d" % i, [128, 512], F32)) for i in range(8)]
        banksb = [b.bitcast(BF16) for b in banks]
        bankB = [Buf("bank%d" % i, excl=True) for i in range(8)]

        sems = {e: es.enter_context(nc.semaphore("s_" + e)) for e in ENGS}
        dsems = [es.enter_context(nc.semaphore("d%d" % k)) for k in range(NDSEM)]

        P = Prog(nc)

        B_hT = [Buf() for _ in range(NT)]
        B_mixT = [[Buf() for _ in range(NT)] for _ in range(8)]
        B_wstage = Buf()
        B_wbf = Buf()
        B_cst = Buf()
        B_misc = Buf()
        B_gate = Buf()
        B_modAB = Buf()
        B_pair = Buf()
        B_stS = [Buf(), Buf()]

        def arena_view(off_bytes, shape, dt):
            n = 1
            for s in shape[1:]:
                n *= s
            esz = 2 if dt == BF16 else 4
            a = arena[:, off_bytes:off_bytes + n * esz].bitcast(dt)
            if len(shape) == 2:
                return a
            if len(shape) == 3:
                return a.rearrange("p (a b) -> p a b", a=shape[1], b=shape[2])
            if len(shape) == 4:
                return a.rearrange("p (a b c) -> p a b c", a=shape[1], b=shape[2], c=shape[3])
            raise ValueError

        xring = Ring([arena_view(36864 + i * 4096, [128, 1024], F32) for i in range(3)])
        xhring = Ring([arena_view(49152 + i * 2048, [128, 1024], BF16) for i in range(2)])
        B_xs = [Buf() for _ in range(NT)]

        M1, L1, M2, L2, IOTA1, IOTA2, COLA, COLB, TRIF, TRIB, BM = [cst[:, i, :] for i in range(11)]

        P.dma(cst[:], cst_in, writes=[B_cst])
        P.dma(ind[:], ind_in, writes=[B_cst])
        P.dma(ropec[:], ropec_in, writes=[B_cst])
        P.dma(ropes[:], ropes_in, writes=[B_cst])
        P.dma(identb[:], identb_in, writes=[B_cst])
        P.dma(identf[:], identf_in, writes=[B_cst])
        P.dma(keep[:], keep_in, writes=[B_cst])
        P.dma(modt[:], modv, writes=[B_cst])
        hlb, hlbb = r512.next()
        P.dma(hlb[:], hlb_in.partition_broadcast(128), writes=[hlbb])
        P.pool(lambda e: e.memset(cneg[:], -0.5), writes=[B_cst])
        P.pool(lambda e: e.memset(zerosb[:], 0.0), writes=[B_cst])
        t_, tb_ = rs.next()
        P.act(lambda e, t_=t_: e.activation(out=t_[:, 0:8], in_=modt[:], func=AF.Tanh, scale=0.5), reads=[B_cst], writes=[tb_])
        P.dve(lambda e, t_=t_: e.scalar_tensor_tensor(out=smod[:], in0=t_[:, 0:8], scalar=1.0, in1=modt[:], op0=ALU.add, op1=ALU.mult),
              reads=[tb_, B_cst], writes=[B_cst])
        P.dve(lambda e: e.tensor_scalar(out=smod[:], in0=smod[:], scalar1=0.5, scalar2=None, op0=ALU.mult), reads=[B_cst], writes=[B_cst])
        P.act(lambda e: e.activation(out=hlb[:], in_=hlb[:], func=AF.Exp), reads=[hlbb], writes=[hlbb])
        den_, denb_ = r256.next()
        P.dve(lambda e: e.tensor_tensor(out=den_[:], in0=hlb[:, 0:256], in1=hlb[:, 256:512], op=ALU.add), reads=[hlbb], writes=[denb_])
        P.dve(lambda e: e.reciprocal(out=den_[:], in_=den_[:]), reads=[denb_], writes=[denb_])
        P.dve(lambda e: e.tensor_tensor(out=hlb[:, 0:256], in0=hlb[:, 0:256], in1=den_[:], op=ALU.mult), reads=[hlbb, denb_], writes=[hlbb])
        P.dve(lambda e: e.tensor_tensor(out=hlb[:, 256:512], in0=hlb[:, 256:512], in1=den_[:], op=ALU.mult), reads=[hlbb, denb_], writes=[hlbb])
        P.dve(lambda e: e.tensor_tensor(out=lball[:, 0, :], in0=hlb[:, 0:256], in1=hlb[:, 0:256], op=ALU.subtract), reads=[hlbb], writes=[B_cst])
        P.dve(lambda e: e.tensor_tensor(out=lball[:, 1, :], in0=hlb[:, 0:256], in1=hlb[:, 256:512], op=ALU.add), reads=[hlbb], writes=[B_cst])
        P.dve(lambda e: e.tensor_tensor(out=lball[:, 1, :], in0=lball[:, 1, :], in1=hlb[:, 0:256], op=ALU.subtract), reads=[hlbb, B_cst], writes=[B_cst])

        def rstd_from_ss(ss_ap, ssb, n, mult, add):
            t1, b1 = rs.next()
            P.dve(lambda e: e.tensor_scalar(out=t1[:, 0:n], in0=ss_ap, scalar1=mult, scalar2=add, op0=ALU.mult, op1=ALU.add),
                  reads=[ssb], writes=[b1])
            t2, b2 = rs.next()
            P.pool(lambda e: e.tensor_tensor(out=t2[:, 0:n], in0=t1[:, 0:n], in1=cneg[:, 0:n], op=ALU.pow), reads=[b1, B_cst], writes=[b2])
            return t2, b2

        def rope(src, srcbufs, dst, dstbufs, T, G, eng_a, eng_b):
            W = G * 64
            t1, b1 = r256.next()
            t2, b2 = r256.next()
            cv_ = ropec[:, T, :]
            sv_ = ropes[:, T, :].rearrange("p (h j i) -> p h j i", h=2, j=2, i=16)
            src3 = src.rearrange("p (g d) -> p g d", g=G, d=64)
            src5 = src.rearrange("p (g h j i) -> p g h j i", g=G, h=2, j=2, i=16)
            t13 = t1[:, 0:W].rearrange("p (g d) -> p g d", g=G, d=64)
            t25 = t2[:, 0:W].rearrange("p (g h j i) -> p g h j i", g=G, h=2, j=2, i=16)
            P.on(eng_a, lambda e: e.tensor_tensor(out=t13, in0=src3, in1=cv_.unsqueeze(1).to_broadcast([128, G, 64]), op=ALU.mult),
                 reads=list(srcbufs) + [B_cst], writes=[b1])
            P.on(eng_b, lambda e: e.tensor_tensor(out=t25[:, :, :, 0, :], in0=src5[:, :, :, 1, :],
                                                  in1=sv_[:, :, 0, :].unsqueeze(1).to_broadcast([128, G, 2, 16]), op=ALU.mult),
                 reads=list(srcbufs) + [B_cst], writes=[b2])
            P.on(eng_b, lambda e: e.tensor_tensor(out=t25[:, :, :, 1, :], in0=src5[:, :, :, 0, :],
                                                  in1=sv_[:, :, 1, :].unsqueeze(1).to_broadcast([128, G, 2, 16]), op=ALU.mult),
                 reads=list(srcbufs) + [B_cst], writes=[b2])
            P.on(eng_a, lambda e: e.tensor_tensor(out=dst, in0=t1[:, 0:W], in1=t2[:, 0:W], op=ALU.add), reads=[b1, b2], writes=list(dstbufs))

        def tslice(T):
            return slice(T * 128, (T + 1) * 128)

        def setup_layer(l):
            stg = [wstage[:, 0:4096].rearrange("p (k n) -> p k n", k=8, n=512), arena_view(16384, [128, 8, 512], F32)]
            stgB = [B_wstage, Buf()]
            smb = arena_view(32768, [128, 8, 128], F32)
            B_smb = Buf()
            P.dve(lambda e: e.tensor_copy(out=smb, in_=smod[:].unsqueeze(2).to_broadcast([128, 8, 128])), reads=[B_cst], writes=[B_smb])
            P.dma(normg[:], normg_in[l], writes=[B_misc])
            P.dma(rdl[:], rdl_in[l].partition_broadcast(128), writes=[B_misc])
            dlam, dlamb = r256.next()
            P.dma(dlam[:], dlam_in[l].partition_broadcast(128), writes=[dlamb])
            P.dma(g4[:, 0:64], qng_in[l].partition_broadcast(128), writes=[B_misc])
            P.dma(g4[:, 64:128], qng_in[l].partition_broadcast(128), writes=[B_misc])
            P.dma(g4[:, 128:192], kng_in[l].partition_broadcast(128), writes=[B_misc])
            P.dma(g4[:, 192:256], kng_in[l].partition_broadcast(128), writes=[B_misc])
            for cb in range(6):
                st_, stb_ = stg[cb % 2], stgB[cb % 2]
                P.dma(st_, wada_in[l][:, :, cb * 512:(cb + 1) * 512], writes=[stb_])
                bt, btb = r512.next()
                P.dma(bt[:], bada_in[l][cb * 512:(cb + 1) * 512].partition_broadcast(128), writes=[btb])
                bk = cb % 4
                for kc in range(8):
                    P.pe(lambda e, kc=kc, st_=st_, bk=bk: e.matmul(banks[bk][:, 0:512], lhsT=smb[:, kc, :], rhs=st_[:, kc, :],
                                                                     start=(kc == 0), stop=(kc == 7)),
                         reads=[B_smb, stb_], writes=[bankB[bk]])
                if cb >= 4:
                    P.dve(lambda e, bk=bk, bt=bt, cb=cb: e.tensor_tensor(out=gate_b[:, (cb - 4) * 512:(cb - 3) * 512], in0=banks[bk][:, 0:512],
                                                                          in1=bt[:], op=ALU.add),
                          reads=[bankB[bk], btb], writes=[B_gate])
                else:
                    P.dve(lambda e, bk=bk, bt=bt: e.tensor_tensor(out=bt[:], in0=banks[bk][:, 0:512], in1=bt[:], op=ALU.add),
                          reads=[bankB[bk], btb], writes=[btb])
                    which = cb // 2
                    tb = 4 + (cb % 2)
                    for jj in range(4):
                        kc = (cb % 2) * 4 + jj
                        P.pe(lambda e, jj=jj, bt=bt, tb=tb: e.transpose(banks[tb][:, jj * 128:(jj + 1) * 128], bt[:, jj * 128:(jj + 1) * 128], identf[:]),
                             reads=[btb, B_cst], writes=[bankB[tb]])
                        P.act(lambda e, jj=jj, tb=tb, which=which, kc=kc: e.copy(out=modT[:, which, kc:kc + 1], in_=banks[tb][:, jj * 128:jj * 128 + 1]),
                              reads=[bankB[tb]], writes=[B_modAB])
            P.dve(lambda e: e.scalar_tensor_tensor(out=modA[:], in0=modT[:, 1, :], scalar=1.0, in1=normg[:], op0=ALU.add, op1=ALU.mult),
                  reads=[B_modAB, B_misc], writes=[B_modAB])
            pr, prb = r256.next()
            P.dve(lambda e: e.tensor_tensor(out=pr[:, 0:64], in0=dlam[:, 0:64], in1=dlam[:, 64:128], op=ALU.mult), reads=[dlamb], writes=[prb])
            P.dve(lambda e: e.tensor_tensor(out=pr[:, 64:128], in0=dlam[:, 128:192], in1=dlam[:, 192:256], op=ALU.mult), reads=[dlamb], writes=[prb])
            s12, s12b = rs.next()
            P.dve(lambda e: e.tensor_reduce(out=s12[:, 0:2], in_=pr[:, 0:128].rearrange("p (a d) -> p a d", a=2, d=64), axis=AX.X, op=ALU.add),
                  reads=[prb], writes=[s12b])
            P.act(lambda e: e.activation(out=s12[:, 0:2], in_=s12[:, 0:2], func=AF.Exp), reads=[s12b], writes=[s12b])
            lam_init = 0.8 - 0.6 * math.exp(-0.3 * l)
            P.dve(lambda e: e.tensor_tensor(out=small[:, 0:1], in0=s12[:, 1:2], in1=s12[:, 0:1], op=ALU.subtract), reads=[s12b], writes=[B_misc])
            P.dve(lambda e: e.tensor_scalar(out=small[:, 0:1], in0=small[:, 0:1], scalar1=-lam_init, scalar2=None, op0=ALU.add),
                  reads=[B_misc], writes=[B_misc])
            P.act(lambda e: e.activation(out=lg[:], in_=rdl[:], func=AF.Exp, scale=-1.0), reads=[B_misc], writes=[B_misc])
            P.act(lambda e: e.activation(out=lg[:], in_=lg[:], func=AF.Ln, bias=1.0), reads=[B_misc], writes=[B_misc])
            P.dve(lambda e: e.tensor_scalar(out=lg[:], in0=lg[:], scalar1=-1.0, scalar2=None, op0=ALU.mult), reads=[B_misc], writes=[B_misc])

        def norm_phase(l):
            src = x_in if l == 0 else xs_scr
            for T in range(NT):
                xt, xb_ = xring.next()
                P.dma(xt[:], src[T * 128:(T + 1) * 128, :], reads=([B_xs[T]] if l > 0 else []), writes=[xb_])
                xh, xhb = xhring.next()
                ss, ssb = rs.next()
                P.act(lambda e, xt=xt, xh=xh, ss=ss: e.activation(out=xh[:], in_=xt[:], func=AF.Square, accum_out=ss[:, 0:1]),
                      reads=[xb_], writes=[xhb, ssb])
                rstd, rb_ = rstd_from_ss(ss[:, 0:1], ssb, 1, 1.0 / D_MODEL, EPS)
                P.dve(lambda e, xt=xt, xh=xh, rstd=rstd: e.tensor_scalar(out=xh[:], in0=xt[:], scalar1=rstd[:, 0:1], scalar2=None, op0=ALU.mult),
                      reads=[xb_, rb_], writes=[xhb])
                bk = 6 + (T % 2)
                for kc in range(8):
                    P.pe(lambda e, kc=kc, xh=xh, bk=bk: e.transpose(banksb[bk][:, kc * 128:(kc + 1) * 128], xh[:, kc * 128:(kc + 1) * 128], identb[:]),
                         reads=[xhb, B_cst], writes=[bankB[bk]])
                for kc in range(8):
                    if kc % 2 == 0:
                        P.dve(lambda e, kc=kc, bk=bk, T=T: e.tensor_scalar(out=hT[:, kc, tslice(T)], in0=banksb[bk][:, kc * 128:(kc + 1) * 128],
                                                                            scalar1=modA[:, kc:kc + 1], scalar2=modT[:, 0, kc:kc + 1],
                                                                            op0=ALU.mult, op1=ALU.add),
                              reads=[bankB[bk], B_modAB], writes=[B_hT[T]])
                    else:
                        P.act(lambda e, kc=kc, bk=bk, T=T: e.activation(out=hT[:, kc, tslice(T)], in_=banksb[bk][:, kc * 128:(kc + 1) * 128],
                                                                         func=AF.Identity, scale=modA[:, kc:kc + 1], bias=modT[:, 0, kc:kc + 1]),
                              reads=[bankB[bk], B_modAB], writes=[B_hT[T]])

        def load_unit_weights(l, u):
            W = UNIT_W[u]
            wb = wbf[:, 0:8 * W].rearrange("p (k n) -> p k n", k=8, n=W)
            engs = ["dve", "pool", "act", "dve", "pool", "act", "dve", "pool"]
            for (a, b) in [(0, 512)] + ([(512, W)] if W > 512 else []):
                wd = b - a
                ws = wstage[:, 0:8 * wd].rearrange("p (k n) -> p k n", k=8, n=wd)
                P.dma(ws, win_in[l][:, :, UNIT_OFF[u] + a:UNIT_OFF[u] + b], writes=[B_wstage])
                for kc in range(8):
                    if engs[kc] == "act":
                        P.act(lambda e, kc=kc, ws=ws, a=a, b=b: e.copy(out=wb[:, kc, a:b], in_=ws[:, kc, :]), reads=[B_wstage], writes=[B_wbf])
                    else:
                        P.on(engs[kc], lambda e, kc=kc, ws=ws, a=a, b=b: e.tensor_copy(out=wb[:, kc, a:b], in_=ws[:, kc, :]), reads=[B_wstage], writes=[B_wbf])
            return wb

        def project(wb, T, bk, c0, c1):
            for kc in range(8):
                P.pe(lambda e, kc=kc: e.matmul(banks[bk][:, 0:c1 - c0], lhsT=hT[:, kc, tslice(T)], rhs=wb[:, kc, c0:c1],
                                               start=(kc == 0), stop=(kc == 7)),
                     reads=[B_hT[T], B_wbf], writes=[bankB[bk]])

        def mixed_out(mt, mtb, chunk, T, tbank, eng="act"):
            P.pe(lambda e: e.transpose(banksb[tbank][:, 0:128], mt, identb[:]), reads=[mtb, B_cst], writes=[bankB[tbank]])
            if eng == "act":
                P.act(lambda e: e.copy(out=mixT[:, chunk, tslice(T)], in_=banksb[tbank][:, 0:128]), reads=[bankB[tbank]], writes=[B_mixT[chunk][T]])
            else:
                P.dve(lambda e: e.tensor_copy(out=mixT[:, chunk, tslice(T)], in_=banksb[tbank][:, 0:128]), reads=[bankB[tbank]], writes=[B_mixT[chunk][T]])

        def diff_unit(l, h, DB):
            u = 4 + h
            chunk = UNIT_CHUNK[u]
            if h == 0:
                P.barrier()
            par = h % 2
            base = par * 27728
            QT = arena_view(base + 0, [128, 2, TOK], BF16)
            KT = arena_view(base + 8192, [128, 2, 2560], BF16)
            V = arena_view(base + 18432, [128, NKT, 130], BF16)
            sg = arena_view(base + 23632, [128, NT, 128], BF16)
            ckst = arena_view(55456, [128, 4, 128], F32)
            cvst = arena_view(57504, [128, 4, 128], F32)
            ckb = arena_view(59552, [128, 4, 128], BF16)
            B_QT, B_KT, B_V, B_sg, B_qm = DB["set"][par]
            B_ck = DB["ck"]
            wb = load_unit_weights(l, u)
            import os as _os
            DD0 = _os.environ.get('DIFFDBG', '')
            if 'nomask' not in DD0:
                for c in range(2):
                    P.dma(QT[64:72, c, :], qmask_in, writes=[B_qm])
                    P.dma(KT[64:72, c, :], kmask_in, writes=[B_qm])
            if 'noctx' not in DD0:
                P.dma(ckst, ck_in[l].rearrange("(t p) n -> p t n", p=128)[:, :, h * 128:(h + 1) * 128], writes=[B_ck])
                P.dma(cvst, cv_in[l].rearrange("(t p) n -> p t n", p=128)[:, :, h * 128:(h + 1) * 128], writes=[B_ck])
                P.pool(lambda e: e.memset(V[:, :, 128:130], 1.0), writes=B_V)
                P.dve(lambda e: e.tensor_copy(out=ckb, in_=ckst), reads=[B_ck], writes=[B_ck])
                for pt in range(4):
                    bk = 5
                    for c in range(2):
                        P.pe(lambda e, pt=pt, c=c, bk=bk: e.transpose(banksb[bk][0:64, c * 128:(c + 1) * 128], ckb[:, pt, c * 64:(c + 1) * 64], identb[:]),
                             reads=[B_ck, B_cst], writes=[bankB[bk]])
                    P.act(lambda e, pt=pt, bk=bk: e.copy(out=KT[0:64, :, 2048 + pt * 128:2048 + (pt + 1) * 128],
                                                         in_=banksb[bk][0:64, 0:256].rearrange("p (c t) -> p c t", c=2, t=128)),
                          reads=[bankB[bk]], writes=[B_KT[16 + pt]])
                    P.any(lambda e, pt=pt: e.tensor_copy(out=V[:, 16 + pt, 0:128], in_=cvst[:, pt, :]), reads=[B_ck], writes=[B_V[16 + pt]])

            if 'noA' in DD0:
                return
            for T in range(NT):
                zb = 6 + (T % 2)
                project(wb, T, zb, 0, 512)
                z = banks[zb]
                sq, sqb = r256.next()
                P.act(lambda e, z=z, sq=sq: e.activation(out=sq[:], in_=z[:, 0:256], func=AF.Square), reads=[bankB[zb]], writes=[sqb])
                ss, ssb = rs.next()
                P.dve(lambda e, sq=sq, ss=ss: e.tensor_reduce(out=ss[:, 0:4], in_=sq[:].rearrange("p (g d) -> p g d", g=4, d=64), axis=AX.X, op=ALU.add),
                      reads=[sqb], writes=[ssb])
                rstd, rb_ = rstd_from_ss(ss[:, 0:4], ssb, 4, 1.0 / 64, EPS)
                nq, nqb = r256.next()
                P.dve(lambda e, z=z, nq=nq, rstd=rstd: e.tensor_tensor(out=nq[:].rearrange("p (g d) -> p g d", g=4, d=64),
                                                                      in0=z[:, 0:256].rearrange("p (g d) -> p g d", g=4, d=64),
                                                                      in1=rstd[:, 0:4].unsqueeze(2).to_broadcast([128, 4, 64]), op=ALU.mult),
                      reads=[bankB[zb], rb_], writes=[nqb])
                P.any(lambda e, nq=nq: e.tensor_tensor(out=nq[:], in0=nq[:], in1=g4[:], op=ALU.mult), reads=[nqb, B_misc], writes=[nqb])
                if 'nonk' not in DD0:
                    P.dma(nk_out[l][T * 128:(T + 1) * 128, h * 128:(h + 1) * 128], nq[:, 128:256], reads=[nqb])
                rt, rtb = rb256.next()
                import os as _os
                rope(nq[:], [nqb], rt[:], [rtb], T, 4, "any", "any")
                if 'noT' not in DD0:
                    tb = 5
                    for g in range(4):
                        P.pe(lambda e, g=g, rt=rt, tb=tb: e.transpose(banksb[tb][0:64, g * 128:(g + 1) * 128], rt[:, g * 64:(g + 1) * 64], identb[:]),
                             reads=[rtb, B_cst], writes=[bankB[tb]])
                    if 'noTq' not in DD0:
                      P.act(lambda e, tb=tb, T=T: e.copy(out=QT[0:64, :, tslice(T)], in_=banksb[tb][0:64, 0:256].rearrange("p (c t) -> p c t", c=2, t=128)),
                          reads=[bankB[tb]], writes=[B_QT[T]])
                    if 'noTk' not in DD0:
                      P.act(lambda e, tb=tb, T=T: e.copy(out=KT[0:64, :, tslice(T)], in_=banksb[tb][0:64, 256:512].rearrange("p (c t) -> p c t", c=2, t=128)),
                          reads=[bankB[tb]], writes=[B_KT[T]])
                vst, vstb = r128.next()
                P.dve(lambda e, z=z, vst=vst: e.tensor_copy(out=vst[:], in_=z[:, 256:384]), reads=[bankB[zb]], writes=[vstb])
                if 'nonk' not in DD0:
                    P.dma(nv_out[l][T * 128:(T + 1) * 128, h * 128:(h + 1) * 128], vst[:], reads=[vstb])
                P.any(lambda e, vst=vst, T=T: e.tensor_copy(out=V[:, T, 0:128], in_=vst[:]), reads=[vstb], writes=[B_V[T]])
                th, thb = r128.next()
                P.act(lambda e, z=z, th=th: e.activation(out=th[:], in_=z[:, 384:512], func=AF.Tanh, scale=0.5), reads=[bankB[zb]], writes=[thb])
                P.dve(lambda e, z=z, th=th, T=T: e.scalar_tensor_tensor(out=sg[:, T, :], in0=th[:], scalar=1.0, in1=z[:, 384:512], op0=ALU.add, op1=ALU.mult),
                      reads=[thb, bankB[zb]], writes=[B_sg[T]])
            import os as _os
            DD = _os.environ.get('DIFFDBG', '')
            if 'noB' in DD:
                return
            KR = 64 if 'k64' in DD else 72
            lam_init = 0.8 - 0.6 * math.exp(-0.3 * l)
            c0 = 0.5 * (1.0 - lam_init)
            OB = [2, 3, 4]

            def acc(c, qi):
                a = c * 4 + qi
                return OB[a // 3], (a % 3) * 160

            for qb in range(4):
                for k in OB:
                    P.pe(lambda e, k=k: e.matmul(banks[k][:, 0:512], lhsT=zerosb[:, 0:128], rhs=zerosb[:, 0:512], start=True, stop=False, skip_group_check=True),
                         reads=[B_cst], writes=[bankB[k]])
                steps = [(c, kt) for c in range(2) for kt in range(NKT)]
                pts = {}

                def emit_st(i):
                    c, kt = steps[i]
                    sbk = i % 2
                    P.pe(lambda e, c=c, kt=kt, sbk=sbk: e.matmul(banks[sbk][:, 0:512], lhsT=KT[0:KR, c, kt * 128:(kt + 1) * 128],
                                                                 rhs=QT[0:KR, c, qb * 512:(qb + 1) * 512], start=True, stop=True),
                         reads=[B_KT[kt], B_qm] + B_QT[qb * 4:qb * 4 + 4], writes=[bankB[sbk]])
                    pt_, ptb = rpt.next()
                    P.act(lambda e, sbk=sbk, pt_=pt_: e.activation(out=pt_[:], in_=banks[sbk][:, 0:512], func=AF.Exp, scale=0.125),
                          reads=[bankB[sbk]], writes=[ptb])
                    pts[i] = (pt_, ptb)

                def emit_pv(i):
                    c, kt = steps[i]
                    pt_, ptb = pts.pop(i)
                    for qi in range(4):
                        bk, off = acc(c, qi)
                        P.pe(lambda e, qi=qi, bk=bk, off=off, pt_=pt_, kt=kt: e.matmul(banks[bk][:, off:off + 129], lhsT=pt_[:, qi * 128:(qi + 1) * 128],
                                                                                       rhs=V[:, kt, 0:129], start=False, stop=(kt == NKT - 1), skip_group_check=True),
                             reads=[ptb, B_V[kt]], writes=[bankB[bk]])

                emit_st(0)
                for i in range(len(steps)):
                    if i + 1 < len(steps):
                        emit_st(i + 1)
                    emit_pv(i)
                for qi in range(4):
                    T = qb * 4 + qi
                    b0, o0 = acc(0, qi)
                    b1, o1 = acc(1, qi)
                    r01, r01b = rs.next()
                    P.dve(lambda e, r01=r01: e.reciprocal(out=r01[:, 0:1], in_=banks[b0][:, o0 + 128:o0 + 129]), reads=[bankB[b0]], writes=[r01b])
                    P.dve(lambda e, r01=r01: e.reciprocal(out=r01[:, 1:2], in_=banks[b1][:, o1 + 128:o1 + 129]), reads=[bankB[b1]], writes=[r01b])
                    P.dve(lambda e, r01=r01: e.tensor_tensor(out=r01[:, 1:2], in0=r01[:, 1:2], in1=small[:, 0:1], op=ALU.mult), reads=[r01b, B_misc], writes=[r01b])
                    d, db = r128.next()
                    P.dve(lambda e, d=d, r01=r01: e.tensor_scalar(out=d[:], in0=banks[b0][:, o0:o0 + 128], scalar1=r01[:, 0:1], scalar2=None, op0=ALU.mult),
                          reads=[bankB[b0], r01b], writes=[db])
                    P.dve(lambda e, d=d, r01=r01: e.scalar_tensor_tensor(out=d[:], in0=banks[b1][:, o1:o1 + 128], scalar=r01[:, 1:2], in1=d[:],
                                                                        op0=ALU.mult, op1=ALU.add),
                          reads=[bankB[b1], r01b, db], writes=[db])
                    jk, jkb = r128.next()
                    ss, ssb = rs.next()
                    P.act(lambda e, d=d, jk=jk, ss=ss: e.activation(out=jk[:], in_=d[:], func=AF.Square, accum_out=ss[:, 0:1]), reads=[db], writes=[jkb, ssb])
                    rstd, rb_ = rstd_from_ss(ss[:, 0:1], ssb, 1, 1.0 / (128 * c0 * c0), EPS / (c0 * c0))
                    mt, mtb = rb128.next()
                    P.dve(lambda e, d=d, rstd=rstd, mt=mt, T=T: e.scalar_tensor_tensor(out=mt[:], in0=d[:], scalar=rstd[:, 0:1], in1=sg[:, T, :],
                                                                                      op0=ALU.mult, op1=ALU.mult),
                          reads=[db, rb_, B_sg[T]], writes=[mtb])
                    mixed_out(mt[:], mtb, chunk, T, 5, eng="dve")

        def ret_unit(l, p, RB):
            u = p
            chunk = UNIT_CHUNK[u]
            if p == 0:
                P.barrier()
            base = p * 28672
            qT = arena_view(base + 0, [128, TOK], BF16)
            kT = arena_view(base + 4096, [128, TOK], BF16)
            ktok = arena_view(base + 8192, [128, NT, 128], BF16)
            v = arena_view(base + 12288, [128, NT, 128], BF16)
            sg = arena_view(base + 16384, [128, NT, 128], BF16)
            Sbf = arena_view(base + 20480, [128, 2, NT, 128], BF16)
            if p == 0:
                dtm_, qd_, kd_, stS_, lgrow_ = dtm, qd, kd, stS, lgrow
            else:
                dtm_ = arena_view(57344, [128, 2, 128], F32)
                qd_ = arena_view(58368, [128, 2, 128], F32)
                kd_ = arena_view(59392, [128, 2, 128], F32)
                stS_ = arena_view(60416, [128, 2, 128], F32)
                lgrow_ = lgrow1
            B_pair_, B_stS_ = RB[p]
            B_q = [Buf() for _ in range(NT)]
            B_k = [Buf() for _ in range(NT)]
            B_kt = [Buf() for _ in range(NT)]
            B_v = [Buf() for _ in range(NT)]
            B_sg = [Buf() for _ in range(NT)]
            B_S = [[Buf() for _ in range(NT)] for _ in range(2)]
            wb = load_unit_weights(l, u)
            for d_ in range(2):
                for hh in range(2):
                    col = d_ * 4 + 2 * p + hh
                    P.dve(lambda e, d_=d_, hh=hh, col=col: e.tensor_copy(out=lgrow_[hh * 64:(hh + 1) * 64, d_:d_ + 1], in_=lg[hh * 64:(hh + 1) * 64, col:col + 1]),
                          reads=[B_misc], writes=[B_pair_])
            for hh in range(2):
                e12t, e12b = r256.next()
                cf = 2 * p + hh
                cb_ = 4 + 2 * p + hh
                P.act(lambda e, cf=cf: e.activation(out=e12t[:, 0:128], in_=M1, func=AF.Exp, scale=lg[:, cf:cf + 1]), reads=[e12b, B_cst, B_misc], writes=[e12b, B_pair_])
                P.act(lambda e, cb_=cb_: e.activation(out=e12t[:, 128:256], in_=M2, func=AF.Exp, scale=lg[:, cb_:cb_ + 1]), reads=[e12b, B_cst, B_misc], writes=[e12b, B_pair_])
                P.dve(lambda e: e.tensor_tensor(out=e12t[:, 0:128], in0=e12t[:, 0:128], in1=L1, op=ALU.mult), reads=[e12b, B_pair_, B_cst], writes=[e12b, B_pair_])
                P.dve(lambda e: e.tensor_tensor(out=e12t[:, 128:256], in0=e12t[:, 128:256], in1=L2, op=ALU.mult), reads=[e12b, B_pair_, B_cst], writes=[e12b, B_pair_])
                P.dve(lambda e, hh=hh: e.tensor_tensor(out=dtm_[:, hh, :], in0=e12t[:, 0:128], in1=e12t[:, 128:256], op=ALU.add), reads=[e12b, B_pair_], writes=[e12b, B_pair_])
                P.act(lambda e, hh=hh, cf=cf: e.activation(out=kd_[:, 0, hh * 64:(hh + 1) * 64], in_=COLA[:, 0:64], func=AF.Exp, scale=lg[:, cf:cf + 1]),
                      reads=[B_cst, B_misc], writes=[B_pair_])
                P.act(lambda e, hh=hh, cb_=cb_: e.activation(out=kd_[:, 1, hh * 64:(hh + 1) * 64], in_=COLB[:, 0:64], func=AF.Exp, scale=lg[:, cb_:cb_ + 1]),
                      reads=[B_cst, B_misc], writes=[B_pair_])
            P.dve(lambda e: e.tensor_scalar(out=kd_[:], in0=kd_[:], scalar1=0.125, scalar2=None, op0=ALU.mult), reads=[B_pair_], writes=[B_pair_])
            P.act(lambda e: e.activation(out=qd_[:, 0, :], in_=IOTA1, func=AF.Exp, scale=lgrow_[:, 0:1]), reads=[B_cst, B_pair_], writes=[B_pair_])
            P.act(lambda e: e.activation(out=qd_[:, 1, :], in_=IOTA2, func=AF.Exp, scale=lgrow_[:, 1:2]), reads=[B_cst, B_pair_], writes=[B_pair_])
            P.act(lambda e: e.activation(out=lgrow_[:, 2:4], in_=lgrow_[:, 0:2], func=AF.Exp, scale=128.0), reads=[B_pair_], writes=[B_pair_])
            for T in range(NT):
                zb = T % 4
                project(wb, T, zb, 0, 512)
                z = banks[zb]
                rq, rqb = rb128.next()
                t1, b1 = r256.next()
                t2, b2 = r256.next()
                cv_ = ropec[:, T, :]
                sv_ = ropes[:, T, :].rearrange("p (h j i) -> p h j i", h=2, j=2, i=16)
                src3 = z[:, 0:256].rearrange("p (g d) -> p g d", g=4, d=64)
                src5 = z[:, 0:256].rearrange("p (g h j i) -> p g h j i", g=4, h=2, j=2, i=16)
                t13 = t1[:].rearrange("p (g d) -> p g d", g=4, d=64)
                t25 = t2[:].rearrange("p (g h j i) -> p g h j i", g=4, h=2, j=2, i=16)
                P.dve(lambda e, t13=t13, src3=src3, cv_=cv_: e.tensor_tensor(out=t13, in0=src3, in1=cv_.unsqueeze(1).to_broadcast([128, 4, 64]), op=ALU.mult),
                      reads=[bankB[zb], B_cst], writes=[b1])
                P.dve(lambda e, t25=t25, src5=src5, sv_=sv_: e.tensor_tensor(out=t25[:, :, :, 0, :], in0=src5[:, :, :, 1, :],
                                                                             in1=sv_[:, :, 0, :].unsqueeze(1).to_broadcast([128, 4, 2, 16]), op=ALU.mult),
                      reads=[bankB[zb], B_cst], writes=[b2])
                P.dve(lambda e, t25=t25, src5=src5, sv_=sv_: e.tensor_tensor(out=t25[:, :, :, 1, :], in0=src5[:, :, :, 0, :],
                                                                             in1=sv_[:, :, 1, :].unsqueeze(1).to_broadcast([128, 4, 2, 16]), op=ALU.mult),
                      reads=[bankB[zb], B_cst], writes=[b2])
                P.any(lambda e, t1=t1, t2=t2, rq=rq: e.tensor_tensor(out=rq[:], in0=t1[:, 0:128], in1=t2[:, 0:128], op=ALU.add), reads=[b1, b2], writes=[rqb])
                P.any(lambda e, t1=t1, t2=t2, T=T: e.tensor_tensor(out=ktok[:, T, :], in0=t1[:, 128:256], in1=t2[:, 128:256], op=ALU.add),
                       reads=[b1, b2], writes=[B_kt[T]])
                tb = 4 + (T % 2)
                P.pe(lambda e, rq=rq, tb=tb: e.transpose(banksb[tb][:, 0:128], rq[:], identb[:]), reads=[rqb, B_cst], writes=[bankB[tb]])
                P.pe(lambda e, T=T, tb=tb: e.transpose(banksb[tb][:, 128:256], ktok[:, T, :], identb[:]), reads=[B_kt[T], B_cst], writes=[bankB[tb]])
                P.act(lambda e, T=T, tb=tb: e.copy(out=qT[:, tslice(T)], in_=banksb[tb][:, 0:128]), reads=[bankB[tb]], writes=[B_q[T]])
                P.act(lambda e, T=T, tb=tb: e.activation(out=kT[:, tslice(T)], in_=banksb[tb][:, 128:256], func=AF.Copy, scale=0.125),
                      reads=[bankB[tb]], writes=[B_k[T]])
                P.act(lambda e, z=z, T=T: e.copy(out=v[:, T, :], in_=z[:, 256:384]), reads=[bankB[zb]], writes=[B_v[T]])
                th, thb = r128.next()
                P.act(lambda e, z=z, th=th: e.activation(out=th[:], in_=z[:, 384:512], func=AF.Tanh, scale=0.5), reads=[bankB[zb]], writes=[thb])
                P.dve(lambda e, z=z, th=th, T=T: e.scalar_tensor_tensor(out=sg[:, T, :], in0=th[:], scalar=1.0, in1=z[:, 384:512], op0=ALU.add, op1=ALU.mult),
                      reads=[thb, bankB[zb]], writes=[B_sg[T]])
            st_in = [srf_in, srb_in]
            st_out = [nsrf_out, nsrb_out]
            for d_ in range(2):
                P.pool(lambda e, d_=d_: e.memset(stS_[:, d_, :], 0.0), writes=[B_stS_[d_]])
                for hh in range(2):
                    P.dma(stS_[hh * 64:(hh + 1) * 64, d_, hh * 64:(hh + 1) * 64], st_in[d_][l, 2 * p + hh], writes=[B_stS_[d_]])
            for step in range(NT):
                for d_ in range(2):
                    T = step if d_ == 0 else NT - 1 - step
                    S = stS_[:, d_, :]
                    kcol = d_ * 16 + T
                    P.dve(lambda e, S=S, kcol=kcol: e.tensor_scalar(out=S, in0=S, scalar1=keep[:, kcol:kcol + 1], scalar2=None, op0=ALU.mult),
                          reads=[B_stS_[d_], B_cst], writes=[B_stS_[d_]])
                    P.act(lambda e, S=S, d_=d_, T=T: e.copy(out=Sbf[:, d_, T, :], in_=S), reads=[B_stS_[d_]], writes=[B_S[d_][T]])
                    kt_, ktb_ = rb128.next()
                    P.any(lambda e, kt_=kt_, T=T, d_=d_: e.tensor_tensor(out=kt_[:], in0=ktok[:, T, :], in1=kd_[:, d_, :], op=ALU.mult),
                           reads=[B_kt[T], B_pair_], writes=[ktb_])
                    ub = d_
                    P.pe(lambda e, kt_=kt_, T=T, ub=ub: e.matmul(banks[ub][:, 0:128], lhsT=kt_[:], rhs=v[:, T, :], start=True, stop=True),
                         reads=[ktb_, B_v[T]], writes=[bankB[ub]])
                    tmp, tmpb = r128.next()
                    P.dve(lambda e, tmp=tmp, ub=ub: e.tensor_tensor(out=tmp[:], in0=banks[ub][:, 0:128], in1=BM, op=ALU.mult),
                          reads=[bankB[ub], B_cst], writes=[tmpb])
                    P.dve(lambda e, S=S, tmp=tmp, d_=d_: e.scalar_tensor_tensor(out=S, in0=S, scalar=lgrow_[:, 2 + d_:3 + d_], in1=tmp[:], op0=ALU.mult, op1=ALU.add),
                          reads=[B_stS_[d_], B_pair_, tmpb], writes=[B_stS_[d_]])
                    is_out = (T % 2 == 1) if d_ == 0 else (T % 2 == 0)
                    if is_out:
                        so, sob = r128.next()
                        P.act(lambda e, so=so, S=S: e.copy(out=so[:], in_=S), reads=[B_stS_[d_]], writes=[sob])
                        for hh in range(2):
                            P.dma(st_out[d_][l, T // 2, 2 * p + hh], so[hh * 64:(hh + 1) * 64, hh * 64:(hh + 1) * 64], reads=[sob])
            for T in range(NT):
                qf, qfb = rb128.next()
                qb_, qbb = rb128.next()
                P.dve(lambda e, qf=qf, T=T: e.tensor_tensor(out=qf[:], in0=qT[:, tslice(T)], in1=qd_[:, 0, :], op=ALU.mult), reads=[B_q[T], B_pair_], writes=[qfb])
                P.any(lambda e, qb_=qb_, T=T: e.tensor_tensor(out=qb_[:], in0=qT[:, tslice(T)], in1=qd_[:, 1, :], op=ALU.mult), reads=[B_q[T], B_pair_], writes=[qbb])
                ob = 6 + (T % 2)
                ab = 2 + (T % 2)
                P.pe(lambda e, qf=qf, T=T, ob=ob: e.matmul(banks[ob][:, 0:128], lhsT=qf[:], rhs=Sbf[:, 0, T, :], start=True, stop=False, skip_group_check=True),
                     reads=[qfb, B_S[0][T]], writes=[bankB[ob]])
                P.pe(lambda e, qb_=qb_, T=T, ob=ob: e.matmul(banks[ob][:, 0:128], lhsT=qb_[:], rhs=Sbf[:, 1, T, :], start=False, stop=False, skip_group_check=True),
                     reads=[qbb, B_S[1][T]], writes=[bankB[ob]])
                am, amb = rb256.next()
                for hh in range(2):
                    P.pe(lambda e, hh=hh, T=T: e.matmul(banks[2 + hh][:, 0:128], lhsT=kT[hh * 64:(hh + 1) * 64, tslice(T)],
                                                        rhs=qT[hh * 64:(hh + 1) * 64, tslice(T)], start=True, stop=True),
                         reads=[B_k[T], B_q[T]], writes=[bankB[2 + hh]])
                    P.dve(lambda e, am=am, hh=hh: e.tensor_tensor(out=am[:, hh * 128:(hh + 1) * 128], in0=banks[2 + hh][:, 0:128], in1=dtm_[:, hh, :], op=ALU.mult),
                          reads=[bankB[2 + hh], B_pair_], writes=[amb])
                for hh in range(2):
                    P.pe(lambda e, hh=hh, am=am, T=T, ob=ob: e.matmul(banks[ob][:, hh * 64:(hh + 1) * 64], lhsT=am[:, hh * 128:(hh + 1) * 128],
                                                                      rhs=v[:, T, hh * 64:(hh + 1) * 64], start=False, stop=(hh == 1), skip_group_check=True),
                         reads=[amb, B_v[T]], writes=[bankB[ob]])
                finish_pair(banks[ob][:, 0:128], [bankB[ob]], sg, B_sg, chunk, T, 0.5, 4 + (T % 2))

        def finish_pair(o_ap, obufs, sg, B_sg, chunk, T, c0, tbank):
            ss, ssb = rs.next()
            jk, jkb = r128.next()
            for hh in range(2):
                P.act(lambda e, hh=hh: e.activation(out=jk[:, hh * 64:(hh + 1) * 64], in_=o_ap[:, hh * 64:(hh + 1) * 64], func=AF.Square,
                                                    accum_out=ss[:, hh:hh + 1]),
                      reads=obufs, writes=[jkb, ssb])
            rstd, rb_ = rstd_from_ss(ss[:, 0:2], ssb, 2, 1.0 / (64 * c0 * c0), EPS / (c0 * c0))
            mt, mtb = rb128.next()
            for hh in range(2):
                P.dve(lambda e, hh=hh: e.scalar_tensor_tensor(out=mt[:, hh * 64:(hh + 1) * 64], in0=o_ap[:, hh * 64:(hh + 1) * 64], scalar=rstd[:, hh:hh + 1],
                                                              in1=sg[:, T, hh * 64:(hh + 1) * 64], op0=ALU.mult, op1=ALU.mult),
                      reads=list(obufs) + [rb_, B_sg[T]], writes=[mtb])
            mixed_out(mt[:], mtb, chunk, T, tbank)

        def hgrn_unit(l, p):
            u = 2 + p
            chunk = UNIT_CHUNK[u]
            P.barrier()
            q = arena_view(0, [128, NT, 128], BF16)
            kk = arena_view(4096, [128, NT, 256], BF16)
            lf = arena_view(12288, [128, NT, 256], F32)
            v = arena_view(28672, [128, NT, 128], BF16)
            sg = arena_view(32768, [128, NT, 128], BF16)
            oacc = arena_view(36864, [128, NT, 128], F32)
            vmall = arena_view(45056, [128, NT, 512], BF16)
            B_vm = [Buf() for _ in range(NT)]
            B_q = [Buf() for _ in range(NT)]
            B_kk = [Buf() for _ in range(NT)]
            B_lf = [Buf() for _ in range(NT)]
            B_v = [Buf() for _ in range(NT)]
            B_sg = [Buf() for _ in range(NT)]
            B_oa = [Buf() for _ in range(NT)]
            wb = load_unit_weights(l, u)
            for half in range(2):
                P.dve(lambda e, half=half: e.tensor_copy(out=lb2[:, half * 128:(half + 1) * 128], in_=lball[:, l, p * 128:(p + 1) * 128]),
                      reads=[B_cst], writes=[B_pair])
            P.dve(lambda e: e.tensor_scalar(out=omlb2[:], in0=lb2[:], scalar1=-1.0, scalar2=1.0, op0=ALU.mult, op1=ALU.add), reads=[B_pair], writes=[B_pair])
            for T in range(NT):
                zb = 2 * (T % 2)
                project(wb, T, zb, 0, 512)
                project(wb, T, zb + 1, 512, 640)
                z = banks[zb]
                z2 = banks[zb + 1]
                u_, ub_ = r512.next()
                P.act(lambda e, z=z, u_=u_: e.activation(out=u_[:, 0:384], in_=z[:, 0:384], func=AF.Exp, scale=-1.0), reads=[bankB[zb]], writes=[ub_])
                P.any(lambda e, u_=u_: e.tensor_scalar(out=u_[:, 0:384], in0=u_[:, 0:384], scalar1=1.0, scalar2=None, op0=ALU.add), reads=[ub_], writes=[ub_])
                P.dve(lambda e, u_=u_: e.reciprocal(out=u_[:, 0:384], in_=u_[:, 0:384]), reads=[ub_], writes=[ub_])
                P.dve(lambda e, u_=u_, z=z, T=T: e.tensor_tensor(out=sg[:, T, :], in0=u_[:, 256:384], in1=z[:, 256:384], op=ALU.mult),
                      reads=[ub_, bankB[zb]], writes=[B_sg[T]])
                f_, fb_ = r256.next()
                P.any(lambda e, u_=u_, f_=f_: e.tensor_tensor(out=f_[:], in0=u_[:, 0:256], in1=omlb2[:], op=ALU.mult), reads=[ub_, B_pair], writes=[fb_])
                P.any(lambda e, f_=f_: e.tensor_tensor(out=f_[:], in0=f_[:], in1=lb2[:], op=ALU.add), reads=[fb_, B_pair], writes=[fb_])
                P.act(lambda e, f_=f_, T=T: e.activation(out=lf[:, T, :], in_=f_[:], func=AF.Ln), reads=[fb_], writes=[B_lf[T]])
                P.dve(lambda e, f_=f_, T=T: e.tensor_scalar(out=kk[:, T, :], in0=f_[:], scalar1=-1.0, scalar2=1.0, op0=ALU.mult, op1=ALU.add),
                      reads=[fb_], writes=[B_kk[T]])
                P.act(lambda e, z=z, T=T: e.activation(out=q[:, T, :], in_=z[:, 384:512], func=AF.Copy, scale=0.125), reads=[bankB[zb]], writes=[B_q[T]])
                P.act(lambda e, z2=z2, T=T: e.copy(out=v[:, T, :], in_=z2[:, 0:128]), reads=[bankB[zb + 1]], writes=[B_v[T]])
            st_in = [shf_in, shb_in]
            st_out = [nshf_out, nshb_out]
            TRI = [TRIF, TRIB]
            for d_ in range(2):
                P.pool(lambda e, d_=d_: e.memset(stS[:, d_, :], 0.0), writes=[B_stS[d_]])
                for hh in range(2):
                    P.dma(stS[hh * 64:(hh + 1) * 64, d_, hh * 64:(hh + 1) * 64], st_in[d_][l, 2 * p + hh], writes=[B_stS[d_]])
            done_first = [False] * NT
            for step in range(NT):
                for d_ in range(2):
                    T = step if d_ == 0 else NT - 1 - step
                    S = stS[:, d_, :]
                    lfd = lf[:, T, d_ * 128:(d_ + 1) * 128]
                    kcol = d_ * 16 + T
                    P.dve(lambda e, S=S, kcol=kcol: e.tensor_scalar(out=S, in0=S, scalar1=keep[:, kcol:kcol + 1], scalar2=None, op0=ALU.mult),
                          reads=[B_stS[d_], B_cst], writes=[B_stS[d_]])
                    sbf, sbfb = rb128.next()
                    P.act(lambda e, S=S, sbf=sbf: e.copy(out=sbf[:], in_=S), reads=[B_stS[d_]], writes=[sbfb])
                    P.pe(lambda e, lfd=lfd, d_=d_: e.matmul(banks[0][:, 0:128], lhsT=TRI[d_], rhs=lfd, start=True, stop=True),
                         reads=[B_cst, B_lf[T]], writes=[bankB[0]])
                    P.pe(lambda e, lfd=lfd: e.matmul(banks[0][:, 128:132], lhsT=lfd, rhs=ind[:], start=True, stop=True),
                         reads=[B_cst, B_lf[T]], writes=[bankB[0]])
                    G_, Gb_ = rs.next()
                    P.act(lambda e, G_=G_: e.activation(out=G_[:, 0:4], in_=banks[0][:, 128:132], func=AF.Exp), reads=[bankB[0]], writes=[Gb_])
                    eq, eqb = r128.next()
                    ek, ekb = r128.next()
                    P.act(lambda e, eq=eq: e.activation(out=eq[:], in_=banks[0][:, 0:128], func=AF.Exp), reads=[bankB[0]], writes=[eqb])
                    P.act(lambda e, ek=ek: e.activation(out=ek[:], in_=banks[0][:, 0:128], func=AF.Exp, scale=-1.0), reads=[bankB[0]], writes=[ekb])
                    qt_, qtb = rb128.next()
                    kt_, ktb = rb128.next()
                    P.dve(lambda e, qt_=qt_, eq=eq, T=T: e.tensor_tensor(out=qt_[:], in0=q[:, T, :], in1=eq[:], op=ALU.mult), reads=[B_q[T], eqb], writes=[qtb])
                    P.any(lambda e, kt_=kt_, ek=ek, T=T, d_=d_: e.tensor_tensor(out=kt_[:], in0=kk[:, T, d_ * 128:(d_ + 1) * 128], in1=ek[:], op=ALU.mult),
                           reads=[B_kk[T], ekb], writes=[ktb])
                    P.pe(lambda e, qt_=qt_: e.transpose(banksb[1][:, 0:128], qt_[:], identb[:]), reads=[qtb, B_cst], writes=[bankB[1]])
                    P.pe(lambda e, kt_=kt_: e.transpose(banksb[1][:, 128:256], kt_[:], identb[:]), reads=[ktb, B_cst], writes=[bankB[1]])
                    qkT, qkTb = rb256.next()
                    P.act(lambda e, qkT=qkT: e.copy(out=qkT[:], in_=banksb[1][:, 0:256]), reads=[bankB[1]], writes=[qkTb])
                    vm, vmb = vmall[:, T, :], B_vm[T]
                    if not done_first[T]:
                        for j in range(4):
                            P.any(lambda e, j=j, vm=vm, T=T: e.tensor_scalar(out=vm[:, j * 128:(j + 1) * 128], in0=v[:, T, :], scalar1=ind[:, j:j + 1],
                                                                             scalar2=None, op0=ALU.mult),
                                  reads=[B_v[T], B_cst], writes=[vmb])
                    P.pe(lambda e, kt_=kt_, vm=vm: e.matmul(banks[2][:, 0:512], lhsT=kt_[:], rhs=vm, start=True, stop=True),
                         reads=[ktb, vmb], writes=[bankB[2]])
                    am, amb = rb256.next()
                    for hh in range(2):
                        abk = 3 + hh
                        P.pe(lambda e, hh=hh, qkT=qkT, abk=abk: e.matmul(banks[abk][:, 0:128], lhsT=qkT[hh * 64:(hh + 1) * 64, 128:256],
                                                                          rhs=qkT[hh * 64:(hh + 1) * 64, 0:128], start=True, stop=True),
                             reads=[qkTb], writes=[bankB[abk]])
                        P.dve(lambda e, am=am, d_=d_, hh=hh, abk=abk: e.tensor_tensor(out=am[:, hh * 128:(hh + 1) * 128], in0=banks[abk][:, 0:128], in1=TRI[d_], op=ALU.mult),
                              reads=[bankB[abk], B_cst], writes=[amb])
                    ob = 5 + d_
                    jorder = [0, 1, 2, 3] if d_ == 0 else [3, 2, 1, 0]
                    cur, curb = sbf, sbfb
                    for ji, j in enumerate(jorder):
                        P.pe(lambda e, j=j, qkT=qkT, cur=cur: e.matmul(banks[ob][32 * j:32 * j + 32, 0:128], lhsT=qkT[:, 32 * j:32 * j + 32], rhs=cur[:],
                                                                       start=True, stop=False, tile_position=(0, 32 * j), skip_group_check=True),
                             reads=[qkTb, curb], writes=[bankB[ob]])
                        tg, tgb = r128.next()
                        P.dve(lambda e, tg=tg, j=j, G_=G_: e.scalar_tensor_tensor(out=tg[:], in0=banks[2][:, j * 128:(j + 1) * 128], scalar=G_[:, j:j + 1], in1=BM,
                                                                                  op0=ALU.mult, op1=ALU.mult),
                              reads=[bankB[2], Gb_, B_cst], writes=[tgb])
                        P.dve(lambda e, S=S, tg=tg, j=j, G_=G_: e.scalar_tensor_tensor(out=S, in0=S, scalar=G_[:, j:j + 1], in1=tg[:], op0=ALU.mult, op1=ALU.add),
                              reads=[B_stS[d_], Gb_, tgb], writes=[B_stS[d_]])
                        if ji < 3:
                            cur, curb = rb128.next()
                            P.act(lambda e, S=S, cur=cur: e.copy(out=cur[:], in_=S), reads=[B_stS[d_]], writes=[curb])
                    for hh in range(2):
                        P.pe(lambda e, hh=hh, am=am, T=T: e.matmul(banks[ob][:, hh * 64:(hh + 1) * 64], lhsT=am[:, hh * 128:(hh + 1) * 128],
                                                                   rhs=v[:, T, hh * 64:(hh + 1) * 64], start=False, stop=(hh == 1), skip_group_check=True),
                             reads=[amb, B_v[T]], writes=[bankB[ob]])
                    is_out = (T % 2 == 1) if d_ == 0 else (T % 2 == 0)
                    if is_out:
                        so, sob = r128.next()
                        P.act(lambda e, so=so, S=S: e.copy(out=so[:], in_=S), reads=[B_stS[d_]], writes=[sob])
                        for hh in range(2):
                            P.dma(st_out[d_][l, T // 2, 2 * p + hh], so[hh * 64:(hh + 1) * 64, hh * 64:(hh + 1) * 64], reads=[sob])
                    if not done_first[T]:
                        done_first[T] = True
                        P.act(lambda e, T=T: e.copy(out=oacc[:, T, :], in_=banks[ob][:, 0:128]), reads=[bankB[ob]], writes=[B_oa[T]])
                    else:
                        ot, otb = r128.next()
                        P.dve(lambda e, ot=ot, T=T: e.tensor_tensor(out=ot[:], in0=banks[ob][:, 0:128], in1=oacc[:, T, :], op=ALU.add),
                              reads=[bankB[ob], B_oa[T]], writes=[otb])
                        finish_pair(ot[:], [otb], sg, B_sg, chunk, T, 1.0, 7)

        def out_phase(l, last):
            P.barrier()
            wo = arena_view(0, [128, 8, 1024], BF16)
            B_wo = Buf()
            for half in range(2):
                ws = wstage[:, 0:4096].rearrange("p (k n) -> p k n", k=8, n=512)
                P.dma(ws, wout_in[l][:, :, half * 512:(half + 1) * 512], writes=[B_wstage])
                for kc in range(8):
                    eng = "any"
                    P.on(eng, lambda e, kc=kc, half=half: e.tensor_copy(out=wo[:, kc, half * 512:(half + 1) * 512], in_=ws[:, kc, :]),
                         reads=[B_wstage], writes=[B_wo])
            src = x_in if l == 0 else xs_scr
            dst = y_out if last else xs_scr
            for T in range(NT):
                xt, xb_ = xring.next()
                P.dma(xt[:], src[T * 128:(T + 1) * 128, :], reads=([B_xs[T]] if l > 0 else []), writes=[xb_])
                for nb in range(2):
                    bk = 2 * (T % 2) + nb
                    for c in range(8):
                        P.pe(lambda e, c=c, nb=nb, bk=bk, T=T: e.matmul(banks[bk][:, 0:512], lhsT=mixT[:, c, tslice(T)], rhs=wo[:, c, nb * 512:(nb + 1) * 512],
                                                                        start=(c == 0), stop=(c == 7)),
                             reads=[B_mixT[c][T], B_wo], writes=[bankB[bk]])
                    tmp, tmpb = r512.next()
                    P.dve(lambda e, tmp=tmp, bk=bk, nb=nb: e.tensor_tensor(out=tmp[:], in0=banks[bk][:, 0:512], in1=gate_b[:, nb * 512:(nb + 1) * 512], op=ALU.mult),
                          reads=[bankB[bk], B_gate], writes=[tmpb])
                    P.any(lambda e, tmp=tmp, xt=xt, nb=nb: e.tensor_tensor(out=xt[:, nb * 512:(nb + 1) * 512], in0=tmp[:], in1=xt[:, nb * 512:(nb + 1) * 512], op=ALU.add),
                           reads=[tmpb, xb_], writes=[xb_])
                P.dma(dst[T * 128:(T + 1) * 128, :], xt[:], reads=[xb_], writes=([] if last else [B_xs[T]]))

        for l in range(L):
            setup_layer(l)
            norm_phase(l)
            RB = [(Buf(), [Buf(), Buf()]) for _ in range(2)]
            for p in range(2):
                if units_enabled is None or ("r%d" % p) in units_enabled:
                    ret_unit(l, p, RB)
            for p in range(2):
                if units_enabled is None or ("g%d" % p) in units_enabled:
                    hgrn_unit(l, p)
            DB = {"set": [([Buf() for _ in range(NT)], [Buf() for _ in range(NKT)], [Buf() for _ in range(NKT)], [Buf() for _ in range(NT)], Buf()) for _ in range(2)], "ck": Buf()}
            for h in range(4):
                if units_enabled is None or ("d%d" % h) in units_enabled:
                    diff_unit(l, h, DB)
            if dbg:
                P.barrier()
                P.dma(dbg_out[l], mixT[:], reads=[b for row in B_mixT for b in row])
            out_phase(l, last=(l == L - 1))

        with nc.Block() as block:
            run = P.build(sems, dsems, reorder=REORDER)
            block.sync(lambda e: run("sp", e))
            block.tensor(lambda e: run("pe", e))
            block.scalar(lambda e: run("act", e))
            block.vector(lambda e: run("dve", e))
            block.gpsimd(lambda e: run("pool", e))
    return nc


def _unit_perm():
    off = dict(rq=0, rk=256, rv=512, rg=768, dq=1024, dk=1536, dv=2048, dg=2560, hq=3072, hff=3328, hfb=3584, hi=3840, hg=4096)
    cols = []
    for p in range(2):
        for n in ("rq", "rk", "rv", "rg"):
            cols += list(range(off[n] + 128 * p, off[n] + 128 * p + 128))
    for p in range(2):
        for n in ("hff", "hfb", "hg", "hq", "hi"):
            cols += list(range(off[n] + 128 * p, off[n] + 128 * p + 128))
    for h in range(4):
        for n in ("dq", "dk", "dv", "dg"):
            cols += list(range(off[n] + 128 * h, off[n] + 128 * h + 128))
    return np.array(cols, dtype=np.int64)


def _constants():
    s = np.arange(128, dtype=np.float32)[:, None]
    t = np.arange(128, dtype=np.float32)[None, :]
    M1 = np.maximum(t - s, 0)
    L1 = (s <= t).astype(np.float32)
    M2 = np.maximum(s - t, 0)
    L2 = (s >= t).astype(np.float32)
    IOTA1 = np.broadcast_to(t + 1, (128, 128))
    IOTA2 = np.broadcast_to(128 - t, (128, 128))
    COLA = np.broadcast_to(127 - s, (128, 128))
    COLB = np.broadcast_to(s, (128, 128))
    same = (np.floor(s / 32) == np.floor(t / 32))
    TRIF = (same & (s <= t)).astype(np.float32)
    TRIB = (same & (s >= t)).astype(np.float32)
    BM = (np.floor(s / 64) == np.floor(t / 64)).astype(np.float32)
    cst = np.stack([M1, L1, M2, L2, IOTA1, IOTA2, COLA, COLB, TRIF, TRIB, BM], axis=1).astype(np.float32)
    ind = (np.floor(np.arange(128)[:, None] / 32) == np.arange(4)[None, :]).astype(np.float32)
    return np.ascontiguousarray(cst), np.ascontiguousarray(ind)


def _rope_tables(sample):
    ropec = np.ones((128, 16, 64), np.float32)
    ropes = np.zeros((128, 16, 64), np.float32)
    if sample:
        tt = np.arange(TOK)
        row = (tt // 64).astype(np.float32)
        col = (tt % 64).astype(np.float32)
        inv = (np.float32(10000.0) ** (-np.arange(16, dtype=np.float32) / np.float32(16))).astype(np.float32)
        ar = (row[:, None] * inv[None, :]).astype(np.float32)
        ac = (col[:, None] * inv[None, :]).astype(np.float32)
        c = np.concatenate([np.cos(ar), np.cos(ar), np.cos(ac), np.cos(ac)], axis=1).astype(np.float32)
        s_ = np.concatenate([-np.sin(ar), np.sin(ar), -np.sin(ac), np.sin(ac)], axis=1).astype(np.float32)
        ropec = np.ascontiguousarray(c.reshape(16, 128, 64).transpose(1, 0, 2))
        ropes = np.ascontiguousarray(s_.reshape(16, 128, 64).transpose(1, 0, 2))
    return ropec, ropes


_NC_CACHE = {}


def kernel(x_prompt, x_sample, c, c_ctx, cache_diff_k, cache_diff_v, state_ret_fwd, state_ret_bwd,
           state_hgrn_fwd, state_hgrn_bwd, norm_g, w_ada, b_ada, w_in, w_out, ret_decay_logit,
           diff_qn_g, diff_kn_g, diff_lambda, hgrn_lb_logit, _dbg=False, _units=None, _L=2):
    f32 = np.float32
    bf = ml_dtypes.bfloat16
    A = lambda a: np.ascontiguousarray(np.asarray(a, dtype=f32))
    x_prompt, x_sample, c, c_ctx = A(x_prompt), A(x_sample), A(c), A(c_ctx)
    perm = _unit_perm()
    w_in_p = A(w_in)[:, :, perm]
    win = np.ascontiguousarray(w_in_p.reshape(2, 8, 128, 4352).transpose(0, 2, 1, 3))
    wada = np.ascontiguousarray(A(w_ada).reshape(2, 8, 128, 3072).transpose(0, 2, 1, 3))
    wout = np.ascontiguousarray(A(w_out).reshape(2, 8, 128, 1024).transpose(0, 2, 1, 3))
    normg = np.ascontiguousarray(A(norm_g).reshape(2, 8, 128).transpose(0, 2, 1))
    cst, ind = _constants()
    shared = dict(
        normg=normg, wada=wada, bada=A(b_ada), win=win, wout=wout, rdl=A(ret_decay_logit).reshape(2, 8),
        qng=A(diff_qn_g), kng=A(diff_kn_g), dlam=A(diff_lambda).reshape(2, 256), hlb=A(hgrn_lb_logit).reshape(512),
        identb=np.eye(128, dtype=f32).astype(bf), identf=np.eye(128, dtype=f32), cst=cst, ind=ind,
    )
    ropec_s, ropes_s = _rope_tables(True)
    ropec_p, ropes_p = _rope_tables(False)
    z64 = np.zeros((2, 4, 64, 64), f32)
    zc = np.zeros((2, 512, 512), f32)
    in_maps = []
    for core in range(8):
        m = dict(shared)
        if core < 4:
            b = core
            m["x"] = x_sample[b]
            m["modv"] = np.ascontiguousarray(c[b].reshape(8, 128).T)
            m["ck"] = np.ascontiguousarray(A(cache_diff_k)[b].reshape(2, 512, 512))
            m["cv"] = np.ascontiguousarray(A(cache_diff_v)[b].reshape(2, 512, 512))
            m["srf"], m["srb"] = A(state_ret_fwd)[b], A(state_ret_bwd)[b]
            m["shf"], m["shb"] = A(state_hgrn_fwd)[b], A(state_hgrn_bwd)[b]
            m["ropec"], m["ropes"] = ropec_s, ropes_s
            m["qmask"] = np.zeros((8, 2048), f32).astype(bf)
            m["kmask"] = np.zeros((8, 2560), f32).astype(bf)
            m["keep"] = np.ones((128, 32), f32)
        else:
            j = core - 4
            m["x"] = np.ascontiguousarray(x_prompt[8 * j:8 * j + 8].reshape(2048, 1024))
            m["modv"] = np.ascontiguousarray(c_ctx.reshape(8, 128).T)
            m["ck"], m["cv"] = zc, zc
            m["srf"], m["srb"], m["shf"], m["shb"] = z64, z64, z64, z64
            m["ropec"], m["ropes"] = ropec_p, ropes_p
            seq = np.arange(2048) // 256
            qm = (seq[None, :] == np.arange(8)[:, None]).astype(f32)
            km = np.full((8, 2560), BIGNEG, f32)
            km[:, :2048] = np.where(seq[None, :] == np.arange(8)[:, None], 0.0, BIGNEG)
            m["qmask"] = qm.astype(bf)
            m["kmask"] = km.astype(bf)
            kf = np.array([0.0 if T % 2 == 0 else 1.0 for T in range(16)], f32)
            kb = np.array([0.0 if T % 2 == 1 else 1.0 for T in range(16)], f32)
            m["keep"] = np.ascontiguousarray(np.broadcast_to(np.concatenate([kf, kb])[None, :], (128, 32)))
        in_maps.append(m)

    key = (_L, _dbg, None if _units is None else tuple(sorted(_units)))
    if key not in _NC_CACHE:
        _NC_CACHE[key] = build_program(L=_L, dbg=_dbg, units_enabled=_units)
    nc = _NC_CACHE[key]
    res = run_bass_kernel_spmd(nc, in_maps, core_ids=list(range(8)))
    R = res.results

    y_sample = np.stack([R[b]["y"] for b in range(4)], axis=0)
    y_prompt = np.concatenate([R[4 + j]["y"].reshape(8, 256, 1024) for j in range(4)], axis=0)
    nk = np.concatenate([R[4 + j]["nk"].reshape(2, 8, 256, 4, 2, 64).transpose(1, 0, 2, 3, 4, 5) for j in range(4)], axis=0)
    nv = np.concatenate([R[4 + j]["nv"].reshape(2, 8, 256, 4, 128).transpose(1, 0, 2, 3, 4) for j in range(4)], axis=0)
    st = []
    for name in ("nsrf", "nsrb", "nshf", "nshb"):
        st.append(np.concatenate([R[4 + j][name].transpose(1, 0, 2, 3, 4) for j in range(4)], axis=0))
    outs = (y_prompt, y_sample, np.ascontiguousarray(nk), np.ascontiguousarray(nv), *[np.ascontiguousarray(s) for s in st])
    if _dbg:
        return outs, [R[i]["dbgmix"] for i in range(8)]
    return outs
```

```python
import math
import types
from contextlib import ExitStack

import numpy as np
import ml_dtypes

import concourse.bass as bass
import concourse.mybir as mybir
from concourse.bass_utils import run_bass_kernel_spmd

F32 = mybir.dt.float32
BF16 = mybir.dt.bfloat16
ALU = mybir.AluOpType
AF = mybir.ActivationFunctionType
AX = mybir.AxisListType

ENGS = ["pe", "act", "dve", "pool", "sp"]
NDSEM = 8
SAME_ENG_SYNC = True
import os as _os0
REORDER = _os0.environ.get('REORDER', '1') == '1'
PSUM_EXCL = _os0.environ.get('PSUM_EXCL', '1') == '1'
REORDER_ENGS = _os0.environ.get('REORDER_ENGS', 'pe,act,dve,pool,sp').split(',')

D_MODEL = 1024
NT = 16
TOK = 2048
NKT = 20
EPS = 1e-6
UNIT_W = [512, 512, 640, 640, 512, 512, 512, 512]
UNIT_OFF = [0, 512, 1024, 1664, 2304, 2816, 3328, 3840]
UNIT_CHUNK = [0, 1, 6, 7, 2, 3, 4, 5]
BIGNEG = -30000.0


class Buf:
    __slots__ = ("w", "r", "name", "excl")

    def __init__(self, name="", excl=False):
        self.w = None
        self.r = []
        self.name = name
        self.excl = excl


class Op:
    __slots__ = ("eng", "fn", "waits", "marked", "semval", "is_dma", "dsem", "dval", "cost", "lat", "idx", "prio",
                 "pos", "succs", "nrem", "ready", "fin", "is_bar", "per_eng", "per_dsem")


class _Probe:
    def __init__(self):
        self.rec = None

    def __getattr__(self, name):
        def f(*a, **k):
            self.rec = (name, a, k)
            return self
        return f


def _nfree(ap):
    n = 1
    for d in ap.shape[1:]:
        n *= int(d)
    return n


def _estimate(eng, fn, is_dma):
    pr = _Probe()
    try:
        fn(pr)
        name, a, k = pr.rec
    except Exception:
        name, a, k = "?", (), {}
    out = k.get("out", a[0] if a else None)
    try:
        if is_dma:
            nbytes = _nfree(out) * int(out.shape[0]) * mybir.dt.size(out.dtype)
            return 120.0, 2200.0 + nbytes / 120.0
        if eng == "pe":
            if name == "transpose":
                return 80.0, 80.0
            rhs = k.get("rhs", a[2] if len(a) > 2 else None)
            lhsT = k.get("lhsT", a[1] if len(a) > 1 else None)
            n = _nfree(rhs)
            c = (max(64, n) / 2.4 + 25.0) * 1.25
            if lhsT.dtype == F32:
                c *= 4.0
            return c, c
        n = _nfree(out)
        if eng == "act":
            c = 190.0 + n / 1.2 + (90.0 if k.get("accum_out") is not None else 0.0)
        elif eng == "dve":
            c = 130.0 + n / 0.7
        else:
            c = 700.0 + n / 0.4
        return c, c
    except Exception:
        return 300.0, 300.0


def _freeze(fn):
    if fn.__closure__ is None:
        return fn
    cells = []
    for c in fn.__closure__:
        try:
            cells.append(types.CellType(c.cell_contents))
        except ValueError:
            cells.append(c)
    return types.FunctionType(fn.__code__, fn.__globals__, fn.__name__, fn.__defaults__, tuple(cells))


LAT_X = 200.0
LAT_S = 50.0


class Prog:
    def __init__(self, nc):
        self.nc = nc
        self.all = []
        self.cur_bar = None
        self.since = []
        self.load = {e: 0.0 for e in ENGS}

    def _new(self, eng):
        op = Op()
        op.eng = eng
        op.fn = None
        op.marked = False
        op.semval = None
        op.is_dma = False
        op.dsem = None
        op.dval = None
        op.cost = 0.0
        op.lat = 0.0
        op.is_bar = False
        op.idx = len(self.all)
        op.waits = []
        self.all.append(op)
        return op

    def barrier(self):
        b = self._new("virt")
        b.is_bar = True
        b.waits = list(self.since)
        self.since = []
        self.cur_bar = b
        self.load = {e: 0.0 for e in ENGS}

    def emit(self, eng, fn, reads=(), writes=(), extra=(), is_dma=False):
        op = self._new(eng)
        op.fn = _freeze(fn)
        op.is_dma = is_dma
        op.cost, op.lat = _estimate(eng, op.fn, is_dma)
        self.load[eng] += op.cost
        waits = set()
        if PSUM_EXCL:
            for b in reads:
                if b.excl:
                    for r in b.r:
                        if r.eng != eng:
                            waits.add(r)
        for b in reads:
            if b.w is not None:
                waits.add(b.w)
        for b in writes:
            if b.w is not None:
                waits.add(b.w)
            for r in b.r:
                waits.add(r)
        for w in extra:
            if w is not None:
                waits.add(w)
        if self.cur_bar is not None:
            waits.add(self.cur_bar)
        waits.discard(op)
        op.waits = list(waits)
        for b in reads:
            b.r.append(op)
        for b in writes:
            b.w = op
            b.r = []
        self.since.append(op)
        return op

    def pe(self, fn, reads=(), writes=(), extra=()):
        return self.emit("pe", fn, reads, writes, extra)

    def act(self, fn, reads=(), writes=(), extra=()):
        return self.emit("act", fn, reads, writes, extra)

    def dve(self, fn, reads=(), writes=(), extra=()):
        return self.emit("dve", fn, reads, writes, extra)

    def pool(self, fn, reads=(), writes=(), extra=()):
        return self.emit("pool", fn, reads, writes, extra)

    def on(self, eng, fn, reads=(), writes=(), extra=()):
        if eng == "any":
            return self.any(fn, reads, writes, extra)
        return self.emit(eng, fn, reads, writes, extra)

    def any(self, fn, reads=(), writes=(), extra=()):
        f = _freeze(fn)
        best = None
        for e in ("dve", "pool"):
            c, _ = _estimate(e, f, False)
            tot = self.load[e] + c
            if best is None or tot < best[0]:
                best = (tot, e)
        return self.emit(best[1], fn, reads, writes, extra)

    def dma(self, out, in_, reads=(), writes=(), extra=()):
        return self.emit("sp", lambda e: e.dma_start(out=out, in_=in_), reads, writes, extra, is_dma=True)

    def schedule(self, reorder=True):
        import heapq
        ops = self.all
        for op in ops:
            op.succs = []
        for op in ops:
            for w in op.waits:
                w.succs.append(op)
        for op in reversed(ops):
            m = 0.0
            for s_ in op.succs:
                l_ = s_.prio + (0.0 if op.is_bar else (LAT_S if s_.eng == op.eng else LAT_X))
                if l_ > m:
                    m = l_
            op.prio = m + op.lat
        order = {e: [] for e in ENGS}
        if not reorder:
            for op in ops:
                if not op.is_bar:
                    order[op.eng].append(op)
            return order
        for op in ops:
            op.nrem = len(op.waits)
            op.ready = 0.0
            op.fin = None
        fixed = [e for e in ENGS if e not in REORDER_ENGS]
        lastop = {}
        for op in ops:
            if op.is_bar or op.eng not in fixed:
                continue
            p_ = lastop.get(op.eng)
            if p_ is not None and p_ not in op.waits:
                p_.succs.append(op)
                op.nrem += 1
            lastop[op.eng] = op
        future = {e: [] for e in ENGS}
        now = {e: [] for e in ENGS}
        free = {e: 0.0 for e in ENGS}

        def release(op):
            for s_ in op.succs:
                if op.is_bar:
                    t = op.fin
                elif s_.is_bar:
                    t = op.fin
                elif s_.eng == op.eng:
                    t = op.fin + (0.0 if op.eng == "pe" else LAT_S)
                else:
                    t = op.fin + LAT_X
                if t > s_.ready:
                    s_.ready = t
                s_.nrem -= 1
                if s_.nrem == 0:
                    if s_.is_bar:
                        s_.fin = s_.ready
                        release(s_)
                    else:
                        heapq.heappush(future[s_.eng], (s_.ready, s_.idx, s_))

        import sys
        sys.setrecursionlimit(100000)
        roots = [op for op in ops if op.nrem == 0]
        for op in roots:
            if op.is_bar:
                op.fin = 0.0
                release(op)
            else:
                heapq.heappush(future[op.eng], (0.0, op.idx, op))
        nleft = sum(1 for op in ops if not op.is_bar)
        while nleft > 0:
            best = None
            for e in ENGS:
                f = future[e]
                nw = now[e]
                while f and f[0][0] <= free[e]:
                    r_, i_, o_ = heapq.heappop(f)
                    heapq.heappush(nw, (-o_.prio, o_.idx, o_))
                if nw:
                    st = free[e]
                elif f:
                    st = f[0][0]
                else:
                    continue
                if best is None or st < best[0]:
                    best = (st, e)
            st, e = best
            if now[e]:
                _, _, op = heapq.heappop(now[e])
            else:
                _, _, op = heapq.heappop(future[e])
            if op.is_dma:
                free[e] = st + op.cost
                op.fin = st + op.lat
            else:
                free[e] = st + op.cost
                op.fin = st + op.cost
            order[e].append(op)
            nleft -= 1
            release(op)
        self.est_ns = max(free.values())
        return order

    def build(self, sems, dsems, reorder=True):
        order = self.schedule(reorder)
        if reorder:
            print('[sched] est_us=%.1f' % (self.est_ns / 1e3), {e: len(order[e]) for e in ENGS})
        for e in ENGS:
            for i, op in enumerate(order[e]):
                op.pos = i

        def skip_same(w_eng, eng):
            return w_eng == eng and (eng == "pe" or not SAME_ENG_SYNC)

        dcnt = [0] * NDSEM
        prev_on_sem = [None] * NDSEM
        dma_prev = {}
        nd = 0
        for op in order["sp"]:
            k = nd % NDSEM
            nd += 1
            op.dsem = k
            dcnt[k] += 16
            op.dval = dcnt[k]
            dma_prev[id(op)] = prev_on_sem[k]
            prev_on_sem[k] = op
        final_dvals = list(dcnt)
        for b in self.all:
            if b.is_bar:
                pe_ = {}
                pd_ = {}
                for w in b.waits:
                    if w.is_bar:
                        continue
                    if w.is_dma:
                        if pd_.get(w.dsem, 0) < w.dval:
                            pd_[w.dsem] = w.dval
                    else:
                        c = pe_.get(w.eng)
                        if c is None or c.pos < w.pos:
                            pe_[w.eng] = w
                b.per_eng = pe_
                b.per_dsem = pd_
        for op in self.all:
            if op.is_bar:
                for w in op.per_eng.values():
                    w.marked = True
                continue
            for w in op.waits:
                if w.is_bar or w.is_dma:
                    continue
                if not skip_same(w.eng, op.eng):
                    w.marked = True
        for e in ENGS:
            cnt = 0
            for op in order[e]:
                if not op.is_dma and op.marked:
                    cnt += 1
                    op.semval = cnt

        def run_engine(ename, eng):
            waited = {}

            def need(semkey, sem, val):
                if waited.get(semkey, 0) >= val:
                    return
                eng.wait_ge(sem, val)
                waited[semkey] = val

            for op in order[ename]:
                for w in op.waits:
                    if w.is_bar:
                        for we, wo in w.per_eng.items():
                            if not (we == ename and ename == "pe"):
                                need(("e", we), sems[we], wo.semval)
                        for k, v in w.per_dsem.items():
                            need(("d", k), dsems[k], v)
                    elif w.is_dma:
                        need(("d", w.dsem), dsems[w.dsem], w.dval)
                    elif not skip_same(w.eng, ename):
                        need(("e", w.eng), sems[w.eng], w.semval)
                if op.is_dma:
                    p = dma_prev[id(op)]
                    if p is not None:
                        need(("d", p.dsem), dsems[p.dsem], p.dval)
                ins = op.fn(eng)
                if op.is_dma:
                    ins.then_inc(dsems[op.dsem], 16)
                elif op.marked:
                    ins.then_inc(sems[ename], 1)
            if ename == "sp":
                for k in range(NDSEM):
                    if final_dvals[k] > 0:
                        need(("d", k), dsems[k], final_dvals[k])

        return run_engine


class Ring:
    def __init__(self, tiles):
        self.tiles = tiles
        self.bufs = [Buf() for _ in tiles]
        self.i = 0

    def next(self):
        k = self.i % len(self.tiles)
        self.i += 1
        return self.tiles[k], self.bufs[k]


def build_program(L=2, dbg=False, units_enabled=None):
    nc = bass.Bass("TRN2", target_bir_lowering=False)

    def din(name, shape, dt=F32):
        return nc.dram_tensor(name, list(shape), dt, kind="ExternalInput").ap()

    def dout(name, shape, dt=F32):
        return nc.dram_tensor(name, list(shape), dt, kind="ExternalOutput").ap()

    x_in = din("x", [TOK, D_MODEL])
    modv = din("modv", [128, 8])
    ck_in = din("ck", [2, 512, 512])
    cv_in = din("cv", [2, 512, 512])
    srf_in = din("srf", [2, 4, 64, 64])
    srb_in = din("srb", [2, 4, 64, 64])
    shf_in = din("shf", [2, 4, 64, 64])
    shb_in = din("shb", [2, 4, 64, 64])
    normg_in = din("normg", [2, 128, 8])
    wada_in = din("wada", [2, 128, 8, 3072])
    bada_in = din("bada", [2, 3072])
    win_in = din("win", [2, 128, 8, 4352])
    wout_in = din("wout", [2, 128, 8, 1024])
    rdl_in = din("rdl", [2, 8])
    qng_in = din("qng", [2, 64])
    kng_in = din("kng", [2, 64])
    dlam_in = din("dlam", [2, 256])
    hlb_in = din("hlb", [512])
    ropec_in = din("ropec", [128, 16, 64])
    ropes_in = din("ropes", [128, 16, 64])
    qmask_in = din("qmask", [8, 2048], BF16)
    kmask_in = din("kmask", [8, 2560], BF16)
    keep_in = din("keep", [128, 32])
    identb_in = din("identb", [128, 128], BF16)
    identf_in = din("identf", [128, 128])
    cst_in = din("cst", [128, 11, 128])
    ind_in = din("ind", [128, 4])

    y_out = dout("y", [TOK, D_MODEL])
    nk_out = dout("nk", [2, TOK, 512])
    nv_out = dout("nv", [2, TOK, 512])
    nsrf_out = dout("nsrf", [2, 8, 4, 64, 64])
    nsrb_out = dout("nsrb", [2, 8, 4, 64, 64])
    nshf_out = dout("nshf", [2, 8, 4, 64, 64])
    nshb_out = dout("nshb", [2, 8, 4, 64, 64])
    xs_scr = nc.dram_tensor("xs_scr", [TOK, D_MODEL], F32, kind="Internal").ap()
    dbg_out = dout("dbgmix", [2, 128, 8, TOK], BF16) if dbg else None

    es = ExitStack()
    with es:
        def sb(name, shape, dt):
            return es.enter_context(nc.sbuf_tensor("sb_" + name, list(shape), dt))

        hT = sb("hT", [128, 8, TOK], BF16)
        mixT = sb("mixT", [128, 8, TOK], BF16)
        wstage = sb("wstage", [128, 8 * 512], F32)
        wbf = sb("wbf", [128, 8 * 640], BF16)
        arena = sb("arena", [128, 61440], mybir.dt.uint8)
        cst = sb("cst", [128, 11, 128], F32)
        ind = sb("ind", [128, 4], F32)
        ropec = sb("ropec", [128, 16, 64], F32)
        ropes = sb("ropes", [128, 16, 64], F32)
        identb = sb("identb", [128, 128], BF16)
        identf = sb("identf", [128, 128], F32)
        keep = sb("keep", [128, 32], F32)
        gate_b = sb("gate_b", [128, 1024], F32)
        modt = sb("modt", [128, 8], F32)
        smod = sb("smod", [128, 8], F32)
        normg = sb("normg", [128, 8], F32)
        modT = sb("modT", [128, 2, 8], F32)
        modA = sb("modA", [128, 8], F32)
        small = sb("small", [128, 64], F32)
        rdl = sb("rdl", [128, 8], F32)
        lg = sb("lg", [128, 8], F32)
        g4 = sb("g4", [128, 256], F32)
        lball = sb("lball", [128, 2, 256], F32)
        lb2 = sb("lb2", [128, 256], F32)
        omlb2 = sb("omlb2", [128, 256], F32)
        cneg = sb("cneg", [128, 8], F32)
        zerosb = sb("zerosb", [128, 512], BF16)
        lgrow = sb("lgrow", [128, 4], F32)
        lgrow1 = sb("lgrow1", [128, 4], F32)
        dtm = sb("dtm", [128, 2, 128], F32)
        qd = sb("qd", [128, 2, 128], F32)
        kd = sb("kd", [128, 2, 128], F32)
        e12 = sb("e12", [128, 2, 128], F32)
        stS = sb("stS", [128, 2, 128], F32)

        n_f32_512 = 3
        r512 = Ring([sb("r512_%d" % i, [128, 512], F32) for i in range(n_f32_512)])
        r256 = Ring([sb("r256_%d" % i, [128, 256], F32) for i in range(8)])
        r128 = Ring([sb("r128_%d" % i, [128, 128], F32) for i in range(8)])
        rb256 = Ring([sb("rb256_%d" % i, [128, 256], BF16) for i in range(3)])
        rb128 = Ring([sb("rb128_%d" % i, [128, 128], BF16) for i in range(8)])
        rpt = Ring([sb("rpt_%d" % i, [128, 512], BF16) for i in range(3)])
        rs = Ring([sb("rs_%d" % i, [128, 8], F32) for i in range(16)])

        banks = [es.enter_context(nc.psum_tensor("bank%d" % i, [128, 512], F32)) for i in range(8)]
        banksb = [b.bitcast(BF16) for b in banks]
        bankB = [Buf("bank%d" % i, excl=True) for i in range(8)]

        sems = {e: es.enter_context(nc.semaphore("s_" + e)) for e in ENGS}
        dsems = [es.enter_context(nc.semaphore("d%d" % k)) for k in range(NDSEM)]

        P = Prog(nc)

        B_hT = [Buf() for _ in range(NT)]
        B_mixT = [[Buf() for _ in range(NT)] for _ in range(8)]
        B_wstage = Buf()
        B_wbf = Buf()
        B_cst = Buf()
        B_misc = Buf()
        B_gate = Buf()
        B_modAB = Buf()
        B_pair = Buf()
        B_stS = [Buf(), Buf()]

        def arena_view(off_bytes, shape, dt):
            n = 1
            for s in shape[1:]:
                n *= s
            esz = 2 if dt == BF16 else 4
            a = arena[:, off_bytes:off_bytes + n * esz].bitcast(dt)
            if len(shape) == 2:
                return a
            if len(shape) == 3:
                return a.rearrange("p (a b) -> p a b", a=shape[1], b=shape[2])
            if len(shape) == 4:
                return a.rearrange("p (a b c) -> p a b c", a=shape[1], b=shape[2], c=shape[3])
            raise ValueError

        xring = Ring([arena_view(36864 + i * 4096, [128, 1024], F32) for i in range(3)])
        xhring = Ring([arena_view(49152 + i * 2048, [128, 1024], BF16) for i in range(2)])
        B_xs = [Buf() for _ in range(NT)]

        M1, L1, M2, L2, IOTA1, IOTA2, COLA, COLB, TRIF, TRIB, BM = [cst[:, i, :] for i in range(11)]

        P.dma(cst[:], cst_in, writes=[B_cst])
        P.dma(ind[:], ind_in, writes=[B_cst])
        P.dma(ropec[:], ropec_in, writes=[B_cst])
        P.dma(ropes[:], ropes_in, writes=[B_cst])
        P.dma(identb[:], identb_in, writes=[B_cst])
        P.dma(identf[:], identf_in, writes=[B_cst])
        P.dma(keep[:], keep_in, writes=[B_cst])
        P.dma(modt[:], modv, writes=[B_cst])
        hlb, hlbb = r512.next()
        P.dma(hlb[:], hlb_in.partition_broadcast(128), writes=[hlbb])
        P.pool(lambda e: e.memset(cneg[:], -0.5), writes=[B_cst])
        P.pool(lambda e: e.memset(zerosb[:], 0.0), writes=[B_cst])
        t_, tb_ = rs.next()
        P.act(lambda e, t_=t_: e.activation(out=t_[:, 0:8], in_=modt[:], func=AF.Tanh, scale=0.5), reads=[B_cst], writes=[tb_])
        P.dve(lambda e, t_=t_: e.scalar_tensor_tensor(out=smod[:], in0=t_[:, 0:8], scalar=1.0, in1=modt[:], op0=ALU.add, op1=ALU.mult),
              reads=[tb_, B_cst], writes=[B_cst])
        P.dve(lambda e: e.tensor_scalar(out=smod[:], in0=smod[:], scalar1=0.5, scalar2=None, op0=ALU.mult), reads=[B_cst], writes=[B_cst])
        P.act(lambda e: e.activation(out=hlb[:], in_=hlb[:], func=AF.Exp), reads=[hlbb], writes=[hlbb])
        den_, denb_ = r256.next()
        P.dve(lambda e: e.tensor_tensor(out=den_[:], in0=hlb[:, 0:256], in1=hlb[:, 256:512], op=ALU.add), reads=[hlbb], writes=[denb_])
        P.dve(lambda e: e.reciprocal(out=den_[:], in_=den_[:]), reads=[denb_], writes=[denb_])
        P.dve(lambda e: e.tensor_tensor(out=hlb[:, 0:256], in0=hlb[:, 0:256], in1=den_[:], op=ALU.mult), reads=[hlbb, denb_], writes=[hlbb])
        P.dve(lambda e: e.tensor_tensor(out=hlb[:, 256:512], in0=hlb[:, 256:512], in1=den_[:], op=ALU.mult), reads=[hlbb, denb_], writes=[hlbb])
        P.dve(lambda e: e.tensor_tensor(out=lball[:, 0, :], in0=hlb[:, 0:256], in1=hlb[:, 0:256], op=ALU.subtract), reads=[hlbb], writes=[B_cst])
        P.dve(lambda e: e.tensor_tensor(out=lball[:, 1, :], in0=hlb[:, 0:256], in1=hlb[:, 256:512], op=ALU.add), reads=[hlbb], writes=[B_cst])
        P.dve(lambda e: e.tensor_tensor(out=lball[:, 1, :], in0=lball[:, 1, :], in1=hlb[:, 0:256], op=ALU.subtract), reads=[hlbb, B_cst], writes=[B_cst])

        def rstd_from_ss(ss_ap, ssb, n, mult, add):
            t1, b1 = rs.next()
            P.dve(lambda e: e.tensor_scalar(out=t1[:, 0:n], in0=ss_ap, scalar1=mult, scalar2=add, op0=ALU.mult, op1=ALU.add),
                  reads=[ssb], writes=[b1])
            t2, b2 = rs.next()
            P.pool(lambda e: e.tensor_tensor(out=t2[:, 0:n], in0=t1[:, 0:n], in1=cneg[:, 0:n], op=ALU.pow), reads=[b1, B_cst], writes=[b2])
            return t2, b2

        def rope(src, srcbufs, dst, dstbufs, T, G, eng_a, eng_b):
            W = G * 64
            t1, b1 = r256.next()
            t2, b2 = r256.next()
            cv_ = ropec[:, T, :]
            sv_ = ropes[:, T, :].rearrange("p (h j i) -> p h j i", h=2, j=2, i=16)
            src3 = src.rearrange("p (g d) -> p g d", g=G, d=64)
            src5 = src.rearrange("p (g h j i) -> p g h j i", g=G, h=2, j=2, i=16)
            t13 = t1[:, 0:W].rearrange("p (g d) -> p g d", g=G, d=64)
            t25 = t2[:, 0:W].rearrange("p (g h j i) -> p g h j i", g=G, h=2, j=2, i=16)
            P.on(eng_a, lambda e: e.tensor_tensor(out=t13, in0=src3, in1=cv_.unsqueeze(1).to_broadcast([128, G, 64]), op=ALU.mult),
                 reads=list(srcbufs) + [B_cst], writes=[b1])
            P.on(eng_b, lambda e: e.tensor_tensor(out=t25[:, :, :, 0, :], in0=src5[:, :, :, 1, :],
                                                  in1=sv_[:, :, 0, :].unsqueeze(1).to_broadcast([128, G, 2, 16]), op=ALU.mult),
                 reads=list(srcbufs) + [B_cst], writes=[b2])
            P.on(eng_b, lambda e: e.tensor_tensor(out=t25[:, :, :, 1, :], in0=src5[:, :, :, 0, :],
                                                  in1=sv_[:, :, 1, :].unsqueeze(1).to_broadcast([128, G, 2, 16]), op=ALU.mult),
                 reads=list(srcbufs) + [B_cst], writes=[b2])
            P.on(eng_a, lambda e: e.tensor_tensor(out=dst, in0=t1[:, 0:W], in1=t2[:, 0:W], op=ALU.add), reads=[b1, b2], writes=list(dstbufs))

        def tslice(T):
            return slice(T * 128, (T + 1) * 128)

        def setup_layer(l):
            stg = [wstage[:, 0:4096].rearrange("p (k n) -> p k n", k=8, n=512), arena_view(16384, [128, 8, 512], F32)]
            stgB = [B_wstage, Buf()]
            smb = arena_view(32768, [128, 8, 128], F32)
            B_smb = Buf()
            P.dve(lambda e: e.tensor_copy(out=smb, in_=smod[:].unsqueeze(2).to_broadcast([128, 8, 128])), reads=[B_cst], writes=[B_smb])
            P.dma(normg[:], normg_in[l], writes=[B_misc])
            P.dma(rdl[:], rdl_in[l].partition_broadcast(128), writes=[B_misc])
            dlam, dlamb = r256.next()
            P.dma(dlam[:], dlam_in[l].partition_broadcast(128), writes=[dlamb])
            P.dma(g4[:, 0:64], qng_in[l].partition_broadcast(128), writes=[B_misc])
            P.dma(g4[:, 64:128], qng_in[l].partition_broadcast(128), writes=[B_misc])
            P.dma(g4[:, 128:192], kng_in[l].partition_broadcast(128), writes=[B_misc])
            P.dma(g4[:, 192:256], kng_in[l].partition_broadcast(128), writes=[B_misc])
            for cb in range(6):
                st_, stb_ = stg[cb % 2], stgB[cb % 2]
                P.dma(st_, wada_in[l][:, :, cb * 512:(cb + 1) * 512], writes=[stb_])
                bt, btb = r512.next()
                P.dma(bt[:], bada_in[l][cb * 512:(cb + 1) * 512].partition_broadcast(128), writes=[btb])
                bk = cb % 4
                for kc in range(8):
                    P.pe(lambda e, kc=kc, st_=st_, bk=bk: e.matmul(banks[bk][:, 0:512], lhsT=smb[:, kc, :], rhs=st_[:, kc, :],
                                                                     start=(kc == 0), stop=(kc == 7)),
                         reads=[B_smb, stb_], writes=[bankB[bk]])
                if cb >= 4:
                    P.dve(lambda e, bk=bk, bt=bt, cb=cb: e.tensor_tensor(out=gate_b[:, (cb - 4) * 512:(cb - 3) * 512], in0=banks[bk][:, 0:512],
                                                                          in1=bt[:], op=ALU.add),
                          reads=[bankB[bk], btb], writes=[B_gate])
                else:
                    P.dve(lambda e, bk=bk, bt=bt: e.tensor_tensor(out=bt[:], in0=banks[bk][:, 0:512], in1=bt[:], op=ALU.add),
                          reads=[bankB[bk], btb], writes=[btb])
                    which = cb // 2
                    tb = 4 + (cb % 2)
                    for jj in range(4):
                        kc = (cb % 2) * 4 + jj
                        P.pe(lambda e, jj=jj, bt=bt, tb=tb: e.transpose(banks[tb][:, jj * 128:(jj + 1) * 128], bt[:, jj * 128:(jj + 1) * 128], identf[:]),
                             reads=[btb, B_cst], writes=[bankB[tb]])
                        P.act(lambda e, jj=jj, tb=tb, which=which, kc=kc: e.copy(out=modT[:, which, kc:kc + 1], in_=banks[tb][:, jj * 128:jj * 128 + 1]),
                              reads=[bankB[tb]], writes=[B_modAB])
            P.dve(lambda e: e.scalar_tensor_tensor(out=modA[:], in0=modT[:, 1, :], scalar=1.0, in1=normg[:], op0=ALU.add, op1=ALU.mult),
                  reads=[B_modAB, B_misc], writes=[B_modAB])
            pr, prb = r256.next()
            P.dve(lambda e: e.tensor_tensor(out=pr[:, 0:64], in0=dlam[:, 0:64], in1=dlam[:, 64:128], op=ALU.mult), reads=[dlamb], writes=[prb])
            P.dve(lambda e: e.tensor_tensor(out=pr[:, 64:128], in0=dlam[:, 128:192], in1=dlam[:, 192:256], op=ALU.mult), reads=[dlamb], writes=[prb])
            s12, s12b = rs.next()
            P.dve(lambda e: e.tensor_reduce(out=s12[:, 0:2], in_=pr[:, 0:128].rearrange("p (a d) -> p a d", a=2, d=64), axis=AX.X, op=ALU.add),
                  reads=[prb], writes=[s12b])
            P.act(lambda e: e.activation(out=s12[:, 0:2], in_=s12[:, 0:2], func=AF.Exp), reads=[s12b], writes=[s12b])
            lam_init = 0.8 - 0.6 * math.exp(-0.3 * l)
            P.dve(lambda e: e.tensor_tensor(out=small[:, 0:1], in0=s12[:, 1:2], in1=s12[:, 0:1], op=ALU.subtract), reads=[s12b], writes=[B_misc])
            P.dve(lambda e: e.tensor_scalar(out=small[:, 0:1], in0=small[:, 0:1], scalar1=-lam_init, scalar2=None, op0=ALU.add),
                  reads=[B_misc], writes=[B_misc])
            P.act(lambda e: e.activation(out=lg[:], in_=rdl[:], func=AF.Exp, scale=-1.0), reads=[B_misc], writes=[B_misc])
            P.act(lambda e: e.activation(out=lg[:], in_=lg[:], func=AF.Ln, bias=1.0), reads=[B_misc], writes=[B_misc])
            P.dve(lambda e: e.tensor_scalar(out=lg[:], in0=lg[:], scalar1=-1.0, scalar2=None, op0=ALU.mult), reads=[B_misc], writes=[B_misc])

        def norm_phase(l):
            src = x_in if l == 0 else xs_scr
            for T in range(NT):
                xt, xb_ = xring.next()
                P.dma(xt[:], src[T * 128:(T + 1) * 128, :], reads=([B_xs[T]] if l > 0 else []), writes=[xb_])
                xh, xhb = xhring.next()
                ss, ssb = rs.next()
                P.act(lambda e, xt=xt, xh=xh, ss=ss: e.activation(out=xh[:], in_=xt[:], func=AF.Square, accum_out=ss[:, 0:1]),
                      reads=[xb_], writes=[xhb, ssb])
                rstd, rb_ = rstd_from_ss(ss[:, 0:1], ssb, 1, 1.0 / D_MODEL, EPS)
                P.dve(lambda e, xt=xt, xh=xh, rstd=rstd: e.tensor_scalar(out=xh[:], in0=xt[:], scalar1=rstd[:, 0:1], scalar2=None, op0=ALU.mult),
                      reads=[xb_, rb_], writes=[xhb])
                bk = 6 + (T % 2)
                for kc in range(8):
                    P.pe(lambda e, kc=kc, xh=xh, bk=bk: e.transpose(banksb[bk][:, kc * 128:(kc + 1) * 128], xh[:, kc * 128:(kc + 1) * 128], identb[:]),
                         reads=[xhb, B_cst], writes=[bankB[bk]])
                for kc in range(8):
                    if kc % 2 == 0:
                        P.dve(lambda e, kc=kc, bk=bk, T=T: e.tensor_scalar(out=hT[:, kc, tslice(T)], in0=banksb[bk][:, kc * 128:(kc + 1) * 128],
                                                                            scalar1=modA[:, kc:kc + 1], scalar2=modT[:, 0, kc:kc + 1],
                                                                            op0=ALU.mult, op1=ALU.add),
                              reads=[bankB[bk], B_modAB], writes=[B_hT[T]])
                    else:
                        P.act(lambda e, kc=kc, bk=bk, T=T: e.activation(out=hT[:, kc, tslice(T)], in_=banksb[bk][:, kc * 128:(kc + 1) * 128],
                                                                         func=AF.Identity, scale=modA[:, kc:kc + 1], bias=modT[:, 0, kc:kc + 1]),
                              reads=[bankB[bk], B_modAB], writes=[B_hT[T]])

        def load_unit_weights(l, u):
            W = UNIT_W[u]
            wb = wbf[:, 0:8 * W].rearrange("p (k n) -> p k n", k=8, n=W)
            engs = ["dve", "pool", "act", "dve", "pool", "act", "dve", "pool"]
            for (a, b) in [(0, 512)] + ([(512, W)] if W > 512 else []):
                wd = b - a
                ws = wstage[:, 0:8 * wd].rearrange("p (k n) -> p k n", k=8, n=wd)
                P.dma(ws, win_in[l][:, :, UNIT_OFF[u] + a:UNIT_OFF[u] + b], writes=[B_wstage])
                for kc in range(8):
                    if engs[kc] == "act":
                        P.act(lambda e, kc=kc, ws=ws, a=a, b=b: e.copy(out=wb[:, kc, a:b], in_=ws[:, kc, :]), reads=[B_wstage], writes=[B_wbf])
                    else:
                        P.on(engs[kc], lambda e, kc=kc, ws=ws, a=a, b=b: e.tensor_copy(out=wb[:, kc, a:b], in_=ws[:, kc, :]), reads=[B_wstage], writes=[B_wbf])
            return wb

        def project(wb, T, bk, c0, c1):
            for kc in range(8):
                P.pe(lambda e, kc=kc: e.matmul(banks[bk][:, 0:c1 - c0], lhsT=hT[:, kc, tslice(T)], rhs=wb[:, kc, c0:c1],
                                               start=(kc == 0), stop=(kc == 7)),
                     reads=[B_hT[T], B_wbf], writes=[bankB[bk]])

        def mixed_out(mt, mtb, chunk, T, tbank, eng="act"):
            P.pe(lambda e: e.transpose(banksb[tbank][:, 0:128], mt, identb[:]), reads=[mtb, B_cst], writes=[bankB[tbank]])
            if eng == "act":
                P.act(lambda e: e.copy(out=mixT[:, chunk, tslice(T)], in_=banksb[tbank][:, 0:128]), reads=[bankB[tbank]], writes=[B_mixT[chunk][T]])
            else:
                P.dve(lambda e: e.tensor_copy(out=mixT[:, chunk, tslice(T)], in_=banksb[tbank][:, 0:128]), reads=[bankB[tbank]], writes=[B_mixT[chunk][T]])

        def diff_unit(l, h, DB):
            u = 4 + h
            chunk = UNIT_CHUNK[u]
            if h == 0:
                P.barrier()
            par = h % 2
            base = par * 27728
            QT = arena_view(base + 0, [128, 2, TOK], BF16)
            KT = arena_view(base + 8192, [128, 2, 2560], BF16)
            V = arena_view(base + 18432, [128, NKT, 130], BF16)
            sg = arena_view(base + 23632, [128, NT, 128], BF16)
            ckst = arena_view(55456, [128, 4, 128], F32)
            cvst = arena_view(57504, [128, 4, 128], F32)
            ckb = arena_view(59552, [128, 4, 128], BF16)
            B_QT, B_KT, B_V, B_sg, B_qm = DB["set"][par]
            B_ck = DB["ck"]
            wb = load_unit_weights(l, u)
            import os as _os
            DD0 = _os.environ.get('DIFFDBG', '')
            if 'nomask' not in DD0:
                for c in range(2):
                    P.dma(QT[64:72, c, :], qmask_in, writes=[B_qm])
                    P.dma(KT[64:72, c, :], kmask_in, writes=[B_qm])
            if 'noctx' not in DD0:
                P.dma(ckst, ck_in[l].rearrange("(t p) n -> p t n", p=128)[:, :, h * 128:(h + 1) * 128], writes=[B_ck])
                P.dma(cvst, cv_in[l].rearrange("(t p) n -> p t n", p=128)[:, :, h * 128:(h + 1) * 128], writes=[B_ck])
                P.pool(lambda e: e.memset(V[:, :, 128:130], 1.0), writes=B_V)
                P.dve(lambda e: e.tensor_copy(out=ckb, in_=ckst), reads=[B_ck], writes=[B_ck])
                for pt in range(4):
                    bk = 5
                    for c in range(2):
                        P.pe(lambda e, pt=pt, c=c, bk=bk: e.transpose(banksb[bk][0:64, c * 128:(c + 1) * 128], ckb[:, pt, c * 64:(c + 1) * 64], identb[:]),
                             reads=[B_ck, B_cst], writes=[bankB[bk]])
                    P.act(lambda e, pt=pt, bk=bk: e.copy(out=KT[0:64, :, 2048 + pt * 128:2048 + (pt + 1) * 128],
                                                         in_=banksb[bk][0:64, 0:256].rearrange("p (c t) -> p c t", c=2, t=128)),
                          reads=[bankB[bk]], writes=[B_KT[16 + pt]])
                    P.any(lambda e, pt=pt: e.tensor_copy(out=V[:, 16 + pt, 0:128], in_=cvst[:, pt, :]), reads=[B_ck], writes=[B_V[16 + pt]])

            if 'noA' in DD0:
                return
            for T in range(NT):
                zb = 6 + (T % 2)
                project(wb, T, zb, 0, 512)
                z = banks[zb]
                sq, sqb = r256.next()
                P.act(lambda e, z=z, sq=sq: e.activation(out=sq[:], in_=z[:, 0:256], func=AF.Square), reads=[bankB[zb]], writes=[sqb])
                ss, ssb = rs.next()
                P.dve(lambda e, sq=sq, ss=ss: e.tensor_reduce(out=ss[:, 0:4], in_=sq[:].rearrange("p (g d) -> p g d", g=4, d=64), axis=AX.X, op=ALU.add),
                      reads=[sqb], writes=[ssb])
                rstd, rb_ = rstd_from_ss(ss[:, 0:4], ssb, 4, 1.0 / 64, EPS)
                nq, nqb = r256.next()
                P.dve(lambda e, z=z, nq=nq, rstd=rstd: e.tensor_tensor(out=nq[:].rearrange("p (g d) -> p g d", g=4, d=64),
                                                                      in0=z[:, 0:256].rearrange("p (g d) -> p g d", g=4, d=64),
                                                                      in1=rstd[:, 0:4].unsqueeze(2).to_broadcast([128, 4, 64]), op=ALU.mult),
                      reads=[bankB[zb], rb_], writes=[nqb])
                P.any(lambda e, nq=nq: e.tensor_tensor(out=nq[:], in0=nq[:], in1=g4[:], op=ALU.mult), reads=[nqb, B_misc], writes=[nqb])
                if 'nonk' not in DD0:
                    P.dma(nk_out[l][T * 128:(T + 1) * 128, h * 128:(h + 1) * 128], nq[:, 128:256], reads=[nqb])
                rt, rtb = rb256.next()
                import os as _os
                rope(nq[:], [nqb], rt[:], [rtb], T, 4, "any", "any")
                if 'noT' not in DD0:
                    tb = 5
                    for g in range(4):
                        P.pe(lambda e, g=g, rt=rt, tb=tb: e.transpose(banksb[tb][0:64, g * 128:(g + 1) * 128], rt[:, g * 64:(g + 1) * 64], identb[:]),
                             reads=[rtb, B_cst], writes=[bankB[tb]])
                    if 'noTq' not in DD0:
                      P.act(lambda e, tb=tb, T=T: e.copy(out=QT[0:64, :, tslice(T)], in_=banksb[tb][0:64, 0:256].rearrange("p (c t) -> p c t", c=2, t=128)),
                          reads=[bankB[tb]], writes=[B_QT[T]])
                    if 'noTk' not in DD0:
                      P.act(lambda e, tb=tb, T=T: e.copy(out=KT[0:64, :, tslice(T)], in_=banksb[tb][0:64, 256:512].rearrange("p (c t) -> p c t", c=2, t=128)),
                          reads=[bankB[tb]], writes=[B_KT[T]])
                vst, vstb = r128.next()
                P.dve(lambda e, z=z, vst=vst: e.tensor_copy(out=vst[:], in_=z[:, 256:384]), reads=[bankB[zb]], writes=[vstb])
                if 'nonk' not in DD0:
                    P.dma(nv_out[l][T * 128:(T + 1) * 128, h * 128:(h + 1) * 128], vst[:], reads=[vstb])
                P.any(lambda e, vst=vst, T=T: e.tensor_copy(out=V[:, T, 0:128], in_=vst[:]), reads=[vstb], writes=[B_V[T]])
                th, thb = r128.next()
                P.act(lambda e, z=z, th=th: e.activation(out=th[:], in_=z[:, 384:512], func=AF.Tanh, scale=0.5), reads=[bankB[zb]], writes=[thb])
                P.dve(lambda e, z=z, th=th, T=T: e.scalar_tensor_tensor(out=sg[:, T, :], in0=th[:], scalar=1.0, in1=z[:, 384:512], op0=ALU.add, op1=ALU.mult),
                      reads=[thb, bankB[zb]], writes=[B_sg[T]])
            import os as _os
            DD = _os.environ.get('DIFFDBG', '')
            if 'noB' in DD:
                return
            KR = 64 if 'k64' in DD else 72
            lam_init = 0.8 - 0.6 * math.exp(-0.3 * l)
            c0 = 0.5 * (1.0 - lam_init)
            OB = [2, 3, 4]

            def acc(c, qi):
                a = c * 4 + qi
                return OB[a // 3], (a % 3) * 160

            for qb in range(4):
                for k in OB:
                    P.pe(lambda e, k=k: e.matmul(banks[k][:, 0:512], lhsT=zerosb[:, 0:128], rhs=zerosb[:, 0:512], start=True, stop=False, skip_group_check=True),
                         reads=[B_cst], writes=[bankB[k]])
                steps = [(c, kt) for c in range(2) for kt in range(NKT)]
                pts = {}

                def emit_st(i):
                    c, kt = steps[i]
                    sbk = i % 2
                    P.pe(lambda e, c=c, kt=kt, sbk=sbk: e.matmul(banks[sbk][:, 0:512], lhsT=KT[0:KR, c, kt * 128:(kt + 1) * 128],
                                                                 rhs=QT[0:KR, c, qb * 512:(qb + 1) * 512], start=True, stop=True),
                         reads=[B_KT[kt], B_qm] + B_QT[qb * 4:qb * 4 + 4], writes=[bankB[sbk]])
                    pt_, ptb = rpt.next()
                    P.act(lambda e, sbk=sbk, pt_=pt_: e.activation(out=pt_[:], in_=banks[sbk][:, 0:512], func=AF.Exp, scale=0.125),
                          reads=[bankB[sbk]], writes=[ptb])
                    pts[i] = (pt_, ptb)

                def emit_pv(i):
                    c, kt = steps[i]
                    pt_, ptb = pts.pop(i)
                    for qi in range(4):
                        bk, off = acc(c, qi)
                        P.pe(lambda e, qi=qi, bk=bk, off=off, pt_=pt_, kt=kt: e.matmul(banks[bk][:, off:off + 129], lhsT=pt_[:, qi * 128:(qi + 1) * 128],
                                                                                       rhs=V[:, kt, 0:129], start=False, stop=(kt == NKT - 1), skip_group_check=True),
                             reads=[ptb, B_V[kt]], writes=[bankB[bk]])

                emit_st(0)
                for i in range(len(steps)):
                    if i + 1 < len(steps):
                        emit_st(i + 1)
                    emit_pv(i)
                for qi in range(4):
                    T = qb * 4 + qi
                    b0, o0 = acc(0, qi)
                    b1, o1 = acc(1, qi)
                    r01, r01b = rs.next()
                    P.dve(lambda e, r01=r01: e.reciprocal(out=r01[:, 0:1], in_=banks[b0][:, o0 + 128:o0 + 129]), reads=[bankB[b0]], writes=[r01b])
                    P.dve(lambda e, r01=r01: e.reciprocal(out=r01[:, 1:2], in_=banks[b1][:, o1 + 128:o1 + 129]), reads=[bankB[b1]], writes=[r01b])
                    P.dve(lambda e, r01=r01: e.tensor_tensor(out=r01[:, 1:2], in0=r01[:, 1:2], in1=small[:, 0:1], op=ALU.mult), reads=[r01b, B_misc], writes=[r01b])
                    d, db = r128.next()
                    P.dve(lambda e, d=d, r01=r01: e.tensor_scalar(out=d[:], in0=banks[b0][:, o0:o0 + 128], scalar1=r01[:, 0:1], scalar2=None, op0=ALU.mult),
                          reads=[bankB[b0], r01b], writes=[db])
                    P.dve(lambda e, d=d, r01=r01: e.scalar_tensor_tensor(out=d[:], in0=banks[b1][:, o1:o1 + 128], scalar=r01[:, 1:2], in1=d[:],
                                                                        op0=ALU.mult, op1=ALU.add),
                          reads=[bankB[b1], r01b, db], writes=[db])
                    jk, jkb = r128.next()
                    ss, ssb = rs.next()
                    P.act(lambda e, d=d, jk=jk, ss=ss: e.activation(out=jk[:], in_=d[:], func=AF.Square, accum_out=ss[:, 0:1]), reads=[db], writes=[jkb, ssb])
                    rstd, rb_ = rstd_from_ss(ss[:, 0:1], ssb, 1, 1.0 / (128 * c0 * c0), EPS / (c0 * c0))
                    mt, mtb = rb128.next()
                    P.dve(lambda e, d=d, rstd=rstd, mt=mt, T=T: e.scalar_tensor_tensor(out=mt[:], in0=d[:], scalar=rstd[:, 0:1], in1=sg[:, T, :],
                                                                                      op0=ALU.mult, op1=ALU.mult),
                          reads=[db, rb_, B_sg[T]], writes=[mtb])
                    mixed_out(mt[:], mtb, chunk, T, 5, eng="dve")

        def ret_unit(l, p, RB):
            u = p
            chunk = UNIT_CHUNK[u]
            if p == 0:
                P.barrier()
            base = p * 28672
            qT = arena_view(base + 0, [128, TOK], BF16)
            kT = arena_view(base + 4096, [128, TOK], BF16)
            ktok = arena_view(base + 8192, [128, NT, 128], BF16)
            v = arena_view(base + 12288, [128, NT, 128], BF16)
            sg = arena_view(base + 16384, [128, NT, 128], BF16)
            Sbf = arena_view(base + 20480, [128, 2, NT, 128], BF16)
            if p == 0:
                dtm_, qd_, kd_, stS_, lgrow_ = dtm, qd, kd, stS, lgrow
            else:
                dtm_ = arena_view(57344, [128, 2, 128], F32)
                qd_ = arena_view(58368, [128, 2, 128], F32)
                kd_ = arena_view(59392, [128, 2, 128], F32)
                stS_ = arena_view(60416, [128, 2, 128], F32)
                lgrow_ = lgrow1
            B_pair_, B_stS_ = RB[p]
            B_q = [Buf() for _ in range(NT)]
            B_k = [Buf() for _ in range(NT)]
            B_kt = [Buf() for _ in range(NT)]
            B_v = [Buf() for _ in range(NT)]
            B_sg = [Buf() for _ in range(NT)]
            B_S = [[Buf() for _ in range(NT)] for _ in range(2)]
            wb = load_unit_weights(l, u)
            for d_ in range(2):
                for hh in range(2):
                    col = d_ * 4 + 2 * p + hh
                    P.dve(lambda e, d_=d_, hh=hh, col=col: e.tensor_copy(out=lgrow_[hh * 64:(hh + 1) * 64, d_:d_ + 1], in_=lg[hh * 64:(hh + 1) * 64, col:col + 1]),
                          reads=[B_misc], writes=[B_pair_])
            for hh in range(2):
                e12t, e12b = r256.next()
                cf = 2 * p + hh
                cb_ = 4 + 2 * p + hh
                P.act(lambda e, cf=cf: e.activation(out=e12t[:, 0:128], in_=M1, func=AF.Exp, scale=lg[:, cf:cf + 1]), reads=[e12b, B_cst, B_misc], writes=[e12b, B_pair_])
                P.act(lambda e, cb_=cb_: e.activation(out=e12t[:, 128:256], in_=M2, func=AF.Exp, scale=lg[:, cb_:cb_ + 1]), reads=[e12b, B_cst, B_misc], writes=[e12b, B_pair_])
                P.dve(lambda e: e.tensor_tensor(out=e12t[:, 0:128], in0=e12t[:, 0:128], in1=L1, op=ALU.mult), reads=[e12b, B_pair_, B_cst], writes=[e12b, B_pair_])
                P.dve(lambda e: e.tensor_tensor(out=e12t[:, 128:256], in0=e12t[:, 128:256], in1=L2, op=ALU.mult), reads=[e12b, B_pair_, B_cst], writes=[e12b, B_pair_])
                P.dve(lambda e, hh=hh: e.tensor_tensor(out=dtm_[:, hh, :], in0=e12t[:, 0:128], in1=e12t[:, 128:256], op=ALU.add), reads=[e12b, B_pair_], writes=[e12b, B_pair_])
                P.act(lambda e, hh=hh, cf=cf: e.activation(out=kd_[:, 0, hh * 64:(hh + 1) * 64], in_=COLA[:, 0:64], func=AF.Exp, scale=lg[:, cf:cf + 1]),
                      reads=[B_cst, B_misc], writes=[B_pair_])
                P.act(lambda e, hh=hh, cb_=cb_: e.activation(out=kd_[:, 1, hh * 64:(hh + 1) * 64], in_=COLB[:, 0:64], func=AF.Exp, scale=lg[:, cb_:cb_ + 1]),
                      reads=[B_cst, B_misc], writes=[B_pair_])
            P.dve(lambda e: e.tensor_scalar(out=kd_[:], in0=kd_[:], scalar1=0.125, scalar2=None, op0=ALU.mult), reads=[B_pair_], writes=[B_pair_])
            P.act(lambda e: e.activation(out=qd_[:, 0, :], in_=IOTA1, func=AF.Exp, scale=lgrow_[:, 0:1]), reads=[B_cst, B_pair_], writes=[B_pair_])
            P.act(lambda e: e.activation(out=qd_[:, 1, :], in_=IOTA2, func=AF.Exp, scale=lgrow_[:, 1:2]), reads=[B_cst, B_pair_], writes=[B_pair_])
            P.act(lambda e: e.activation(out=lgrow_[:, 2:4], in_=lgrow_[:, 0:2], func=AF.Exp, scale=128.0), reads=[B_pair_], writes=[B_pair_])
            for T in range(NT):
                zb = T % 4
                project(wb, T, zb, 0, 512)
                z = banks[zb]
                rq, rqb = rb128.next()
                t1, b1 = r256.next()
                t2, b2 = r256.next()
                cv_ = ropec[:, T, :]
                sv_ = ropes[:, T, :].rearrange("p (h j i) -> p h j i", h=2, j=2, i=16)
                src3 = z[:, 0:256].rearrange("p (g d) -> p g d", g=4, d=64)
                src5 = z[:, 0:256].rearrange("p (g h j i) -> p g h j i", g=4, h=2, j=2, i=16)
                t13 = t1[:].rearrange("p (g d) -> p g d", g=4, d=64)
                t25 = t2[:].rearrange("p (g h j i) -> p g h j i", g=4, h=2, j=2, i=16)
                P.dve(lambda e, t13=t13, src3=src3, cv_=cv_: e.tensor_tensor(out=t13, in0=src3, in1=cv_.unsqueeze(1).to_broadcast([128, 4, 64]), op=ALU.mult),
                      reads=[bankB[zb], B_cst], writes=[b1])
                P.dve(lambda e, t25=t25, src5=src5, sv_=sv_: e.tensor_tensor(out=t25[:, :, :, 0, :], in0=src5[:, :, :, 1, :],
                                                                             in1=sv_[:, :, 0, :].unsqueeze(1).to_broadcast([128, 4, 2, 16]), op=ALU.mult),
                      reads=[bankB[zb], B_cst], writes=[b2])
                P.dve(lambda e, t25=t25, src5=src5, sv_=sv_: e.tensor_tensor(out=t25[:, :, :, 1, :], in0=src5[:, :, :, 0, :],
                                                                             in1=sv_[:, :, 1, :].unsqueeze(1).to_broadcast([128, 4, 2, 16]), op=ALU.mult),
                      reads=[bankB[zb], B_cst], writes=[b2])
                P.any(lambda e, t1=t1, t2=t2, rq=rq: e.tensor_tensor(out=rq[:], in0=t1[:, 0:128], in1=t2[:, 0:128], op=ALU.add), reads=[b1, b2], writes=[rqb])
                P.any(lambda e, t1=t1, t2=t2, T=T: e.tensor_tensor(out=ktok[:, T, :], in0=t1[:, 128:256], in1=t2[:, 128:256], op=ALU.add),
                       reads=[b1, b2], writes=[B_kt[T]])
                tb = 4 + (T % 2)
                P.pe(lambda e, rq=rq, tb=tb: e.transpose(banksb[tb][:, 0:128], rq[:], identb[:]), reads=[rqb, B_cst], writes=[bankB[tb]])
                P.pe(lambda e, T=T, tb=tb: e.transpose(banksb[tb][:, 128:256], ktok[:, T, :], identb[:]), reads=[B_kt[T], B_cst], writes=[bankB[tb]])
                P.act(lambda e, T=T, tb=tb: e.copy(out=qT[:, tslice(T)], in_=banksb[tb][:, 0:128]), reads=[bankB[tb]], writes=[B_q[T]])
                P.act(lambda e, T=T, tb=tb: e.activation(out=kT[:, tslice(T)], in_=banksb[tb][:, 128:256], func=AF.Copy, scale=0.125),
                      reads=[bankB[tb]], writes=[B_k[T]])
                P.act(lambda e, z=z, T=T: e.copy(out=v[:, T, :], in_=z[:, 256:384]), reads=[bankB[zb]], writes=[B_v[T]])
                th, thb = r128.next()
                P.act(lambda e, z=z, th=th: e.activation(out=th[:], in_=z[:, 384:512], func=AF.Tanh, scale=0.5), reads=[bankB[zb]], writes=[thb])
                P.dve(lambda e, z=z, th=th, T=T: e.scalar_tensor_tensor(out=sg[:, T, :], in0=th[:], scalar=1.0, in1=z[:, 384:512], op0=ALU.add, op1=ALU.mult),
                      reads=[thb, bankB[zb]], writes=[B_sg[T]])
            st_in = [srf_in, srb_in]
            st_out = [nsrf_out, nsrb_out]
            for d_ in range(2):
                P.pool(lambda e, d_=d_: e.memset(stS_[:, d_, :], 0.0), writes=[B_stS_[d_]])
                for hh in range(2):
                    P.dma(stS_[hh * 64:(hh + 1) * 64, d_, hh * 64:(hh + 1) * 64], st_in[d_][l, 2 * p + hh], writes=[B_stS_[d_]])
            for step in range(NT):
                for d_ in range(2):
                    T = step if d_ == 0 else NT - 1 - step
                    S = stS_[:, d_, :]
                    kcol = d_ * 16 + T
                    P.dve(lambda e, S=S, kcol=kcol: e.tensor_scalar(out=S, in0=S, scalar1=keep[:, kcol:kcol + 1], scalar2=None, op0=ALU.mult),
                          reads=[B_stS_[d_], B_cst], writes=[B_stS_[d_]])
                    P.act(lambda e, S=S, d_=d_, T=T: e.copy(out=Sbf[:, d_, T, :], in_=S), reads=[B_stS_[d_]], writes=[B_S[d_][T]])
                    kt_, ktb_ = rb128.next()
                    P.any(lambda e, kt_=kt_, T=T, d_=d_: e.tensor_tensor(out=kt_[:], in0=ktok[:, T, :], in1=kd_[:, d_, :], op=ALU.mult),
                           reads=[B_kt[T], B_pair_], writes=[ktb_])
                    ub = d_
                    P.pe(lambda e, kt_=kt_, T=T, ub=ub: e.matmul(banks[ub][:, 0:128], lhsT=kt_[:], rhs=v[:, T, :], start=True, stop=True),
                         reads=[ktb_, B_v[T]], writes=[bankB[ub]])
                    tmp, tmpb = r128.next()
                    P.dve(lambda e, tmp=tmp, ub=ub: e.tensor_tensor(out=tmp[:], in0=banks[ub][:, 0:128], in1=BM, op=ALU.mult),
                          reads=[bankB[ub], B_cst], writes=[tmpb])
                    P.dve(lambda e, S=S, tmp=tmp, d_=d_: e.scalar_tensor_tensor(out=S, in0=S, scalar=lgrow_[:, 2 + d_:3 + d_], in1=tmp[:], op0=ALU.mult, op1=ALU.add),
                          reads=[B_stS_[d_], B_pair_, tmpb], writes=[B_stS_[d_]])
                    is_out = (T % 2 == 1) if d_ == 0 else (T % 2 == 0)
                    if is_out:
                        so, sob = r128.next()
                        P.act(lambda e, so=so, S=S: e.copy(out=so[:], in_=S), reads=[B_stS_[d_]], writes=[sob])
                        for hh in range(2):
                            P.dma(st_out[d_][l, T // 2, 2 * p + hh], so[hh * 64:(hh + 1) * 64, hh * 64:(hh + 1) * 64], reads=[sob])
            for T in range(NT):
                qf, qfb = rb128.next()
                qb_, qbb = rb128.next()
                P.dve(lambda e, qf=qf, T=T: e.tensor_tensor(out=qf[:], in0=qT[:, tslice(T)], in1=qd_[:, 0, :], op=ALU.mult), reads=[B_q[T], B_pair_], writes=[qfb])
                P.any(lambda e, qb_=qb_, T=T: e.tensor_tensor(out=qb_[:], in0=qT[:, tslice(T)], in1=qd_[:, 1, :], op=ALU.mult), reads=[B_q[T], B_pair_], writes=[qbb])
                ob = 6 + (T % 2)
                ab = 2 + (T % 2)
                P.pe(lambda e, qf=qf, T=T, ob=ob: e.matmul(banks[ob][:, 0:128], lhsT=qf[:], rhs=Sbf[:, 0, T, :], start=True, stop=False, skip_group_check=True),
                     reads=[qfb, B_S[0][T]], writes=[bankB[ob]])
                P.pe(lambda e, qb_=qb_, T=T, ob=ob: e.matmul(banks[ob][:, 0:128], lhsT=qb_[:], rhs=Sbf[:, 1, T, :], start=False, stop=False, skip_group_check=True),
                     reads=[qbb, B_S[1][T]], writes=[bankB[ob]])
                am, amb = rb256.next()
                for hh in range(2):
                    P.pe(lambda e, hh=hh, T=T: e.matmul(banks[2 + hh][:, 0:128], lhsT=kT[hh * 64:(hh + 1) * 64, tslice(T)],
                                                        rhs=qT[hh * 64:(hh + 1) * 64, tslice(T)], start=True, stop=True),
                         reads=[B_k[T], B_q[T]], writes=[bankB[2 + hh]])
                    P.dve(lambda e, am=am, hh=hh: e.tensor_tensor(out=am[:, hh * 128:(hh + 1) * 128], in0=banks[2 + hh][:, 0:128], in1=dtm_[:, hh, :], op=ALU.mult),
                          reads=[bankB[2 + hh], B_pair_], writes=[amb])
                for hh in range(2):
                    P.pe(lambda e, hh=hh, am=am, T=T, ob=ob: e.matmul(banks[ob][:, hh * 64:(hh + 1) * 64], lhsT=am[:, hh * 128:(hh + 1) * 128],
                                                                      rhs=v[:, T, hh * 64:(hh + 1) * 64], start=False, stop=(hh == 1), skip_group_check=True),
                         reads=[amb, B_v[T]], writes=[bankB[ob]])
                finish_pair(banks[ob][:, 0:128], [bankB[ob]], sg, B_sg, chunk, T, 0.5, 4 + (T % 2))

        def finish_pair(o_ap, obufs, sg, B_sg, chunk, T, c0, tbank):
            ss, ssb = rs.next()
            jk, jkb = r128.next()
            for hh in range(2):
                P.act(lambda e, hh=hh: e.activation(out=jk[:, hh * 64:(hh + 1) * 64], in_=o_ap[:, hh * 64:(hh + 1) * 64], func=AF.Square,
                                                    accum_out=ss[:, hh:hh + 1]),
                      reads=obufs, writes=[jkb, ssb])
            rstd, rb_ = rstd_from_ss(ss[:, 0:2], ssb, 2, 1.0 / (64 * c0 * c0), EPS / (c0 * c0))
            mt, mtb = rb128.next()
            for hh in range(2):
                P.dve(lambda e, hh=hh: e.scalar_tensor_tensor(out=mt[:, hh * 64:(hh + 1) * 64], in0=o_ap[:, hh * 64:(hh + 1) * 64], scalar=rstd[:, hh:hh + 1],
                                                              in1=sg[:, T, hh * 64:(hh + 1) * 64], op0=ALU.mult, op1=ALU.mult),
                      reads=list(obufs) + [rb_, B_sg[T]], writes=[mtb])
            mixed_out(mt[:], mtb, chunk, T, tbank)

        def hgrn_unit(l, p):
            u = 2 + p
            chunk = UNIT_CHUNK[u]
            P.barrier()
            q = arena_view(0, [128, NT, 128], BF16)
            kk = arena_view(4096, [128, NT, 256], BF16)
            lf = arena_view(12288, [128, NT, 256], F32)
            v = arena_view(28672, [128, NT, 128], BF16)
            sg = arena_view(32768, [128, NT, 128], BF16)
            oacc = arena_view(36864, [128, NT, 128], F32)
            vmall = arena_view(45056, [128, NT, 512], BF16)
            B_vm = [Buf() for _ in range(NT)]
            B_q = [Buf() for _ in range(NT)]
            B_kk = [Buf() for _ in range(NT)]
            B_lf = [Buf() for _ in range(NT)]
            B_v = [Buf() for _ in range(NT)]
            B_sg = [Buf() for _ in range(NT)]
            B_oa = [Buf() for _ in range(NT)]
            wb = load_unit_weights(l, u)
            for half in range(2):
                P.dve(lambda e, half=half: e.tensor_copy(out=lb2[:, half * 128:(half + 1) * 128], in_=lball[:, l, p * 128:(p + 1) * 128]),
                      reads=[B_cst], writes=[B_pair])
            P.dve(lambda e: e.tensor_scalar(out=omlb2[:], in0=lb2[:], scalar1=-1.0, scalar2=1.0, op0=ALU.mult, op1=ALU.add), reads=[B_pair], writes=[B_pair])
            for T in range(NT):
                zb = 2 * (T % 2)
                project(wb, T, zb, 0, 512)
                project(wb, T, zb + 1, 512, 640)
                z = banks[zb]
                z2 = banks[zb + 1]
                u_, ub_ = r512.next()
                P.act(lambda e, z=z, u_=u_: e.activation(out=u_[:, 0:384], in_=z[:, 0:384], func=AF.Exp, scale=-1.0), reads=[bankB[zb]], writes=[ub_])
                P.act(lambda e, u_=u_: e.activation(out=u_[:, 0:384], in_=u_[:, 0:384], func=AF.Ln, bias=1.0), reads=[ub_], writes=[ub_])
                P.act(lambda e, u_=u_: e.activation(out=u_[:, 0:384], in_=u_[:, 0:384], func=AF.Exp, scale=-1.0), reads=[ub_], writes=[ub_])
                P.dve(lambda e, u_=u_, z=z, T=T: e.tensor_tensor(out=sg[:, T, :], in0=u_[:, 256:384], in1=z[:, 256:384], op=ALU.mult),
                      reads=[ub_, bankB[zb]], writes=[B_sg[T]])
                f_, fb_ = r256.next()
                P.any(lambda e, u_=u_, f_=f_: e.tensor_tensor(out=f_[:], in0=u_[:, 0:256], in1=omlb2[:], op=ALU.mult), reads=[ub_, B_pair], writes=[fb_])
                P.any(lambda e, f_=f_: e.tensor_tensor(out=f_[:], in0=f_[:], in1=lb2[:], op=ALU.add), reads=[fb_, B_pair], writes=[fb_])
                P.act(lambda e, f_=f_, T=T: e.activation(out=lf[:, T, :], in_=f_[:], func=AF.Ln), reads=[fb_], writes=[B_lf[T]])
                P.act(lambda e, f_=f_, T=T: e.activation(out=kk[:, T, :], in_=f_[:], func=AF.Identity, scale=-1.0, bias=1.0),
                      reads=[fb_], writes=[B_kk[T]])
                P.act(lambda e, z=z, T=T: e.activation(out=q[:, T, :], in_=z[:, 384:512], func=AF.Copy, scale=0.125), reads=[bankB[zb]], writes=[B_q[T]])
                P.act(lambda e, z2=z2, T=T: e.copy(out=v[:, T, :], in_=z2[:, 0:128]), reads=[bankB[zb + 1]], writes=[B_v[T]])
            st_in = [shf_in, shb_in]
            st_out = [nshf_out, nshb_out]
            TRI = [TRIF, TRIB]
            for d_ in range(2):
                P.pool(lambda e, d_=d_: e.memset(stS[:, d_, :], 0.0), writes=[B_stS[d_]])
                for hh in range(2):
                    P.dma(stS[hh * 64:(hh + 1) * 64, d_, hh * 64:(hh + 1) * 64], st_in[d_][l, 2 * p + hh], writes=[B_stS[d_]])
            done_first = [False] * NT
            for step in range(NT):
                for d_ in range(2):
                    T = step if d_ == 0 else NT - 1 - step
                    S = stS[:, d_, :]
                    lfd = lf[:, T, d_ * 128:(d_ + 1) * 128]
                    kcol = d_ * 16 + T
                    P.dve(lambda e, S=S, kcol=kcol: e.tensor_scalar(out=S, in0=S, scalar1=keep[:, kcol:kcol + 1], scalar2=None, op0=ALU.mult),
                          reads=[B_stS[d_], B_cst], writes=[B_stS[d_]])
                    sbf, sbfb = rb128.next()
                    P.act(lambda e, S=S, sbf=sbf: e.copy(out=sbf[:], in_=S), reads=[B_stS[d_]], writes=[sbfb])
                    P.pe(lambda e, lfd=lfd, d_=d_: e.matmul(banks[0][:, 0:128], lhsT=TRI[d_], rhs=lfd, start=True, stop=True),
                         reads=[B_cst, B_lf[T]], writes=[bankB[0]])
                    P.pe(lambda e, lfd=lfd: e.matmul(banks[0][:, 128:132], lhsT=lfd, rhs=ind[:], start=True, stop=True),
                         reads=[B_cst, B_lf[T]], writes=[bankB[0]])
                    G_, Gb_ = rs.next()
                    P.act(lambda e, G_=G_: e.activation(out=G_[:, 0:4], in_=banks[0][:, 128:132], func=AF.Exp), reads=[bankB[0]], writes=[Gb_])
                    eq, eqb = r128.next()
                    ek, ekb = r128.next()
                    P.act(lambda e, eq=eq: e.activation(out=eq[:], in_=banks[0][:, 0:128], func=AF.Exp), reads=[bankB[0]], writes=[eqb])
                    P.act(lambda e, ek=ek: e.activation(out=ek[:], in_=banks[0][:, 0:128], func=AF.Exp, scale=-1.0), reads=[bankB[0]], writes=[ekb])
                    qt_, qtb = rb128.next()
                    kt_, ktb = rb128.next()
                    P.dve(lambda e, qt_=qt_, eq=eq, T=T: e.tensor_tensor(out=qt_[:], in0=q[:, T, :], in1=eq[:], op=ALU.mult), reads=[B_q[T], eqb], writes=[qtb])
                    P.any(lambda e, kt_=kt_, ek=ek, T=T, d_=d_: e.tensor_tensor(out=kt_[:], in0=kk[:, T, d_ * 128:(d_ + 1) * 128], in1=ek[:], op=ALU.mult),
                           reads=[B_kk[T], ekb], writes=[ktb])
                    P.pe(lambda e, qt_=qt_: e.transpose(banksb[1][:, 0:128], qt_[:], identb[:]), reads=[qtb, B_cst], writes=[bankB[1]])
                    P.pe(lambda e, kt_=kt_: e.transpose(banksb[1][:, 128:256], kt_[:], identb[:]), reads=[ktb, B_cst], writes=[bankB[1]])
                    qkT, qkTb = rb256.next()
                    P.act(lambda e, qkT=qkT: e.copy(out=qkT[:], in_=banksb[1][:, 0:256]), reads=[bankB[1]], writes=[qkTb])
                    vm, vmb = vmall[:, T, :], B_vm[T]
                    if not done_first[T]:
                        for j in range(4):
                            P.any(lambda e, j=j, vm=vm, T=T: e.tensor_scalar(out=vm[:, j * 128:(j + 1) * 128], in0=v[:, T, :], scalar1=ind[:, j:j + 1],
                                                                             scalar2=None, op0=ALU.mult),
                                  reads=[B_v[T], B_cst], writes=[vmb])
                    P.pe(lambda e, kt_=kt_, vm=vm: e.matmul(banks[2][:, 0:512], lhsT=kt_[:], rhs=vm, start=True, stop=True),
                         reads=[ktb, vmb], writes=[bankB[2]])
                    am, amb = rb256.next()
                    for hh in range(2):
                        abk = 3 + hh
                        P.pe(lambda e, hh=hh, qkT=qkT, abk=abk: e.matmul(banks[abk][:, 0:128], lhsT=qkT[hh * 64:(hh + 1) * 64, 128:256],
                                                                          rhs=qkT[hh * 64:(hh + 1) * 64, 0:128], start=True, stop=True),
                             reads=[qkTb], writes=[bankB[abk]])
                        P.dve(lambda e, am=am, d_=d_, hh=hh, abk=abk: e.tensor_tensor(out=am[:, hh * 128:(hh + 1) * 128], in0=banks[abk][:, 0:128], in1=TRI[d_], op=ALU.mult),
                              reads=[bankB[abk], B_cst], writes=[amb])
                    ob = 5 + d_
                    jorder = [0, 1, 2, 3] if d_ == 0 else [3, 2, 1, 0]
                    cur, curb = sbf, sbfb
                    for ji, j in enumerate(jorder):
                        P.pe(lambda e, j=j, qkT=qkT, cur=cur: e.matmul(banks[ob][32 * j:32 * j + 32, 0:128], lhsT=qkT[:, 32 * j:32 * j + 32], rhs=cur[:],
                                                                       start=True, stop=False, tile_position=(0, 32 * j), skip_group_check=True),
                             reads=[qkTb, curb], writes=[bankB[ob]])
                        tg, tgb = r128.next()
                        P.dve(lambda e, tg=tg, j=j, G_=G_: e.scalar_tensor_tensor(out=tg[:], in0=banks[2][:, j * 128:(j + 1) * 128], scalar=G_[:, j:j + 1], in1=BM,
                                                                                  op0=ALU.mult, op1=ALU.mult),
                              reads=[bankB[2], Gb_, B_cst], writes=[tgb])
                        P.dve(lambda e, S=S, tg=tg, j=j, G_=G_: e.scalar_tensor_tensor(out=S, in0=S, scalar=G_[:, j:j + 1], in1=tg[:], op0=ALU.mult, op1=ALU.add),
                              reads=[B_stS[d_], Gb_, tgb], writes=[B_stS[d_]])
                        if ji < 3:
                            cur, curb = rb128.next()
                            P.act(lambda e, S=S, cur=cur: e.copy(out=cur[:], in_=S), reads=[B_stS[d_]], writes=[curb])
                    for hh in range(2):
                        P.pe(lambda e, hh=hh, am=am, T=T: e.matmul(banks[ob][:, hh * 64:(hh + 1) * 64], lhsT=am[:, hh * 128:(hh + 1) * 128],
                                                                   rhs=v[:, T, hh * 64:(hh + 1) * 64], start=False, stop=(hh == 1), skip_group_check=True),
                             reads=[amb, B_v[T]], writes=[bankB[ob]])
                    is_out = (T % 2 == 1) if d_ == 0 else (T % 2 == 0)
                    if is_out:
                        so, sob = r128.next()
                        P.act(lambda e, so=so, S=S: e.copy(out=so[:], in_=S), reads=[B_stS[d_]], writes=[sob])
                        for hh in range(2):
                            P.dma(st_out[d_][l, T // 2, 2 * p + hh], so[hh * 64:(hh + 1) * 64, hh * 64:(hh + 1) * 64], reads=[sob])
                    if not done_first[T]:
                        done_first[T] = True
                        P.act(lambda e, T=T: e.copy(out=oacc[:, T, :], in_=banks[ob][:, 0:128]), reads=[bankB[ob]], writes=[B_oa[T]])
                    else:
                        ot, otb = r128.next()
                        P.dve(lambda e, ot=ot, T=T: e.tensor_tensor(out=ot[:], in0=banks[ob][:, 0:128], in1=oacc[:, T, :], op=ALU.add),
                              reads=[bankB[ob], B_oa[T]], writes=[otb])
                        finish_pair(ot[:], [otb], sg, B_sg, chunk, T, 1.0, 7)

        def out_phase(l, last):
            P.barrier()
            wo = arena_view(0, [128, 8, 1024], BF16)
            B_wo = Buf()
            for half in range(2):
                ws = wstage[:, 0:4096].rearrange("p (k n) -> p k n", k=8, n=512)
                P.dma(ws, wout_in[l][:, :, half * 512:(half + 1) * 512], writes=[B_wstage])
                for kc in range(8):
                    eng = "any"
                    P.on(eng, lambda e, kc=kc, half=half: e.tensor_copy(out=wo[:, kc, half * 512:(half + 1) * 512], in_=ws[:, kc, :]),
                         reads=[B_wstage], writes=[B_wo])
            src = x_in if l == 0 else xs_scr
            dst = y_out if last else xs_scr
            for T in range(NT):
                xt, xb_ = xring.next()
                P.dma(xt[:], src[T * 128:(T + 1) * 128, :], reads=([B_xs[T]] if l > 0 else []), writes=[xb_])
                for nb in range(2):
                    bk = 2 * (T % 2) + nb
                    for c in range(8):
                        P.pe(lambda e, c=c, nb=nb, bk=bk, T=T: e.matmul(banks[bk][:, 0:512], lhsT=mixT[:, c, tslice(T)], rhs=wo[:, c, nb * 512:(nb + 1) * 512],
                                                                        start=(c == 0), stop=(c == 7)),
                             reads=[B_mixT[c][T], B_wo], writes=[bankB[bk]])
                    tmp, tmpb = r512.next()
                    P.dve(lambda e, tmp=tmp, bk=bk, nb=nb: e.tensor_tensor(out=tmp[:], in0=banks[bk][:, 0:512], in1=gate_b[:, nb * 512:(nb + 1) * 512], op=ALU.mult),
                          reads=[bankB[bk], B_gate], writes=[tmpb])
                    P.any(lambda e, tmp=tmp, xt=xt, nb=nb: e.tensor_tensor(out=xt[:, nb * 512:(nb + 1) * 512], in0=tmp[:], in1=xt[:, nb * 512:(nb + 1) * 512], op=ALU.add),
                           reads=[tmpb, xb_], writes=[xb_])
                P.dma(dst[T * 128:(T + 1) * 128, :], xt[:], reads=[xb_], writes=([] if last else [B_xs[T]]))

        for l in range(L):
            setup_layer(l)
            norm_phase(l)
            RB = [(Buf(), [Buf(), Buf()]) for _ in range(2)]
            for p in range(2):
                if units_enabled is None or ("r%d" % p) in units_enabled:
                    ret_unit(l, p, RB)
            for p in range(2):
                if units_enabled is None or ("g%d" % p) in units_enabled:
                    hgrn_unit(l, p)
            DB = {"set": [([Buf() for _ in range(NT)], [Buf() for _ in range(NKT)], [Buf() for _ in range(NKT)], [Buf() for _ in range(NT)], Buf()) for _ in range(2)], "ck": Buf()}
            for h in range(4):
                if units_enabled is None or ("d%d" % h) in units_enabled:
                    diff_unit(l, h, DB)
            if dbg:
                P.barrier()
                P.dma(dbg_out[l], mixT[:], reads=[b for row in B_mixT for b in row])
            out_phase(l, last=(l == L - 1))

        with nc.Block() as block:
            run = P.build(sems, dsems, reorder=REORDER)
            block.sync(lambda e: run("sp", e))
            block.tensor(lambda e: run("pe", e))
            block.scalar(lambda e: run("act", e))
            block.vector(lambda e: run("dve", e))
            block.gpsimd(lambda e: run("pool", e))
    return nc


def _unit_perm():
    off = dict(rq=0, rk=256, rv=512, rg=768, dq=1024, dk=1536, dv=2048, dg=2560, hq=3072, hff=3328, hfb=3584, hi=3840, hg=4096)
    cols = []
    for p in range(2):
        for n in ("rq", "rk", "rv", "rg"):
            cols += list(range(off[n] + 128 * p, off[n] + 128 * p + 128))
    for p in range(2):
        for n in ("hff", "hfb", "hg", "hq", "hi"):
            cols += list(range(off[n] + 128 * p, off[n] + 128 * p + 128))
    for h in range(4):
        for n in ("dq", "dk", "dv", "dg"):
            cols += list(range(off[n] + 128 * h, off[n] + 128 * h + 128))
    return np.array(cols, dtype=np.int64)


def _constants():
    s = np.arange(128, dtype=np.float32)[:, None]
    t = np.arange(128, dtype=np.float32)[None, :]
    M1 = np.maximum(t - s, 0)
    L1 = (s <= t).astype(np.float32)
    M2 = np.maximum(s - t, 0)
    L2 = (s >= t).astype(np.float32)
    IOTA1 = np.broadcast_to(t + 1, (128, 128))
    IOTA2 = np.broadcast_to(128 - t, (128, 128))
    COLA = np.broadcast_to(127 - s, (128, 128))
    COLB = np.broadcast_to(s, (128, 128))
    same = (np.floor(s / 32) == np.floor(t / 32))
    TRIF = (same & (s <= t)).astype(np.float32)
    TRIB = (same & (s >= t)).astype(np.float32)
    BM = (np.floor(s / 64) == np.floor(t / 64)).astype(np.float32)
    cst = np.stack([M1, L1, M2, L2, IOTA1, IOTA2, COLA, COLB, TRIF, TRIB, BM], axis=1).astype(np.float32)
    ind = (np.floor(np.arange(128)[:, None] / 32) == np.arange(4)[None, :]).astype(np.float32)
    return np.ascontiguousarray(cst), np.ascontiguousarray(ind)


def _rope_tables(sample):
    ropec = np.ones((128, 16, 64), np.float32)
    ropes = np.zeros((128, 16, 64), np.float32)
    if sample:
        tt = np.arange(TOK)
        row = (tt // 64).astype(np.float32)
        col = (tt % 64).astype(np.float32)
        inv = (np.float32(10000.0) ** (-np.arange(16, dtype=np.float32) / np.float32(16))).astype(np.float32)
        ar = (row[:, None] * inv[None, :]).astype(np.float32)
        ac = (col[:, None] * inv[None, :]).astype(np.float32)
        c = np.concatenate([np.cos(ar), np.cos(ar), np.cos(ac), np.cos(ac)], axis=1).astype(np.float32)
        s_ = np.concatenate([-np.sin(ar), np.sin(ar), -np.sin(ac), np.sin(ac)], axis=1).astype(np.float32)
        ropec = np.ascontiguousarray(c.reshape(16, 128, 64).transpose(1, 0, 2))
        ropes = np.ascontiguousarray(s_.reshape(16, 128, 64).transpose(1, 0, 2))
    return ropec, ropes


_NC_CACHE = {}


def kernel(x_prompt, x_sample, c, c_ctx, cache_diff_k, cache_diff_v, state_ret_fwd, state_ret_bwd,
           state_hgrn_fwd, state_hgrn_bwd, norm_g, w_ada, b_ada, w_in, w_out, ret_decay_logit,
           diff_qn_g, diff_kn_g, diff_lambda, hgrn_lb_logit, _dbg=False, _units=None, _L=2):
    f32 = np.float32
    bf = ml_dtypes.bfloat16
    A = lambda a: np.ascontiguousarray(np.asarray(a, dtype=f32))
    x_prompt, x_sample, c, c_ctx = A(x_prompt), A(x_sample), A(c), A(c_ctx)
    perm = _unit_perm()
    w_in_p = A(w_in)[:, :, perm]
    win = np.ascontiguousarray(w_in_p.reshape(2, 8, 128, 4352).transpose(0, 2, 1, 3))
    wada = np.ascontiguousarray(A(w_ada).reshape(2, 8, 128, 3072).transpose(0, 2, 1, 3))
    wout = np.ascontiguousarray(A(w_out).reshape(2, 8, 128, 1024).transpose(0, 2, 1, 3))
    normg = np.ascontiguousarray(A(norm_g).reshape(2, 8, 128).transpose(0, 2, 1))
    cst, ind = _constants()
    shared = dict(
        normg=normg, wada=wada, bada=A(b_ada), win=win, wout=wout, rdl=A(ret_decay_logit).reshape(2, 8),
        qng=A(diff_qn_g), kng=A(diff_kn_g), dlam=A(diff_lambda).reshape(2, 256), hlb=A(hgrn_lb_logit).reshape(512),
        identb=np.eye(128, dtype=f32).astype(bf), identf=np.eye(128, dtype=f32), cst=cst, ind=ind,
    )
    ropec_s, ropes_s = _rope_tables(True)
    ropec_p, ropes_p = _rope_tables(False)
    z64 = np.zeros((2, 4, 64, 64), f32)
    zc = np.zeros((2, 512, 512), f32)
    in_maps = []
    for core in range(8):
        m = dict(shared)
        if core < 4:
            b = core
            m["x"] = x_sample[b]
            m["modv"] = np.ascontiguousarray(c[b].reshape(8, 128).T)
            m["ck"] = np.ascontiguousarray(A(cache_diff_k)[b].reshape(2, 512, 512))
            m["cv"] = np.ascontiguousarray(A(cache_diff_v)[b].reshape(2, 512, 512))
            m["srf"], m["srb"] = A(state_ret_fwd)[b], A(state_ret_bwd)[b]
            m["shf"], m["shb"] = A(state_hgrn_fwd)[b], A(state_hgrn_bwd)[b]
            m["ropec"], m["ropes"] = ropec_s, ropes_s
            m["qmask"] = np.zeros((8, 2048), f32).astype(bf)
            m["kmask"] = np.zeros((8, 2560), f32).astype(bf)
            m["keep"] = np.ones((128, 32), f32)
        else:
            j = core - 4
            m["x"] = np.ascontiguousarray(x_prompt[8 * j:8 * j + 8].reshape(2048, 1024))
            m["modv"] = np.ascontiguousarray(c_ctx.reshape(8, 128).T)
            m["ck"], m["cv"] = zc, zc
            m["srf"], m["srb"], m["shf"], m["shb"] = z64, z64, z64, z64
            m["ropec"], m["ropes"] = ropec_p, ropes_p
            seq = np.arange(2048) // 256
            qm = (seq[None, :] == np.arange(8)[:, None]).astype(f32)
            km = np.full((8, 2560), BIGNEG, f32)
            km[:, :2048] = np.where(seq[None, :] == np.arange(8)[:, None], 0.0, BIGNEG)
            m["qmask"] = qm.astype(bf)
            m["kmask"] = km.astype(bf)
            kf = np.array([0.0 if T % 2 == 0 else 1.0 for T in range(16)], f32)
            kb = np.array([0.0 if T % 2 == 1 else 1.0 for T in range(16)], f32)
            m["keep"] = np.ascontiguousarray(np.broadcast_to(np.concatenate([kf, kb])[None, :], (128, 32)))
        in_maps.append(m)

    key = (_L, _dbg, None if _units is None else tuple(sorted(_units)))
    if key not in _NC_CACHE:
        _NC_CACHE[key] = build_program(L=_L, dbg=_dbg, units_enabled=_units)
    nc = _NC_CACHE[key]
    res = run_bass_kernel_spmd(nc, in_maps, core_ids=list(range(8)))
    R = res.results

    y_sample = np.stack([R[b]["y"] for b in range(4)], axis=0)
    y_prompt = np.concatenate([R[4 + j]["y"].reshape(8, 256, 1024) for j in range(4)], axis=0)
    nk = np.concatenate([R[4 + j]["nk"].reshape(2, 8, 256, 4, 2, 64).transpose(1, 0, 2, 3, 4, 5) for j in range(4)], axis=0)
    nv = np.concatenate([R[4 + j]["nv"].reshape(2, 8, 256, 4, 128).transpose(1, 0, 2, 3, 4) for j in range(4)], axis=0)
    st = []
    for name in ("nsrf", "nsrb", "nshf", "nshb"):
        st.append(np.concatenate([R[4 + j][name].transpose(1, 0, 2, 3, 4) for j in range(4)], axis=0))
    outs = (y_prompt, y_sample, np.ascontiguousarray(nk), np.ascontiguousarray(nv), *[np.ascontiguousarray(s) for s in st])
    if _dbg:
        return outs, [R[i]["dbgmix"] for i in range(8)]
    return outs
```

```python
import math
import types
from contextlib import ExitStack

import numpy as np
import ml_dtypes

import concourse.bass as bass
import concourse.mybir as mybir
from concourse.bass_utils import run_bass_kernel_spmd

F32 = mybir.dt.float32
BF16 = mybir.dt.bfloat16
ALU = mybir.AluOpType
AF = mybir.ActivationFunctionType
AX = mybir.AxisListType

ENGS = ["pe", "act", "dve", "pool", "sp"]
NDSEM = 8
SAME_ENG_SYNC = True
import os as _os0
REORDER = _os0.environ.get('REORDER', '1') == '1'
PSUM_EXCL = _os0.environ.get('PSUM_EXCL', '1') == '1'
REORDER_ENGS = _os0.environ.get('REORDER_ENGS', 'pe,act,dve,pool,sp').split(',')

D_MODEL = 1024
NT = 16
TOK = 2048
NKT = 20
EPS = 1e-6
UNIT_W = [512, 512, 640, 640, 512, 512, 512, 512]
UNIT_OFF = [0, 512, 1024, 1664, 2304, 2816, 3328, 3840]
UNIT_CHUNK = [0, 1, 6, 7, 2, 3, 4, 5]
BIGNEG = -30000.0


class Buf:
    __slots__ = ("w", "r", "name", "excl")

    def __init__(self, name="", excl=False):
        self.w = None
        self.r = []
        self.name = name
        self.excl = excl


class Op:
    __slots__ = ("eng", "fn", "waits", "marked", "semval", "is_dma", "dsem", "dval", "cost", "lat", "idx", "prio",
                 "pos", "succs", "nrem", "ready", "fin", "is_bar", "per_eng", "per_dsem")


class _Probe:
    def __init__(self):
        self.rec = None

    def __getattr__(self, name):
        def f(*a, **k):
            self.rec = (name, a, k)
            return self
        return f


def _nfree(ap):
    n = 1
    for d in ap.shape[1:]:
        n *= int(d)
    return n


def _estimate(eng, fn, is_dma):
    pr = _Probe()
    try:
        fn(pr)
        name, a, k = pr.rec
    except Exception:
        name, a, k = "?", (), {}
    out = k.get("out", a[0] if a else None)
    try:
        if is_dma:
            nbytes = _nfree(out) * int(out.shape[0]) * mybir.dt.size(out.dtype)
            return 120.0, 2200.0 + nbytes / 120.0
        if eng == "pe":
            if name == "transpose":
                return 80.0, 80.0
            rhs = k.get("rhs", a[2] if len(a) > 2 else None)
            lhsT = k.get("lhsT", a[1] if len(a) > 1 else None)
            n = _nfree(rhs)
            c = (max(64, n) / 2.4 + 25.0) * 1.25
            if lhsT.dtype == F32:
                c *= 4.0
            return c, c
        n = _nfree(out)
        if eng == "act":
            c = 190.0 + n / 1.2 + (90.0 if k.get("accum_out") is not None else 0.0)
        elif eng == "dve":
            c = 130.0 + n / 0.7
        else:
            c = 700.0 + n / 0.4
        return c, c
    except Exception:
        return 300.0, 300.0


def _freeze(fn):
    if fn.__closure__ is None:
        return fn
    cells = []
    for c in fn.__closure__:
        try:
            cells.append(types.CellType(c.cell_contents))
        except ValueError:
            cells.append(c)
    return types.FunctionType(fn.__code__, fn.__globals__, fn.__name__, fn.__defaults__, tuple(cells))


LAT_X = 200.0
LAT_S = 50.0


class Prog:
    def __init__(self, nc):
        self.nc = nc
        self.all = []
        self.cur_bar = None
        self.since = []
        self.load = {e: 0.0 for e in ENGS}

    def _new(self, eng):
        op = Op()
        op.eng = eng
        op.fn = None
        op.marked = False
        op.semval = None
        op.is_dma = False
        op.dsem = None
        op.dval = None
        op.cost = 0.0
        op.lat = 0.0
        op.is_bar = False
        op.idx = len(self.all)
        op.waits = []
        self.all.append(op)
        return op

    def barrier(self):
        b = self._new("virt")
        b.is_bar = True
        b.waits = list(self.since)
        self.since = []
        self.cur_bar = b
        self.load = {e: 0.0 for e in ENGS}

    def emit(self, eng, fn, reads=(), writes=(), extra=(), is_dma=False):
        op = self._new(eng)
        op.fn = _freeze(fn)
        op.is_dma = is_dma
        op.cost, op.lat = _estimate(eng, op.fn, is_dma)
        self.load[eng] += op.cost
        waits = set()
        if PSUM_EXCL:
            for b in reads:
                if b.excl:
                    for r in b.r:
                        if r.eng != eng:
                            waits.add(r)
        for b in reads:
            if b.w is not None:
                waits.add(b.w)
        for b in writes:
            if b.w is not None:
                waits.add(b.w)
            for r in b.r:
                waits.add(r)
        for w in extra:
            if w is not None:
                waits.add(w)
        if self.cur_bar is not None:
            waits.add(self.cur_bar)
        waits.discard(op)
        op.waits = list(waits)
        for b in reads:
            b.r.append(op)
        for b in writes:
            b.w = op
            b.r = []
        self.since.append(op)
        return op

    def pe(self, fn, reads=(), writes=(), extra=()):
        return self.emit("pe", fn, reads, writes, extra)

    def act(self, fn, reads=(), writes=(), extra=()):
        return self.emit("act", fn, reads, writes, extra)

    def dve(self, fn, reads=(), writes=(), extra=()):
        return self.emit("dve", fn, reads, writes, extra)

    def pool(self, fn, reads=(), writes=(), extra=()):
        return self.emit("pool", fn, reads, writes, extra)

    def on(self, eng, fn, reads=(), writes=(), extra=()):
        if eng == "any":
            return self.any(fn, reads, writes, extra)
        return self.emit(eng, fn, reads, writes, extra)

    def any(self, fn, reads=(), writes=(), extra=()):
        f = _freeze(fn)
        best = None
        for e in ("dve", "pool"):
            c, _ = _estimate(e, f, False)
            tot = self.load[e] + c
            if best is None or tot < best[0]:
                best = (tot, e)
        return self.emit(best[1], fn, reads, writes, extra)

    def dma(self, out, in_, reads=(), writes=(), extra=()):
        return self.emit("sp", lambda e: e.dma_start(out=out, in_=in_), reads, writes, extra, is_dma=True)

    def schedule(self, reorder=True):
        import heapq
        ops = self.all
        for op in ops:
            op.succs = []
        for op in ops:
            for w in op.waits:
                w.succs.append(op)
        for op in reversed(ops):
            m = 0.0
            for s_ in op.succs:
                l_ = s_.prio + (0.0 if op.is_bar else (LAT_S if s_.eng == op.eng else LAT_X))
                if l_ > m:
                    m = l_
            op.prio = m + op.lat
        order = {e: [] for e in ENGS}
        if not reorder:
            for op in ops:
                if not op.is_bar:
                    order[op.eng].append(op)
            return order
        for op in ops:
            op.nrem = len(op.waits)
            op.ready = 0.0
            op.fin = None
        fixed = [e for e in ENGS if e not in REORDER_ENGS]
        lastop = {}
        for op in ops:
            if op.is_bar or op.eng not in fixed:
                continue
            p_ = lastop.get(op.eng)
            if p_ is not None and p_ not in op.waits:
                p_.succs.append(op)
                op.nrem += 1
            lastop[op.eng] = op
        future = {e: [] for e in ENGS}
        now = {e: [] for e in ENGS}
        free = {e: 0.0 for e in ENGS}

        def release(op):
            for s_ in op.succs:
                if op.is_bar:
                    t = op.fin
                elif s_.is_bar:
                    t = op.fin
                elif s_.eng == op.eng:
                    t = op.fin + (0.0 if op.eng == "pe" else LAT_S)
                else:
                    t = op.fin + LAT_X
                if t > s_.ready:
                    s_.ready = t
                s_.nrem -= 1
                if s_.nrem == 0:
                    if s_.is_bar:
                        s_.fin = s_.ready
                        release(s_)
                    else:
                        heapq.heappush(future[s_.eng], (s_.ready, s_.idx, s_))

        import sys
        sys.setrecursionlimit(100000)
        roots = [op for op in ops if op.nrem == 0]
        for op in roots:
            if op.is_bar:
                op.fin = 0.0
                release(op)
            else:
                heapq.heappush(future[op.eng], (0.0, op.idx, op))
        nleft = sum(1 for op in ops if not op.is_bar)
        while nleft > 0:
            best = None
            for e in ENGS:
                f = future[e]
                nw = now[e]
                while f and f[0][0] <= free[e]:
                    r_, i_, o_ = heapq.heappop(f)
                    heapq.heappush(nw, (-o_.prio, o_.idx, o_))
                if nw:
                    st = free[e]
                elif f:
                    st = f[0][0]
                else:
                    continue
                if best is None or st < best[0]:
                    best = (st, e)
            st, e = best
            if now[e]:
                _, _, op = heapq.heappop(now[e])
            else:
                _, _, op = heapq.heappop(future[e])
            if op.is_dma:
                free[e] = st + op.cost
                op.fin = st + op.lat
            else:
                free[e] = st + op.cost
                op.fin = st + op.cost
            order[e].append(op)
            nleft -= 1
            release(op)
        self.est_ns = max(free.values())
        return order

    def build(self, sems, dsems, reorder=True):
        order = self.schedule(reorder)
        if reorder:
            print('[sched] est_us=%.1f' % (self.est_ns / 1e3), {e: len(order[e]) for e in ENGS})
        for e in ENGS:
            for i, op in enumerate(order[e]):
                op.pos = i

        def skip_same(w_eng, eng):
            return w_eng == eng and (eng == "pe" or not SAME_ENG_SYNC)

        dcnt = [0] * NDSEM
        prev_on_sem = [None] * NDSEM
        dma_prev = {}
        nd = 0
        for op in order["sp"]:
            k = nd % NDSEM
            nd += 1
            op.dsem = k
            dcnt[k] += 16
            op.dval = dcnt[k]
            dma_prev[id(op)] = prev_on_sem[k]
            prev_on_sem[k] = op
        final_dvals = list(dcnt)
        for b in self.all:
            if b.is_bar:
                pe_ = {}
                pd_ = {}
                for w in b.waits:
                    if w.is_bar:
                        continue
                    if w.is_dma:
                        if pd_.get(w.dsem, 0) < w.dval:
                            pd_[w.dsem] = w.dval
                    else:
                        c = pe_.get(w.eng)
                        if c is None or c.pos < w.pos:
                            pe_[w.eng] = w
                b.per_eng = pe_
                b.per_dsem = pd_
        for op in self.all:
            if op.is_bar:
                for w in op.per_eng.values():
                    w.marked = True
                continue
            for w in op.waits:
                if w.is_bar or w.is_dma:
                    continue
                if not skip_same(w.eng, op.eng):
                    w.marked = True
        for e in ENGS:
            cnt = 0
            for op in order[e]:
                if not op.is_dma and op.marked:
                    cnt += 1
                    op.semval = cnt

        def run_engine(ename, eng):
            waited = {}

            def need(semkey, sem, val):
                if waited.get(semkey, 0) >= val:
                    return
                eng.wait_ge(sem, val)
                waited[semkey] = val

            for op in order[ename]:
                for w in op.waits:
                    if w.is_bar:
                        for we, wo in w.per_eng.items():
                            if not (we == ename and ename == "pe"):
                                need(("e", we), sems[we], wo.semval)
                        for k, v in w.per_dsem.items():
                            need(("d", k), dsems[k], v)
                    elif w.is_dma:
                        need(("d", w.dsem), dsems[w.dsem], w.dval)
                    elif not skip_same(w.eng, ename):
                        need(("e", w.eng), sems[w.eng], w.semval)
                if op.is_dma:
                    p = dma_prev[id(op)]
                    if p is not None:
                        need(("d", p.dsem), dsems[p.dsem], p.dval)
                ins = op.fn(eng)
                if op.is_dma:
                    ins.then_inc(dsems[op.dsem], 16)
                elif op.marked:
                    ins.then_inc(sems[ename], 1)
            if ename == "sp":
                for k in range(NDSEM):
                    if final_dvals[k] > 0:
                        need(("d", k), dsems[k], final_dvals[k])

        return run_engine


class Ring:
    def __init__(self, tiles):
        self.tiles = tiles
        self.bufs = [Buf() for _ in tiles]
        self.i = 0

    def next(self):
        k = self.i % len(self.tiles)
        self.i += 1
        return self.tiles[k], self.bufs[k]


def build_program(L=2, dbg=False, units_enabled=None):
    nc = bass.Bass("TRN2", target_bir_lowering=False)

    def din(name, shape, dt=F32):
        return nc.dram_tensor(name, list(shape), dt, kind="ExternalInput").ap()

    def dout(name, shape, dt=F32):
        return nc.dram_tensor(name, list(shape), dt, kind="ExternalOutput").ap()

    x_in = din("x", [TOK, D_MODEL])
    modv = din("modv", [128, 8])
    ck_in = din("ck", [2, 512, 512])
    cv_in = din("cv", [2, 512, 512])
    srf_in = din("srf", [2, 4, 64, 64])
    srb_in = din("srb", [2, 4, 64, 64])
    shf_in = din("shf", [2, 4, 64, 64])
    shb_in = din("shb", [2, 4, 64, 64])
    normg_in = din("normg", [2, 128, 8])
    wada_in = din("wada", [2, 128, 8, 3072])
    bada_in = din("bada", [2, 3072])
    win_in = din("win", [2, 128, 8, 4352])
    wout_in = din("wout", [2, 128, 8, 1024])
    rdl_in = din("rdl", [2, 8])
    qng_in = din("qng", [2, 64])
    kng_in = din("kng", [2, 64])
    dlam_in = din("dlam", [2, 256])
    hlb_in = din("hlb", [512])
    ropec_in = din("ropec", [128, 16, 64])
    ropes_in = din("ropes", [128, 16, 64])
    qmask_in = din("qmask", [8, 2048], BF16)
    kmask_in = din("kmask", [8, 2560], BF16)
    keep_in = din("keep", [128, 32])
    identb_in = din("identb", [128, 128], BF16)
    identf_in = din("identf", [128, 128])
    cst_in = din("cst", [128, 11, 128])
    ind_in = din("ind", [128, 4])

    y_out = dout("y", [TOK, D_MODEL])
    nk_out = dout("nk", [2, TOK, 512])
    nv_out = dout("nv", [2, TOK, 512])
    nsrf_out = dout("nsrf", [2, 8, 4, 64, 64])
    nsrb_out = dout("nsrb", [2, 8, 4, 64, 64])
    nshf_out = dout("nshf", [2, 8, 4, 64, 64])
    nshb_out = dout("nshb", [2, 8, 4, 64, 64])
    xs_scr = nc.dram_tensor("xs_scr", [TOK, D_MODEL], F32, kind="Internal").ap()
    dbg_out = dout("dbgmix", [2, 128, 8, TOK], BF16) if dbg else None

    es = ExitStack()
    with es:
        def sb(name, shape, dt):
            return es.enter_context(nc.sbuf_tensor("sb_" + name, list(shape), dt))

        hT = sb("hT", [128, 8, TOK], BF16)
        mixT = sb("mixT", [128, 8, TOK], BF16)
        wstage = sb("wstage", [128, 8 * 512], F32)
        wbf = sb("wbf", [128, 8 * 640], BF16)
        arena = sb("arena", [128, 61440], mybir.dt.uint8)
        cst = sb("cst", [128, 11, 128], F32)
        ind = sb("ind", [128, 4], F32)
        ropec = sb("ropec", [128, 16, 64], F32)
        ropes = sb("ropes", [128, 16, 64], F32)
        identb = sb("identb", [128, 128], BF16)
        identf = sb("identf", [128, 128], F32)
        keep = sb("keep", [128, 32], F32)
        gate_b = sb("gate_b", [128, 1024], F32)
        modt = sb("modt", [128, 8], F32)
        smod = sb("smod", [128, 8], F32)
        normg = sb("normg", [128, 8], F32)
        modT = sb("modT", [128, 2, 8], F32)
        modA = sb("modA", [128, 8], F32)
        small = sb("small", [128, 64], F32)
        rdl = sb("rdl", [128, 8], F32)
        lg = sb("lg", [128, 8], F32)
        g4 = sb("g4", [128, 256], F32)
        lball = sb("lball", [128, 2, 256], F32)
        lb2 = sb("lb2", [128, 256], F32)
        omlb2 = sb("omlb2", [128, 256], F32)
        cneg = sb("cneg", [128, 8], F32)
        zerosb = sb("zerosb", [128, 512], BF16)
        lgrow = sb("lgrow", [128, 4], F32)
        lgrow1 = sb("lgrow1", [128, 4], F32)
        dtm = sb("dtm", [128, 2, 128], F32)
        qd = sb("qd", [128, 2, 128], F32)
        kd = sb("kd", [128, 2, 128], F32)
        e12 = sb("e12", [128, 2, 128], F32)
        stS = sb("stS", [128, 2, 128], F32)

        n_f32_512 = 3
        r512 = Ring([sb("r512_%d" % i, [128, 512], F32) for i in range(n_f32_512)])
        r256 = Ring([sb("r256_%d" % i, [128, 256], F32) for i in range(8)])
        r128 = Ring([sb("r128_%d" % i, [128, 128], F32) for i in range(8)])
        rb256 = Ring([sb("rb256_%d" % i, [128, 256], BF16) for i in range(3)])
        rb128 = Ring([sb("rb128_%d" % i, [128, 128], BF16) for i in range(8)])
        rpt = Ring([sb("rpt_%d" % i, [128, 512], BF16) for i in range(3)])
        rs = Ring([sb("rs_%d" % i, [128, 8], F32) for i in range(16)])

        banks = [es.enter_context(nc.psum_tensor("bank%d" % i, [128, 512], F32)) for i in range(8)]
        banksb = [b.bitcast(BF16) for b in banks]
        bankB = [Buf("bank%d" % i, excl=True) for i in range(8)]

        sems = {e: es.enter_context(nc.semaphore("s_" + e)) for e in ENGS}
        dsems = [es.enter_context(nc.semaphore("d%d" % k)) for k in range(NDSEM)]

        P = Prog(nc)

        B_hT = [Buf() for _ in range(NT)]
        B_mixT = [[Buf() for _ in range(NT)] for _ in range(8)]
        B_wstage = Buf()
        B_wbf = Buf()
        B_cst = Buf()
        B_misc = Buf()
        B_gate = Buf()
        B_modAB = Buf()
        B_pair = Buf()
        B_stS = [Buf(), Buf()]

        def arena_view(off_bytes, shape, dt):
            n = 1
            for s in shape[1:]:
                n *= s
            esz = 2 if dt == BF16 else 4
            a = arena[:, off_bytes:off_bytes + n * esz].bitcast(dt)
            if len(shape) == 2:
                return a
            if len(shape) == 3:
                return a.rearrange("p (a b) -> p a b", a=shape[1], b=shape[2])
            if len(shape) == 4:
                return a.rearrange("p (a b c) -> p a b c", a=shape[1], b=shape[2], c=shape[3])
            raise ValueError

        xring = Ring([arena_view(36864 + i * 4096, [128, 1024], F32) for i in range(3)])
        xhring = Ring([arena_view(49152 + i * 2048, [128, 1024], BF16) for i in range(2)])
        B_xs = [Buf() for _ in range(NT)]

        M1, L1, M2, L2, IOTA1, IOTA2, COLA, COLB, TRIF, TRIB, BM = [cst[:, i, :] for i in range(11)]

        P.dma(cst[:], cst_in, writes=[B_cst])
        P.dma(ind[:], ind_in, writes=[B_cst])
        P.dma(ropec[:], ropec_in, writes=[B_cst])
        P.dma(ropes[:], ropes_in, writes=[B_cst])
        P.dma(identb[:], identb_in, writes=[B_cst])
        P.dma(identf[:], identf_in, writes=[B_cst])
        P.dma(keep[:], keep_in, writes=[B_cst])
        P.dma(modt[:], modv, writes=[B_cst])
        hlb, hlbb = r512.next()
        P.dma(hlb[:], hlb_in.partition_broadcast(128), writes=[hlbb])
        P.pool(lambda e: e.memset(cneg[:], -0.5), writes=[B_cst])
        P.pool(lambda e: e.memset(zerosb[:], 0.0), writes=[B_cst])
        t_, tb_ = rs.next()
        P.act(lambda e, t_=t_: e.activation(out=t_[:, 0:8], in_=modt[:], func=AF.Tanh, scale=0.5), reads=[B_cst], writes=[tb_])
        P.dve(lambda e, t_=t_: e.scalar_tensor_tensor(out=smod[:], in0=t_[:, 0:8], scalar=1.0, in1=modt[:], op0=ALU.add, op1=ALU.mult),
              reads=[tb_, B_cst], writes=[B_cst])
        P.dve(lambda e: e.tensor_scalar(out=smod[:], in0=smod[:], scalar1=0.5, scalar2=None, op0=ALU.mult), reads=[B_cst], writes=[B_cst])
        P.act(lambda e: e.activation(out=hlb[:], in_=hlb[:], func=AF.Exp), reads=[hlbb], writes=[hlbb])
        den_, denb_ = r256.next()
        P.dve(lambda e: e.tensor_tensor(out=den_[:], in0=hlb[:, 0:256], in1=hlb[:, 256:512], op=ALU.add), reads=[hlbb], writes=[denb_])
        P.dve(lambda e: e.reciprocal(out=den_[:], in_=den_[:]), reads=[denb_], writes=[denb_])
        P.dve(lambda e: e.tensor_tensor(out=hlb[:, 0:256], in0=hlb[:, 0:256], in1=den_[:], op=ALU.mult), reads=[hlbb, denb_], writes=[hlbb])
        P.dve(lambda e: e.tensor_tensor(out=hlb[:, 256:512], in0=hlb[:, 256:512], in1=den_[:], op=ALU.mult), reads=[hlbb, denb_], writes=[hlbb])
        P.dve(lambda e: e.tensor_tensor(out=lball[:, 0, :], in0=hlb[:, 0:256], in1=hlb[:, 0:256], op=ALU.subtract), reads=[hlbb], writes=[B_cst])
        P.dve(lambda e: e.tensor_tensor(out=lball[:, 1, :], in0=hlb[:, 0:256], in1=hlb[:, 256:512], op=ALU.add), reads=[hlbb], writes=[B_cst])
        P.dve(lambda e: e.tensor_tensor(out=lball[:, 1, :], in0=lball[:, 1, :], in1=hlb[:, 0:256], op=ALU.subtract), reads=[hlbb, B_cst], writes=[B_cst])

        def rstd_from_ss(ss_ap, ssb, n, mult, add):
            t1, b1 = rs.next()
            P.dve(lambda e: e.tensor_scalar(out=t1[:, 0:n], in0=ss_ap, scalar1=mult, scalar2=add, op0=ALU.mult, op1=ALU.add),
                  reads=[ssb], writes=[b1])
            t2, b2 = rs.next()
            P.pool(lambda e: e.tensor_tensor(out=t2[:, 0:n], in0=t1[:, 0:n], in1=cneg[:, 0:n], op=ALU.pow), reads=[b1, B_cst], writes=[b2])
            return t2, b2

        def rope(src, srcbufs, dst, dstbufs, T, G, eng_a, eng_b):
            W = G * 64
            t1, b1 = r256.next()
            t2, b2 = r256.next()
            cv_ = ropec[:, T, :]
            sv_ = ropes[:, T, :].rearrange("p (h j i) -> p h j i", h=2, j=2, i=16)
            src3 = src.rearrange("p (g d) -> p g d", g=G, d=64)
            src5 = src.rearrange("p (g h j i) -> p g h j i", g=G, h=2, j=2, i=16)
            t13 = t1[:, 0:W].rearrange("p (g d) -> p g d", g=G, d=64)
            t25 = t2[:, 0:W].rearrange("p (g h j i) -> p g h j i", g=G, h=2, j=2, i=16)
            P.on(eng_a, lambda e: e.tensor_tensor(out=t13, in0=src3, in1=cv_.unsqueeze(1).to_broadcast([128, G, 64]), op=ALU.mult),
                 reads=list(srcbufs) + [B_cst], writes=[b1])
            P.on(eng_b, lambda e: e.tensor_tensor(out=t25[:, :, :, 0, :], in0=src5[:, :, :, 1, :],
                                                  in1=sv_[:, :, 0, :].unsqueeze(1).to_broadcast([128, G, 2, 16]), op=ALU.mult),
                 reads=list(srcbufs) + [B_cst], writes=[b2])
            P.on(eng_b, lambda e: e.tensor_tensor(out=t25[:, :, :, 1, :], in0=src5[:, :, :, 0, :],
                                                  in1=sv_[:, :, 1, :].unsqueeze(1).to_broadcast([128, G, 2, 16]), op=ALU.mult),
                 reads=list(srcbufs) + [B_cst], writes=[b2])
            P.on(eng_a, lambda e: e.tensor_tensor(out=dst, in0=t1[:, 0:W], in1=t2[:, 0:W], op=ALU.add), reads=[b1, b2], writes=list(dstbufs))

        def tslice(T):
            return slice(T * 128, (T + 1) * 128)

        def setup_layer(l):
            stg = [wstage[:, 0:4096].rearrange("p (k n) -> p k n", k=8, n=512), arena_view(16384, [128, 8, 512], F32)]
            stgB = [B_wstage, Buf()]
            smb = arena_view(32768, [128, 8, 128], F32)
            B_smb = Buf()
            P.dve(lambda e: e.tensor_copy(out=smb, in_=smod[:].unsqueeze(2).to_broadcast([128, 8, 128])), reads=[B_cst], writes=[B_smb])
            P.dma(normg[:], normg_in[l], writes=[B_misc])
            P.dma(rdl[:], rdl_in[l].partition_broadcast(128), writes=[B_misc])
            dlam, dlamb = r256.next()
            P.dma(dlam[:], dlam_in[l].partition_broadcast(128), writes=[dlamb])
            P.dma(g4[:, 0:64], qng_in[l].partition_broadcast(128), writes=[B_misc])
            P.dma(g4[:, 64:128], qng_in[l].partition_broadcast(128), writes=[B_misc])
            P.dma(g4[:, 128:192], kng_in[l].partition_broadcast(128), writes=[B_misc])
            P.dma(g4[:, 192:256], kng_in[l].partition_broadcast(128), writes=[B_misc])
            for cb in range(6):
                st_, stb_ = stg[cb % 2], stgB[cb % 2]
                P.dma(st_, wada_in[l][:, :, cb * 512:(cb + 1) * 512], writes=[stb_])
                bt, btb = r512.next()
                P.dma(bt[:], bada_in[l][cb * 512:(cb + 1) * 512].partition_broadcast(128), writes=[btb])
                bk = cb % 4
                for kc in range(8):
                    P.pe(lambda e, kc=kc, st_=st_, bk=bk: e.matmul(banks[bk][:, 0:512], lhsT=smb[:, kc, :], rhs=st_[:, kc, :],
                                                                     start=(kc == 0), stop=(kc == 7)),
                         reads=[B_smb, stb_], writes=[bankB[bk]])
                if cb >= 4:
                    P.dve(lambda e, bk=bk, bt=bt, cb=cb: e.tensor_tensor(out=gate_b[:, (cb - 4) * 512:(cb - 3) * 512], in0=banks[bk][:, 0:512],
                                                                          in1=bt[:], op=ALU.add),
                          reads=[bankB[bk], btb], writes=[B_gate])
                else:
                    P.dve(lambda e, bk=bk, bt=bt: e.tensor_tensor(out=bt[:], in0=banks[bk][:, 0:512], in1=bt[:], op=ALU.add),
                          reads=[bankB[bk], btb], writes=[btb])
                    which = cb // 2
                    tb = 4 + (cb % 2)
                    for jj in range(4):
                        kc = (cb % 2) * 4 + jj
                        P.pe(lambda e, jj=jj, bt=bt, tb=tb: e.transpose(banks[tb][:, jj * 128:(jj + 1) * 128], bt[:, jj * 128:(jj + 1) * 128], identf[:]),
                             reads=[btb, B_cst], writes=[bankB[tb]])
                        P.act(lambda e, jj=jj, tb=tb, which=which, kc=kc: e.copy(out=modT[:, which, kc:kc + 1], in_=banks[tb][:, jj * 128:jj * 128 + 1]),
                              reads=[bankB[tb]], writes=[B_modAB])
            P.dve(lambda e: e.scalar_tensor_tensor(out=modA[:], in0=modT[:, 1, :], scalar=1.0, in1=normg[:], op0=ALU.add, op1=ALU.mult),
                  reads=[B_modAB, B_misc], writes=[B_modAB])
            pr, prb = r256.next()
            P.dve(lambda e: e.tensor_tensor(out=pr[:, 0:64], in0=dlam[:, 0:64], in1=dlam[:, 64:128], op=ALU.mult), reads=[dlamb], writes=[prb])
            P.dve(lambda e: e.tensor_tensor(out=pr[:, 64:128], in0=dlam[:, 128:192], in1=dlam[:, 192:256], op=ALU.mult), reads=[dlamb], writes=[prb])
            s12, s12b = rs.next()
            P.dve(lambda e: e.tensor_reduce(out=s12[:, 0:2], in_=pr[:, 0:128].rearrange("p (a d) -> p a d", a=2, d=64), axis=AX.X, op=ALU.add),
                  reads=[prb], writes=[s12b])
            P.act(lambda e: e.activation(out=s12[:, 0:2], in_=s12[:, 0:2], func=AF.Exp), reads=[s12b], writes=[s12b])
            lam_init = 0.8 - 0.6 * math.exp(-0.3 * l)
            P.dve(lambda e: e.tensor_tensor(out=small[:, 0:1], in0=s12[:, 1:2], in1=s12[:, 0:1], op=ALU.subtract), reads=[s12b], writes=[B_misc])
            P.dve(lambda e: e.tensor_scalar(out=small[:, 0:1], in0=small[:, 0:1], scalar1=-lam_init, scalar2=None, op0=ALU.add),
                  reads=[B_misc], writes=[B_misc])
            P.act(lambda e: e.activation(out=lg[:], in_=rdl[:], func=AF.Exp, scale=-1.0), reads=[B_misc], writes=[B_misc])
            P.act(lambda e: e.activation(out=lg[:], in_=lg[:], func=AF.Ln, bias=1.0), reads=[B_misc], writes=[B_misc])
            P.dve(lambda e: e.tensor_scalar(out=lg[:], in0=lg[:], scalar1=-1.0, scalar2=None, op0=ALU.mult), reads=[B_misc], writes=[B_misc])

        def norm_phase(l):
            src = x_in if l == 0 else xs_scr
            for T in range(NT):
                xt, xb_ = xring.next()
                P.dma(xt[:], src[T * 128:(T + 1) * 128, :], reads=([B_xs[T]] if l > 0 else []), writes=[xb_])
                xh, xhb = xhring.next()
                ss, ssb = rs.next()
                P.act(lambda e, xt=xt, xh=xh, ss=ss: e.activation(out=xh[:], in_=xt[:], func=AF.Square, accum_out=ss[:, 0:1]),
                      reads=[xb_], writes=[xhb, ssb])
                rstd, rb_ = rstd_from_ss(ss[:, 0:1], ssb, 1, 1.0 / D_MODEL, EPS)
                P.dve(lambda e, xt=xt, xh=xh, rstd=rstd: e.tensor_scalar(out=xh[:], in0=xt[:], scalar1=rstd[:, 0:1], scalar2=None, op0=ALU.mult),
                      reads=[xb_, rb_], writes=[xhb])
                bk = 6 + (T % 2)
                for kc in range(8):
                    P.pe(lambda e, kc=kc, xh=xh, bk=bk: e.transpose(banksb[bk][:, kc * 128:(kc + 1) * 128], xh[:, kc * 128:(kc + 1) * 128], identb[:]),
                         reads=[xhb, B_cst], writes=[bankB[bk]])
                for kc in range(8):
                    if kc % 2 == 0:
                        P.dve(lambda e, kc=kc, bk=bk, T=T: e.tensor_scalar(out=hT[:, kc, tslice(T)], in0=banksb[bk][:, kc * 128:(kc + 1) * 128],
                                                                            scalar1=modA[:, kc:kc + 1], scalar2=modT[:, 0, kc:kc + 1],
                                                                            op0=ALU.mult, op1=ALU.add),
                              reads=[bankB[bk], B_modAB], writes=[B_hT[T]])
                    else:
                        P.act(lambda e, kc=kc, bk=bk, T=T: e.activation(out=hT[:, kc, tslice(T)], in_=banksb[bk][:, kc * 128:(kc + 1) * 128],
                                                                         func=AF.Identity, scale=modA[:, kc:kc + 1], bias=modT[:, 0, kc:kc + 1]),
                              reads=[bankB[bk], B_modAB], writes=[B_hT[T]])

        def load_unit_weights(l, u):
            W = UNIT_W[u]
            wb = wbf[:, 0:8 * W].rearrange("p (k n) -> p k n", k=8, n=W)
            engs = ["dve", "pool", "act", "dve", "pool", "act", "dve", "pool"]
            for (a, b) in [(0, 512)] + ([(512, W)] if W > 512 else []):
                wd = b - a
                ws = wstage[:, 0:8 * wd].rearrange("p (k n) -> p k n", k=8, n=wd)
                P.dma(ws, win_in[l][:, :, UNIT_OFF[u] + a:UNIT_OFF[u] + b], writes=[B_wstage])
                for kc in range(8):
                    if engs[kc] == "act":
                        P.act(lambda e, kc=kc, ws=ws, a=a, b=b: e.copy(out=wb[:, kc, a:b], in_=ws[:, kc, :]), reads=[B_wstage], writes=[B_wbf])
                    else:
                        P.on(engs[kc], lambda e, kc=kc, ws=ws, a=a, b=b: e.tensor_copy(out=wb[:, kc, a:b], in_=ws[:, kc, :]), reads=[B_wstage], writes=[B_wbf])
            return wb

        def project(wb, T, bk, c0, c1):
            for kc in range(8):
                P.pe(lambda e, kc=kc: e.matmul(banks[bk][:, 0:c1 - c0], lhsT=hT[:, kc, tslice(T)], rhs=wb[:, kc, c0:c1],
                                               start=(kc == 0), stop=(kc == 7)),
                     reads=[B_hT[T], B_wbf], writes=[bankB[bk]])

        def mixed_out(mt, mtb, chunk, T, tbank, eng="act"):
            P.pe(lambda e: e.transpose(banksb[tbank][:, 0:128], mt, identb[:]), reads=[mtb, B_cst], writes=[bankB[tbank]])
            if eng == "act":
                P.act(lambda e: e.copy(out=mixT[:, chunk, tslice(T)], in_=banksb[tbank][:, 0:128]), reads=[bankB[tbank]], writes=[B_mixT[chunk][T]])
            else:
                P.dve(lambda e: e.tensor_copy(out=mixT[:, chunk, tslice(T)], in_=banksb[tbank][:, 0:128]), reads=[bankB[tbank]], writes=[B_mixT[chunk][T]])

        def diff_unit(l, h, DB):
            u = 4 + h
            chunk = UNIT_CHUNK[u]
            if h == 0:
                P.barrier()
            par = h % 2
            base = par * 27728
            QT = arena_view(base + 0, [128, 2, TOK], BF16)
            KT = arena_view(base + 8192, [128, 2, 2560], BF16)
            V = arena_view(base + 18432, [128, NKT, 130], BF16)
            sg = arena_view(base + 23632, [128, NT, 128], BF16)
            ckst = arena_view(55456, [128, 4, 128], F32)
            cvst = arena_view(57504, [128, 4, 128], F32)
            ckb = arena_view(59552, [128, 4, 128], BF16)
            B_QT, B_KT, B_V, B_sg, B_qm = DB["set"][par]
            B_ck = DB["ck"]
            wb = load_unit_weights(l, u)
            import os as _os
            DD0 = _os.environ.get('DIFFDBG', '')
            if 'nomask' not in DD0:
                for c in range(2):
                    P.dma(QT[64:72, c, :], qmask_in, writes=[B_qm])
                    P.dma(KT[64:72, c, :], kmask_in, writes=[B_qm])
            if 'noctx' not in DD0:
                P.dma(ckst, ck_in[l].rearrange("(t p) n -> p t n", p=128)[:, :, h * 128:(h + 1) * 128], writes=[B_ck])
                P.dma(cvst, cv_in[l].rearrange("(t p) n -> p t n", p=128)[:, :, h * 128:(h + 1) * 128], writes=[B_ck])
                P.pool(lambda e: e.memset(V[:, :, 128:130], 1.0), writes=B_V)
                P.dve(lambda e: e.tensor_copy(out=ckb, in_=ckst), reads=[B_ck], writes=[B_ck])
                for pt in range(4):
                    bk = 5
                    for c in range(2):
                        P.pe(lambda e, pt=pt, c=c, bk=bk: e.transpose(banksb[bk][0:64, c * 128:(c + 1) * 128], ckb[:, pt, c * 64:(c + 1) * 64], identb[:]),
                             reads=[B_ck, B_cst], writes=[bankB[bk]])
                    P.act(lambda e, pt=pt, bk=bk: e.copy(out=KT[0:64, :, 2048 + pt * 128:2048 + (pt + 1) * 128],
                                                         in_=banksb[bk][0:64, 0:256].rearrange("p (c t) -> p c t", c=2, t=128)),
                          reads=[bankB[bk]], writes=[B_KT[16 + pt]])
                    P.any(lambda e, pt=pt: e.tensor_copy(out=V[:, 16 + pt, 0:128], in_=cvst[:, pt, :]), reads=[B_ck], writes=[B_V[16 + pt]])

            if 'noA' in DD0:
                return
            for T in range(NT):
                zb = 6 + (T % 2)
                project(wb, T, zb, 0, 512)
                z = banks[zb]
                sq, sqb = r256.next()
                P.act(lambda e, z=z, sq=sq: e.activation(out=sq[:], in_=z[:, 0:256], func=AF.Square), reads=[bankB[zb]], writes=[sqb])
                ss, ssb = rs.next()
                P.dve(lambda e, sq=sq, ss=ss: e.tensor_reduce(out=ss[:, 0:4], in_=sq[:].rearrange("p (g d) -> p g d", g=4, d=64), axis=AX.X, op=ALU.add),
                      reads=[sqb], writes=[ssb])
                rstd, rb_ = rstd_from_ss(ss[:, 0:4], ssb, 4, 1.0 / 64, EPS)
                nq, nqb = r256.next()
                P.dve(lambda e, z=z, nq=nq, rstd=rstd: e.tensor_tensor(out=nq[:].rearrange("p (g d) -> p g d", g=4, d=64),
                                                                      in0=z[:, 0:256].rearrange("p (g d) -> p g d", g=4, d=64),
                                                                      in1=rstd[:, 0:4].unsqueeze(2).to_broadcast([128, 4, 64]), op=ALU.mult),
                      reads=[bankB[zb], rb_], writes=[nqb])
                P.any(lambda e, nq=nq: e.tensor_tensor(out=nq[:], in0=nq[:], in1=g4[:], op=ALU.mult), reads=[nqb, B_misc], writes=[nqb])
                if 'nonk' not in DD0:
                    P.dma(nk_out[l][T * 128:(T + 1) * 128, h * 128:(h + 1) * 128], nq[:, 128:256], reads=[nqb])
                rt, rtb = rb256.next()
                import os as _os
                rope(nq[:], [nqb], rt[:], [rtb], T, 4, "any", "any")
                if 'noT' not in DD0:
                    tb = 5
                    for g in range(4):
                        P.pe(lambda e, g=g, rt=rt, tb=tb: e.transpose(banksb[tb][0:64, g * 128:(g + 1) * 128], rt[:, g * 64:(g + 1) * 64], identb[:]),
                             reads=[rtb, B_cst], writes=[bankB[tb]])
                    if 'noTq' not in DD0:
                      P.act(lambda e, tb=tb, T=T: e.copy(out=QT[0:64, :, tslice(T)], in_=banksb[tb][0:64, 0:256].rearrange("p (c t) -> p c t", c=2, t=128)),
                          reads=[bankB[tb]], writes=[B_QT[T]])
                    if 'noTk' not in DD0:
                      P.act(lambda e, tb=tb, T=T: e.copy(out=KT[0:64, :, tslice(T)], in_=banksb[tb][0:64, 256:512].rearrange("p (c t) -> p c t", c=2, t=128)),
                          reads=[bankB[tb]], writes=[B_KT[T]])
                vst, vstb = r128.next()
                P.dve(lambda e, z=z, vst=vst: e.tensor_copy(out=vst[:], in_=z[:, 256:384]), reads=[bankB[zb]], writes=[vstb])
                if 'nonk' not in DD0:
                    P.dma(nv_out[l][T * 128:(T + 1) * 128, h * 128:(h + 1) * 128], vst[:], reads=[vstb])
                P.any(lambda e, vst=vst, T=T: e.tensor_copy(out=V[:, T, 0:128], in_=vst[:]), reads=[vstb], writes=[B_V[T]])
                th, thb = r128.next()
                P.act(lambda e, z=z, th=th: e.activation(out=th[:], in_=z[:, 384:512], func=AF.Tanh, scale=0.5), reads=[bankB[zb]], writes=[thb])
                P.dve(lambda e, z=z, th=th, T=T: e.scalar_tensor_tensor(out=sg[:, T, :], in0=th[:], scalar=1.0, in1=z[:, 384:512], op0=ALU.add, op1=ALU.mult),
                      reads=[thb, bankB[zb]], writes=[B_sg[T]])
            import os as _os
            DD = _os.environ.get('DIFFDBG', '')
            if 'noB' in DD:
                return
            KR = 64 if 'k64' in DD else 72
            lam_init = 0.8 - 0.6 * math.exp(-0.3 * l)
            c0 = 0.5 * (1.0 - lam_init)
            OB = [2, 3, 4]

            def acc(c, qi):
                a = c * 4 + qi
                return OB[a // 3], (a % 3) * 160

            for qb in range(4):
                for k in OB:
                    P.pe(lambda e, k=k: e.matmul(banks[k][:, 0:512], lhsT=zerosb[:, 0:128], rhs=zerosb[:, 0:512], start=True, stop=False, skip_group_check=True),
                         reads=[B_cst], writes=[bankB[k]])
                steps = [(c, kt) for c in range(2) for kt in range(NKT)]
                pts = {}

                def emit_st(i):
                    c, kt = steps[i]
                    sbk = i % 2
                    P.pe(lambda e, c=c, kt=kt, sbk=sbk: e.matmul(banks[sbk][:, 0:512], lhsT=KT[0:KR, c, kt * 128:(kt + 1) * 128],
                                                                 rhs=QT[0:KR, c, qb * 512:(qb + 1) * 512], start=True, stop=True),
                         reads=[B_KT[kt], B_qm] + B_QT[qb * 4:qb * 4 + 4], writes=[bankB[sbk]])
                    pt_, ptb = rpt.next()
                    P.act(lambda e, sbk=sbk, pt_=pt_: e.activation(out=pt_[:], in_=banks[sbk][:, 0:512], func=AF.Exp, scale=0.125),
                          reads=[bankB[sbk]], writes=[ptb])
                    pts[i] = (pt_, ptb)

                def emit_pv(i):
                    c, kt = steps[i]
                    pt_, ptb = pts.pop(i)
                    for qi in range(4):
                        bk, off = acc(c, qi)
                        P.pe(lambda e, qi=qi, bk=bk, off=off, pt_=pt_, kt=kt: e.matmul(banks[bk][:, off:off + 129], lhsT=pt_[:, qi * 128:(qi + 1) * 128],
                                                                                       rhs=V[:, kt, 0:129], start=False, stop=(kt == NKT - 1), skip_group_check=True),
                             reads=[ptb, B_V[kt]], writes=[bankB[bk]])

                emit_st(0)
                for i in range(len(steps)):
                    if i + 1 < len(steps):
                        emit_st(i + 1)
                    emit_pv(i)
                for qi in range(4):
                    T = qb * 4 + qi
                    b0, o0 = acc(0, qi)
                    b1, o1 = acc(1, qi)
                    r01, r01b = rs.next()
                    P.dve(lambda e, r01=r01: e.reciprocal(out=r01[:, 0:1], in_=banks[b0][:, o0 + 128:o0 + 129]), reads=[bankB[b0]], writes=[r01b])
                    P.dve(lambda e, r01=r01: e.reciprocal(out=r01[:, 1:2], in_=banks[b1][:, o1 + 128:o1 + 129]), reads=[bankB[b1]], writes=[r01b])
                    P.dve(lambda e, r01=r01: e.tensor_tensor(out=r01[:, 1:2], in0=r01[:, 1:2], in1=small[:, 0:1], op=ALU.mult), reads=[r01b, B_misc], writes=[r01b])
                    d, db = r128.next()
                    P.dve(lambda e, d=d, r01=r01: e.tensor_scalar(out=d[:], in0=banks[b0][:, o0:o0 + 128], scalar1=r01[:, 0:1], scalar2=None, op0=ALU.mult),
                          reads=[bankB[b0], r01b], writes=[db])
                    P.dve(lambda e, d=d, r01=r01: e.scalar_tensor_tensor(out=d[:], in0=banks[b1][:, o1:o1 + 128], scalar=r01[:, 1:2], in1=d[:],
                                                                        op0=ALU.mult, op1=ALU.add),
                          reads=[bankB[b1], r01b, db], writes=[db])
                    jk, jkb = r128.next()
                    ss, ssb = rs.next()
                    P.act(lambda e, d=d, jk=jk, ss=ss: e.activation(out=jk[:], in_=d[:], func=AF.Square, accum_out=ss[:, 0:1]), reads=[db], writes=[jkb, ssb])
                    rstd, rb_ = rstd_from_ss(ss[:, 0:1], ssb, 1, 1.0 / (128 * c0 * c0), EPS / (c0 * c0))
                    mt, mtb = rb128.next()
                    P.dve(lambda e, d=d, rstd=rstd, mt=mt, T=T: e.scalar_tensor_tensor(out=mt[:], in0=d[:], scalar=rstd[:, 0:1], in1=sg[:, T, :],
                                                                                      op0=ALU.mult, op1=ALU.mult),
                          reads=[db, rb_, B_sg[T]], writes=[mtb])
                    mixed_out(mt[:], mtb, chunk, T, 5, eng="dve")

        def ret_unit(l, p, RB):
            u = p
            chunk = UNIT_CHUNK[u]
            if p == 0:
                P.barrier()
            base = p * 28672
            qT = arena_view(base + 0, [128, TOK], BF16)
            kT = arena_view(base + 4096, [128, TOK], BF16)
            ktok = arena_view(base + 8192, [128, NT, 128], BF16)
            v = arena_view(base + 12288, [128, NT, 128], BF16)
            sg = arena_view(base + 16384, [128, NT, 128], BF16)
            Sbf = arena_view(base + 20480, [128, 2, NT, 128], BF16)
            if p == 0:
                dtm_, qd_, kd_, stS_, lgrow_ = dtm, qd, kd, stS, lgrow
            else:
                dtm_ = arena_view(57344, [128, 2, 128], F32)
                qd_ = arena_view(58368, [128, 2, 128], F32)
                kd_ = arena_view(59392, [128, 2, 128], F32)
                stS_ = arena_view(60416, [128, 2, 128], F32)
                lgrow_ = lgrow1
            B_pair_, B_stS_ = RB[p]
            B_q = [Buf() for _ in range(NT)]
            B_k = [Buf() for _ in range(NT)]
            B_kt = [Buf() for _ in range(NT)]
            B_v = [Buf() for _ in range(NT)]
            B_sg = [Buf() for _ in range(NT)]
            B_S = [[Buf() for _ in range(NT)] for _ in range(2)]
            wb = load_unit_weights(l, u)
            for d_ in range(2):
                for hh in range(2):
                    col = d_ * 4 + 2 * p + hh
                    P.dve(lambda e, d_=d_, hh=hh, col=col: e.tensor_copy(out=lgrow_[hh * 64:(hh + 1) * 64, d_:d_ + 1], in_=lg[hh * 64:(hh + 1) * 64, col:col + 1]),
                          reads=[B_misc], writes=[B_pair_])
            for hh in range(2):
                e12t, e12b = r256.next()
                cf = 2 * p + hh
                cb_ = 4 + 2 * p + hh
                P.act(lambda e, cf=cf: e.activation(out=e12t[:, 0:128], in_=M1, func=AF.Exp, scale=lg[:, cf:cf + 1]), reads=[e12b, B_cst, B_misc], writes=[e12b, B_pair_])
                P.act(lambda e, cb_=cb_: e.activation(out=e12t[:, 128:256], in_=M2, func=AF.Exp, scale=lg[:, cb_:cb_ + 1]), reads=[e12b, B_cst, B_misc], writes=[e12b, B_pair_])
                P.dve(lambda e: e.tensor_tensor(out=e12t[:, 0:128], in0=e12t[:, 0:128], in1=L1, op=ALU.mult), reads=[e12b, B_pair_, B_cst], writes=[e12b, B_pair_])
                P.dve(lambda e: e.tensor_tensor(out=e12t[:, 128:256], in0=e12t[:, 128:256], in1=L2, op=ALU.mult), reads=[e12b, B_pair_, B_cst], writes=[e12b, B_pair_])
                P.dve(lambda e, hh=hh: e.tensor_tensor(out=dtm_[:, hh, :], in0=e12t[:, 0:128], in1=e12t[:, 128:256], op=ALU.add), reads=[e12b, B_pair_], writes=[e12b, B_pair_])
                P.act(lambda e, hh=hh, cf=cf: e.activation(out=kd_[:, 0, hh * 64:(hh + 1) * 64], in_=COLA[:, 0:64], func=AF.Exp, scale=lg[:, cf:cf + 1]),
                      reads=[B_cst, B_misc], writes=[B_pair_])
                P.act(lambda e, hh=hh, cb_=cb_: e.activation(out=kd_[:, 1, hh * 64:(hh + 1) * 64], in_=COLB[:, 0:64], func=AF.Exp, scale=lg[:, cb_:cb_ + 1]),
                      reads=[B_cst, B_misc], writes=[B_pair_])
            P.dve(lambda e: e.tensor_scalar(out=kd_[:], in0=kd_[:], scalar1=0.125, scalar2=None, op0=ALU.mult), reads=[B_pair_], writes=[B_pair_])
            P.act(lambda e: e.activation(out=qd_[:, 0, :], in_=IOTA1, func=AF.Exp, scale=lgrow_[:, 0:1]), reads=[B_cst, B_pair_], writes=[B_pair_])
            P.act(lambda e: e.activation(out=qd_[:, 1, :], in_=IOTA2, func=AF.Exp, scale=lgrow_[:, 1:2]), reads=[B_cst, B_pair_], writes=[B_pair_])
            P.act(lambda e: e.activation(out=lgrow_[:, 2:4], in_=lgrow_[:, 0:2], func=AF.Exp, scale=128.0), reads=[B_pair_], writes=[B_pair_])
            for T in range(NT):
                zb = T % 4
                project(wb, T, zb, 0, 512)
                z = banks[zb]
                rq, rqb = rb128.next()
                t1, b1 = r256.next()
                t2, b2 = r256.next()
                cv_ = ropec[:, T, :]
                sv_ = ropes[:, T, :].rearrange("p (h j i) -> p h j i", h=2, j=2, i=16)
                src3 = z[:, 0:256].rearrange("p (g d) -> p g d", g=4, d=64)
                src5 = z[:, 0:256].rearrange("p (g h j i) -> p g h j i", g=4, h=2, j=2, i=16)
                t13 = t1[:].rearrange("p (g d) -> p g d", g=4, d=64)
                t25 = t2[:].rearrange("p (g h j i) -> p g h j i", g=4, h=2, j=2, i=16)
                P.dve(lambda e, t13=t13, src3=src3, cv_=cv_: e.tensor_tensor(out=t13, in0=src3, in1=cv_.unsqueeze(1).to_broadcast([128, 4, 64]), op=ALU.mult),
                      reads=[bankB[zb], B_cst], writes=[b1])
                P.dve(lambda e, t25=t25, src5=src5, sv_=sv_: e.tensor_tensor(out=t25[:, :, :, 0, :], in0=src5[:, :, :, 1, :],
                                                                             in1=sv_[:, :, 0, :].unsqueeze(1).to_broadcast([128, 4, 2, 16]), op=ALU.mult),
                      reads=[bankB[zb], B_cst], writes=[b2])
                P.dve(lambda e, t25=t25, src5=src5, sv_=sv_: e.tensor_tensor(out=t25[:, :, :, 1, :], in0=src5[:, :, :, 0, :],
                                                                             in1=sv_[:, :, 1, :].unsqueeze(1).to_broadcast([128, 4, 2, 16]), op=ALU.mult),
                      reads=[bankB[zb], B_cst], writes=[b2])
                P.any(lambda e, t1=t1, t2=t2, rq=rq: e.tensor_tensor(out=rq[:], in0=t1[:, 0:128], in1=t2[:, 0:128], op=ALU.add), reads=[b1, b2], writes=[rqb])
                P.any(lambda e, t1=t1, t2=t2, T=T: e.tensor_tensor(out=ktok[:, T, :], in0=t1[:, 128:256], in1=t2[:, 128:256], op=ALU.add),
                       reads=[b1, b2], writes=[B_kt[T]])
                tb = 4 + (T % 2)
                P.pe(lambda e, rq=rq, tb=tb: e.transpose(banksb[tb][:, 0:128], rq[:], identb[:]), reads=[rqb, B_cst], writes=[bankB[tb]])
                P.pe(lambda e, T=T, tb=tb: e.transpose(banksb[tb][:, 128:256], ktok[:, T, :], identb[:]), reads=[B_kt[T], B_cst], writes=[bankB[tb]])
                P.act(lambda e, T=T, tb=tb: e.copy(out=qT[:, tslice(T)], in_=banksb[tb][:, 0:128]), reads=[bankB[tb]], writes=[B_q[T]])
                P.act(lambda e, T=T, tb=tb: e.activation(out=kT[:, tslice(T)], in_=banksb[tb][:, 128:256], func=AF.Copy, scale=0.125),
                      reads=[bankB[tb]], writes=[B_k[T]])
                P.act(lambda e, z=z, T=T: e.copy(out=v[:, T, :], in_=z[:, 256:384]), reads=[bankB[zb]], writes=[B_v[T]])
                th, thb = r128.next()
                P.act(lambda e, z=z, th=th: e.activation(out=th[:], in_=z[:, 384:512], func=AF.Tanh, scale=0.5), reads=[bankB[zb]], writes=[thb])
                P.dve(lambda e, z=z, th=th, T=T: e.scalar_tensor_tensor(out=sg[:, T, :], in0=th[:], scalar=1.0, in1=z[:, 384:512], op0=ALU.add, op1=ALU.mult),
                      reads=[thb, bankB[zb]], writes=[B_sg[T]])
            st_in = [srf_in, srb_in]
            st_out = [nsrf_out, nsrb_out]
            for d_ in range(2):
                P.pool(lambda e, d_=d_: e.memset(stS_[:, d_, :], 0.0), writes=[B_stS_[d_]])
                for hh in range(2):
                    P.dma(stS_[hh * 64:(hh + 1) * 64, d_, hh * 64:(hh + 1) * 64], st_in[d_][l, 2 * p + hh], writes=[B_stS_[d_]])
            for step in range(NT):
                for d_ in range(2):
                    T = step if d_ == 0 else NT - 1 - step
                    S = stS_[:, d_, :]
                    kcol = d_ * 16 + T
                    P.dve(lambda e, S=S, kcol=kcol: e.tensor_scalar(out=S, in0=S, scalar1=keep[:, kcol:kcol + 1], scalar2=None, op0=ALU.mult),
                          reads=[B_stS_[d_], B_cst], writes=[B_stS_[d_]])
                    P.act(lambda e, S=S, d_=d_, T=T: e.copy(out=Sbf[:, d_, T, :], in_=S), reads=[B_stS_[d_]], writes=[B_S[d_][T]])
                    kt_, ktb_ = rb128.next()
                    P.any(lambda e, kt_=kt_, T=T, d_=d_: e.tensor_tensor(out=kt_[:], in0=ktok[:, T, :], in1=kd_[:, d_, :], op=ALU.mult),
                           reads=[B_kt[T], B_pair_], writes=[ktb_])
                    ub = d_
                    P.pe(lambda e, kt_=kt_, T=T, ub=ub: e.matmul(banks[ub][:, 0:128], lhsT=kt_[:], rhs=v[:, T, :], start=True, stop=True),
                         reads=[ktb_, B_v[T]], writes=[bankB[ub]])
                    tmp, tmpb = r128.next()
                    P.dve(lambda e, tmp=tmp, ub=ub: e.tensor_tensor(out=tmp[:], in0=banks[ub][:, 0:128], in1=BM, op=ALU.mult),
                          reads=[bankB[ub], B_cst], writes=[tmpb])
                    P.dve(lambda e, S=S, tmp=tmp, d_=d_: e.scalar_tensor_tensor(out=S, in0=S, scalar=lgrow_[:, 2 + d_:3 + d_], in1=tmp[:], op0=ALU.mult, op1=ALU.add),
                          reads=[B_stS_[d_], B_pair_, tmpb], writes=[B_stS_[d_]])
                    is_out = (T % 2 == 1) if d_ == 0 else (T % 2 == 0)
                    if is_out:
                        so, sob = r128.next()
                        P.act(lambda e, so=so, S=S: e.copy(out=so[:], in_=S), reads=[B_stS_[d_]], writes=[sob])
                        for hh in range(2):
                            P.dma(st_out[d_][l, T // 2, 2 * p + hh], so[hh * 64:(hh + 1) * 64, hh * 64:(hh + 1) * 64], reads=[sob])
            for T in range(NT):
                qf, qfb = rb128.next()
                qb_, qbb = rb128.next()
                P.dve(lambda e, qf=qf, T=T: e.tensor_tensor(out=qf[:], in0=qT[:, tslice(T)], in1=qd_[:, 0, :], op=ALU.mult), reads=[B_q[T], B_pair_], writes=[qfb])
                P.any(lambda e, qb_=qb_, T=T: e.tensor_tensor(out=qb_[:], in0=qT[:, tslice(T)], in1=qd_[:, 1, :], op=ALU.mult), reads=[B_q[T], B_pair_], writes=[qbb])
                ob = 6 + (T % 2)
                ab = 2 + (T % 2)
                P.pe(lambda e, qf=qf, T=T, ob=ob: e.matmul(banks[ob][:, 0:128], lhsT=qf[:], rhs=Sbf[:, 0, T, :], start=True, stop=False, skip_group_check=True),
                     reads=[qfb, B_S[0][T]], writes=[bankB[ob]])
                P.pe(lambda e, qb_=qb_, T=T, ob=ob: e.matmul(banks[ob][:, 0:128], lhsT=qb_[:], rhs=Sbf[:, 1, T, :], start=False, stop=False, skip_group_check=True),
                     reads=[qbb, B_S[1][T]], writes=[bankB[ob]])
                am, amb = rb256.next()
                for hh in range(2):
                    P.pe(lambda e, hh=hh, T=T: e.matmul(banks[2 + hh][:, 0:128], lhsT=kT[hh * 64:(hh + 1) * 64, tslice(T)],
                                                        rhs=qT[hh * 64:(hh + 1) * 64, tslice(T)], start=True, stop=True),
                         reads=[B_k[T], B_q[T]], writes=[bankB[2 + hh]])
                    P.dve(lambda e, am=am, hh=hh: e.tensor_tensor(out=am[:, hh * 128:(hh + 1) * 128], in0=banks[2 + hh][:, 0:128], in1=dtm_[:, hh, :], op=ALU.mult),
                          reads=[bankB[2 + hh], B_pair_], writes=[amb])
                for hh in range(2):
                    P.pe(lambda e, hh=hh, am=am, T=T, ob=ob: e.matmul(banks[ob][:, hh * 64:(hh + 1) * 64], lhsT=am[:, hh * 128:(hh + 1) * 128],
                                                                      rhs=v[:, T, hh * 64:(hh + 1) * 64], start=False, stop=(hh == 1), skip_group_check=True),
                         reads=[amb, B_v[T]], writes=[bankB[ob]])
                finish_pair(banks[ob][:, 0:128], [bankB[ob]], sg, B_sg, chunk, T, 0.5, 4 + (T % 2))

        def finish_pair(o_ap, obufs, sg, B_sg, chunk, T, c0, tbank, act_rstd=False):
            ss, ssb = rs.next()
            jk, jkb = r128.next()
            for hh in range(2):
                P.act(lambda e, hh=hh: e.activation(out=jk[:, hh * 64:(hh + 1) * 64], in_=o_ap[:, hh * 64:(hh + 1) * 64], func=AF.Square,
                                                    accum_out=ss[:, hh:hh + 1]),
                      reads=obufs, writes=[jkb, ssb])
            if act_rstd:
                rstd, rb_ = rs.next()
                P.act(lambda e: e.activation(out=rstd[:, 0:2], in_=ss[:, 0:2], func=AF.Ln, scale=1.0 / (64 * c0 * c0), bias=EPS / (c0 * c0)),
                      reads=[ssb], writes=[rb_])
                P.act(lambda e: e.activation(out=rstd[:, 0:2], in_=rstd[:, 0:2], func=AF.Exp, scale=-0.5), reads=[rb_], writes=[rb_])
            else:
                rstd, rb_ = rstd_from_ss(ss[:, 0:2], ssb, 2, 1.0 / (64 * c0 * c0), EPS / (c0 * c0))
            mt, mtb = rb128.next()
            for hh in range(2):
                P.dve(lambda e, hh=hh: e.scalar_tensor_tensor(out=mt[:, hh * 64:(hh + 1) * 64], in0=o_ap[:, hh * 64:(hh + 1) * 64], scalar=rstd[:, hh:hh + 1],
                                                              in1=sg[:, T, hh * 64:(hh + 1) * 64], op0=ALU.mult, op1=ALU.mult),
                      reads=list(obufs) + [rb_, B_sg[T]], writes=[mtb])
            mixed_out(mt[:], mtb, chunk, T, tbank)

        def hgrn_unit(l, p):
            u = 2 + p
            chunk = UNIT_CHUNK[u]
            P.barrier()
            q = arena_view(0, [128, NT, 128], BF16)
            kk = arena_view(4096, [128, NT, 256], BF16)
            lf = arena_view(12288, [128, NT, 256], F32)
            v = arena_view(28672, [128, NT, 128], BF16)
            sg = arena_view(32768, [128, NT, 128], BF16)
            oacc = arena_view(36864, [128, NT, 128], F32)
            vmall = arena_view(45056, [128, NT, 512], BF16)
            B_vm = [Buf() for _ in range(NT)]
            B_q = [Buf() for _ in range(NT)]
            B_kk = [Buf() for _ in range(NT)]
            B_lf = [Buf() for _ in range(NT)]
            B_v = [Buf() for _ in range(NT)]
            B_sg = [Buf() for _ in range(NT)]
            B_oa = [Buf() for _ in range(NT)]
            wb = load_unit_weights(l, u)
            for half in range(2):
                P.dve(lambda e, half=half: e.tensor_copy(out=lb2[:, half * 128:(half + 1) * 128], in_=lball[:, l, p * 128:(p + 1) * 128]),
                      reads=[B_cst], writes=[B_pair])
            P.dve(lambda e: e.tensor_scalar(out=omlb2[:], in0=lb2[:], scalar1=-1.0, scalar2=1.0, op0=ALU.mult, op1=ALU.add), reads=[B_pair], writes=[B_pair])
            for T in range(NT):
                zb = 2 * (T % 2)
                project(wb, T, zb, 0, 512)
                project(wb, T, zb + 1, 512, 640)
                z = banks[zb]
                z2 = banks[zb + 1]
                u_, ub_ = r512.next()
                P.act(lambda e, z=z, u_=u_: e.activation(out=u_[:, 0:384], in_=z[:, 0:384], func=AF.Exp, scale=-1.0), reads=[bankB[zb]], writes=[ub_])
                P.act(lambda e, u_=u_: e.activation(out=u_[:, 0:384], in_=u_[:, 0:384], func=AF.Ln, bias=1.0), reads=[ub_], writes=[ub_])
                P.act(lambda e, u_=u_: e.activation(out=u_[:, 0:384], in_=u_[:, 0:384], func=AF.Exp, scale=-1.0), reads=[ub_], writes=[ub_])
                P.dve(lambda e, u_=u_, z=z, T=T: e.tensor_tensor(out=sg[:, T, :], in0=u_[:, 256:384], in1=z[:, 256:384], op=ALU.mult),
                      reads=[ub_, bankB[zb]], writes=[B_sg[T]])
                f_, fb_ = r256.next()
                P.any(lambda e, u_=u_, f_=f_: e.tensor_tensor(out=f_[:], in0=u_[:, 0:256], in1=omlb2[:], op=ALU.mult), reads=[ub_, B_pair], writes=[fb_])
                P.any(lambda e, f_=f_: e.tensor_tensor(out=f_[:], in0=f_[:], in1=lb2[:], op=ALU.add), reads=[fb_, B_pair], writes=[fb_])
                P.act(lambda e, f_=f_, T=T: e.activation(out=lf[:, T, :], in_=f_[:], func=AF.Ln), reads=[fb_], writes=[B_lf[T]])
                P.act(lambda e, f_=f_, T=T: e.activation(out=kk[:, T, :], in_=f_[:], func=AF.Identity, scale=-1.0, bias=1.0),
                      reads=[fb_], writes=[B_kk[T]])
                P.act(lambda e, z=z, T=T: e.activation(out=q[:, T, :], in_=z[:, 384:512], func=AF.Copy, scale=0.125), reads=[bankB[zb]], writes=[B_q[T]])
                P.act(lambda e, z2=z2, T=T: e.copy(out=v[:, T, :], in_=z2[:, 0:128]), reads=[bankB[zb + 1]], writes=[B_v[T]])
            st_in = [shf_in, shb_in]
            st_out = [nshf_out, nshb_out]
            TRI = [TRIF, TRIB]
            for d_ in range(2):
                P.pool(lambda e, d_=d_: e.memset(stS[:, d_, :], 0.0), writes=[B_stS[d_]])
                for hh in range(2):
                    P.dma(stS[hh * 64:(hh + 1) * 64, d_, hh * 64:(hh + 1) * 64], st_in[d_][l, 2 * p + hh], writes=[B_stS[d_]])
            done_first = [False] * NT
            for step in range(NT):
                for d_ in range(2):
                    T = step if d_ == 0 else NT - 1 - step
                    S = stS[:, d_, :]
                    lfd = lf[:, T, d_ * 128:(d_ + 1) * 128]
                    kcol = d_ * 16 + T
                    P.dve(lambda e, S=S, kcol=kcol: e.tensor_scalar(out=S, in0=S, scalar1=keep[:, kcol:kcol + 1], scalar2=None, op0=ALU.mult),
                          reads=[B_stS[d_], B_cst], writes=[B_stS[d_]])
                    sbf, sbfb = rb128.next()
                    P.act(lambda e, S=S, sbf=sbf: e.copy(out=sbf[:], in_=S), reads=[B_stS[d_]], writes=[sbfb])
                    P.pe(lambda e, lfd=lfd, d_=d_: e.matmul(banks[0][:, 0:128], lhsT=TRI[d_], rhs=lfd, start=True, stop=True),
                         reads=[B_cst, B_lf[T]], writes=[bankB[0]])
                    P.pe(lambda e, lfd=lfd: e.matmul(banks[0][:, 128:132], lhsT=lfd, rhs=ind[:], start=True, stop=True),
                         reads=[B_cst, B_lf[T]], writes=[bankB[0]])
                    G_, Gb_ = rs.next()
                    P.act(lambda e, G_=G_: e.activation(out=G_[:, 0:4], in_=banks[0][:, 128:132], func=AF.Exp), reads=[bankB[0]], writes=[Gb_])
                    eq, eqb = r128.next()
                    ek, ekb = r128.next()
                    P.act(lambda e, eq=eq: e.activation(out=eq[:], in_=banks[0][:, 0:128], func=AF.Exp), reads=[bankB[0]], writes=[eqb])
                    P.act(lambda e, ek=ek: e.activation(out=ek[:], in_=banks[0][:, 0:128], func=AF.Exp, scale=-1.0), reads=[bankB[0]], writes=[ekb])
                    qt_, qtb = rb128.next()
                    kt_, ktb = rb128.next()
                    P.dve(lambda e, qt_=qt_, eq=eq, T=T: e.tensor_tensor(out=qt_[:], in0=q[:, T, :], in1=eq[:], op=ALU.mult), reads=[B_q[T], eqb], writes=[qtb])
                    P.any(lambda e, kt_=kt_, ek=ek, T=T, d_=d_: e.tensor_tensor(out=kt_[:], in0=kk[:, T, d_ * 128:(d_ + 1) * 128], in1=ek[:], op=ALU.mult),
                           reads=[B_kk[T], ekb], writes=[ktb])
                    P.pe(lambda e, qt_=qt_: e.transpose(banksb[1][:, 0:128], qt_[:], identb[:]), reads=[qtb, B_cst], writes=[bankB[1]])
                    P.pe(lambda e, kt_=kt_: e.transpose(banksb[1][:, 128:256], kt_[:], identb[:]), reads=[ktb, B_cst], writes=[bankB[1]])
                    qkT, qkTb = rb256.next()
                    P.act(lambda e, qkT=qkT: e.copy(out=qkT[:], in_=banksb[1][:, 0:256]), reads=[bankB[1]], writes=[qkTb])
                    vm, vmb = vmall[:, T, :], B_vm[T]
                    if not done_first[T]:
                        for j in range(4):
                            P.any(lambda e, j=j, vm=vm, T=T: e.tensor_scalar(out=vm[:, j * 128:(j + 1) * 128], in0=v[:, T, :], scalar1=ind[:, j:j + 1],
                                                                             scalar2=None, op0=ALU.mult),
                                  reads=[B_v[T], B_cst], writes=[vmb])
                    P.pe(lambda e, kt_=kt_, vm=vm: e.matmul(banks[2][:, 0:512], lhsT=kt_[:], rhs=vm, start=True, stop=True),
                         reads=[ktb, vmb], writes=[bankB[2]])
                    am, amb = rb256.next()
                    for hh in range(2):
                        abk = 3 + hh
                        P.pe(lambda e, hh=hh, qkT=qkT, abk=abk: e.matmul(banks[abk][:, 0:128], lhsT=qkT[hh * 64:(hh + 1) * 64, 128:256],
                                                                          rhs=qkT[hh * 64:(hh + 1) * 64, 0:128], start=True, stop=True),
                             reads=[qkTb], writes=[bankB[abk]])
                        P.dve(lambda e, am=am, d_=d_, hh=hh, abk=abk: e.tensor_tensor(out=am[:, hh * 128:(hh + 1) * 128], in0=banks[abk][:, 0:128], in1=TRI[d_], op=ALU.mult),
                              reads=[bankB[abk], B_cst], writes=[amb])
                    ob = 5 + d_
                    jorder = [0, 1, 2, 3] if d_ == 0 else [3, 2, 1, 0]
                    cur, curb = sbf, sbfb
                    for ji, j in enumerate(jorder):
                        P.pe(lambda e, j=j, qkT=qkT, cur=cur: e.matmul(banks[ob][32 * j:32 * j + 32, 0:128], lhsT=qkT[:, 32 * j:32 * j + 32], rhs=cur[:],
                                                                       start=True, stop=False, tile_position=(0, 32 * j), skip_group_check=True),
                             reads=[qkTb, curb], writes=[bankB[ob]])
                        tg, tgb = r128.next()
                        P.dve(lambda e, tg=tg, j=j, G_=G_: e.scalar_tensor_tensor(out=tg[:], in0=banks[2][:, j * 128:(j + 1) * 128], scalar=G_[:, j:j + 1], in1=BM,
                                                                                  op0=ALU.mult, op1=ALU.mult),
                              reads=[bankB[2], Gb_, B_cst], writes=[tgb])
                        P.dve(lambda e, S=S, tg=tg, j=j, G_=G_: e.scalar_tensor_tensor(out=S, in0=S, scalar=G_[:, j:j + 1], in1=tg[:], op0=ALU.mult, op1=ALU.add),
                              reads=[B_stS[d_], Gb_, tgb], writes=[B_stS[d_]])
                        if ji < 3:
                            cur, curb = rb128.next()
                            P.act(lambda e, S=S, cur=cur: e.copy(out=cur[:], in_=S), reads=[B_stS[d_]], writes=[curb])
                    for hh in range(2):
                        P.pe(lambda e, hh=hh, am=am, T=T: e.matmul(banks[ob][:, hh * 64:(hh + 1) * 64], lhsT=am[:, hh * 128:(hh + 1) * 128],
                                                                   rhs=v[:, T, hh * 64:(hh + 1) * 64], start=False, stop=(hh == 1), skip_group_check=True),
                             reads=[amb, B_v[T]], writes=[bankB[ob]])
                    is_out = (T % 2 == 1) if d_ == 0 else (T % 2 == 0)
                    if is_out:
                        so, sob = r128.next()
                        P.act(lambda e, so=so, S=S: e.copy(out=so[:], in_=S), reads=[B_stS[d_]], writes=[sob])
                        for hh in range(2):
                            P.dma(st_out[d_][l, T // 2, 2 * p + hh], so[hh * 64:(hh + 1) * 64, hh * 64:(hh + 1) * 64], reads=[sob])
                    if not done_first[T]:
                        done_first[T] = True
                        P.act(lambda e, T=T: e.copy(out=oacc[:, T, :], in_=banks[ob][:, 0:128]), reads=[bankB[ob]], writes=[B_oa[T]])
                    else:
                        ot, otb = r128.next()
                        P.dve(lambda e, ot=ot, T=T: e.tensor_tensor(out=ot[:], in0=banks[ob][:, 0:128], in1=oacc[:, T, :], op=ALU.add),
                              reads=[bankB[ob], B_oa[T]], writes=[otb])
                        finish_pair(ot[:], [otb], sg, B_sg, chunk, T, 1.0, 7, act_rstd=True)

        def out_phase(l, last):
            P.barrier()
            wo = arena_view(0, [128, 8, 1024], BF16)
            B_wo = Buf()
            for half in range(2):
                ws = wstage[:, 0:4096].rearrange("p (k n) -> p k n", k=8, n=512)
                P.dma(ws, wout_in[l][:, :, half * 512:(half + 1) * 512], writes=[B_wstage])
                for kc in range(8):
                    eng = "any"
                    P.on(eng, lambda e, kc=kc, half=half: e.tensor_copy(out=wo[:, kc, half * 512:(half + 1) * 512], in_=ws[:, kc, :]),
                         reads=[B_wstage], writes=[B_wo])
            src = x_in if l == 0 else xs_scr
            dst = y_out if last else xs_scr
            for T in range(NT):
                xt, xb_ = xring.next()
                P.dma(xt[:], src[T * 128:(T + 1) * 128, :], reads=([B_xs[T]] if l > 0 else []), writes=[xb_])
                for nb in range(2):
                    bk = 2 * (T % 2) + nb
                    for c in range(8):
                        P.pe(lambda e, c=c, nb=nb, bk=bk, T=T: e.matmul(banks[bk][:, 0:512], lhsT=mixT[:, c, tslice(T)], rhs=wo[:, c, nb * 512:(nb + 1) * 512],
                                                                        start=(c == 0), stop=(c == 7)),
                             reads=[B_mixT[c][T], B_wo], writes=[bankB[bk]])
                    tmp, tmpb = r512.next()
                    P.dve(lambda e, tmp=tmp, bk=bk, nb=nb: e.tensor_tensor(out=tmp[:], in0=banks[bk][:, 0:512], in1=gate_b[:, nb * 512:(nb + 1) * 512], op=ALU.mult),
                          reads=[bankB[bk], B_gate], writes=[tmpb])
                    P.any(lambda e, tmp=tmp, xt=xt, nb=nb: e.tensor_tensor(out=xt[:, nb * 512:(nb + 1) * 512], in0=tmp[:], in1=xt[:, nb * 512:(nb + 1) * 512], op=ALU.add),
                           reads=[tmpb, xb_], writes=[xb_])
                P.dma(dst[T * 128:(T + 1) * 128, :], xt[:], reads=[xb_], writes=([] if last else [B_xs[T]]))

        for l in range(L):
            setup_layer(l)
            norm_phase(l)
            RB = [(Buf(), [Buf(), Buf()]) for _ in range(2)]
            for p in range(2):
                if units_enabled is None or ("r%d" % p) in units_enabled:
                    ret_unit(l, p, RB)
            for p in range(2):
                if units_enabled is None or ("g%d" % p) in units_enabled:
                    hgrn_unit(l, p)
            DB = {"set": [([Buf() for _ in range(NT)], [Buf() for _ in range(NKT)], [Buf() for _ in range(NKT)], [Buf() for _ in range(NT)], Buf()) for _ in range(2)], "ck": Buf()}
            for h in range(4):
                if units_enabled is None or ("d%d" % h) in units_enabled:
                    diff_unit(l, h, DB)
            if dbg:
                P.barrier()
                P.dma(dbg_out[l], mixT[:], reads=[b for row in B_mixT for b in row])
            out_phase(l, last=(l == L - 1))

        with nc.Block() as block:
            run = P.build(sems, dsems, reorder=REORDER)
            block.sync(lambda e: run("sp", e))
            block.tensor(lambda e: run("pe", e))
            block.scalar(lambda e: run("act", e))
            block.vector(lambda e: run("dve", e))
            block.gpsimd(lambda e: run("pool", e))
    return nc


def _unit_perm():
    off = dict(rq=0, rk=256, rv=512, rg=768, dq=1024, dk=1536, dv=2048, dg=2560, hq=3072, hff=3328, hfb=3584, hi=3840, hg=4096)
    cols = []
    for p in range(2):
        for n in ("rq", "rk", "rv", "rg"):
            cols += list(range(off[n] + 128 * p, off[n] + 128 * p + 128))
    for p in range(2):
        for n in ("hff", "hfb", "hg", "hq", "hi"):
            cols += list(range(off[n] + 128 * p, off[n] + 128 * p + 128))
    for h in range(4):
        for n in ("dq", "dk", "dv", "dg"):
            cols += list(range(off[n] + 128 * h, off[n] + 128 * h + 128))
    return np.array(cols, dtype=np.int64)


def _constants():
    s = np.arange(128, dtype=np.float32)[:, None]
    t = np.arange(128, dtype=np.float32)[None, :]
    M1 = np.maximum(t - s, 0)
    L1 = (s <= t).astype(np.float32)
    M2 = np.maximum(s - t, 0)
    L2 = (s >= t).astype(np.float32)
    IOTA1 = np.broadcast_to(t + 1, (128, 128))
    IOTA2 = np.broadcast_to(128 - t, (128, 128))
    COLA = np.broadcast_to(127 - s, (128, 128))
    COLB = np.broadcast_to(s, (128, 128))
    same = (np.floor(s / 32) == np.floor(t / 32))
    TRIF = (same & (s <= t)).astype(np.float32)
    TRIB = (same & (s >= t)).astype(np.float32)
    BM = (np.floor(s / 64) == np.floor(t / 64)).astype(np.float32)
    cst = np.stack([M1, L1, M2, L2, IOTA1, IOTA2, COLA, COLB, TRIF, TRIB, BM], axis=1).astype(np.float32)
    ind = (np.floor(np.arange(128)[:, None] / 32) == np.arange(4)[None, :]).astype(np.float32)
    return np.ascontiguousarray(cst), np.ascontiguousarray(ind)


def _rope_tables(sample):
    ropec = np.ones((128, 16, 64), np.float32)
    ropes = np.zeros((128, 16, 64), np.float32)
    if sample:
        tt = np.arange(TOK)
        row = (tt // 64).astype(np.float32)
        col = (tt % 64).astype(np.float32)
        inv = (np.float32(10000.0) ** (-np.arange(16, dtype=np.float32) / np.float32(16))).astype(np.float32)
        ar = (row[:, None] * inv[None, :]).astype(np.float32)
        ac = (col[:, None] * inv[None, :]).astype(np.float32)
        c = np.concatenate([np.cos(ar), np.cos(ar), np.cos(ac), np.cos(ac)], axis=1).astype(np.float32)
        s_ = np.concatenate([-np.sin(ar), np.sin(ar), -np.sin(ac), np.sin(ac)], axis=1).astype(np.float32)
        ropec = np.ascontiguousarray(c.reshape(16, 128, 64).transpose(1, 0, 2))
        ropes = np.ascontiguousarray(s_.reshape(16, 128, 64).transpose(1, 0, 2))
    return ropec, ropes


_NC_CACHE = {}


def kernel(x_prompt, x_sample, c, c_ctx, cache_diff_k, cache_diff_v, state_ret_fwd, state_ret_bwd,
           state_hgrn_fwd, state_hgrn_bwd, norm_g, w_ada, b_ada, w_in, w_out, ret_decay_logit,
           diff_qn_g, diff_kn_g, diff_lambda, hgrn_lb_logit, _dbg=False, _units=None, _L=2):
    f32 = np.float32
    bf = ml_dtypes.bfloat16
    A = lambda a: np.ascontiguousarray(np.asarray(a, dtype=f32))
    x_prompt, x_sample, c, c_ctx = A(x_prompt), A(x_sample), A(c), A(c_ctx)
    perm = _unit_perm()
    w_in_p = A(w_in)[:, :, perm]
    win = np.ascontiguousarray(w_in_p.reshape(2, 8, 128, 4352).transpose(0, 2, 1, 3))
    wada = np.ascontiguousarray(A(w_ada).reshape(2, 8, 128, 3072).transpose(0, 2, 1, 3))
    wout = np.ascontiguousarray(A(w_out).reshape(2, 8, 128, 1024).transpose(0, 2, 1, 3))
    normg = np.ascontiguousarray(A(norm_g).reshape(2, 8, 128).transpose(0, 2, 1))
    cst, ind = _constants()
    shared = dict(
        normg=normg, wada=wada, bada=A(b_ada), win=win, wout=wout, rdl=A(ret_decay_logit).reshape(2, 8),
        qng=A(diff_qn_g), kng=A(diff_kn_g), dlam=A(diff_lambda).reshape(2, 256), hlb=A(hgrn_lb_logit).reshape(512),
        identb=np.eye(128, dtype=f32).astype(bf), identf=np.eye(128, dtype=f32), cst=cst, ind=ind,
    )
    ropec_s, ropes_s = _rope_tables(True)
    ropec_p, ropes_p = _rope_tables(False)
    z64 = np.zeros((2, 4, 64, 64), f32)
    zc = np.zeros((2, 512, 512), f32)
    in_maps = []
    for core in range(8):
        m = dict(shared)
        if core < 4:
            b = core
            m["x"] = x_sample[b]
            m["modv"] = np.ascontiguousarray(c[b].reshape(8, 128).T)
            m["ck"] = np.ascontiguousarray(A(cache_diff_k)[b].reshape(2, 512, 512))
            m["cv"] = np.ascontiguousarray(A(cache_diff_v)[b].reshape(2, 512, 512))
            m["srf"], m["srb"] = A(state_ret_fwd)[b], A(state_ret_bwd)[b]
            m["shf"], m["shb"] = A(state_hgrn_fwd)[b], A(state_hgrn_bwd)[b]
            m["ropec"], m["ropes"] = ropec_s, ropes_s
            m["qmask"] = np.zeros((8, 2048), f32).astype(bf)
            m["kmask"] = np.zeros((8, 2560), f32).astype(bf)
            m["keep"] = np.ones((128, 32), f32)
        else:
            j = core - 4
            m["x"] = np.ascontiguousarray(x_prompt[8 * j:8 * j + 8].reshape(2048, 1024))
            m["modv"] = np.ascontiguousarray(c_ctx.reshape(8, 128).T)
            m["ck"], m["cv"] = zc, zc
            m["srf"], m["srb"], m["shf"], m["shb"] = z64, z64, z64, z64
            m["ropec"], m["ropes"] = ropec_p, ropes_p
            seq = np.arange(2048) // 256
            qm = (seq[None, :] == np.arange(8)[:, None]).astype(f32)
            km = np.full((8, 2560), BIGNEG, f32)
            km[:, :2048] = np.where(seq[None, :] == np.arange(8)[:, None], 0.0, BIGNEG)
            m["qmask"] = qm.astype(bf)
            m["kmask"] = km.astype(bf)
            kf = np.array([0.0 if T % 2 == 0 else 1.0 for T in range(16)], f32)
            kb = np.array([0.0 if T % 2 == 1 else 1.0 for T in range(16)], f32)
            m["keep"] = np.ascontiguousarray(np.broadcast_to(np.concatenate([kf, kb])[None, :], (128, 32)))
        in_maps.append(m)

    key = (_L, _dbg, None if _units is None else tuple(sorted(_units)))
    if key not in _NC_CACHE:
        _NC_CACHE[key] = build_program(L=_L, dbg=_dbg, units_enabled=_units)
    nc = _NC_CACHE[key]
    res = run_bass_kernel_spmd(nc, in_maps, core_ids=list(range(8)))
    R = res.results

    y_sample = np.stack([R[b]["y"] for b in range(4)], axis=0)
    y_prompt = np.concatenate([R[4 + j]["y"].reshape(8, 256, 1024) for j in range(4)], axis=0)
    nk = np.concatenate([R[4 + j]["nk"].reshape(2, 8, 256, 4, 2, 64).transpose(1, 0, 2, 3, 4, 5) for j in range(4)], axis=0)
    nv = np.concatenate([R[4 + j]["nv"].reshape(2, 8, 256, 4, 128).transpose(1, 0, 2, 3, 4) for j in range(4)], axis=0)
    st = []
    for name in ("nsrf", "nsrb", "nshf", "nshb"):
        st.append(np.concatenate([R[4 + j][name].transpose(1, 0, 2, 3, 4) for j in range(4)], axis=0))
    outs = (y_prompt, y_sample, np.ascontiguousarray(nk), np.ascontiguousarray(nv), *[np.ascontiguousarray(s) for s in st])
    if _dbg:
        return outs, [R[i]["dbgmix"] for i in range(8)]
    return outs
```

```python
import math
import types
from contextlib import ExitStack

import numpy as np
import ml_dtypes

import concourse.bass as bass
import concourse.mybir as mybir
from concourse.bass_utils import run_bass_kernel_spmd

F32 = mybir.dt.float32
BF16 = mybir.dt.bfloat16
ALU = mybir.AluOpType
AF = mybir.ActivationFunctionType
AX = mybir.AxisListType

ENGS = ["pe", "act", "dve", "pool", "sp"]
NDSEM = 8
SAME_ENG_SYNC = True
import os as _os0
REORDER = _os0.environ.get('REORDER', '1') == '1'
PSUM_EXCL = _os0.environ.get('PSUM_EXCL', '1') == '1'
REORDER_ENGS = _os0.environ.get('REORDER_ENGS', 'pe,act,dve,pool,sp').split(',')

D_MODEL = 1024
NT = 16
TOK = 2048
NKT = 20
EPS = 1e-6
UNIT_W = [512, 512, 640, 640, 512, 512, 512, 512]
UNIT_OFF = [0, 512, 1024, 1664, 2304, 2816, 3328, 3840]
UNIT_CHUNK = [0, 1, 6, 7, 2, 3, 4, 5]
BIGNEG = -30000.0


class Buf:
    __slots__ = ("w", "r", "name", "excl")

    def __init__(self, name="", excl=False):
        self.w = None
        self.r = []
        self.name = name
        self.excl = excl


class Op:
    __slots__ = ("eng", "fn", "waits", "marked", "semval", "is_dma", "dsem", "dval", "cost", "lat", "idx", "prio",
                 "pos", "succs", "nrem", "ready", "fin", "is_bar", "per_eng", "per_dsem")


class _Probe:
    def __init__(self):
        self.rec = None

    def __getattr__(self, name):
        def f(*a, **k):
            self.rec = (name, a, k)
            return self
        return f


def _nfree(ap):
    n = 1
    for d in ap.shape[1:]:
        n *= int(d)
    return n


def _estimate(eng, fn, is_dma):
    pr = _Probe()
    try:
        fn(pr)
        name, a, k = pr.rec
    except Exception:
        name, a, k = "?", (), {}
    out = k.get("out", a[0] if a else None)
    try:
        if is_dma:
            nbytes = _nfree(out) * int(out.shape[0]) * mybir.dt.size(out.dtype)
            return 120.0, 2200.0 + nbytes / 120.0
        if eng == "pe":
            if name == "transpose":
                return 80.0, 80.0
            rhs = k.get("rhs", a[2] if len(a) > 2 else None)
            lhsT = k.get("lhsT", a[1] if len(a) > 1 else None)
            n = _nfree(rhs)
            c = (max(64, n) / 2.4 + 25.0) * 1.25
            if lhsT.dtype == F32:
                c *= 4.0
            return c, c
        n = _nfree(out)
        if eng == "act":
            c = 190.0 + n / 1.2 + (90.0 if k.get("accum_out") is not None else 0.0)
        elif eng == "dve":
            c = 130.0 + n / 0.7
        else:
            c = 700.0 + n / 0.4
        return c, c
    except Exception:
        return 300.0, 300.0


def _freeze(fn):
    if fn.__closure__ is None:
        return fn
    cells = []
    for c in fn.__closure__:
        try:
            cells.append(types.CellType(c.cell_contents))
        except ValueError:
            cells.append(c)
    return types.FunctionType(fn.__code__, fn.__globals__, fn.__name__, fn.__defaults__, tuple(cells))


LAT_X = 200.0
LAT_S = 50.0


class Prog:
    def __init__(self, nc):
        self.nc = nc
        self.all = []
        self.cur_bar = None
        self.since = []
        self.load = {e: 0.0 for e in ENGS}

    def _new(self, eng):
        op = Op()
        op.eng = eng
        op.fn = None
        op.marked = False
        op.semval = None
        op.is_dma = False
        op.dsem = None
        op.dval = None
        op.cost = 0.0
        op.lat = 0.0
        op.is_bar = False
        op.idx = len(self.all)
        op.waits = []
        self.all.append(op)
        return op

    def barrier(self):
        b = self._new("virt")
        b.is_bar = True
        b.waits = list(self.since)
        self.since = []
        self.cur_bar = b
        self.load = {e: 0.0 for e in ENGS}

    def emit(self, eng, fn, reads=(), writes=(), extra=(), is_dma=False):
        op = self._new(eng)
        op.fn = _freeze(fn)
        op.is_dma = is_dma
        op.cost, op.lat = _estimate(eng, op.fn, is_dma)
        self.load[eng] += op.cost
        waits = set()
        if PSUM_EXCL:
            for b in reads:
                if b.excl:
                    for r in b.r:
                        if r.eng != eng:
                            waits.add(r)
        for b in reads:
            if b.w is not None:
                waits.add(b.w)
        for b in writes:
            if b.w is not None:
                waits.add(b.w)
            for r in b.r:
                waits.add(r)
        for w in extra:
            if w is not None:
                waits.add(w)
        if self.cur_bar is not None:
            waits.add(self.cur_bar)
        waits.discard(op)
        op.waits = list(waits)
        for b in reads:
            b.r.append(op)
        for b in writes:
            b.w = op
            b.r = []
        self.since.append(op)
        return op

    def pe(self, fn, reads=(), writes=(), extra=()):
        return self.emit("pe", fn, reads, writes, extra)

    def act(self, fn, reads=(), writes=(), extra=()):
        return self.emit("act", fn, reads, writes, extra)

    def dve(self, fn, reads=(), writes=(), extra=()):
        return self.emit("dve", fn, reads, writes, extra)

    def pool(self, fn, reads=(), writes=(), extra=()):
        return self.emit("pool", fn, reads, writes, extra)

    def on(self, eng, fn, reads=(), writes=(), extra=()):
        if eng == "any":
            return self.any(fn, reads, writes, extra)
        return self.emit(eng, fn, reads, writes, extra)

    def any(self, fn, reads=(), writes=(), extra=()):
        f = _freeze(fn)
        best = None
        for e in ("dve", "pool"):
            c, _ = _estimate(e, f, False)
            tot = self.load[e] + c
            if best is None or tot < best[0]:
                best = (tot, e)
        return self.emit(best[1], fn, reads, writes, extra)

    def dma(self, out, in_, reads=(), writes=(), extra=()):
        return self.emit("sp", lambda e: e.dma_start(out=out, in_=in_), reads, writes, extra, is_dma=True)

    def schedule(self, reorder=True):
        import heapq
        ops = self.all
        for op in ops:
            op.succs = []
        for op in ops:
            for w in op.waits:
                w.succs.append(op)
        for op in reversed(ops):
            m = 0.0
            for s_ in op.succs:
                l_ = s_.prio + (0.0 if op.is_bar else (LAT_S if s_.eng == op.eng else LAT_X))
                if l_ > m:
                    m = l_
            op.prio = m + op.lat
        order = {e: [] for e in ENGS}
        if not reorder:
            for op in ops:
                if not op.is_bar:
                    order[op.eng].append(op)
            return order
        for op in ops:
            op.nrem = len(op.waits)
            op.ready = 0.0
            op.fin = None
        fixed = [e for e in ENGS if e not in REORDER_ENGS]
        lastop = {}
        for op in ops:
            if op.is_bar or op.eng not in fixed:
                continue
            p_ = lastop.get(op.eng)
            if p_ is not None and p_ not in op.waits:
                p_.succs.append(op)
                op.nrem += 1
            lastop[op.eng] = op
        future = {e: [] for e in ENGS}
        now = {e: [] for e in ENGS}
        free = {e: 0.0 for e in ENGS}

        def release(op):
            for s_ in op.succs:
                if op.is_bar:
                    t = op.fin
                elif s_.is_bar:
                    t = op.fin
                elif s_.eng == op.eng:
                    t = op.fin + (0.0 if op.eng == "pe" else LAT_S)
                else:
                    t = op.fin + LAT_X
                if t > s_.ready:
                    s_.ready = t
                s_.nrem -= 1
                if s_.nrem == 0:
                    if s_.is_bar:
                        s_.fin = s_.ready
                        release(s_)
                    else:
                        heapq.heappush(future[s_.eng], (s_.ready, s_.idx, s_))

        import sys
        sys.setrecursionlimit(100000)
        roots = [op for op in ops if op.nrem == 0]
        for op in roots:
            if op.is_bar:
                op.fin = 0.0
                release(op)
            else:
                heapq.heappush(future[op.eng], (0.0, op.idx, op))
        nleft = sum(1 for op in ops if not op.is_bar)
        while nleft > 0:
            best = None
            for e in ENGS:
                f = future[e]
                nw = now[e]
                while f and f[0][0] <= free[e]:
                    r_, i_, o_ = heapq.heappop(f)
                    heapq.heappush(nw, (-o_.prio, o_.idx, o_))
                if nw:
                    st = free[e]
                elif f:
                    st = f[0][0]
                else:
                    continue
                if best is None or st < best[0]:
                    best = (st, e)
            st, e = best
            if now[e]:
                _, _, op = heapq.heappop(now[e])
            else:
                _, _, op = heapq.heappop(future[e])
            if op.is_dma:
                free[e] = st + op.cost
                op.fin = st + op.lat
            else:
                free[e] = st + op.cost
                op.fin = st + op.cost
            order[e].append(op)
            nleft -= 1
            release(op)
        self.est_ns = max(free.values())
        return order

    def build(self, sems, dsems, reorder=True):
        order = self.schedule(reorder)
        if reorder:
            print('[sched] est_us=%.1f' % (self.est_ns / 1e3), {e: len(order[e]) for e in ENGS})
        for e in ENGS:
            for i, op in enumerate(order[e]):
                op.pos = i

        def skip_same(w_eng, eng):
            return w_eng == eng and (eng == "pe" or not SAME_ENG_SYNC)

        dcnt = [0] * NDSEM
        prev_on_sem = [None] * NDSEM
        dma_prev = {}
        nd = 0
        for op in order["sp"]:
            k = nd % NDSEM
            nd += 1
            op.dsem = k
            dcnt[k] += 16
            op.dval = dcnt[k]
            dma_prev[id(op)] = prev_on_sem[k]
            prev_on_sem[k] = op
        final_dvals = list(dcnt)
        for b in self.all:
            if b.is_bar:
                pe_ = {}
                pd_ = {}
                for w in b.waits:
                    if w.is_bar:
                        continue
                    if w.is_dma:
                        if pd_.get(w.dsem, 0) < w.dval:
                            pd_[w.dsem] = w.dval
                    else:
                        c = pe_.get(w.eng)
                        if c is None or c.pos < w.pos:
                            pe_[w.eng] = w
                b.per_eng = pe_
                b.per_dsem = pd_
        for op in self.all:
            if op.is_bar:
                for w in op.per_eng.values():
                    w.marked = True
                continue
            for w in op.waits:
                if w.is_bar or w.is_dma:
                    continue
                if not skip_same(w.eng, op.eng):
                    w.marked = True
        for e in ENGS:
            cnt = 0
            for op in order[e]:
                if not op.is_dma and op.marked:
                    cnt += 1
                    op.semval = cnt

        def run_engine(ename, eng):
            waited = {}

            def need(semkey, sem, val):
                if waited.get(semkey, 0) >= val:
                    return
                eng.wait_ge(sem, val)
                waited[semkey] = val

            for op in order[ename]:
                for w in op.waits:
                    if w.is_bar:
                        for we, wo in w.per_eng.items():
                            if not (we == ename and ename == "pe"):
                                need(("e", we), sems[we], wo.semval)
                        for k, v in w.per_dsem.items():
                            need(("d", k), dsems[k], v)
                    elif w.is_dma:
                        need(("d", w.dsem), dsems[w.dsem], w.dval)
                    elif not skip_same(w.eng, ename):
                        need(("e", w.eng), sems[w.eng], w.semval)
                if op.is_dma:
                    p = dma_prev[id(op)]
                    if p is not None:
                        need(("d", p.dsem), dsems[p.dsem], p.dval)
                ins = op.fn(eng)
                if op.is_dma:
                    ins.then_inc(dsems[op.dsem], 16)
                elif op.marked:
                    ins.then_inc(sems[ename], 1)
            if ename == "sp":
                for k in range(NDSEM):
                    if final_dvals[k] > 0:
                        need(("d", k), dsems[k], final_dvals[k])

        return run_engine


class Ring:
    def __init__(self, tiles):
        self.tiles = tiles
        self.bufs = [Buf() for _ in tiles]
        self.i = 0

    def next(self):
        k = self.i % len(self.tiles)
        self.i += 1
        return self.tiles[k], self.bufs[k]


def build_program(L=2, dbg=False, units_enabled=None):
    nc = bass.Bass("TRN2", target_bir_lowering=False)

    def din(name, shape, dt=F32):
        return nc.dram_tensor(name, list(shape), dt, kind="ExternalInput").ap()

    def dout(name, shape, dt=F32):
        return nc.dram_tensor(name, list(shape), dt, kind="ExternalOutput").ap()

    x_in = din("x", [TOK, D_MODEL])
    modv = din("modv", [128, 8])
    ck_in = din("ck", [2, 512, 512])
    cv_in = din("cv", [2, 512, 512])
    srf_in = din("srf", [2, 4, 64, 64])
    srb_in = din("srb", [2, 4, 64, 64])
    shf_in = din("shf", [2, 4, 64, 64])
    shb_in = din("shb", [2, 4, 64, 64])
    normg_in = din("normg", [2, 128, 8])
    wada_in = din("wada", [2, 128, 8, 3072])
    bada_in = din("bada", [2, 3072])
    win_in = din("win", [2, 128, 8, 4352])
    wout_in = din("wout", [2, 128, 8, 1024])
    rdl_in = din("rdl", [2, 8])
    qng_in = din("qng", [2, 64])
    kng_in = din("kng", [2, 64])
    dlam_in = din("dlam", [2, 256])
    hlb_in = din("hlb", [512])
    ropec_in = din("ropec", [128, 16, 64])
    ropes_in = din("ropes", [128, 16, 64])
    qmask_in = din("qmask", [8, 2048], BF16)
    kmask_in = din("kmask", [8, 2560], BF16)
    keep_in = din("keep", [128, 32])
    identb_in = din("identb", [128, 128], BF16)
    identf_in = din("identf", [128, 128])
    cst_in = din("cst", [128, 11, 128])
    ind_in = din("ind", [128, 4])

    y_out = dout("y", [TOK, D_MODEL])
    nk_out = dout("nk", [2, TOK, 512])
    nv_out = dout("nv", [2, TOK, 512])
    nsrf_out = dout("nsrf", [2, 8, 4, 64, 64])
    nsrb_out = dout("nsrb", [2, 8, 4, 64, 64])
    nshf_out = dout("nshf", [2, 8, 4, 64, 64])
    nshb_out = dout("nshb", [2, 8, 4, 64, 64])
    xs_scr = nc.dram_tensor("xs_scr", [TOK, D_MODEL], F32, kind="Internal").ap()
    dbg_out = dout("dbgmix", [2, 128, 8, TOK], BF16) if dbg else None

    es = ExitStack()
    with es:
        def sb(name, shape, dt):
            return es.enter_context(nc.sbuf_tensor("sb_" + name, list(shape), dt))

        hT = sb("hT", [128, 8, TOK], BF16)
        mixT = sb("mixT", [128, 8, TOK], BF16)
        wstage = sb("wstage", [128, 8 * 512], F32)
        wbf = sb("wbf", [128, 8 * 640], BF16)
        arena = sb("arena", [128, 61440], mybir.dt.uint8)
        cst = sb("cst", [128, 11, 128], F32)
        ind = sb("ind", [128, 4], F32)
        ropec = sb("ropec", [128, 16, 64], F32)
        ropes = sb("ropes", [128, 16, 64], F32)
        identb = sb("identb", [128, 128], BF16)
        identf = sb("identf", [128, 128], F32)
        keep = sb("keep", [128, 32], F32)
        gate_b = sb("gate_b", [128, 1024], F32)
        modt = sb("modt", [128, 8], F32)
        smod = sb("smod", [128, 8], F32)
        normg = sb("normg", [128, 8], F32)
        modT = sb("modT", [128, 2, 8], F32)
        modA = sb("modA", [128, 8], F32)
        small = sb("small", [128, 64], F32)
        rdl = sb("rdl", [128, 8], F32)
        lg = sb("lg", [128, 8], F32)
        g4 = sb("g4", [128, 256], F32)
        lball = sb("lball", [128, 2, 256], F32)
        lb2 = sb("lb2", [128, 256], F32)
        omlb2 = sb("omlb2", [128, 256], F32)
        cneg = sb("cneg", [128, 8], F32)
        zerosb = sb("zerosb", [128, 512], BF16)
        lgrow = sb("lgrow", [128, 4], F32)
        lgrow1 = sb("lgrow1", [128, 4], F32)
        dtm = sb("dtm", [128, 2, 128], F32)
        qd = sb("qd", [128, 2, 128], F32)
        kd = sb("kd", [128, 2, 128], F32)
        e12 = sb("e12", [128, 2, 128], F32)
        stS = sb("stS", [128, 2, 128], F32)

        n_f32_512 = 3
        r512 = Ring([sb("r512_%d" % i, [128, 512], F32) for i in range(n_f32_512)])
        r256 = Ring([sb("r256_%d" % i, [128, 256], F32) for i in range(8)])
        r128 = Ring([sb("r128_%d" % i, [128, 128], F32) for i in range(8)])
        rb256 = Ring([sb("rb256_%d" % i, [128, 256], BF16) for i in range(3)])
        rb128 = Ring([sb("rb128_%d" % i, [128, 128], BF16) for i in range(8)])
        rpt = Ring([sb("rpt_%d" % i, [128, 512], BF16) for i in range(3)])
        rs = Ring([sb("rs_%d" % i, [128, 8], F32) for i in range(16)])

        banks = [es.enter_context(nc.psum_tensor("bank%d" % i, [128, 512], F32)) for i in range(8)]
        banksb = [b.bitcast(BF16) for b in banks]
        bankB = [Buf("bank%d" % i, excl=True) for i in range(8)]

        sems = {e: es.enter_context(nc.semaphore("s_" + e)) for e in ENGS}
        dsems = [es.enter_context(nc.semaphore("d%d" % k)) for k in range(NDSEM)]

        P = Prog(nc)

        B_hT = [Buf() for _ in range(NT)]
        B_mixT = [[Buf() for _ in range(NT)] for _ in range(8)]
        B_wstage = Buf()
        B_wbf = Buf()
        B_cst = Buf()
        B_misc = Buf()
        B_gate = Buf()
        B_modAB = Buf()
        B_pair = Buf()
        B_stS = [Buf(), Buf()]

        def arena_view(off_bytes, shape, dt):
            n = 1
            for s in shape[1:]:
                n *= s
            esz = 2 if dt == BF16 else 4
            a = arena[:, off_bytes:off_bytes + n * esz].bitcast(dt)
            if len(shape) == 2:
                return a
            if len(shape) == 3:
                return a.rearrange("p (a b) -> p a b", a=shape[1], b=shape[2])
            if len(shape) == 4:
                return a.rearrange("p (a b c) -> p a b c", a=shape[1], b=shape[2], c=shape[3])
            raise ValueError

        xring = Ring([arena_view(36864 + i * 4096, [128, 1024], F32) for i in range(3)])
        xhring = Ring([arena_view(49152 + i * 2048, [128, 1024], BF16) for i in range(2)])
        B_xs = [Buf() for _ in range(NT)]

        M1, L1, M2, L2, IOTA1, IOTA2, COLA, COLB, TRIF, TRIB, BM = [cst[:, i, :] for i in range(11)]

        P.dma(cst[:], cst_in, writes=[B_cst])
        P.dma(ind[:], ind_in, writes=[B_cst])
        P.dma(ropec[:], ropec_in, writes=[B_cst])
        P.dma(ropes[:], ropes_in, writes=[B_cst])
        P.dma(identb[:], identb_in, writes=[B_cst])
        P.dma(identf[:], identf_in, writes=[B_cst])
        P.dma(keep[:], keep_in, writes=[B_cst])
        P.dma(modt[:], modv, writes=[B_cst])
        hlb, hlbb = r512.next()
        P.dma(hlb[:], hlb_in.partition_broadcast(128), writes=[hlbb])
        P.pool(lambda e: e.memset(cneg[:], -0.5), writes=[B_cst])
        P.pool(lambda e: e.memset(zerosb[:], 0.0), writes=[B_cst])
        t_, tb_ = rs.next()
        P.act(lambda e, t_=t_: e.activation(out=t_[:, 0:8], in_=modt[:], func=AF.Tanh, scale=0.5), reads=[B_cst], writes=[tb_])
        P.dve(lambda e, t_=t_: e.scalar_tensor_tensor(out=smod[:], in0=t_[:, 0:8], scalar=1.0, in1=modt[:], op0=ALU.add, op1=ALU.mult),
              reads=[tb_, B_cst], writes=[B_cst])
        P.dve(lambda e: e.tensor_scalar(out=smod[:], in0=smod[:], scalar1=0.5, scalar2=None, op0=ALU.mult), reads=[B_cst], writes=[B_cst])
        P.act(lambda e: e.activation(out=hlb[:], in_=hlb[:], func=AF.Exp), reads=[hlbb], writes=[hlbb])
        den_, denb_ = r256.next()
        P.dve(lambda e: e.tensor_tensor(out=den_[:], in0=hlb[:, 0:256], in1=hlb[:, 256:512], op=ALU.add), reads=[hlbb], writes=[denb_])
        P.dve(lambda e: e.reciprocal(out=den_[:], in_=den_[:]), reads=[denb_], writes=[denb_])
        P.dve(lambda e: e.tensor_tensor(out=hlb[:, 0:256], in0=hlb[:, 0:256], in1=den_[:], op=ALU.mult), reads=[hlbb, denb_], writes=[hlbb])
        P.dve(lambda e: e.tensor_tensor(out=hlb[:, 256:512], in0=hlb[:, 256:512], in1=den_[:], op=ALU.mult), reads=[hlbb, denb_], writes=[hlbb])
        P.dve(lambda e: e.tensor_tensor(out=lball[:, 0, :], in0=hlb[:, 0:256], in1=hlb[:, 0:256], op=ALU.subtract), reads=[hlbb], writes=[B_cst])
        P.dve(lambda e: e.tensor_tensor(out=lball[:, 1, :], in0=hlb[:, 0:256], in1=hlb[:, 256:512], op=ALU.add), reads=[hlbb], writes=[B_cst])
        P.dve(lambda e: e.tensor_tensor(out=lball[:, 1, :], in0=lball[:, 1, :], in1=hlb[:, 0:256], op=ALU.subtract), reads=[hlbb, B_cst], writes=[B_cst])

        def rstd_from_ss(ss_ap, ssb, n, mult, add):
            t1, b1 = rs.next()
            P.dve(lambda e: e.tensor_scalar(out=t1[:, 0:n], in0=ss_ap, scalar1=mult, scalar2=add, op0=ALU.mult, op1=ALU.add),
                  reads=[ssb], writes=[b1])
            t2, b2 = rs.next()
            P.pool(lambda e: e.tensor_tensor(out=t2[:, 0:n], in0=t1[:, 0:n], in1=cneg[:, 0:n], op=ALU.pow), reads=[b1, B_cst], writes=[b2])
            return t2, b2

        def rope(src, srcbufs, dst, dstbufs, T, G, eng_a, eng_b):
            W = G * 64
            t1, b1 = r256.next()
            t2, b2 = r256.next()
            cv_ = ropec[:, T, :]
            sv_ = ropes[:, T, :].rearrange("p (h j i) -> p h j i", h=2, j=2, i=16)
            src3 = src.rearrange("p (g d) -> p g d", g=G, d=64)
            src5 = src.rearrange("p (g h j i) -> p g h j i", g=G, h=2, j=2, i=16)
            t13 = t1[:, 0:W].rearrange("p (g d) -> p g d", g=G, d=64)
            t25 = t2[:, 0:W].rearrange("p (g h j i) -> p g h j i", g=G, h=2, j=2, i=16)
            P.on(eng_a, lambda e: e.tensor_tensor(out=t13, in0=src3, in1=cv_.unsqueeze(1).to_broadcast([128, G, 64]), op=ALU.mult),
                 reads=list(srcbufs) + [B_cst], writes=[b1])
            P.on(eng_b, lambda e: e.tensor_tensor(out=t25[:, :, :, 0, :], in0=src5[:, :, :, 1, :],
                                                  in1=sv_[:, :, 0, :].unsqueeze(1).to_broadcast([128, G, 2, 16]), op=ALU.mult),
                 reads=list(srcbufs) + [B_cst], writes=[b2])
            P.on(eng_b, lambda e: e.tensor_tensor(out=t25[:, :, :, 1, :], in0=src5[:, :, :, 0, :],
                                                  in1=sv_[:, :, 1, :].unsqueeze(1).to_broadcast([128, G, 2, 16]), op=ALU.mult),
                 reads=list(srcbufs) + [B_cst], writes=[b2])
            P.on(eng_a, lambda e: e.tensor_tensor(out=dst, in0=t1[:, 0:W], in1=t2[:, 0:W], op=ALU.add), reads=[b1, b2], writes=list(dstbufs))

        def tslice(T):
            return slice(T * 128, (T + 1) * 128)

        def setup_layer(l):
            stg = [wstage[:, 0:4096].rearrange("p (k n) -> p k n", k=8, n=512), arena_view(16384, [128, 8, 512], F32)]
            stgB = [B_wstage, Buf()]
            smb = arena_view(32768, [128, 8, 128], F32)
            B_smb = Buf()
            P.dve(lambda e: e.tensor_copy(out=smb, in_=smod[:].unsqueeze(2).to_broadcast([128, 8, 128])), reads=[B_cst], writes=[B_smb])
            P.dma(normg[:], normg_in[l], writes=[B_misc])
            P.dma(rdl[:], rdl_in[l].partition_broadcast(128), writes=[B_misc])
            dlam, dlamb = r256.next()
            P.dma(dlam[:], dlam_in[l].partition_broadcast(128), writes=[dlamb])
            P.dma(g4[:, 0:64], qng_in[l].partition_broadcast(128), writes=[B_misc])
            P.dma(g4[:, 64:128], qng_in[l].partition_broadcast(128), writes=[B_misc])
            P.dma(g4[:, 128:192], kng_in[l].partition_broadcast(128), writes=[B_misc])
            P.dma(g4[:, 192:256], kng_in[l].partition_broadcast(128), writes=[B_misc])
            for cb in range(6):
                st_, stb_ = stg[cb % 2], stgB[cb % 2]
                P.dma(st_, wada_in[l][:, :, cb * 512:(cb + 1) * 512], writes=[stb_])
                bt, btb = r512.next()
                P.dma(bt[:], bada_in[l][cb * 512:(cb + 1) * 512].partition_broadcast(128), writes=[btb])
                bk = cb % 4
                for kc in range(8):
                    P.pe(lambda e, kc=kc, st_=st_, bk=bk: e.matmul(banks[bk][:, 0:512], lhsT=smb[:, kc, :], rhs=st_[:, kc, :],
                                                                     start=(kc == 0), stop=(kc == 7)),
                         reads=[B_smb, stb_], writes=[bankB[bk]])
                if cb >= 4:
                    P.dve(lambda e, bk=bk, bt=bt, cb=cb: e.tensor_tensor(out=gate_b[:, (cb - 4) * 512:(cb - 3) * 512], in0=banks[bk][:, 0:512],
                                                                          in1=bt[:], op=ALU.add),
                          reads=[bankB[bk], btb], writes=[B_gate])
                else:
                    P.dve(lambda e, bk=bk, bt=bt: e.tensor_tensor(out=bt[:], in0=banks[bk][:, 0:512], in1=bt[:], op=ALU.add),
                          reads=[bankB[bk], btb], writes=[btb])
                    which = cb // 2
                    tb = 4 + (cb % 2)
                    for jj in range(4):
                        kc = (cb % 2) * 4 + jj
                        P.pe(lambda e, jj=jj, bt=bt, tb=tb: e.transpose(banks[tb][:, jj * 128:(jj + 1) * 128], bt[:, jj * 128:(jj + 1) * 128], identf[:]),
                             reads=[btb, B_cst], writes=[bankB[tb]])
                        P.act(lambda e, jj=jj, tb=tb, which=which, kc=kc: e.copy(out=modT[:, which, kc:kc + 1], in_=banks[tb][:, jj * 128:jj * 128 + 1]),
                              reads=[bankB[tb]], writes=[B_modAB])
            P.dve(lambda e: e.scalar_tensor_tensor(out=modA[:], in0=modT[:, 1, :], scalar=1.0, in1=normg[:], op0=ALU.add, op1=ALU.mult),
                  reads=[B_modAB, B_misc], writes=[B_modAB])
            pr, prb = r256.next()
            P.dve(lambda e: e.tensor_tensor(out=pr[:, 0:64], in0=dlam[:, 0:64], in1=dlam[:, 64:128], op=ALU.mult), reads=[dlamb], writes=[prb])
            P.dve(lambda e: e.tensor_tensor(out=pr[:, 64:128], in0=dlam[:, 128:192], in1=dlam[:, 192:256], op=ALU.mult), reads=[dlamb], writes=[prb])
            s12, s12b = rs.next()
            P.dve(lambda e: e.tensor_reduce(out=s12[:, 0:2], in_=pr[:, 0:128].rearrange("p (a d) -> p a d", a=2, d=64), axis=AX.X, op=ALU.add),
                  reads=[prb], writes=[s12b])
            P.act(lambda e: e.activation(out=s12[:, 0:2], in_=s12[:, 0:2], func=AF.Exp), reads=[s12b], writes=[s12b])
            lam_init = 0.8 - 0.6 * math.exp(-0.3 * l)
            P.dve(lambda e: e.tensor_tensor(out=small[:, 0:1], in0=s12[:, 1:2], in1=s12[:, 0:1], op=ALU.subtract), reads=[s12b], writes=[B_misc])
            P.dve(lambda e: e.tensor_scalar(out=small[:, 0:1], in0=small[:, 0:1], scalar1=-lam_init, scalar2=None, op0=ALU.add),
                  reads=[B_misc], writes=[B_misc])
            P.act(lambda e: e.activation(out=lg[:], in_=rdl[:], func=AF.Exp, scale=-1.0), reads=[B_misc], writes=[B_misc])
            P.act(lambda e: e.activation(out=lg[:], in_=lg[:], func=AF.Ln, bias=1.0), reads=[B_misc], writes=[B_misc])
            P.dve(lambda e: e.tensor_scalar(out=lg[:], in0=lg[:], scalar1=-1.0, scalar2=None, op0=ALU.mult), reads=[B_misc], writes=[B_misc])

        def norm_phase(l):
            src = x_in if l == 0 else xs_scr
            for T in range(NT):
                xt, xb_ = xring.next()
                P.dma(xt[:], src[T * 128:(T + 1) * 128, :], reads=([B_xs[T]] if l > 0 else []), writes=[xb_])
                xh, xhb = xhring.next()
                ss, ssb = rs.next()
                P.act(lambda e, xt=xt, xh=xh, ss=ss: e.activation(out=xh[:], in_=xt[:], func=AF.Square, accum_out=ss[:, 0:1]),
                      reads=[xb_], writes=[xhb, ssb])
                rstd, rb_ = rstd_from_ss(ss[:, 0:1], ssb, 1, 1.0 / D_MODEL, EPS)
                P.dve(lambda e, xt=xt, xh=xh, rstd=rstd: e.tensor_scalar(out=xh[:], in0=xt[:], scalar1=rstd[:, 0:1], scalar2=None, op0=ALU.mult),
                      reads=[xb_, rb_], writes=[xhb])
                bk = 6 + (T % 2)
                for kc in range(8):
                    P.pe(lambda e, kc=kc, xh=xh, bk=bk: e.transpose(banksb[bk][:, kc * 128:(kc + 1) * 128], xh[:, kc * 128:(kc + 1) * 128], identb[:]),
                         reads=[xhb, B_cst], writes=[bankB[bk]])
                for kc in range(8):
                    if kc % 2 == 0:
                        P.dve(lambda e, kc=kc, bk=bk, T=T: e.tensor_scalar(out=hT[:, kc, tslice(T)], in0=banksb[bk][:, kc * 128:(kc + 1) * 128],
                                                                            scalar1=modA[:, kc:kc + 1], scalar2=modT[:, 0, kc:kc + 1],
                                                                            op0=ALU.mult, op1=ALU.add),
                              reads=[bankB[bk], B_modAB], writes=[B_hT[T]])
                    else:
                        P.act(lambda e, kc=kc, bk=bk, T=T: e.activation(out=hT[:, kc, tslice(T)], in_=banksb[bk][:, kc * 128:(kc + 1) * 128],
                                                                         func=AF.Identity, scale=modA[:, kc:kc + 1], bias=modT[:, 0, kc:kc + 1]),
                              reads=[bankB[bk], B_modAB], writes=[B_hT[T]])

        def load_unit_weights(l, u):
            W = UNIT_W[u]
            wb = wbf[:, 0:8 * W].rearrange("p (k n) -> p k n", k=8, n=W)
            engs = ["dve", "pool", "act", "dve", "pool", "act", "dve", "pool"]
            for (a, b) in [(0, 512)] + ([(512, W)] if W > 512 else []):
                wd = b - a
                ws = wstage[:, 0:8 * wd].rearrange("p (k n) -> p k n", k=8, n=wd)
                P.dma(ws, win_in[l][:, :, UNIT_OFF[u] + a:UNIT_OFF[u] + b], writes=[B_wstage])
                for kc in range(8):
                    if engs[kc] == "act":
                        P.act(lambda e, kc=kc, ws=ws, a=a, b=b: e.copy(out=wb[:, kc, a:b], in_=ws[:, kc, :]), reads=[B_wstage], writes=[B_wbf])
                    else:
                        P.on(engs[kc], lambda e, kc=kc, ws=ws, a=a, b=b: e.tensor_copy(out=wb[:, kc, a:b], in_=ws[:, kc, :]), reads=[B_wstage], writes=[B_wbf])
            return wb

        def project(wb, T, bk, c0, c1):
            for kc in range(8):
                P.pe(lambda e, kc=kc: e.matmul(banks[bk][:, 0:c1 - c0], lhsT=hT[:, kc, tslice(T)], rhs=wb[:, kc, c0:c1],
                                               start=(kc == 0), stop=(kc == 7)),
                     reads=[B_hT[T], B_wbf], writes=[bankB[bk]])

        def mixed_out(mt, mtb, chunk, T, tbank, eng="act"):
            P.pe(lambda e: e.transpose(banksb[tbank][:, 0:128], mt, identb[:]), reads=[mtb, B_cst], writes=[bankB[tbank]])
            if eng == "act":
                P.act(lambda e: e.copy(out=mixT[:, chunk, tslice(T)], in_=banksb[tbank][:, 0:128]), reads=[bankB[tbank]], writes=[B_mixT[chunk][T]])
            else:
                P.dve(lambda e: e.tensor_copy(out=mixT[:, chunk, tslice(T)], in_=banksb[tbank][:, 0:128]), reads=[bankB[tbank]], writes=[B_mixT[chunk][T]])

        def diff_unit(l, h, DB):
            u = 4 + h
            chunk = UNIT_CHUNK[u]
            if h == 0:
                P.barrier()
            par = h % 2
            base = par * 27728
            QT = arena_view(base + 0, [128, 2, TOK], BF16)
            KT = arena_view(base + 8192, [128, 2, 2560], BF16)
            V = arena_view(base + 18432, [128, NKT, 130], BF16)
            sg = arena_view(base + 23632, [128, NT, 128], BF16)
            ckst = arena_view(55456, [128, 4, 128], F32)
            cvst = arena_view(57504, [128, 4, 128], F32)
            ckb = arena_view(59552, [128, 4, 128], BF16)
            B_QT, B_KT, B_V, B_sg, B_qm = DB["set"][par]
            B_ck = DB["ck"]
            wb = load_unit_weights(l, u)
            import os as _os
            DD0 = _os.environ.get('DIFFDBG', '')
            if 'nomask' not in DD0:
                for c in range(2):
                    P.dma(QT[64:72, c, :], qmask_in, writes=[B_qm])
                    P.dma(KT[64:72, c, :], kmask_in, writes=[B_qm])
            if 'noctx' not in DD0:
                P.dma(ckst, ck_in[l].rearrange("(t p) n -> p t n", p=128)[:, :, h * 128:(h + 1) * 128], writes=[B_ck])
                P.dma(cvst, cv_in[l].rearrange("(t p) n -> p t n", p=128)[:, :, h * 128:(h + 1) * 128], writes=[B_ck])
                P.pool(lambda e: e.memset(V[:, :, 128:130], 1.0), writes=B_V)
                P.dve(lambda e: e.tensor_copy(out=ckb, in_=ckst), reads=[B_ck], writes=[B_ck])
                for pt in range(4):
                    bk = 5
                    for c in range(2):
                        P.pe(lambda e, pt=pt, c=c, bk=bk: e.transpose(banksb[bk][0:64, c * 128:(c + 1) * 128], ckb[:, pt, c * 64:(c + 1) * 64], identb[:]),
                             reads=[B_ck, B_cst], writes=[bankB[bk]])
                    P.act(lambda e, pt=pt, bk=bk: e.copy(out=KT[0:64, :, 2048 + pt * 128:2048 + (pt + 1) * 128],
                                                         in_=banksb[bk][0:64, 0:256].rearrange("p (c t) -> p c t", c=2, t=128)),
                          reads=[bankB[bk]], writes=[B_KT[16 + pt]])
                    P.any(lambda e, pt=pt: e.tensor_copy(out=V[:, 16 + pt, 0:128], in_=cvst[:, pt, :]), reads=[B_ck], writes=[B_V[16 + pt]])

            if 'noA' in DD0:
                return
            for T in range(NT):
                zb = 6 + (T % 2)
                project(wb, T, zb, 0, 512)
                z = banks[zb]
                sq, sqb = r256.next()
                P.act(lambda e, z=z, sq=sq: e.activation(out=sq[:], in_=z[:, 0:256], func=AF.Square), reads=[bankB[zb]], writes=[sqb])
                ss, ssb = rs.next()
                P.dve(lambda e, sq=sq, ss=ss: e.tensor_reduce(out=ss[:, 0:4], in_=sq[:].rearrange("p (g d) -> p g d", g=4, d=64), axis=AX.X, op=ALU.add),
                      reads=[sqb], writes=[ssb])
                rstd, rb_ = rstd_from_ss(ss[:, 0:4], ssb, 4, 1.0 / 64, EPS)
                nq, nqb = r256.next()
                P.dve(lambda e, z=z, nq=nq, rstd=rstd: e.tensor_tensor(out=nq[:].rearrange("p (g d) -> p g d", g=4, d=64),
                                                                      in0=z[:, 0:256].rearrange("p (g d) -> p g d", g=4, d=64),
                                                                      in1=rstd[:, 0:4].unsqueeze(2).to_broadcast([128, 4, 64]), op=ALU.mult),
                      reads=[bankB[zb], rb_], writes=[nqb])
                P.any(lambda e, nq=nq: e.tensor_tensor(out=nq[:], in0=nq[:], in1=g4[:], op=ALU.mult), reads=[nqb, B_misc], writes=[nqb])
                if 'nonk' not in DD0:
                    P.dma(nk_out[l][T * 128:(T + 1) * 128, h * 128:(h + 1) * 128], nq[:, 128:256], reads=[nqb])
                rt, rtb = rb256.next()
                import os as _os
                rope(nq[:], [nqb], rt[:], [rtb], T, 4, "any", "any")
                if 'noT' not in DD0:
                    tb = 5
                    for g in range(4):
                        P.pe(lambda e, g=g, rt=rt, tb=tb: e.transpose(banksb[tb][0:64, g * 128:(g + 1) * 128], rt[:, g * 64:(g + 1) * 64], identb[:]),
                             reads=[rtb, B_cst], writes=[bankB[tb]])
                    if 'noTq' not in DD0:
                      P.act(lambda e, tb=tb, T=T: e.copy(out=QT[0:64, :, tslice(T)], in_=banksb[tb][0:64, 0:256].rearrange("p (c t) -> p c t", c=2, t=128)),
                          reads=[bankB[tb]], writes=[B_QT[T]])
                    if 'noTk' not in DD0:
                      P.act(lambda e, tb=tb, T=T: e.copy(out=KT[0:64, :, tslice(T)], in_=banksb[tb][0:64, 256:512].rearrange("p (c t) -> p c t", c=2, t=128)),
                          reads=[bankB[tb]], writes=[B_KT[T]])
                vst, vstb = r128.next()
                P.dve(lambda e, z=z, vst=vst: e.tensor_copy(out=vst[:], in_=z[:, 256:384]), reads=[bankB[zb]], writes=[vstb])
                if 'nonk' not in DD0:
                    P.dma(nv_out[l][T * 128:(T + 1) * 128, h * 128:(h + 1) * 128], vst[:], reads=[vstb])
                P.any(lambda e, vst=vst, T=T: e.tensor_copy(out=V[:, T, 0:128], in_=vst[:]), reads=[vstb], writes=[B_V[T]])
                th, thb = r128.next()
                P.act(lambda e, z=z, th=th: e.activation(out=th[:], in_=z[:, 384:512], func=AF.Tanh, scale=0.5), reads=[bankB[zb]], writes=[thb])
                P.dve(lambda e, z=z, th=th, T=T: e.scalar_tensor_tensor(out=sg[:, T, :], in0=th[:], scalar=1.0, in1=z[:, 384:512], op0=ALU.add, op1=ALU.mult),
                      reads=[thb, bankB[zb]], writes=[B_sg[T]])
            import os as _os
            DD = _os.environ.get('DIFFDBG', '')
            if 'noB' in DD:
                return
            KR = 64 if 'k64' in DD else 72
            lam_init = 0.8 - 0.6 * math.exp(-0.3 * l)
            c0 = 0.5 * (1.0 - lam_init)
            OB = [2, 3, 4]

            def acc(c, qi):
                a = c * 4 + qi
                return OB[a // 3], (a % 3) * 160

            for qb in range(4):
                for k in OB:
                    P.pe(lambda e, k=k: e.matmul(banks[k][:, 0:512], lhsT=zerosb[:, 0:128], rhs=zerosb[:, 0:512], start=True, stop=False, skip_group_check=True),
                         reads=[B_cst], writes=[bankB[k]])
                steps = [(c, kt) for c in range(2) for kt in range(NKT)]
                pts = {}

                def emit_st(i):
                    c, kt = steps[i]
                    sbk = i % 2
                    P.pe(lambda e, c=c, kt=kt, sbk=sbk: e.matmul(banks[sbk][:, 0:512], lhsT=KT[0:KR, c, kt * 128:(kt + 1) * 128],
                                                                 rhs=QT[0:KR, c, qb * 512:(qb + 1) * 512], start=True, stop=True),
                         reads=[B_KT[kt], B_qm] + B_QT[qb * 4:qb * 4 + 4], writes=[bankB[sbk]])
                    pt_, ptb = rpt.next()
                    P.act(lambda e, sbk=sbk, pt_=pt_: e.activation(out=pt_[:], in_=banks[sbk][:, 0:512], func=AF.Exp, scale=0.125),
                          reads=[bankB[sbk]], writes=[ptb])
                    pts[i] = (pt_, ptb)

                def emit_pv(i):
                    c, kt = steps[i]
                    pt_, ptb = pts.pop(i)
                    for qi in range(4):
                        bk, off = acc(c, qi)
                        P.pe(lambda e, qi=qi, bk=bk, off=off, pt_=pt_, kt=kt: e.matmul(banks[bk][:, off:off + 129], lhsT=pt_[:, qi * 128:(qi + 1) * 128],
                                                                                       rhs=V[:, kt, 0:129], start=False, stop=(kt == NKT - 1), skip_group_check=True),
                             reads=[ptb, B_V[kt]], writes=[bankB[bk]])

                emit_st(0)
                for i in range(len(steps)):
                    if i + 1 < len(steps):
                        emit_st(i + 1)
                    emit_pv(i)
                for qi in range(4):
                    T = qb * 4 + qi
                    b0, o0 = acc(0, qi)
                    b1, o1 = acc(1, qi)
                    r01, r01b = rs.next()
                    P.dve(lambda e, r01=r01: e.reciprocal(out=r01[:, 0:1], in_=banks[b0][:, o0 + 128:o0 + 129]), reads=[bankB[b0]], writes=[r01b])
                    P.dve(lambda e, r01=r01: e.reciprocal(out=r01[:, 1:2], in_=banks[b1][:, o1 + 128:o1 + 129]), reads=[bankB[b1]], writes=[r01b])
                    P.dve(lambda e, r01=r01: e.tensor_tensor(out=r01[:, 1:2], in0=r01[:, 1:2], in1=small[:, 0:1], op=ALU.mult), reads=[r01b, B_misc], writes=[r01b])
                    d, db = r128.next()
                    P.dve(lambda e, d=d, r01=r01: e.tensor_scalar(out=d[:], in0=banks[b0][:, o0:o0 + 128], scalar1=r01[:, 0:1], scalar2=None, op0=ALU.mult),
                          reads=[bankB[b0], r01b], writes=[db])
                    P.dve(lambda e, d=d, r01=r01: e.scalar_tensor_tensor(out=d[:], in0=banks[b1][:, o1:o1 + 128], scalar=r01[:, 1:2], in1=d[:],
                                                                        op0=ALU.mult, op1=ALU.add),
                          reads=[bankB[b1], r01b, db], writes=[db])
                    jk, jkb = r128.next()
                    ss, ssb = rs.next()
                    P.act(lambda e, d=d, jk=jk, ss=ss: e.activation(out=jk[:], in_=d[:], func=AF.Square, accum_out=ss[:, 0:1]), reads=[db], writes=[jkb, ssb])
                    rstd, rb_ = rstd_from_ss(ss[:, 0:1], ssb, 1, 1.0 / (128 * c0 * c0), EPS / (c0 * c0))
                    mt, mtb = rb128.next()
                    P.dve(lambda e, d=d, rstd=rstd, mt=mt, T=T: e.scalar_tensor_tensor(out=mt[:], in0=d[:], scalar=rstd[:, 0:1], in1=sg[:, T, :],
                                                                                      op0=ALU.mult, op1=ALU.mult),
                          reads=[db, rb_, B_sg[T]], writes=[mtb])
                    mixed_out(mt[:], mtb, chunk, T, 5, eng="dve")

        def ret_unit(l, p, RB):
            u = p
            chunk = UNIT_CHUNK[u]
            if p == 0:
                P.barrier()
            base = p * 28672
            qT = arena_view(base + 0, [128, TOK], BF16)
            kT = arena_view(base + 4096, [128, TOK], BF16)
            ktok = arena_view(base + 8192, [128, NT, 128], BF16)
            v = arena_view(base + 12288, [128, NT, 128], BF16)
            sg = arena_view(base + 16384, [128, NT, 128], BF16)
            Sbf = arena_view(base + 20480, [128, 2, NT, 128], BF16)
            if p == 0:
                dtm_, qd_, kd_, stS_, lgrow_ = dtm, qd, kd, stS, lgrow
            else:
                dtm_ = arena_view(57344, [128, 2, 128], F32)
                qd_ = arena_view(58368, [128, 2, 128], F32)
                kd_ = arena_view(59392, [128, 2, 128], F32)
                stS_ = arena_view(60416, [128, 2, 128], F32)
                lgrow_ = lgrow1
            B_pair_, B_stS_ = RB[p]
            B_q = [Buf() for _ in range(NT)]
            B_k = [Buf() for _ in range(NT)]
            B_kt = [Buf() for _ in range(NT)]
            B_v = [Buf() for _ in range(NT)]
            B_sg = [Buf() for _ in range(NT)]
            B_S = [[Buf() for _ in range(NT)] for _ in range(2)]
            wb = load_unit_weights(l, u)
            for d_ in range(2):
                for hh in range(2):
                    col = d_ * 4 + 2 * p + hh
                    P.dve(lambda e, d_=d_, hh=hh, col=col: e.tensor_copy(out=lgrow_[hh * 64:(hh + 1) * 64, d_:d_ + 1], in_=lg[hh * 64:(hh + 1) * 64, col:col + 1]),
                          reads=[B_misc], writes=[B_pair_])
            for hh in range(2):
                e12t, e12b = r256.next()
                cf = 2 * p + hh
                cb_ = 4 + 2 * p + hh
                P.act(lambda e, cf=cf: e.activation(out=e12t[:, 0:128], in_=M1, func=AF.Exp, scale=lg[:, cf:cf + 1]), reads=[e12b, B_cst, B_misc], writes=[e12b, B_pair_])
                P.act(lambda e, cb_=cb_: e.activation(out=e12t[:, 128:256], in_=M2, func=AF.Exp, scale=lg[:, cb_:cb_ + 1]), reads=[e12b, B_cst, B_misc], writes=[e12b, B_pair_])
                P.dve(lambda e: e.tensor_tensor(out=e12t[:, 0:128], in0=e12t[:, 0:128], in1=L1, op=ALU.mult), reads=[e12b, B_pair_, B_cst], writes=[e12b, B_pair_])
                P.dve(lambda e: e.tensor_tensor(out=e12t[:, 128:256], in0=e12t[:, 128:256], in1=L2, op=ALU.mult), reads=[e12b, B_pair_, B_cst], writes=[e12b, B_pair_])
                P.dve(lambda e, hh=hh: e.tensor_tensor(out=dtm_[:, hh, :], in0=e12t[:, 0:128], in1=e12t[:, 128:256], op=ALU.add), reads=[e12b, B_pair_], writes=[e12b, B_pair_])
                P.act(lambda e, hh=hh, cf=cf: e.activation(out=kd_[:, 0, hh * 64:(hh + 1) * 64], in_=COLA[:, 0:64], func=AF.Exp, scale=lg[:, cf:cf + 1]),
                      reads=[B_cst, B_misc], writes=[B_pair_])
                P.act(lambda e, hh=hh, cb_=cb_: e.activation(out=kd_[:, 1, hh * 64:(hh + 1) * 64], in_=COLB[:, 0:64], func=AF.Exp, scale=lg[:, cb_:cb_ + 1]),
                      reads=[B_cst, B_misc], writes=[B_pair_])
            P.dve(lambda e: e.tensor_scalar(out=kd_[:], in0=kd_[:], scalar1=0.125, scalar2=None, op0=ALU.mult), reads=[B_pair_], writes=[B_pair_])
            P.act(lambda e: e.activation(out=qd_[:, 0, :], in_=IOTA1, func=AF.Exp, scale=lgrow_[:, 0:1]), reads=[B_cst, B_pair_], writes=[B_pair_])
            P.act(lambda e: e.activation(out=qd_[:, 1, :], in_=IOTA2, func=AF.Exp, scale=lgrow_[:, 1:2]), reads=[B_cst, B_pair_], writes=[B_pair_])
            P.act(lambda e: e.activation(out=lgrow_[:, 2:4], in_=lgrow_[:, 0:2], func=AF.Exp, scale=128.0), reads=[B_pair_], writes=[B_pair_])
            for T in range(NT):
                zb = T % 4
                project(wb, T, zb, 0, 512)
                z = banks[zb]
                rq, rqb = rb128.next()
                t1, b1 = r256.next()
                t2, b2 = r256.next()
                cv_ = ropec[:, T, :]
                sv_ = ropes[:, T, :].rearrange("p (h j i) -> p h j i", h=2, j=2, i=16)
                src3 = z[:, 0:256].rearrange("p (g d) -> p g d", g=4, d=64)
                src5 = z[:, 0:256].rearrange("p (g h j i) -> p g h j i", g=4, h=2, j=2, i=16)
                t13 = t1[:].rearrange("p (g d) -> p g d", g=4, d=64)
                t25 = t2[:].rearrange("p (g h j i) -> p g h j i", g=4, h=2, j=2, i=16)
                P.dve(lambda e, t13=t13, src3=src3, cv_=cv_: e.tensor_tensor(out=t13, in0=src3, in1=cv_.unsqueeze(1).to_broadcast([128, 4, 64]), op=ALU.mult),
                      reads=[bankB[zb], B_cst], writes=[b1])
                P.dve(lambda e, t25=t25, src5=src5, sv_=sv_: e.tensor_tensor(out=t25[:, :, :, 0, :], in0=src5[:, :, :, 1, :],
                                                                             in1=sv_[:, :, 0, :].unsqueeze(1).to_broadcast([128, 4, 2, 16]), op=ALU.mult),
                      reads=[bankB[zb], B_cst], writes=[b2])
                P.dve(lambda e, t25=t25, src5=src5, sv_=sv_: e.tensor_tensor(out=t25[:, :, :, 1, :], in0=src5[:, :, :, 0, :],
                                                                             in1=sv_[:, :, 1, :].unsqueeze(1).to_broadcast([128, 4, 2, 16]), op=ALU.mult),
                      reads=[bankB[zb], B_cst], writes=[b2])
                P.any(lambda e, t1=t1, t2=t2, rq=rq: e.tensor_tensor(out=rq[:], in0=t1[:, 0:128], in1=t2[:, 0:128], op=ALU.add), reads=[b1, b2], writes=[rqb])
                P.any(lambda e, t1=t1, t2=t2, T=T: e.tensor_tensor(out=ktok[:, T, :], in0=t1[:, 128:256], in1=t2[:, 128:256], op=ALU.add),
                       reads=[b1, b2], writes=[B_kt[T]])
                tb = 4 + (T % 2)
                P.pe(lambda e, rq=rq, tb=tb: e.transpose(banksb[tb][:, 0:128], rq[:], identb[:]), reads=[rqb, B_cst], writes=[bankB[tb]])
                P.pe(lambda e, T=T, tb=tb: e.transpose(banksb[tb][:, 128:256], ktok[:, T, :], identb[:]), reads=[B_kt[T], B_cst], writes=[bankB[tb]])
                P.act(lambda e, T=T, tb=tb: e.copy(out=qT[:, tslice(T)], in_=banksb[tb][:, 0:128]), reads=[bankB[tb]], writes=[B_q[T]])
                P.act(lambda e, T=T, tb=tb: e.activation(out=kT[:, tslice(T)], in_=banksb[tb][:, 128:256], func=AF.Copy, scale=0.125),
                      reads=[bankB[tb]], writes=[B_k[T]])
                P.act(lambda e, z=z, T=T: e.copy(out=v[:, T, :], in_=z[:, 256:384]), reads=[bankB[zb]], writes=[B_v[T]])
                th, thb = r128.next()
                P.act(lambda e, z=z, th=th: e.activation(out=th[:], in_=z[:, 384:512], func=AF.Tanh, scale=0.5), reads=[bankB[zb]], writes=[thb])
                P.dve(lambda e, z=z, th=th, T=T: e.scalar_tensor_tensor(out=sg[:, T, :], in0=th[:], scalar=1.0, in1=z[:, 384:512], op0=ALU.add, op1=ALU.mult),
                      reads=[thb, bankB[zb]], writes=[B_sg[T]])
            st_in = [srf_in, srb_in]
            st_out = [nsrf_out, nsrb_out]
            for d_ in range(2):
                P.pool(lambda e, d_=d_: e.memset(stS_[:, d_, :], 0.0), writes=[B_stS_[d_]])
                for hh in range(2):
                    P.dma(stS_[hh * 64:(hh + 1) * 64, d_, hh * 64:(hh + 1) * 64], st_in[d_][l, 2 * p + hh], writes=[B_stS_[d_]])
            for step in range(NT):
                for d_ in range(2):
                    T = step if d_ == 0 else NT - 1 - step
                    S = stS_[:, d_, :]
                    kcol = d_ * 16 + T
                    P.dve(lambda e, S=S, kcol=kcol: e.tensor_scalar(out=S, in0=S, scalar1=keep[:, kcol:kcol + 1], scalar2=None, op0=ALU.mult),
                          reads=[B_stS_[d_], B_cst], writes=[B_stS_[d_]])
                    P.act(lambda e, S=S, d_=d_, T=T: e.copy(out=Sbf[:, d_, T, :], in_=S), reads=[B_stS_[d_]], writes=[B_S[d_][T]])
                    kt_, ktb_ = rb128.next()
                    P.any(lambda e, kt_=kt_, T=T, d_=d_: e.tensor_tensor(out=kt_[:], in0=ktok[:, T, :], in1=kd_[:, d_, :], op=ALU.mult),
                           reads=[B_kt[T], B_pair_], writes=[ktb_])
                    ub = d_
                    P.pe(lambda e, kt_=kt_, T=T, ub=ub: e.matmul(banks[ub][:, 0:128], lhsT=kt_[:], rhs=v[:, T, :], start=True, stop=True),
                         reads=[ktb_, B_v[T]], writes=[bankB[ub]])
                    tmp, tmpb = r128.next()
                    P.dve(lambda e, tmp=tmp, ub=ub: e.tensor_tensor(out=tmp[:], in0=banks[ub][:, 0:128], in1=BM, op=ALU.mult),
                          reads=[bankB[ub], B_cst], writes=[tmpb])
                    P.dve(lambda e, S=S, tmp=tmp, d_=d_: e.scalar_tensor_tensor(out=S, in0=S, scalar=lgrow_[:, 2 + d_:3 + d_], in1=tmp[:], op0=ALU.mult, op1=ALU.add),
                          reads=[B_stS_[d_], B_pair_, tmpb], writes=[B_stS_[d_]])
                    is_out = (T % 2 == 1) if d_ == 0 else (T % 2 == 0)
                    if is_out:
                        so, sob = r128.next()
                        P.act(lambda e, so=so, S=S: e.copy(out=so[:], in_=S), reads=[B_stS_[d_]], writes=[sob])
                        for hh in range(2):
                            P.dma(st_out[d_][l, T // 2, 2 * p + hh], so[hh * 64:(hh + 1) * 64, hh * 64:(hh + 1) * 64], reads=[sob])
            for T in range(NT):
                qf, qfb = rb128.next()
                qb_, qbb = rb128.next()
                P.dve(lambda e, qf=qf, T=T: e.tensor_tensor(out=qf[:], in0=qT[:, tslice(T)], in1=qd_[:, 0, :], op=ALU.mult), reads=[B_q[T], B_pair_], writes=[qfb])
                P.any(lambda e, qb_=qb_, T=T: e.tensor_tensor(out=qb_[:], in0=qT[:, tslice(T)], in1=qd_[:, 1, :], op=ALU.mult), reads=[B_q[T], B_pair_], writes=[qbb])
                ob = 6 + (T % 2)
                ab = 2 + (T % 2)
                P.pe(lambda e, qf=qf, T=T, ob=ob: e.matmul(banks[ob][:, 0:128], lhsT=qf[:], rhs=Sbf[:, 0, T, :], start=True, stop=False, skip_group_check=True),
                     reads=[qfb, B_S[0][T]], writes=[bankB[ob]])
                P.pe(lambda e, qb_=qb_, T=T, ob=ob: e.matmul(banks[ob][:, 0:128], lhsT=qb_[:], rhs=Sbf[:, 1, T, :], start=False, stop=False, skip_group_check=True),
                     reads=[qbb, B_S[1][T]], writes=[bankB[ob]])
                am, amb = rb256.next()
                for hh in range(2):
                    P.pe(lambda e, hh=hh, T=T: e.matmul(banks[2 + hh][:, 0:128], lhsT=kT[hh * 64:(hh + 1) * 64, tslice(T)],
                                                        rhs=qT[hh * 64:(hh + 1) * 64, tslice(T)], start=True, stop=True),
                         reads=[B_k[T], B_q[T]], writes=[bankB[2 + hh]])
                    P.dve(lambda e, am=am, hh=hh: e.tensor_tensor(out=am[:, hh * 128:(hh + 1) * 128], in0=banks[2 + hh][:, 0:128], in1=dtm_[:, hh, :], op=ALU.mult),
                          reads=[bankB[2 + hh], B_pair_], writes=[amb])
                for hh in range(2):
                    P.pe(lambda e, hh=hh, am=am, T=T, ob=ob: e.matmul(banks[ob][:, hh * 64:(hh + 1) * 64], lhsT=am[:, hh * 128:(hh + 1) * 128],
                                                                      rhs=v[:, T, hh * 64:(hh + 1) * 64], start=False, stop=(hh == 1), skip_group_check=True),
                         reads=[amb, B_v[T]], writes=[bankB[ob]])
                finish_pair(banks[ob][:, 0:128], [bankB[ob]], sg, B_sg, chunk, T, 0.5, 4 + (T % 2))

        def finish_pair(o_ap, obufs, sg, B_sg, chunk, T, c0, tbank):
            ss, ssb = rs.next()
            jk, jkb = r128.next()
            for hh in range(2):
                P.act(lambda e, hh=hh: e.activation(out=jk[:, hh * 64:(hh + 1) * 64], in_=o_ap[:, hh * 64:(hh + 1) * 64], func=AF.Square,
                                                    accum_out=ss[:, hh:hh + 1]),
                      reads=obufs, writes=[jkb, ssb])
            rstd, rb_ = rstd_from_ss(ss[:, 0:2], ssb, 2, 1.0 / (64 * c0 * c0), EPS / (c0 * c0))
            mt, mtb = rb128.next()
            for hh in range(2):
                P.dve(lambda e, hh=hh: e.scalar_tensor_tensor(out=mt[:, hh * 64:(hh + 1) * 64], in0=o_ap[:, hh * 64:(hh + 1) * 64], scalar=rstd[:, hh:hh + 1],
                                                              in1=sg[:, T, hh * 64:(hh + 1) * 64], op0=ALU.mult, op1=ALU.mult),
                      reads=list(obufs) + [rb_, B_sg[T]], writes=[mtb])
            mixed_out(mt[:], mtb, chunk, T, tbank)

        def hgrn_unit(l, p):
            u = 2 + p
            chunk = UNIT_CHUNK[u]
            P.barrier()
            q = arena_view(0, [128, NT, 128], BF16)
            kk = arena_view(4096, [128, NT, 256], BF16)
            lf = arena_view(12288, [128, NT, 256], F32)
            v = arena_view(28672, [128, NT, 128], BF16)
            sg = arena_view(32768, [128, NT, 128], BF16)
            oacc = arena_view(36864, [128, NT, 128], F32)
            vmall = arena_view(45056, [128, NT, 512], BF16)
            B_vm = [Buf() for _ in range(NT)]
            B_q = [Buf() for _ in range(NT)]
            B_kk = [Buf() for _ in range(NT)]
            B_lf = [Buf() for _ in range(NT)]
            B_v = [Buf() for _ in range(NT)]
            B_sg = [Buf() for _ in range(NT)]
            B_oa = [Buf() for _ in range(NT)]
            wb = load_unit_weights(l, u)
            for half in range(2):
                P.dve(lambda e, half=half: e.tensor_copy(out=lb2[:, half * 128:(half + 1) * 128], in_=lball[:, l, p * 128:(p + 1) * 128]),
                      reads=[B_cst], writes=[B_pair])
            P.dve(lambda e: e.tensor_scalar(out=omlb2[:], in0=lb2[:], scalar1=-1.0, scalar2=1.0, op0=ALU.mult, op1=ALU.add), reads=[B_pair], writes=[B_pair])
            for T in range(NT):
                zb = 2 * (T % 2)
                project(wb, T, zb, 0, 512)
                project(wb, T, zb + 1, 512, 640)
                z = banks[zb]
                z2 = banks[zb + 1]
                u_, ub_ = r512.next()
                P.act(lambda e, z=z, u_=u_: e.activation(out=u_[:, 0:384], in_=z[:, 0:384], func=AF.Exp, scale=-1.0), reads=[bankB[zb]], writes=[ub_])
                P.act(lambda e, u_=u_: e.activation(out=u_[:, 0:384], in_=u_[:, 0:384], func=AF.Ln, bias=1.0), reads=[ub_], writes=[ub_])
                P.act(lambda e, u_=u_: e.activation(out=u_[:, 0:384], in_=u_[:, 0:384], func=AF.Exp, scale=-1.0), reads=[ub_], writes=[ub_])
                P.dve(lambda e, u_=u_, z=z, T=T: e.tensor_tensor(out=sg[:, T, :], in0=u_[:, 256:384], in1=z[:, 256:384], op=ALU.mult),
                      reads=[ub_, bankB[zb]], writes=[B_sg[T]])
                f_, fb_ = r256.next()
                P.any(lambda e, u_=u_, f_=f_: e.tensor_tensor(out=f_[:], in0=u_[:, 0:256], in1=omlb2[:], op=ALU.mult), reads=[ub_, B_pair], writes=[fb_])
                P.any(lambda e, f_=f_: e.tensor_tensor(out=f_[:], in0=f_[:], in1=lb2[:], op=ALU.add), reads=[fb_, B_pair], writes=[fb_])
                P.act(lambda e, f_=f_, T=T: e.activation(out=lf[:, T, :], in_=f_[:], func=AF.Ln), reads=[fb_], writes=[B_lf[T]])
                P.act(lambda e, f_=f_, T=T: e.activation(out=kk[:, T, :], in_=f_[:], func=AF.Identity, scale=-1.0, bias=1.0),
                      reads=[fb_], writes=[B_kk[T]])
                P.act(lambda e, z=z, T=T: e.activation(out=q[:, T, :], in_=z[:, 384:512], func=AF.Copy, scale=0.125), reads=[bankB[zb]], writes=[B_q[T]])
                P.act(lambda e, z2=z2, T=T: e.copy(out=v[:, T, :], in_=z2[:, 0:128]), reads=[bankB[zb + 1]], writes=[B_v[T]])
            st_in = [shf_in, shb_in]
            st_out = [nshf_out, nshb_out]
            TRI = [TRIF, TRIB]
            for d_ in range(2):
                P.pool(lambda e, d_=d_: e.memset(stS[:, d_, :], 0.0), writes=[B_stS[d_]])
                for hh in range(2):
                    P.dma(stS[hh * 64:(hh + 1) * 64, d_, hh * 64:(hh + 1) * 64], st_in[d_][l, 2 * p + hh], writes=[B_stS[d_]])
            done_first = [False] * NT
            for step in range(NT):
                for d_ in range(2):
                    T = step if d_ == 0 else NT - 1 - step
                    S = stS[:, d_, :]
                    lfd = lf[:, T, d_ * 128:(d_ + 1) * 128]
                    kcol = d_ * 16 + T
                    P.dve(lambda e, S=S, kcol=kcol: e.tensor_scalar(out=S, in0=S, scalar1=keep[:, kcol:kcol + 1], scalar2=None, op0=ALU.mult),
                          reads=[B_stS[d_], B_cst], writes=[B_stS[d_]])
                    sbf, sbfb = rb128.next()
                    P.act(lambda e, S=S, sbf=sbf: e.copy(out=sbf[:], in_=S), reads=[B_stS[d_]], writes=[sbfb])
                    P.pe(lambda e, lfd=lfd, d_=d_: e.matmul(banks[0][:, 0:128], lhsT=TRI[d_], rhs=lfd, start=True, stop=True),
                         reads=[B_cst, B_lf[T]], writes=[bankB[0]])
                    P.pe(lambda e, lfd=lfd: e.matmul(banks[0][:, 128:132], lhsT=lfd, rhs=ind[:], start=True, stop=True),
                         reads=[B_cst, B_lf[T]], writes=[bankB[0]])
                    G_, Gb_ = rs.next()
                    P.act(lambda e, G_=G_: e.activation(out=G_[:, 0:4], in_=banks[0][:, 128:132], func=AF.Exp), reads=[bankB[0]], writes=[Gb_])
                    eq, eqb = r128.next()
                    ek, ekb = r128.next()
                    P.act(lambda e, eq=eq: e.activation(out=eq[:], in_=banks[0][:, 0:128], func=AF.Exp), reads=[bankB[0]], writes=[eqb])
                    P.act(lambda e, ek=ek: e.activation(out=ek[:], in_=banks[0][:, 0:128], func=AF.Exp, scale=-1.0), reads=[bankB[0]], writes=[ekb])
                    qt_, qtb = rb128.next()
                    kt_, ktb = rb128.next()
                    P.dve(lambda e, qt_=qt_, eq=eq, T=T: e.tensor_tensor(out=qt_[:], in0=q[:, T, :], in1=eq[:], op=ALU.mult), reads=[B_q[T], eqb], writes=[qtb])
                    P.any(lambda e, kt_=kt_, ek=ek, T=T, d_=d_: e.tensor_tensor(out=kt_[:], in0=kk[:, T, d_ * 128:(d_ + 1) * 128], in1=ek[:], op=ALU.mult),
                           reads=[B_kk[T], ekb], writes=[ktb])
                    P.pe(lambda e, qt_=qt_: e.transpose(banksb[1][:, 0:128], qt_[:], identb[:]), reads=[qtb, B_cst], writes=[bankB[1]])
                    P.pe(lambda e, kt_=kt_: e.transpose(banksb[1][:, 128:256], kt_[:], identb[:]), reads=[ktb, B_cst], writes=[bankB[1]])
                    qkT, qkTb = rb256.next()
                    P.act(lambda e, qkT=qkT: e.copy(out=qkT[:], in_=banksb[1][:, 0:256]), reads=[bankB[1]], writes=[qkTb])
                    vm, vmb = vmall[:, T, :], B_vm[T]
                    if not done_first[T]:
                        for j in range(4):
                            P.act(lambda e, j=j, vm=vm, T=T: e.activation(out=vm[:, j * 128:(j + 1) * 128], in_=v[:, T, :], func=AF.Identity,
                                                                          scale=ind[:, j:j + 1]),
                                  reads=[B_v[T], B_cst], writes=[vmb])
                    P.pe(lambda e, kt_=kt_, vm=vm: e.matmul(banks[2][:, 0:512], lhsT=kt_[:], rhs=vm, start=True, stop=True),
                         reads=[ktb, vmb], writes=[bankB[2]])
                    am, amb = rb256.next()
                    for hh in range(2):
                        abk = 3 + hh
                        P.pe(lambda e, hh=hh, qkT=qkT, abk=abk: e.matmul(banks[abk][:, 0:128], lhsT=qkT[hh * 64:(hh + 1) * 64, 128:256],
                                                                          rhs=qkT[hh * 64:(hh + 1) * 64, 0:128], start=True, stop=True),
                             reads=[qkTb], writes=[bankB[abk]])
                        P.dve(lambda e, am=am, d_=d_, hh=hh, abk=abk: e.tensor_tensor(out=am[:, hh * 128:(hh + 1) * 128], in0=banks[abk][:, 0:128], in1=TRI[d_], op=ALU.mult),
                              reads=[bankB[abk], B_cst], writes=[amb])
                    ob = 5 + d_
                    jorder = [0, 1, 2, 3] if d_ == 0 else [3, 2, 1, 0]
                    cur, curb = sbf, sbfb
                    for ji, j in enumerate(jorder):
                        P.pe(lambda e, j=j, qkT=qkT, cur=cur: e.matmul(banks[ob][32 * j:32 * j + 32, 0:128], lhsT=qkT[:, 32 * j:32 * j + 32], rhs=cur[:],
                                                                       start=True, stop=False, tile_position=(0, 32 * j), skip_group_check=True),
                             reads=[qkTb, curb], writes=[bankB[ob]])
                        tg, tgb = r128.next()
                        P.dve(lambda e, tg=tg, j=j, G_=G_: e.scalar_tensor_tensor(out=tg[:], in0=banks[2][:, j * 128:(j + 1) * 128], scalar=G_[:, j:j + 1], in1=BM,
                                                                                  op0=ALU.mult, op1=ALU.mult),
                              reads=[bankB[2], Gb_, B_cst], writes=[tgb])
                        P.dve(lambda e, S=S, tg=tg, j=j, G_=G_: e.scalar_tensor_tensor(out=S, in0=S, scalar=G_[:, j:j + 1], in1=tg[:], op0=ALU.mult, op1=ALU.add),
                              reads=[B_stS[d_], Gb_, tgb], writes=[B_stS[d_]])
                        if ji < 3:
                            cur, curb = rb128.next()
                            P.act(lambda e, S=S, cur=cur: e.copy(out=cur[:], in_=S), reads=[B_stS[d_]], writes=[curb])
                    for hh in range(2):
                        P.pe(lambda e, hh=hh, am=am, T=T: e.matmul(banks[ob][:, hh * 64:(hh + 1) * 64], lhsT=am[:, hh * 128:(hh + 1) * 128],
                                                                   rhs=v[:, T, hh * 64:(hh + 1) * 64], start=False, stop=(hh == 1), skip_group_check=True),
                             reads=[amb, B_v[T]], writes=[bankB[ob]])
                    is_out = (T % 2 == 1) if d_ == 0 else (T % 2 == 0)
                    if is_out:
                        so, sob = r128.next()
                        P.act(lambda e, so=so, S=S: e.copy(out=so[:], in_=S), reads=[B_stS[d_]], writes=[sob])
                        for hh in range(2):
                            P.dma(st_out[d_][l, T // 2, 2 * p + hh], so[hh * 64:(hh + 1) * 64, hh * 64:(hh + 1) * 64], reads=[sob])
                    if not done_first[T]:
                        done_first[T] = True
                        P.act(lambda e, T=T: e.copy(out=oacc[:, T, :], in_=banks[ob][:, 0:128]), reads=[bankB[ob]], writes=[B_oa[T]])
                    else:
                        ot, otb = r128.next()
                        P.dve(lambda e, ot=ot, T=T: e.tensor_tensor(out=ot[:], in0=banks[ob][:, 0:128], in1=oacc[:, T, :], op=ALU.add),
                              reads=[bankB[ob], B_oa[T]], writes=[otb])
                        finish_pair(ot[:], [otb], sg, B_sg, chunk, T, 1.0, 7)

        def out_phase(l, last):
            P.barrier()
            wo = arena_view(0, [128, 8, 1024], BF16)
            B_wo = Buf()
            for half in range(2):
                ws = wstage[:, 0:4096].rearrange("p (k n) -> p k n", k=8, n=512)
                P.dma(ws, wout_in[l][:, :, half * 512:(half + 1) * 512], writes=[B_wstage])
                for kc in range(8):
                    eng = "any"
                    P.on(eng, lambda e, kc=kc, half=half: e.tensor_copy(out=wo[:, kc, half * 512:(half + 1) * 512], in_=ws[:, kc, :]),
                         reads=[B_wstage], writes=[B_wo])
            src = x_in if l == 0 else xs_scr
            dst = y_out if last else xs_scr
            for T in range(NT):
                xt, xb_ = xring.next()
                P.dma(xt[:], src[T * 128:(T + 1) * 128, :], reads=([B_xs[T]] if l > 0 else []), writes=[xb_])
                for nb in range(2):
                    bk = 2 * (T % 2) + nb
                    for c in range(8):
                        P.pe(lambda e, c=c, nb=nb, bk=bk, T=T: e.matmul(banks[bk][:, 0:512], lhsT=mixT[:, c, tslice(T)], rhs=wo[:, c, nb * 512:(nb + 1) * 512],
                                                                        start=(c == 0), stop=(c == 7)),
                             reads=[B_mixT[c][T], B_wo], writes=[bankB[bk]])
                    tmp, tmpb = r512.next()
                    P.dve(lambda e, tmp=tmp, bk=bk, nb=nb: e.tensor_tensor(out=tmp[:], in0=banks[bk][:, 0:512], in1=gate_b[:, nb * 512:(nb + 1) * 512], op=ALU.mult),
                          reads=[bankB[bk], B_gate], writes=[tmpb])
                    P.any(lambda e, tmp=tmp, xt=xt, nb=nb: e.tensor_tensor(out=xt[:, nb * 512:(nb + 1) * 512], in0=tmp[:], in1=xt[:, nb * 512:(nb + 1) * 512], op=ALU.add),
                           reads=[tmpb, xb_], writes=[xb_])
                P.dma(dst[T * 128:(T + 1) * 128, :], xt[:], reads=[xb_], writes=([] if last else [B_xs[T]]))

        for l in range(L):
            setup_layer(l)
            norm_phase(l)
            RB = [(Buf(), [Buf(), Buf()]) for _ in range(2)]
            for p in range(2):
                if units_enabled is None or ("r%d" % p) in units_enabled:
                    ret_unit(l, p, RB)
            for p in range(2):
                if units_enabled is None or ("g%d" % p) in units_enabled:
                    hgrn_unit(l, p)
            DB = {"set": [([Buf() for _ in range(NT)], [Buf() for _ in range(NKT)], [Buf() for _ in range(NKT)], [Buf() for _ in range(NT)], Buf()) for _ in range(2)], "ck": Buf()}
            for h in range(4):
                if units_enabled is None or ("d%d" % h) in units_enabled:
                    diff_unit(l, h, DB)
            if dbg:
                P.barrier()
                P.dma(dbg_out[l], mixT[:], reads=[b for row in B_mixT for b in row])
            out_phase(l, last=(l == L - 1))

        with nc.Block() as block:
            run = P.build(sems, dsems, reorder=REORDER)
            block.sync(lambda e: run("sp", e))
            block.tensor(lambda e: run("pe", e))
            block.scalar(lambda e: run("act", e))
            block.vector(lambda e: run("dve", e))
            block.gpsimd(lambda e: run("pool", e))
    return nc


def _unit_perm():
    off = dict(rq=0, rk=256, rv=512, rg=768, dq=1024, dk=1536, dv=2048, dg=2560, hq=3072, hff=3328, hfb=3584, hi=3840, hg=4096)
    cols = []
    for p in range(2):
        for n in ("rq", "rk", "rv", "rg"):
            cols += list(range(off[n] + 128 * p, off[n] + 128 * p + 128))
    for p in range(2):
        for n in ("hff", "hfb", "hg", "hq", "hi"):
            cols += list(range(off[n] + 128 * p, off[n] + 128 * p + 128))
    for h in range(4):
        for n in ("dq", "dk", "dv", "dg"):
            cols += list(range(off[n] + 128 * h, off[n] + 128 * h + 128))
    return np.array(cols, dtype=np.int64)


def _constants():
    s = np.arange(128, dtype=np.float32)[:, None]
    t = np.arange(128, dtype=np.float32)[None, :]
    M1 = np.maximum(t - s, 0)
    L1 = (s <= t).astype(np.float32)
    M2 = np.maximum(s - t, 0)
    L2 = (s >= t).astype(np.float32)
    IOTA1 = np.broadcast_to(t + 1, (128, 128))
    IOTA2 = np.broadcast_to(128 - t, (128, 128))
    COLA = np.broadcast_to(127 - s, (128, 128))
    COLB = np.broadcast_to(s, (128, 128))
    same = (np.floor(s / 32) == np.floor(t / 32))
    TRIF = (same & (s <= t)).astype(np.float32)
    TRIB = (same & (s >= t)).astype(np.float32)
    BM = (np.floor(s / 64) == np.floor(t / 64)).astype(np.float32)
    cst = np.stack([M1, L1, M2, L2, IOTA1, IOTA2, COLA, COLB, TRIF, TRIB, BM], axis=1).astype(np.float32)
    ind = (np.floor(np.arange(128)[:, None] / 32) == np.arange(4)[None, :]).astype(np.float32)
    return np.ascontiguousarray(cst), np.ascontiguousarray(ind)


def _rope_tables(sample):
    ropec = np.ones((128, 16, 64), np.float32)
    ropes = np.zeros((128, 16, 64), np.float32)
    if sample:
        tt = np.arange(TOK)
        row = (tt // 64).astype(np.float32)
        col = (tt % 64).astype(np.float32)
        inv = (np.float32(10000.0) ** (-np.arange(16, dtype=np.float32) / np.float32(16))).astype(np.float32)
        ar = (row[:, None] * inv[None, :]).astype(np.float32)
        ac = (col[:, None] * inv[None, :]).astype(np.float32)
        c = np.concatenate([np.cos(ar), np.cos(ar), np.cos(ac), np.cos(ac)], axis=1).astype(np.float32)
        s_ = np.concatenate([-np.sin(ar), np.sin(ar), -np.sin(ac), np.sin(ac)], axis=1).astype(np.float32)
        ropec = np.ascontiguousarray(c.reshape(16, 128, 64).transpose(1, 0, 2))
        ropes = np.ascontiguousarray(s_.reshape(16, 128, 64).transpose(1, 0, 2))
    return ropec, ropes


_NC_CACHE = {}


def kernel(x_prompt, x_sample, c, c_ctx, cache_diff_k, cache_diff_v, state_ret_fwd, state_ret_bwd,
           state_hgrn_fwd, state_hgrn_bwd, norm_g, w_ada, b_ada, w_in, w_out, ret_decay_logit,
           diff_qn_g, diff_kn_g, diff_lambda, hgrn_lb_logit, _dbg=False, _units=None, _L=2):
    f32 = np.float32
    bf = ml_dtypes.bfloat16
    A = lambda a: np.ascontiguousarray(np.asarray(a, dtype=f32))
    x_prompt, x_sample, c, c_ctx = A(x_prompt), A(x_sample), A(c), A(c_ctx)
    perm = _unit_perm()
    w_in_p = A(w_in)[:, :, perm]
    win = np.ascontiguousarray(w_in_p.reshape(2, 8, 128, 4352).transpose(0, 2, 1, 3))
    wada = np.ascontiguousarray(A(w_ada).reshape(2, 8, 128, 3072).transpose(0, 2, 1, 3))
    wout = np.ascontiguousarray(A(w_out).reshape(2, 8, 128, 1024).transpose(0, 2, 1, 3))
    normg = np.ascontiguousarray(A(norm_g).reshape(2, 8, 128).transpose(0, 2, 1))
    cst, ind = _constants()
    shared = dict(
        normg=normg, wada=wada, bada=A(b_ada), win=win, wout=wout, rdl=A(ret_decay_logit).reshape(2, 8),
        qng=A(diff_qn_g), kng=A(diff_kn_g), dlam=A(diff_lambda).reshape(2, 256), hlb=A(hgrn_lb_logit).reshape(512),
        identb=np.eye(128, dtype=f32).astype(bf), identf=np.eye(128, dtype=f32), cst=cst, ind=ind,
    )
    ropec_s, ropes_s = _rope_tables(True)
    ropec_p, ropes_p = _rope_tables(False)
    z64 = np.zeros((2, 4, 64, 64), f32)
    zc = np.zeros((2, 512, 512), f32)
    in_maps = []
    for core in range(8):
        m = dict(shared)
        if core < 4:
            b = core
            m["x"] = x_sample[b]
            m["modv"] = np.ascontiguousarray(c[b].reshape(8, 128).T)
            m["ck"] = np.ascontiguousarray(A(cache_diff_k)[b].reshape(2, 512, 512))
            m["cv"] = np.ascontiguousarray(A(cache_diff_v)[b].reshape(2, 512, 512))
            m["srf"], m["srb"] = A(state_ret_fwd)[b], A(state_ret_bwd)[b]
            m["shf"], m["shb"] = A(state_hgrn_fwd)[b], A(state_hgrn_bwd)[b]
            m["ropec"], m["ropes"] = ropec_s, ropes_s
            m["qmask"] = np.zeros((8, 2048), f32).astype(bf)
            m["kmask"] = np.zeros((8, 2560), f32).astype(bf)
            m["keep"] = np.ones((128, 32), f32)
        else:
            j = core - 4
            m["x"] = np.ascontiguousarray(x_prompt[8 * j:8 * j + 8].reshape(2048, 1024))
            m["modv"] = np.ascontiguousarray(c_ctx.reshape(8, 128).T)
            m["ck"], m["cv"] = zc, zc
            m["srf"], m["srb"], m["shf"], m["shb"] = z64, z64, z64, z64
            m["ropec"], m["ropes"] = ropec_p, ropes_p
            seq = np.arange(2048) // 256
            qm = (seq[None, :] == np.arange(8)[:, None]).astype(f32)
            km = np.full((8, 2560), BIGNEG, f32)
            km[:, :2048] = np.where(seq[None, :] == np.arange(8)[:, None], 0.0, BIGNEG)
            m["qmask"] = qm.astype(bf)
            m["kmask"] = km.astype(bf)
            kf = np.array([0.0 if T % 2 == 0 else 1.0 for T in range(16)], f32)
            kb = np.array([0.0 if T % 2 == 1 else 1.0 for T in range(16)], f32)
            m["keep"] = np.ascontiguousarray(np.broadcast_to(np.concatenate([kf, kb])[None, :], (128, 32)))
        in_maps.append(m)

    key = (_L, _dbg, None if _units is None else tuple(sorted(_units)))
    if key not in _NC_CACHE:
        _NC_CACHE[key] = build_program(L=_L, dbg=_dbg, units_enabled=_units)
    nc = _NC_CACHE[key]
    res = run_bass_kernel_spmd(nc, in_maps, core_ids=list(range(8)))
    R = res.results

    y_sample = np.stack([R[b]["y"] for b in range(4)], axis=0)
    y_prompt = np.concatenate([R[4 + j]["y"].reshape(8, 256, 1024) for j in range(4)], axis=0)
    nk = np.concatenate([R[4 + j]["nk"].reshape(2, 8, 256, 4, 2, 64).transpose(1, 0, 2, 3, 4, 5) for j in range(4)], axis=0)
    nv = np.concatenate([R[4 + j]["nv"].reshape(2, 8, 256, 4, 128).transpose(1, 0, 2, 3, 4) for j in range(4)], axis=0)
    st = []
    for name in ("nsrf", "nsrb", "nshf", "nshb"):
        st.append(np.concatenate([R[4 + j][name].transpose(1, 0, 2, 3, 4) for j in range(4)], axis=0))
    outs = (y_prompt, y_sample, np.ascontiguousarray(nk), np.ascontiguousarray(nv), *[np.ascontiguousarray(s) for s in st])
    if _dbg:
        return outs, [R[i]["dbgmix"] for i in range(8)]
    return outs
```

```python
import math
import types
from contextlib import ExitStack

import numpy as np
import ml_dtypes

import concourse.bass as bass
import concourse.mybir as mybir
from concourse.bass_utils import run_bass_kernel_spmd

F32 = mybir.dt.float32
BF16 = mybir.dt.bfloat16
ALU = mybir.AluOpType
AF = mybir.ActivationFunctionType
AX = mybir.AxisListType

ENGS = ["pe", "act", "dve", "pool", "sp"]
NDSEM = 8
SAME_ENG_SYNC = True
import os as _os0
REORDER = _os0.environ.get('REORDER', '1') == '1'
PSUM_EXCL = _os0.environ.get('PSUM_EXCL', '1') == '1'
REORDER_ENGS = _os0.environ.get('REORDER_ENGS', 'pe,act,dve,pool,sp').split(',')

D_MODEL = 1024
NT = 16
TOK = 2048
NKT = 20
EPS = 1e-6
UNIT_W = [512, 512, 640, 640, 512, 512, 512, 512]
UNIT_OFF = [0, 512, 1024, 1664, 2304, 2816, 3328, 3840]
UNIT_CHUNK = [0, 1, 6, 7, 2, 3, 4, 5]
BIGNEG = -30000.0


class Buf:
    __slots__ = ("w", "r", "name", "excl")

    def __init__(self, name="", excl=False):
        self.w = None
        self.r = []
        self.name = name
        self.excl = excl


class Op:
    __slots__ = ("eng", "fn", "waits", "marked", "semval", "is_dma", "dsem", "dval", "cost", "lat", "idx", "prio",
                 "pos", "succs", "nrem", "ready", "fin", "is_bar", "per_eng", "per_dsem")


class _Probe:
    def __init__(self):
        self.rec = None

    def __getattr__(self, name):
        def f(*a, **k):
            self.rec = (name, a, k)
            return self
        return f


def _nfree(ap):
    n = 1
    for d in ap.shape[1:]:
        n *= int(d)
    return n


def _estimate(eng, fn, is_dma):
    pr = _Probe()
    try:
        fn(pr)
        name, a, k = pr.rec
    except Exception:
        name, a, k = "?", (), {}
    out = k.get("out", a[0] if a else None)
    try:
        if is_dma:
            nbytes = _nfree(out) * int(out.shape[0]) * mybir.dt.size(out.dtype)
            return 120.0, 2200.0 + nbytes / 120.0
        if eng == "pe":
            if name == "transpose":
                return 80.0, 80.0
            rhs = k.get("rhs", a[2] if len(a) > 2 else None)
            lhsT = k.get("lhsT", a[1] if len(a) > 1 else None)
            n = _nfree(rhs)
            c = (max(64, n) / 2.4 + 25.0) * 1.25
            if lhsT.dtype == F32:
                c *= 4.0
            return c, c
        n = _nfree(out)
        if eng == "act":
            c = 190.0 + n / 1.2 + (90.0 if k.get("accum_out") is not None else 0.0)
        elif eng == "dve":
            c = 130.0 + n / 0.7
        else:
            c = 700.0 + n / 0.4
        return c, c
    except Exception:
        return 300.0, 300.0


def _freeze(fn):
    if fn.__closure__ is None:
        return fn
    cells = []
    for c in fn.__closure__:
        try:
            cells.append(types.CellType(c.cell_contents))
        except ValueError:
            cells.append(c)
    return types.FunctionType(fn.__code__, fn.__globals__, fn.__name__, fn.__defaults__, tuple(cells))


LAT_X = 200.0
LAT_S = 50.0


class Prog:
    def __init__(self, nc):
        self.nc = nc
        self.all = []
        self.cur_bar = None
        self.since = []
        self.load = {e: 0.0 for e in ENGS}

    def _new(self, eng):
        op = Op()
        op.eng = eng
        op.fn = None
        op.marked = False
        op.semval = None
        op.is_dma = False
        op.dsem = None
        op.dval = None
        op.cost = 0.0
        op.lat = 0.0
        op.is_bar = False
        op.idx = len(self.all)
        op.waits = []
        self.all.append(op)
        return op

    def barrier(self):
        b = self._new("virt")
        b.is_bar = True
        b.waits = list(self.since)
        self.since = []
        self.cur_bar = b
        self.load = {e: 0.0 for e in ENGS}

    def emit(self, eng, fn, reads=(), writes=(), extra=(), is_dma=False):
        op = self._new(eng)
        op.fn = _freeze(fn)
        op.is_dma = is_dma
        op.cost, op.lat = _estimate(eng, op.fn, is_dma)
        self.load[eng] += op.cost
        waits = set()
        if PSUM_EXCL:
            for b in reads:
                if b.excl:
                    for r in b.r:
                        if r.eng != eng:
                            waits.add(r)
        for b in reads:
            if b.w is not None:
                waits.add(b.w)
        for b in writes:
            if b.w is not None:
                waits.add(b.w)
            for r in b.r:
                waits.add(r)
        for w in extra:
            if w is not None:
                waits.add(w)
        if self.cur_bar is not None:
            waits.add(self.cur_bar)
        waits.discard(op)
        op.waits = list(waits)
        for b in reads:
            b.r.append(op)
        for b in writes:
            b.w = op
            b.r = []
        self.since.append(op)
        return op

    def pe(self, fn, reads=(), writes=(), extra=()):
        return self.emit("pe", fn, reads, writes, extra)

    def act(self, fn, reads=(), writes=(), extra=()):
        return self.emit("act", fn, reads, writes, extra)

    def dve(self, fn, reads=(), writes=(), extra=()):
        return self.emit("dve", fn, reads, writes, extra)

    def pool(self, fn, reads=(), writes=(), extra=()):
        return self.emit("pool", fn, reads, writes, extra)

    def on(self, eng, fn, reads=(), writes=(), extra=()):
        if eng == "any":
            return self.any(fn, reads, writes, extra)
        return self.emit(eng, fn, reads, writes, extra)

    def any(self, fn, reads=(), writes=(), extra=()):
        f = _freeze(fn)
        best = None
        for e in ("dve", "pool"):
            c, _ = _estimate(e, f, False)
            tot = self.load[e] + c
            if best is None or tot < best[0]:
                best = (tot, e)
        return self.emit(best[1], fn, reads, writes, extra)

    def dma(self, out, in_, reads=(), writes=(), extra=()):
        return self.emit("sp", lambda e: e.dma_start(out=out, in_=in_), reads, writes, extra, is_dma=True)

    def schedule(self, reorder=True):
        import heapq
        ops = self.all
        for op in ops:
            op.succs = []
        for op in ops:
            for w in op.waits:
                w.succs.append(op)
        for op in reversed(ops):
            m = 0.0
            for s_ in op.succs:
                l_ = s_.prio + (0.0 if op.is_bar else (LAT_S if s_.eng == op.eng else LAT_X))
                if l_ > m:
                    m = l_
            op.prio = m + op.lat
        order = {e: [] for e in ENGS}
        if not reorder:
            for op in ops:
                if not op.is_bar:
                    order[op.eng].append(op)
            return order
        for op in ops:
            op.nrem = len(op.waits)
            op.ready = 0.0
            op.fin = None
        fixed = [e for e in ENGS if e not in REORDER_ENGS]
        lastop = {}
        for op in ops:
            if op.is_bar or op.eng not in fixed:
                continue
            p_ = lastop.get(op.eng)
            if p_ is not None and p_ not in op.waits:
                p_.succs.append(op)
                op.nrem += 1
            lastop[op.eng] = op
        future = {e: [] for e in ENGS}
        now = {e: [] for e in ENGS}
        free = {e: 0.0 for e in ENGS}

        def release(op):
            for s_ in op.succs:
                if op.is_bar:
                    t = op.fin
                elif s_.is_bar:
                    t = op.fin
                elif s_.eng == op.eng:
                    t = op.fin + (0.0 if op.eng == "pe" else LAT_S)
                else:
                    t = op.fin + LAT_X
                if t > s_.ready:
                    s_.ready = t
                s_.nrem -= 1
                if s_.nrem == 0:
                    if s_.is_bar:
                        s_.fin = s_.ready
                        release(s_)
                    else:
                        heapq.heappush(future[s_.eng], (s_.ready, s_.idx, s_))

        import sys
        sys.setrecursionlimit(100000)
        roots = [op for op in ops if op.nrem == 0]
        for op in roots:
            if op.is_bar:
                op.fin = 0.0
                release(op)
            else:
                heapq.heappush(future[op.eng], (0.0, op.idx, op))
        nleft = sum(1 for op in ops if not op.is_bar)
        while nleft > 0:
            best = None
            for e in ENGS:
                f = future[e]
                nw = now[e]
                while f and f[0][0] <= free[e]:
                    r_, i_, o_ = heapq.heappop(f)
                    heapq.heappush(nw, (-o_.prio, o_.idx, o_))
                if nw:
                    st = free[e]
                elif f:
                    st = f[0][0]
                else:
                    continue
                if best is None or st < best[0]:
                    best = (st, e)
            st, e = best
            if now[e]:
                _, _, op = heapq.heappop(now[e])
            else:
                _, _, op = heapq.heappop(future[e])
            if op.is_dma:
                free[e] = st + op.cost
                op.fin = st + op.lat
            else:
                free[e] = st + op.cost
                op.fin = st + op.cost
            order[e].append(op)
            nleft -= 1
            release(op)
        self.est_ns = max(free.values())
        return order

    def build(self, sems, dsems, reorder=True):
        order = self.schedule(reorder)
        if reorder:
            print('[sched] est_us=%.1f' % (self.est_ns / 1e3), {e: len(order[e]) for e in ENGS})
        for e in ENGS:
            for i, op in enumerate(order[e]):
                op.pos = i

        def skip_same(w_eng, eng):
            return w_eng == eng and (eng == "pe" or not SAME_ENG_SYNC)

        dcnt = [0] * NDSEM
        prev_on_sem = [None] * NDSEM
        dma_prev = {}
        nd = 0
        for op in order["sp"]:
            k = nd % NDSEM
            nd += 1
            op.dsem = k
            dcnt[k] += 16
            op.dval = dcnt[k]
            dma_prev[id(op)] = prev_on_sem[k]
            prev_on_sem[k] = op
        final_dvals = list(dcnt)
        for b in self.all:
            if b.is_bar:
                pe_ = {}
                pd_ = {}
                for w in b.waits:
                    if w.is_bar:
                        continue
                    if w.is_dma:
                        if pd_.get(w.dsem, 0) < w.dval:
                            pd_[w.dsem] = w.dval
                    else:
                        c = pe_.get(w.eng)
                        if c is None or c.pos < w.pos:
                            pe_[w.eng] = w
                b.per_eng = pe_
                b.per_dsem = pd_
        for op in self.all:
            if op.is_bar:
                for w in op.per_eng.values():
                    w.marked = True
                continue
            for w in op.waits:
                if w.is_bar or w.is_dma:
                    continue
                if not skip_same(w.eng, op.eng):
                    w.marked = True
        for e in ENGS:
            cnt = 0
            for op in order[e]:
                if not op.is_dma and op.marked:
                    cnt += 1
                    op.semval = cnt

        def run_engine(ename, eng):
            waited = {}

            def need(semkey, sem, val):
                if waited.get(semkey, 0) >= val:
                    return
                eng.wait_ge(sem, val)
                waited[semkey] = val

            for op in order[ename]:
                for w in op.waits:
                    if w.is_bar:
                        for we, wo in w.per_eng.items():
                            if not (we == ename and ename == "pe"):
                                need(("e", we), sems[we], wo.semval)
                        for k, v in w.per_dsem.items():
                            need(("d", k), dsems[k], v)
                    elif w.is_dma:
                        need(("d", w.dsem), dsems[w.dsem], w.dval)
                    elif not skip_same(w.eng, ename):
                        need(("e", w.eng), sems[w.eng], w.semval)
                if op.is_dma:
                    p = dma_prev[id(op)]
                    if p is not None:
                        need(("d", p.dsem), dsems[p.dsem], p.dval)
                ins = op.fn(eng)
                if op.is_dma:
                    ins.then_inc(dsems[op.dsem], 16)
                elif op.marked:
                    ins.then_inc(sems[ename], 1)
            if ename == "sp":
                for k in range(NDSEM):
                    if final_dvals[k] > 0:
                        need(("d", k), dsems[k], final_dvals[k])

        return run_engine


class Ring:
    def __init__(self, tiles):
        self.tiles = tiles
        self.bufs = [Buf() for _ in tiles]
        self.i = 0

    def next(self):
        k = self.i % len(self.tiles)
        self.i += 1
        return self.tiles[k], self.bufs[k]


def build_program(L=2, dbg=False, units_enabled=None):
    nc = bass.Bass("TRN2", target_bir_lowering=False)

    def din(name, shape, dt=F32):
        return nc.dram_tensor(name, list(shape), dt, kind="ExternalInput").ap()

    def dout(name, shape, dt=F32):
        return nc.dram_tensor(name, list(shape), dt, kind="ExternalOutput").ap()

    x_in = din("x", [TOK, D_MODEL])
    modv = din("modv", [128, 8])
    ck_in = din("ck", [2, 512, 512])
    cv_in = din("cv", [2, 512, 512])
    srf_in = din("srf", [2, 4, 64, 64])
    srb_in = din("srb", [2, 4, 64, 64])
    shf_in = din("shf", [2, 4, 64, 64])
    shb_in = din("shb", [2, 4, 64, 64])
    normg_in = din("normg", [2, 128, 8])
    wada_in = din("wada", [2, 128, 8, 3072])
    bada_in = din("bada", [2, 3072])
    win_in = din("win", [2, 128, 8, 4352])
    wout_in = din("wout", [2, 128, 8, 1024])
    rdl_in = din("rdl", [2, 8])
    qng_in = din("qng", [2, 64])
    kng_in = din("kng", [2, 64])
    dlam_in = din("dlam", [2, 256])
    hlb_in = din("hlb", [512])
    ropec_in = din("ropec", [128, 16, 64])
    ropes_in = din("ropes", [128, 16, 64])
    qmask_in = din("qmask", [8, 2048], BF16)
    kmask_in = din("kmask", [8, 2560], BF16)
    keep_in = din("keep", [128, 32])
    identb_in = din("identb", [128, 128], BF16)
    identf_in = din("identf", [128, 128])
    cst_in = din("cst", [128, 11, 128])
    ind_in = din("ind", [128, 4])

    y_out = dout("y", [TOK, D_MODEL])
    nk_out = dout("nk", [2, TOK, 512])
    nv_out = dout("nv", [2, TOK, 512])
    nsrf_out = dout("nsrf", [2, 8, 4, 64, 64])
    nsrb_out = dout("nsrb", [2, 8, 4, 64, 64])
    nshf_out = dout("nshf", [2, 8, 4, 64, 64])
    nshb_out = dout("nshb", [2, 8, 4, 64, 64])
    xs_scr = nc.dram_tensor("xs_scr", [TOK, D_MODEL], F32, kind="Internal").ap()
    dbg_out = dout("dbgmix", [2, 128, 8, TOK], BF16) if dbg else None

    es = ExitStack()
    with es:
        def sb(name, shape, dt):
            return es.enter_context(nc.sbuf_tensor("sb_" + name, list(shape), dt))

        hT = sb("hT", [128, 8, TOK], BF16)
        mixT = sb("mixT", [128, 8, TOK], BF16)
        wstage = sb("wstage", [128, 8 * 512], F32)
        wbf = sb("wbf", [128, 8 * 640], BF16)
        arena = sb("arena", [128, 61440], mybir.dt.uint8)
        cst = sb("cst", [128, 11, 128], F32)
        ind = sb("ind", [128, 4], F32)
        ropec = sb("ropec", [128, 16, 64], F32)
        ropes = sb("ropes", [128, 16, 64], F32)
        identb = sb("identb", [128, 128], BF16)
        identf = sb("identf", [128, 128], F32)
        keep = sb("keep", [128, 32], F32)
        gate_b = sb("gate_b", [128, 1024], F32)
        modt = sb("modt", [128, 8], F32)
        smod = sb("smod", [128, 8], F32)
        normg = sb("normg", [128, 8], F32)
        modT = sb("modT", [128, 2, 8], F32)
        modA = sb("modA", [128, 8], F32)
        small = sb("small", [128, 64], F32)
        rdl = sb("rdl", [128, 8], F32)
        lg = sb("lg", [128, 8], F32)
        g4 = sb("g4", [128, 256], F32)
        lball = sb("lball", [128, 2, 256], F32)
        lb2 = sb("lb2", [128, 256], F32)
        omlb2 = sb("omlb2", [128, 256], F32)
        cneg = sb("cneg", [128, 8], F32)
        zerosb = sb("zerosb", [128, 512], BF16)
        lgrow = sb("lgrow", [128, 4], F32)
        lgrow1 = sb("lgrow1", [128, 4], F32)
        dtm = sb("dtm", [128, 2, 128], F32)
        qd = sb("qd", [128, 2, 128], F32)
        kd = sb("kd", [128, 2, 128], F32)
        e12 = sb("e12", [128, 2, 128], F32)
        stS = sb("stS", [128, 2, 128], F32)

        n_f32_512 = 3
        r512 = Ring([sb("r512_%d" % i, [128, 512], F32) for i in range(n_f32_512)])
        r256 = Ring([sb("r256_%d" % i, [128, 256], F32) for i in range(8)])
        r128 = Ring([sb("r128_%d" % i, [128, 128], F32) for i in range(8)])
        rb256 = Ring([sb("rb256_%d" % i, [128, 256], BF16) for i in range(3)])
        rb128 = Ring([sb("rb128_%d" % i, [128, 128], BF16) for i in range(8)])
        rpt = Ring([sb("rpt_%d" % i, [128, 512], BF16) for i in range(3)])
        rs = Ring([sb("rs_%d" % i, [128, 8], F32) for i in range(16)])

        banks = [es.enter_context(nc.psum_tensor("bank%d" % i, [128, 512], F32)) for i in range(8)]
        banksb = [b.bitcast(BF16) for b in banks]
        bankB = [Buf("bank%d" % i, excl=True) for i in range(8)]

        sems = {e: es.enter_context(nc.semaphore("s_" + e)) for e in ENGS}
        dsems = [es.enter_context(nc.semaphore("d%d" % k)) for k in range(NDSEM)]

        P = Prog(nc)

        B_hT = [Buf() for _ in range(NT)]
        B_mixT = [[Buf() for _ in range(NT)] for _ in range(8)]
        B_wstage = Buf()
        B_wbf = Buf()
        B_cst = Buf()
        B_misc = Buf()
        B_gate = Buf()
        B_modAB = Buf()
        B_pair = Buf()
        B_stS = [Buf(), Buf()]

        def arena_view(off_bytes, shape, dt):
            n = 1
            for s in shape[1:]:
                n *= s
            esz = 2 if dt == BF16 else 4
            a = arena[:, off_bytes:off_bytes + n * esz].bitcast(dt)
            if len(shape) == 2:
                return a
            if len(shape) == 3:
                return a.rearrange("p (a b) -> p a b", a=shape[1], b=shape[2])
            if len(shape) == 4:
                return a.rearrange("p (a b c) -> p a b c", a=shape[1], b=shape[2], c=shape[3])
            raise ValueError

        xring = Ring([arena_view(36864 + i * 4096, [128, 1024], F32) for i in range(3)])
        xhring = Ring([arena_view(49152 + i * 2048, [128, 1024], BF16) for i in range(2)])
        B_xs = [Buf() for _ in range(NT)]

        M1, L1, M2, L2, IOTA1, IOTA2, COLA, COLB, TRIF, TRIB, BM = [cst[:, i, :] for i in range(11)]

        P.dma(cst[:], cst_in, writes=[B_cst])
        P.dma(ind[:], ind_in, writes=[B_cst])
        P.dma(ropec[:], ropec_in, writes=[B_cst])
        P.dma(ropes[:], ropes_in, writes=[B_cst])
        P.dma(identb[:], identb_in, writes=[B_cst])
        P.dma(identf[:], identf_in, writes=[B_cst])
        P.dma(keep[:], keep_in, writes=[B_cst])
        P.dma(modt[:], modv, writes=[B_cst])
        hlb, hlbb = r512.next()
        P.dma(hlb[:], hlb_in.partition_broadcast(128), writes=[hlbb])
        P.pool(lambda e: e.memset(cneg[:], -0.5), writes=[B_cst])
        P.pool(lambda e: e.memset(zerosb[:], 0.0), writes=[B_cst])
        t_, tb_ = rs.next()
        P.act(lambda e, t_=t_: e.activation(out=t_[:, 0:8], in_=modt[:], func=AF.Tanh, scale=0.5), reads=[B_cst], writes=[tb_])
        P.dve(lambda e, t_=t_: e.scalar_tensor_tensor(out=smod[:], in0=t_[:, 0:8], scalar=1.0, in1=modt[:], op0=ALU.add, op1=ALU.mult),
              reads=[tb_, B_cst], writes=[B_cst])
        P.dve(lambda e: e.tensor_scalar(out=smod[:], in0=smod[:], scalar1=0.5, scalar2=None, op0=ALU.mult), reads=[B_cst], writes=[B_cst])
        P.act(lambda e: e.activation(out=hlb[:], in_=hlb[:], func=AF.Exp), reads=[hlbb], writes=[hlbb])
        den_, denb_ = r256.next()
        P.dve(lambda e: e.tensor_tensor(out=den_[:], in0=hlb[:, 0:256], in1=hlb[:, 256:512], op=ALU.add), reads=[hlbb], writes=[denb_])
        P.dve(lambda e: e.reciprocal(out=den_[:], in_=den_[:]), reads=[denb_], writes=[denb_])
        P.dve(lambda e: e.tensor_tensor(out=hlb[:, 0:256], in0=hlb[:, 0:256], in1=den_[:], op=ALU.mult), reads=[hlbb, denb_], writes=[hlbb])
        P.dve(lambda e: e.tensor_tensor(out=hlb[:, 256:512], in0=hlb[:, 256:512], in1=den_[:], op=ALU.mult), reads=[hlbb, denb_], writes=[hlbb])
        P.dve(lambda e: e.tensor_tensor(out=lball[:, 0, :], in0=hlb[:, 0:256], in1=hlb[:, 0:256], op=ALU.subtract), reads=[hlbb], writes=[B_cst])
        P.dve(lambda e: e.tensor_tensor(out=lball[:, 1, :], in0=hlb[:, 0:256], in1=hlb[:, 256:512], op=ALU.add), reads=[hlbb], writes=[B_cst])
        P.dve(lambda e: e.tensor_tensor(out=lball[:, 1, :], in0=lball[:, 1, :], in1=hlb[:, 0:256], op=ALU.subtract), reads=[hlbb, B_cst], writes=[B_cst])

        def rstd_from_ss(ss_ap, ssb, n, mult, add):
            t1, b1 = rs.next()
            P.dve(lambda e: e.tensor_scalar(out=t1[:, 0:n], in0=ss_ap, scalar1=mult, scalar2=add, op0=ALU.mult, op1=ALU.add),
                  reads=[ssb], writes=[b1])
            t2, b2 = rs.next()
            P.pool(lambda e: e.tensor_tensor(out=t2[:, 0:n], in0=t1[:, 0:n], in1=cneg[:, 0:n], op=ALU.pow), reads=[b1, B_cst], writes=[b2])
            return t2, b2

        def rope(src, srcbufs, dst, dstbufs, T, G, eng_a, eng_b):
            W = G * 64
            t1, b1 = r256.next()
            t2, b2 = r256.next()
            cv_ = ropec[:, T, :]
            sv_ = ropes[:, T, :].rearrange("p (h j i) -> p h j i", h=2, j=2, i=16)
            src3 = src.rearrange("p (g d) -> p g d", g=G, d=64)
            src5 = src.rearrange("p (g h j i) -> p g h j i", g=G, h=2, j=2, i=16)
            t13 = t1[:, 0:W].rearrange("p (g d) -> p g d", g=G, d=64)
            t25 = t2[:, 0:W].rearrange("p (g h j i) -> p g h j i", g=G, h=2, j=2, i=16)
            P.on(eng_a, lambda e: e.tensor_tensor(out=t13, in0=src3, in1=cv_.unsqueeze(1).to_broadcast([128, G, 64]), op=ALU.mult),
                 reads=list(srcbufs) + [B_cst], writes=[b1])
            P.on(eng_b, lambda e: e.tensor_tensor(out=t25[:, :, :, 0, :], in0=src5[:, :, :, 1, :],
                                                  in1=sv_[:, :, 0, :].unsqueeze(1).to_broadcast([128, G, 2, 16]), op=ALU.mult),
                 reads=list(srcbufs) + [B_cst], writes=[b2])
            P.on(eng_b, lambda e: e.tensor_tensor(out=t25[:, :, :, 1, :], in0=src5[:, :, :, 0, :],
                                                  in1=sv_[:, :, 1, :].unsqueeze(1).to_broadcast([128, G, 2, 16]), op=ALU.mult),
                 reads=list(srcbufs) + [B_cst], writes=[b2])
            P.on(eng_a, lambda e: e.tensor_tensor(out=dst, in0=t1[:, 0:W], in1=t2[:, 0:W], op=ALU.add), reads=[b1, b2], writes=list(dstbufs))

        def tslice(T):
            return slice(T * 128, (T + 1) * 128)

        def setup_layer(l):
            stg = [wstage[:, 0:4096].rearrange("p (k n) -> p k n", k=8, n=512), arena_view(16384, [128, 8, 512], F32)]
            stgB = [B_wstage, Buf()]
            smb = arena_view(32768, [128, 8, 128], F32)
            B_smb = Buf()
            P.dve(lambda e: e.tensor_copy(out=smb, in_=smod[:].unsqueeze(2).to_broadcast([128, 8, 128])), reads=[B_cst], writes=[B_smb])
            P.dma(normg[:], normg_in[l], writes=[B_misc])
            P.dma(rdl[:], rdl_in[l].partition_broadcast(128), writes=[B_misc])
            dlam, dlamb = r256.next()
            P.dma(dlam[:], dlam_in[l].partition_broadcast(128), writes=[dlamb])
            P.dma(g4[:, 0:64], qng_in[l].partition_broadcast(128), writes=[B_misc])
            P.dma(g4[:, 64:128], qng_in[l].partition_broadcast(128), writes=[B_misc])
            P.dma(g4[:, 128:192], kng_in[l].partition_broadcast(128), writes=[B_misc])
            P.dma(g4[:, 192:256], kng_in[l].partition_broadcast(128), writes=[B_misc])
            for cb in range(6):
                st_, stb_ = stg[cb % 2], stgB[cb % 2]
                P.dma(st_, wada_in[l][:, :, cb * 512:(cb + 1) * 512], writes=[stb_])
                bt, btb = r512.next()
                P.dma(bt[:], bada_in[l][cb * 512:(cb + 1) * 512].partition_broadcast(128), writes=[btb])
                bk = cb % 4
                for kc in range(8):
                    P.pe(lambda e, kc=kc, st_=st_, bk=bk: e.matmul(banks[bk][:, 0:512], lhsT=smb[:, kc, :], rhs=st_[:, kc, :],
                                                                     start=(kc == 0), stop=(kc == 7)),
                         reads=[B_smb, stb_], writes=[bankB[bk]])
                if cb >= 4:
                    P.dve(lambda e, bk=bk, bt=bt, cb=cb: e.tensor_tensor(out=gate_b[:, (cb - 4) * 512:(cb - 3) * 512], in0=banks[bk][:, 0:512],
                                                                          in1=bt[:], op=ALU.add),
                          reads=[bankB[bk], btb], writes=[B_gate])
                else:
                    P.dve(lambda e, bk=bk, bt=bt: e.tensor_tensor(out=bt[:], in0=banks[bk][:, 0:512], in1=bt[:], op=ALU.add),
                          reads=[bankB[bk], btb], writes=[btb])
                    which = cb // 2
                    tb = 4 + (cb % 2)
                    for jj in range(4):
                        kc = (cb % 2) * 4 + jj
                        P.pe(lambda e, jj=jj, bt=bt, tb=tb: e.transpose(banks[tb][:, jj * 128:(jj + 1) * 128], bt[:, jj * 128:(jj + 1) * 128], identf[:]),
                             reads=[btb, B_cst], writes=[bankB[tb]])
                        P.act(lambda e, jj=jj, tb=tb, which=which, kc=kc: e.copy(out=modT[:, which, kc:kc + 1], in_=banks[tb][:, jj * 128:jj * 128 + 1]),
                              reads=[bankB[tb]], writes=[B_modAB])
            P.dve(lambda e: e.scalar_tensor_tensor(out=modA[:], in0=modT[:, 1, :], scalar=1.0, in1=normg[:], op0=ALU.add, op1=ALU.mult),
                  reads=[B_modAB, B_misc], writes=[B_modAB])
            pr, prb = r256.next()
            P.dve(lambda e: e.tensor_tensor(out=pr[:, 0:64], in0=dlam[:, 0:64], in1=dlam[:, 64:128], op=ALU.mult), reads=[dlamb], writes=[prb])
            P.dve(lambda e: e.tensor_tensor(out=pr[:, 64:128], in0=dlam[:, 128:192], in1=dlam[:, 192:256], op=ALU.mult), reads=[dlamb], writes=[prb])
            s12, s12b = rs.next()
            P.dve(lambda e: e.tensor_reduce(out=s12[:, 0:2], in_=pr[:, 0:128].rearrange("p (a d) -> p a d", a=2, d=64), axis=AX.X, op=ALU.add),
                  reads=[prb], writes=[s12b])
            P.act(lambda e: e.activation(out=s12[:, 0:2], in_=s12[:, 0:2], func=AF.Exp), reads=[s12b], writes=[s12b])
            lam_init = 0.8 - 0.6 * math.exp(-0.3 * l)
            P.dve(lambda e: e.tensor_tensor(out=small[:, 0:1], in0=s12[:, 1:2], in1=s12[:, 0:1], op=ALU.subtract), reads=[s12b], writes=[B_misc])
            P.dve(lambda e: e.tensor_scalar(out=small[:, 0:1], in0=small[:, 0:1], scalar1=-lam_init, scalar2=None, op0=ALU.add),
                  reads=[B_misc], writes=[B_misc])
            P.act(lambda e: e.activation(out=lg[:], in_=rdl[:], func=AF.Exp, scale=-1.0), reads=[B_misc], writes=[B_misc])
            P.act(lambda e: e.activation(out=lg[:], in_=lg[:], func=AF.Ln, bias=1.0), reads=[B_misc], writes=[B_misc])
            P.dve(lambda e: e.tensor_scalar(out=lg[:], in0=lg[:], scalar1=-1.0, scalar2=None, op0=ALU.mult), reads=[B_misc], writes=[B_misc])

        def norm_phase(l):
            src = x_in if l == 0 else xs_scr
            for T in range(NT):
                xt, xb_ = xring.next()
                P.dma(xt[:], src[T * 128:(T + 1) * 128, :], reads=([B_xs[T]] if l > 0 else []), writes=[xb_])
                xh, xhb = xhring.next()
                ss, ssb = rs.next()
                P.act(lambda e, xt=xt, xh=xh, ss=ss: e.activation(out=xh[:], in_=xt[:], func=AF.Square, accum_out=ss[:, 0:1]),
                      reads=[xb_], writes=[xhb, ssb])
                rstd, rb_ = rstd_from_ss(ss[:, 0:1], ssb, 1, 1.0 / D_MODEL, EPS)
                P.dve(lambda e, xt=xt, xh=xh, rstd=rstd: e.tensor_scalar(out=xh[:], in0=xt[:], scalar1=rstd[:, 0:1], scalar2=None, op0=ALU.mult),
                      reads=[xb_, rb_], writes=[xhb])
                bk = 6 + (T % 2)
                for kc in range(8):
                    P.pe(lambda e, kc=kc, xh=xh, bk=bk: e.transpose(banksb[bk][:, kc * 128:(kc + 1) * 128], xh[:, kc * 128:(kc + 1) * 128], identb[:]),
                         reads=[xhb, B_cst], writes=[bankB[bk]])
                for kc in range(8):
                    if kc % 2 == 0:
                        P.dve(lambda e, kc=kc, bk=bk, T=T: e.tensor_scalar(out=hT[:, kc, tslice(T)], in0=banksb[bk][:, kc * 128:(kc + 1) * 128],
                                                                            scalar1=modA[:, kc:kc + 1], scalar2=modT[:, 0, kc:kc + 1],
                                                                            op0=ALU.mult, op1=ALU.add),
                              reads=[bankB[bk], B_modAB], writes=[B_hT[T]])
                    else:
                        P.act(lambda e, kc=kc, bk=bk, T=T: e.activation(out=hT[:, kc, tslice(T)], in_=banksb[bk][:, kc * 128:(kc + 1) * 128],
                                                                         func=AF.Identity, scale=modA[:, kc:kc + 1], bias=modT[:, 0, kc:kc + 1]),
                              reads=[bankB[bk], B_modAB], writes=[B_hT[T]])

        def load_unit_weights(l, u):
            W = UNIT_W[u]
            wb = wbf[:, 0:8 * W].rearrange("p (k n) -> p k n", k=8, n=W)
            engs = (["dve", "act"] * 4) if u < 4 else (["dve"] * 8)
            for (a, b) in [(0, 512)] + ([(512, W)] if W > 512 else []):
                wd = b - a
                ws = wstage[:, 0:8 * wd].rearrange("p (k n) -> p k n", k=8, n=wd)
                P.dma(ws, win_in[l][:, :, UNIT_OFF[u] + a:UNIT_OFF[u] + b], writes=[B_wstage])
                for kc in range(8):
                    if engs[kc] == "act":
                        P.act(lambda e, kc=kc, ws=ws, a=a, b=b: e.copy(out=wb[:, kc, a:b], in_=ws[:, kc, :]), reads=[B_wstage], writes=[B_wbf])
                    else:
                        P.on(engs[kc], lambda e, kc=kc, ws=ws, a=a, b=b: e.tensor_copy(out=wb[:, kc, a:b], in_=ws[:, kc, :]), reads=[B_wstage], writes=[B_wbf])
            return wb

        def project(wb, T, bk, c0, c1):
            for kc in range(8):
                P.pe(lambda e, kc=kc: e.matmul(banks[bk][:, 0:c1 - c0], lhsT=hT[:, kc, tslice(T)], rhs=wb[:, kc, c0:c1],
                                               start=(kc == 0), stop=(kc == 7)),
                     reads=[B_hT[T], B_wbf], writes=[bankB[bk]])

        def mixed_out(mt, mtb, chunk, T, tbank, eng="act"):
            P.pe(lambda e: e.transpose(banksb[tbank][:, 0:128], mt, identb[:]), reads=[mtb, B_cst], writes=[bankB[tbank]])
            if eng == "act":
                P.act(lambda e: e.copy(out=mixT[:, chunk, tslice(T)], in_=banksb[tbank][:, 0:128]), reads=[bankB[tbank]], writes=[B_mixT[chunk][T]])
            else:
                P.dve(lambda e: e.tensor_copy(out=mixT[:, chunk, tslice(T)], in_=banksb[tbank][:, 0:128]), reads=[bankB[tbank]], writes=[B_mixT[chunk][T]])

        def diff_unit(l, h, DB):
            u = 4 + h
            chunk = UNIT_CHUNK[u]
            if h == 0:
                P.barrier()
            par = h % 2
            base = par * 27728
            QT = arena_view(base + 0, [128, 2, TOK], BF16)
            KT = arena_view(base + 8192, [128, 2, 2560], BF16)
            V = arena_view(base + 18432, [128, NKT, 130], BF16)
            sg = arena_view(base + 23632, [128, NT, 128], BF16)
            ckst = arena_view(55456, [128, 4, 128], F32)
            cvst = arena_view(57504, [128, 4, 128], F32)
            ckb = arena_view(59552, [128, 4, 128], BF16)
            B_QT, B_KT, B_V, B_sg, B_qm = DB["set"][par]
            B_ck = DB["ck"]
            wb = load_unit_weights(l, u)
            import os as _os
            DD0 = _os.environ.get('DIFFDBG', '')
            if 'nomask' not in DD0:
                for c in range(2):
                    P.dma(QT[64:72, c, :], qmask_in, writes=[B_qm])
                    P.dma(KT[64:72, c, :], kmask_in, writes=[B_qm])
            if 'noctx' not in DD0:
                P.dma(ckst, ck_in[l].rearrange("(t p) n -> p t n", p=128)[:, :, h * 128:(h + 1) * 128], writes=[B_ck])
                P.dma(cvst, cv_in[l].rearrange("(t p) n -> p t n", p=128)[:, :, h * 128:(h + 1) * 128], writes=[B_ck])
                P.pool(lambda e: e.memset(V[:, :, 128:130], 1.0), writes=B_V)
                P.dve(lambda e: e.tensor_copy(out=ckb, in_=ckst), reads=[B_ck], writes=[B_ck])
                for pt in range(4):
                    bk = 5
                    for c in range(2):
                        P.pe(lambda e, pt=pt, c=c, bk=bk: e.transpose(banksb[bk][0:64, c * 128:(c + 1) * 128], ckb[:, pt, c * 64:(c + 1) * 64], identb[:]),
                             reads=[B_ck, B_cst], writes=[bankB[bk]])
                    P.act(lambda e, pt=pt, bk=bk: e.copy(out=KT[0:64, :, 2048 + pt * 128:2048 + (pt + 1) * 128],
                                                         in_=banksb[bk][0:64, 0:256].rearrange("p (c t) -> p c t", c=2, t=128)),
                          reads=[bankB[bk]], writes=[B_KT[16 + pt]])
                    P.any(lambda e, pt=pt: e.tensor_copy(out=V[:, 16 + pt, 0:128], in_=cvst[:, pt, :]), reads=[B_ck], writes=[B_V[16 + pt]])

            if 'noA' in DD0:
                return
            for T in range(NT):
                zb = 6 + (T % 2)
                project(wb, T, zb, 0, 512)
                z = banks[zb]
                sq, sqb = r256.next()
                P.act(lambda e, z=z, sq=sq: e.activation(out=sq[:], in_=z[:, 0:256], func=AF.Square), reads=[bankB[zb]], writes=[sqb])
                ss, ssb = rs.next()
                P.dve(lambda e, sq=sq, ss=ss: e.tensor_reduce(out=ss[:, 0:4], in_=sq[:].rearrange("p (g d) -> p g d", g=4, d=64), axis=AX.X, op=ALU.add),
                      reads=[sqb], writes=[ssb])
                rstd, rb_ = rstd_from_ss(ss[:, 0:4], ssb, 4, 1.0 / 64, EPS)
                nq, nqb = r256.next()
                P.dve(lambda e, z=z, nq=nq, rstd=rstd: e.tensor_tensor(out=nq[:].rearrange("p (g d) -> p g d", g=4, d=64),
                                                                      in0=z[:, 0:256].rearrange("p (g d) -> p g d", g=4, d=64),
                                                                      in1=rstd[:, 0:4].unsqueeze(2).to_broadcast([128, 4, 64]), op=ALU.mult),
                      reads=[bankB[zb], rb_], writes=[nqb])
                P.any(lambda e, nq=nq: e.tensor_tensor(out=nq[:], in0=nq[:], in1=g4[:], op=ALU.mult), reads=[nqb, B_misc], writes=[nqb])
                if 'nonk' not in DD0:
                    P.dma(nk_out[l][T * 128:(T + 1) * 128, h * 128:(h + 1) * 128], nq[:, 128:256], reads=[nqb])
                rt, rtb = rb256.next()
                import os as _os
                rope(nq[:], [nqb], rt[:], [rtb], T, 4, "any", "any")
                if 'noT' not in DD0:
                    tb = 5
                    for g in range(4):
                        P.pe(lambda e, g=g, rt=rt, tb=tb: e.transpose(banksb[tb][0:64, g * 128:(g + 1) * 128], rt[:, g * 64:(g + 1) * 64], identb[:]),
                             reads=[rtb, B_cst], writes=[bankB[tb]])
                    if 'noTq' not in DD0:
                      P.act(lambda e, tb=tb, T=T: e.copy(out=QT[0:64, :, tslice(T)], in_=banksb[tb][0:64, 0:256].rearrange("p (c t) -> p c t", c=2, t=128)),
                          reads=[bankB[tb]], writes=[B_QT[T]])
                    if 'noTk' not in DD0:
                      P.act(lambda e, tb=tb, T=T: e.copy(out=KT[0:64, :, tslice(T)], in_=banksb[tb][0:64, 256:512].rearrange("p (c t) -> p c t", c=2, t=128)),
                          reads=[bankB[tb]], writes=[B_KT[T]])
                vst, vstb = r128.next()
                P.dve(lambda e, z=z, vst=vst: e.tensor_copy(out=vst[:], in_=z[:, 256:384]), reads=[bankB[zb]], writes=[vstb])
                if 'nonk' not in DD0:
                    P.dma(nv_out[l][T * 128:(T + 1) * 128, h * 128:(h + 1) * 128], vst[:], reads=[vstb])
                P.any(lambda e, vst=vst, T=T: e.tensor_copy(out=V[:, T, 0:128], in_=vst[:]), reads=[vstb], writes=[B_V[T]])
                th, thb = r128.next()
                P.act(lambda e, z=z, th=th: e.activation(out=th[:], in_=z[:, 384:512], func=AF.Tanh, scale=0.5), reads=[bankB[zb]], writes=[thb])
                P.dve(lambda e, z=z, th=th, T=T: e.scalar_tensor_tensor(out=sg[:, T, :], in0=th[:], scalar=1.0, in1=z[:, 384:512], op0=ALU.add, op1=ALU.mult),
                      reads=[thb, bankB[zb]], writes=[B_sg[T]])
            import os as _os
            DD = _os.environ.get('DIFFDBG', '')
            if 'noB' in DD:
                return
            KR = 64 if 'k64' in DD else 72
            lam_init = 0.8 - 0.6 * math.exp(-0.3 * l)
            c0 = 0.5 * (1.0 - lam_init)
            OB = [2, 3, 4]

            def acc(c, qi):
                a = c * 4 + qi
                return OB[a // 3], (a % 3) * 160

            for qb in range(4):
                for k in OB:
                    P.pe(lambda e, k=k: e.matmul(banks[k][:, 0:512], lhsT=zerosb[:, 0:128], rhs=zerosb[:, 0:512], start=True, stop=False, skip_group_check=True),
                         reads=[B_cst], writes=[bankB[k]])
                steps = [(c, kt) for c in range(2) for kt in range(NKT)]
                pts = {}

                def emit_st(i):
                    c, kt = steps[i]
                    sbk = i % 2
                    P.pe(lambda e, c=c, kt=kt, sbk=sbk: e.matmul(banks[sbk][:, 0:512], lhsT=KT[0:KR, c, kt * 128:(kt + 1) * 128],
                                                                 rhs=QT[0:KR, c, qb * 512:(qb + 1) * 512], start=True, stop=True),
                         reads=[B_KT[kt], B_qm] + B_QT[qb * 4:qb * 4 + 4], writes=[bankB[sbk]])
                    pt_, ptb = rpt.next()
                    P.act(lambda e, sbk=sbk, pt_=pt_: e.activation(out=pt_[:], in_=banks[sbk][:, 0:512], func=AF.Exp, scale=0.125),
                          reads=[bankB[sbk]], writes=[ptb])
                    pts[i] = (pt_, ptb)

                def emit_pv(i):
                    c, kt = steps[i]
                    pt_, ptb = pts.pop(i)
                    for qi in range(4):
                        bk, off = acc(c, qi)
                        P.pe(lambda e, qi=qi, bk=bk, off=off, pt_=pt_, kt=kt: e.matmul(banks[bk][:, off:off + 129], lhsT=pt_[:, qi * 128:(qi + 1) * 128],
                                                                                       rhs=V[:, kt, 0:129], start=False, stop=(kt == NKT - 1), skip_group_check=True),
                             reads=[ptb, B_V[kt]], writes=[bankB[bk]])

                emit_st(0)
                for i in range(len(steps)):
                    if i + 1 < len(steps):
                        emit_st(i + 1)
                    emit_pv(i)
                for qi in range(4):
                    T = qb * 4 + qi
                    b0, o0 = acc(0, qi)
                    b1, o1 = acc(1, qi)
                    r01, r01b = rs.next()
                    P.dve(lambda e, r01=r01: e.reciprocal(out=r01[:, 0:1], in_=banks[b0][:, o0 + 128:o0 + 129]), reads=[bankB[b0]], writes=[r01b])
                    P.dve(lambda e, r01=r01: e.reciprocal(out=r01[:, 1:2], in_=banks[b1][:, o1 + 128:o1 + 129]), reads=[bankB[b1]], writes=[r01b])
                    P.dve(lambda e, r01=r01: e.tensor_tensor(out=r01[:, 1:2], in0=r01[:, 1:2], in1=small[:, 0:1], op=ALU.mult), reads=[r01b, B_misc], writes=[r01b])
                    d, db = r128.next()
                    P.dve(lambda e, d=d, r01=r01: e.tensor_scalar(out=d[:], in0=banks[b0][:, o0:o0 + 128], scalar1=r01[:, 0:1], scalar2=None, op0=ALU.mult),
                          reads=[bankB[b0], r01b], writes=[db])
                    P.dve(lambda e, d=d, r01=r01: e.scalar_tensor_tensor(out=d[:], in0=banks[b1][:, o1:o1 + 128], scalar=r01[:, 1:2], in1=d[:],
                                                                        op0=ALU.mult, op1=ALU.add),
                          reads=[bankB[b1], r01b, db], writes=[db])
                    jk, jkb = r128.next()
                    ss, ssb = rs.next()
                    P.act(lambda e, d=d, jk=jk, ss=ss: e.activation(out=jk[:], in_=d[:], func=AF.Square, accum_out=ss[:, 0:1]), reads=[db], writes=[jkb, ssb])
                    rstd, rb_ = rstd_from_ss(ss[:, 0:1], ssb, 1, 1.0 / (128 * c0 * c0), EPS / (c0 * c0))
                    mt, mtb = rb128.next()
                    P.dve(lambda e, d=d, rstd=rstd, mt=mt, T=T: e.scalar_tensor_tensor(out=mt[:], in0=d[:], scalar=rstd[:, 0:1], in1=sg[:, T, :],
                                                                                      op0=ALU.mult, op1=ALU.mult),
                          reads=[db, rb_, B_sg[T]], writes=[mtb])
                    mixed_out(mt[:], mtb, chunk, T, 5, eng="dve")

        def ret_unit(l, p, RB):
            u = p
            chunk = UNIT_CHUNK[u]
            if p == 0:
                P.barrier()
            base = p * 28672
            qT = arena_view(base + 0, [128, TOK], BF16)
            kT = arena_view(base + 4096, [128, TOK], BF16)
            ktok = arena_view(base + 8192, [128, NT, 128], BF16)
            v = arena_view(base + 12288, [128, NT, 128], BF16)
            sg = arena_view(base + 16384, [128, NT, 128], BF16)
            Sbf = arena_view(base + 20480, [128, 2, NT, 128], BF16)
            if p == 0:
                dtm_, qd_, kd_, stS_, lgrow_ = dtm, qd, kd, stS, lgrow
            else:
                dtm_ = arena_view(57344, [128, 2, 128], F32)
                qd_ = arena_view(58368, [128, 2, 128], F32)
                kd_ = arena_view(59392, [128, 2, 128], F32)
                stS_ = arena_view(60416, [128, 2, 128], F32)
                lgrow_ = lgrow1
            B_pair_, B_stS_ = RB[p]
            B_q = [Buf() for _ in range(NT)]
            B_k = [Buf() for _ in range(NT)]
            B_kt = [Buf() for _ in range(NT)]
            B_v = [Buf() for _ in range(NT)]
            B_sg = [Buf() for _ in range(NT)]
            B_S = [[Buf() for _ in range(NT)] for _ in range(2)]
            wb = load_unit_weights(l, u)
            for d_ in range(2):
                for hh in range(2):
                    col = d_ * 4 + 2 * p + hh
                    P.dve(lambda e, d_=d_, hh=hh, col=col: e.tensor_copy(out=lgrow_[hh * 64:(hh + 1) * 64, d_:d_ + 1], in_=lg[hh * 64:(hh + 1) * 64, col:col + 1]),
                          reads=[B_misc], writes=[B_pair_])
            for hh in range(2):
                e12t, e12b = r256.next()
                cf = 2 * p + hh
                cb_ = 4 + 2 * p + hh
                P.act(lambda e, cf=cf: e.activation(out=e12t[:, 0:128], in_=M1, func=AF.Exp, scale=lg[:, cf:cf + 1]), reads=[e12b, B_cst, B_misc], writes=[e12b, B_pair_])
                P.act(lambda e, cb_=cb_: e.activation(out=e12t[:, 128:256], in_=M2, func=AF.Exp, scale=lg[:, cb_:cb_ + 1]), reads=[e12b, B_cst, B_misc], writes=[e12b, B_pair_])
                P.dve(lambda e: e.tensor_tensor(out=e12t[:, 0:128], in0=e12t[:, 0:128], in1=L1, op=ALU.mult), reads=[e12b, B_pair_, B_cst], writes=[e12b, B_pair_])
                P.dve(lambda e: e.tensor_tensor(out=e12t[:, 128:256], in0=e12t[:, 128:256], in1=L2, op=ALU.mult), reads=[e12b, B_pair_, B_cst], writes=[e12b, B_pair_])
                P.dve(lambda e, hh=hh: e.tensor_tensor(out=dtm_[:, hh, :], in0=e12t[:, 0:128], in1=e12t[:, 128:256], op=ALU.add), reads=[e12b, B_pair_], writes=[e12b, B_pair_])
                P.act(lambda e, hh=hh, cf=cf: e.activation(out=kd_[:, 0, hh * 64:(hh + 1) * 64], in_=COLA[:, 0:64], func=AF.Exp, scale=lg[:, cf:cf + 1]),
                      reads=[B_cst, B_misc], writes=[B_pair_])
                P.act(lambda e, hh=hh, cb_=cb_: e.activation(out=kd_[:, 1, hh * 64:(hh + 1) * 64], in_=COLB[:, 0:64], func=AF.Exp, scale=lg[:, cb_:cb_ + 1]),
                      reads=[B_cst, B_misc], writes=[B_pair_])
            P.dve(lambda e: e.tensor_scalar(out=kd_[:], in0=kd_[:], scalar1=0.125, scalar2=None, op0=ALU.mult), reads=[B_pair_], writes=[B_pair_])
            P.act(lambda e: e.activation(out=qd_[:, 0, :], in_=IOTA1, func=AF.Exp, scale=lgrow_[:, 0:1]), reads=[B_cst, B_pair_], writes=[B_pair_])
            P.act(lambda e: e.activation(out=qd_[:, 1, :], in_=IOTA2, func=AF.Exp, scale=lgrow_[:, 1:2]), reads=[B_cst, B_pair_], writes=[B_pair_])
            P.act(lambda e: e.activation(out=lgrow_[:, 2:4], in_=lgrow_[:, 0:2], func=AF.Exp, scale=128.0), reads=[B_pair_], writes=[B_pair_])
            for T in range(NT):
                zb = T % 4
                project(wb, T, zb, 0, 512)
                z = banks[zb]
                rq, rqb = rb128.next()
                t1, b1 = r256.next()
                t2, b2 = r256.next()
                cv_ = ropec[:, T, :]
                sv_ = ropes[:, T, :].rearrange("p (h j i) -> p h j i", h=2, j=2, i=16)
                src3 = z[:, 0:256].rearrange("p (g d) -> p g d", g=4, d=64)
                src5 = z[:, 0:256].rearrange("p (g h j i) -> p g h j i", g=4, h=2, j=2, i=16)
                t13 = t1[:].rearrange("p (g d) -> p g d", g=4, d=64)
                t25 = t2[:].rearrange("p (g h j i) -> p g h j i", g=4, h=2, j=2, i=16)
                P.dve(lambda e, t13=t13, src3=src3, cv_=cv_: e.tensor_tensor(out=t13, in0=src3, in1=cv_.unsqueeze(1).to_broadcast([128, 4, 64]), op=ALU.mult),
                      reads=[bankB[zb], B_cst], writes=[b1])
                P.dve(lambda e, t25=t25, src5=src5, sv_=sv_: e.tensor_tensor(out=t25[:, :, :, 0, :], in0=src5[:, :, :, 1, :],
                                                                             in1=sv_[:, :, 0, :].unsqueeze(1).to_broadcast([128, 4, 2, 16]), op=ALU.mult),
                      reads=[bankB[zb], B_cst], writes=[b2])
                P.dve(lambda e, t25=t25, src5=src5, sv_=sv_: e.tensor_tensor(out=t25[:, :, :, 1, :], in0=src5[:, :, :, 0, :],
                                                                             in1=sv_[:, :, 1, :].unsqueeze(1).to_broadcast([128, 4, 2, 16]), op=ALU.mult),
                      reads=[bankB[zb], B_cst], writes=[b2])
                P.any(lambda e, t1=t1, t2=t2, rq=rq: e.tensor_tensor(out=rq[:], in0=t1[:, 0:128], in1=t2[:, 0:128], op=ALU.add), reads=[b1, b2], writes=[rqb])
                P.any(lambda e, t1=t1, t2=t2, T=T: e.tensor_tensor(out=ktok[:, T, :], in0=t1[:, 128:256], in1=t2[:, 128:256], op=ALU.add),
                       reads=[b1, b2], writes=[B_kt[T]])
                tb = 4 + (T % 2)
                P.pe(lambda e, rq=rq, tb=tb: e.transpose(banksb[tb][:, 0:128], rq[:], identb[:]), reads=[rqb, B_cst], writes=[bankB[tb]])
                P.pe(lambda e, T=T, tb=tb: e.transpose(banksb[tb][:, 128:256], ktok[:, T, :], identb[:]), reads=[B_kt[T], B_cst], writes=[bankB[tb]])
                P.act(lambda e, T=T, tb=tb: e.copy(out=qT[:, tslice(T)], in_=banksb[tb][:, 0:128]), reads=[bankB[tb]], writes=[B_q[T]])
                P.act(lambda e, T=T, tb=tb: e.activation(out=kT[:, tslice(T)], in_=banksb[tb][:, 128:256], func=AF.Copy, scale=0.125),
                      reads=[bankB[tb]], writes=[B_k[T]])
                P.act(lambda e, z=z, T=T: e.copy(out=v[:, T, :], in_=z[:, 256:384]), reads=[bankB[zb]], writes=[B_v[T]])
                th, thb = r128.next()
                P.act(lambda e, z=z, th=th: e.activation(out=th[:], in_=z[:, 384:512], func=AF.Tanh, scale=0.5), reads=[bankB[zb]], writes=[thb])
                P.dve(lambda e, z=z, th=th, T=T: e.scalar_tensor_tensor(out=sg[:, T, :], in0=th[:], scalar=1.0, in1=z[:, 384:512], op0=ALU.add, op1=ALU.mult),
                      reads=[thb, bankB[zb]], writes=[B_sg[T]])
            st_in = [srf_in, srb_in]
            st_out = [nsrf_out, nsrb_out]
            for d_ in range(2):
                P.pool(lambda e, d_=d_: e.memset(stS_[:, d_, :], 0.0), writes=[B_stS_[d_]])
                for hh in range(2):
                    P.dma(stS_[hh * 64:(hh + 1) * 64, d_, hh * 64:(hh + 1) * 64], st_in[d_][l, 2 * p + hh], writes=[B_stS_[d_]])
            for step in range(NT):
                for d_ in range(2):
                    T = step if d_ == 0 else NT - 1 - step
                    S = stS_[:, d_, :]
                    kcol = d_ * 16 + T
                    P.dve(lambda e, S=S, kcol=kcol: e.tensor_scalar(out=S, in0=S, scalar1=keep[:, kcol:kcol + 1], scalar2=None, op0=ALU.mult),
                          reads=[B_stS_[d_], B_cst], writes=[B_stS_[d_]])
                    P.act(lambda e, S=S, d_=d_, T=T: e.copy(out=Sbf[:, d_, T, :], in_=S), reads=[B_stS_[d_]], writes=[B_S[d_][T]])
                    kt_, ktb_ = rb128.next()
                    P.any(lambda e, kt_=kt_, T=T, d_=d_: e.tensor_tensor(out=kt_[:], in0=ktok[:, T, :], in1=kd_[:, d_, :], op=ALU.mult),
                           reads=[B_kt[T], B_pair_], writes=[ktb_])
                    ub = d_
                    P.pe(lambda e, kt_=kt_, T=T, ub=ub: e.matmul(banks[ub][:, 0:128], lhsT=kt_[:], rhs=v[:, T, :], start=True, stop=True),
                         reads=[ktb_, B_v[T]], writes=[bankB[ub]])
                    tmp, tmpb = r128.next()
                    P.dve(lambda e, tmp=tmp, ub=ub: e.tensor_tensor(out=tmp[:], in0=banks[ub][:, 0:128], in1=BM, op=ALU.mult),
                          reads=[bankB[ub], B_cst], writes=[tmpb])
                    P.dve(lambda e, S=S, tmp=tmp, d_=d_: e.scalar_tensor_tensor(out=S, in0=S, scalar=lgrow_[:, 2 + d_:3 + d_], in1=tmp[:], op0=ALU.mult, op1=ALU.add),
                          reads=[B_stS_[d_], B_pair_, tmpb], writes=[B_stS_[d_]])
                    is_out = (T % 2 == 1) if d_ == 0 else (T % 2 == 0)
                    if is_out:
                        so, sob = r128.next()
                        P.act(lambda e, so=so, S=S: e.copy(out=so[:], in_=S), reads=[B_stS_[d_]], writes=[sob])
                        for hh in range(2):
                            P.dma(st_out[d_][l, T // 2, 2 * p + hh], so[hh * 64:(hh + 1) * 64, hh * 64:(hh + 1) * 64], reads=[sob])
            for T in range(NT):
                qf, qfb = rb128.next()
                qb_, qbb = rb128.next()
                P.dve(lambda e, qf=qf, T=T: e.tensor_tensor(out=qf[:], in0=qT[:, tslice(T)], in1=qd_[:, 0, :], op=ALU.mult), reads=[B_q[T], B_pair_], writes=[qfb])
                P.any(lambda e, qb_=qb_, T=T: e.tensor_tensor(out=qb_[:], in0=qT[:, tslice(T)], in1=qd_[:, 1, :], op=ALU.mult), reads=[B_q[T], B_pair_], writes=[qbb])
                ob = 6 + (T % 2)
                ab = 2 + (T % 2)
                P.pe(lambda e, qf=qf, T=T, ob=ob: e.matmul(banks[ob][:, 0:128], lhsT=qf[:], rhs=Sbf[:, 0, T, :], start=True, stop=False, skip_group_check=True),
                     reads=[qfb, B_S[0][T]], writes=[bankB[ob]])
                P.pe(lambda e, qb_=qb_, T=T, ob=ob: e.matmul(banks[ob][:, 0:128], lhsT=qb_[:], rhs=Sbf[:, 1, T, :], start=False, stop=False, skip_group_check=True),
                     reads=[qbb, B_S[1][T]], writes=[bankB[ob]])
                am, amb = rb256.next()
                for hh in range(2):
                    P.pe(lambda e, hh=hh, T=T: e.matmul(banks[2 + hh][:, 0:128], lhsT=kT[hh * 64:(hh + 1) * 64, tslice(T)],
                                                        rhs=qT[hh * 64:(hh + 1) * 64, tslice(T)], start=True, stop=True),
                         reads=[B_k[T], B_q[T]], writes=[bankB[2 + hh]])
                    P.dve(lambda e, am=am, hh=hh: e.tensor_tensor(out=am[:, hh * 128:(hh + 1) * 128], in0=banks[2 + hh][:, 0:128], in1=dtm_[:, hh, :], op=ALU.mult),
                          reads=[bankB[2 + hh], B_pair_], writes=[amb])
                for hh in range(2):
                    P.pe(lambda e, hh=hh, am=am, T=T, ob=ob: e.matmul(banks[ob][:, hh * 64:(hh + 1) * 64], lhsT=am[:, hh * 128:(hh + 1) * 128],
                                                                      rhs=v[:, T, hh * 64:(hh + 1) * 64], start=False, stop=(hh == 1), skip_group_check=True),
                         reads=[amb, B_v[T]], writes=[bankB[ob]])
                finish_pair(banks[ob][:, 0:128], [bankB[ob]], sg, B_sg, chunk, T, 0.5, 4 + (T % 2))

        def finish_pair(o_ap, obufs, sg, B_sg, chunk, T, c0, tbank):
            ss, ssb = rs.next()
            jk, jkb = r128.next()
            for hh in range(2):
                P.act(lambda e, hh=hh: e.activation(out=jk[:, hh * 64:(hh + 1) * 64], in_=o_ap[:, hh * 64:(hh + 1) * 64], func=AF.Square,
                                                    accum_out=ss[:, hh:hh + 1]),
                      reads=obufs, writes=[jkb, ssb])
            rstd, rb_ = rstd_from_ss(ss[:, 0:2], ssb, 2, 1.0 / (64 * c0 * c0), EPS / (c0 * c0))
            mt, mtb = rb128.next()
            for hh in range(2):
                P.dve(lambda e, hh=hh: e.scalar_tensor_tensor(out=mt[:, hh * 64:(hh + 1) * 64], in0=o_ap[:, hh * 64:(hh + 1) * 64], scalar=rstd[:, hh:hh + 1],
                                                              in1=sg[:, T, hh * 64:(hh + 1) * 64], op0=ALU.mult, op1=ALU.mult),
                      reads=list(obufs) + [rb_, B_sg[T]], writes=[mtb])
            mixed_out(mt[:], mtb, chunk, T, tbank)

        def hgrn_unit(l, p):
            u = 2 + p
            chunk = UNIT_CHUNK[u]
            P.barrier()
            q = arena_view(0, [128, NT, 128], BF16)
            kk = arena_view(4096, [128, NT, 256], BF16)
            lf = arena_view(12288, [128, NT, 256], F32)
            v = arena_view(28672, [128, NT, 128], BF16)
            sg = arena_view(32768, [128, NT, 128], BF16)
            oacc = arena_view(36864, [128, NT, 128], F32)
            vmall = arena_view(45056, [128, NT, 512], BF16)
            B_vm = [Buf() for _ in range(NT)]
            B_q = [Buf() for _ in range(NT)]
            B_kk = [Buf() for _ in range(NT)]
            B_lf = [Buf() for _ in range(NT)]
            B_v = [Buf() for _ in range(NT)]
            B_sg = [Buf() for _ in range(NT)]
            B_oa = [Buf() for _ in range(NT)]
            wb = load_unit_weights(l, u)
            for half in range(2):
                P.dve(lambda e, half=half: e.tensor_copy(out=lb2[:, half * 128:(half + 1) * 128], in_=lball[:, l, p * 128:(p + 1) * 128]),
                      reads=[B_cst], writes=[B_pair])
            P.dve(lambda e: e.tensor_scalar(out=omlb2[:], in0=lb2[:], scalar1=-1.0, scalar2=1.0, op0=ALU.mult, op1=ALU.add), reads=[B_pair], writes=[B_pair])
            for T in range(NT):
                zb = 2 * (T % 2)
                project(wb, T, zb, 0, 512)
                project(wb, T, zb + 1, 512, 640)
                z = banks[zb]
                z2 = banks[zb + 1]
                u_, ub_ = r512.next()
                P.act(lambda e, z=z, u_=u_: e.activation(out=u_[:, 0:384], in_=z[:, 0:384], func=AF.Exp, scale=-1.0), reads=[bankB[zb]], writes=[ub_])
                P.act(lambda e, u_=u_: e.activation(out=u_[:, 0:384], in_=u_[:, 0:384], func=AF.Ln, bias=1.0), reads=[ub_], writes=[ub_])
                P.act(lambda e, u_=u_: e.activation(out=u_[:, 0:384], in_=u_[:, 0:384], func=AF.Exp, scale=-1.0), reads=[ub_], writes=[ub_])
                P.dve(lambda e, u_=u_, z=z, T=T: e.tensor_tensor(out=sg[:, T, :], in0=u_[:, 256:384], in1=z[:, 256:384], op=ALU.mult),
                      reads=[ub_, bankB[zb]], writes=[B_sg[T]])
                f_, fb_ = r256.next()
                P.any(lambda e, u_=u_, f_=f_: e.tensor_tensor(out=f_[:], in0=u_[:, 0:256], in1=omlb2[:], op=ALU.mult), reads=[ub_, B_pair], writes=[fb_])
                P.any(lambda e, f_=f_: e.tensor_tensor(out=f_[:], in0=f_[:], in1=lb2[:], op=ALU.add), reads=[fb_, B_pair], writes=[fb_])
                P.act(lambda e, f_=f_, T=T: e.activation(out=lf[:, T, :], in_=f_[:], func=AF.Ln), reads=[fb_], writes=[B_lf[T]])
                P.act(lambda e, f_=f_, T=T: e.activation(out=kk[:, T, :], in_=f_[:], func=AF.Identity, scale=-1.0, bias=1.0),
                      reads=[fb_], writes=[B_kk[T]])
                P.act(lambda e, z=z, T=T: e.activation(out=q[:, T, :], in_=z[:, 384:512], func=AF.Copy, scale=0.125), reads=[bankB[zb]], writes=[B_q[T]])
                P.act(lambda e, z2=z2, T=T: e.copy(out=v[:, T, :], in_=z2[:, 0:128]), reads=[bankB[zb + 1]], writes=[B_v[T]])
            st_in = [shf_in, shb_in]
            st_out = [nshf_out, nshb_out]
            TRI = [TRIF, TRIB]
            for d_ in range(2):
                P.pool(lambda e, d_=d_: e.memset(stS[:, d_, :], 0.0), writes=[B_stS[d_]])
                for hh in range(2):
                    P.dma(stS[hh * 64:(hh + 1) * 64, d_, hh * 64:(hh + 1) * 64], st_in[d_][l, 2 * p + hh], writes=[B_stS[d_]])
            done_first = [False] * NT
            for step in range(NT):
                for d_ in range(2):
                    T = step if d_ == 0 else NT - 1 - step
                    S = stS[:, d_, :]
                    lfd = lf[:, T, d_ * 128:(d_ + 1) * 128]
                    kcol = d_ * 16 + T
                    P.dve(lambda e, S=S, kcol=kcol: e.tensor_scalar(out=S, in0=S, scalar1=keep[:, kcol:kcol + 1], scalar2=None, op0=ALU.mult),
                          reads=[B_stS[d_], B_cst], writes=[B_stS[d_]])
                    sbf, sbfb = rb128.next()
                    P.act(lambda e, S=S, sbf=sbf: e.copy(out=sbf[:], in_=S), reads=[B_stS[d_]], writes=[sbfb])
                    P.pe(lambda e, lfd=lfd, d_=d_: e.matmul(banks[0][:, 0:128], lhsT=TRI[d_], rhs=lfd, start=True, stop=True),
                         reads=[B_cst, B_lf[T]], writes=[bankB[0]])
                    P.pe(lambda e, lfd=lfd: e.matmul(banks[0][:, 128:132], lhsT=lfd, rhs=ind[:], start=True, stop=True),
                         reads=[B_cst, B_lf[T]], writes=[bankB[0]])
                    G_, Gb_ = rs.next()
                    P.act(lambda e, G_=G_: e.activation(out=G_[:, 0:4], in_=banks[0][:, 128:132], func=AF.Exp), reads=[bankB[0]], writes=[Gb_])
                    eq, eqb = r128.next()
                    ek, ekb = r128.next()
                    P.act(lambda e, eq=eq: e.activation(out=eq[:], in_=banks[0][:, 0:128], func=AF.Exp), reads=[bankB[0]], writes=[eqb])
                    P.act(lambda e, ek=ek: e.activation(out=ek[:], in_=banks[0][:, 0:128], func=AF.Exp, scale=-1.0), reads=[bankB[0]], writes=[ekb])
                    qt_, qtb = rb128.next()
                    kt_, ktb = rb128.next()
                    P.dve(lambda e, qt_=qt_, eq=eq, T=T: e.tensor_tensor(out=qt_[:], in0=q[:, T, :], in1=eq[:], op=ALU.mult), reads=[B_q[T], eqb], writes=[qtb])
                    P.any(lambda e, kt_=kt_, ek=ek, T=T, d_=d_: e.tensor_tensor(out=kt_[:], in0=kk[:, T, d_ * 128:(d_ + 1) * 128], in1=ek[:], op=ALU.mult),
                           reads=[B_kk[T], ekb], writes=[ktb])
                    P.pe(lambda e, qt_=qt_: e.transpose(banksb[1][:, 0:128], qt_[:], identb[:]), reads=[qtb, B_cst], writes=[bankB[1]])
                    P.pe(lambda e, kt_=kt_: e.transpose(banksb[1][:, 128:256], kt_[:], identb[:]), reads=[ktb, B_cst], writes=[bankB[1]])
                    qkT, qkTb = rb256.next()
                    P.act(lambda e, qkT=qkT: e.copy(out=qkT[:], in_=banksb[1][:, 0:256]), reads=[bankB[1]], writes=[qkTb])
                    vm, vmb = vmall[:, T, :], B_vm[T]
                    if not done_first[T]:
                        for j in range(4):
                            P.act(lambda e, j=j, vm=vm, T=T: e.activation(out=vm[:, j * 128:(j + 1) * 128], in_=v[:, T, :], func=AF.Identity,
                                                                          scale=ind[:, j:j + 1]),
                                  reads=[B_v[T], B_cst], writes=[vmb])
                    P.pe(lambda e, kt_=kt_, vm=vm: e.matmul(banks[2][:, 0:512], lhsT=kt_[:], rhs=vm, start=True, stop=True),
                         reads=[ktb, vmb], writes=[bankB[2]])
                    am, amb = rb256.next()
                    for hh in range(2):
                        abk = 3 + hh
                        P.pe(lambda e, hh=hh, qkT=qkT, abk=abk: e.matmul(banks[abk][:, 0:128], lhsT=qkT[hh * 64:(hh + 1) * 64, 128:256],
                                                                          rhs=qkT[hh * 64:(hh + 1) * 64, 0:128], start=True, stop=True),
                             reads=[qkTb], writes=[bankB[abk]])
                        P.dve(lambda e, am=am, d_=d_, hh=hh, abk=abk: e.tensor_tensor(out=am[:, hh * 128:(hh + 1) * 128], in0=banks[abk][:, 0:128], in1=TRI[d_], op=ALU.mult),
                              reads=[bankB[abk], B_cst], writes=[amb])
                    ob = 5 + d_
                    jorder = [0, 1, 2, 3] if d_ == 0 else [3, 2, 1, 0]
                    cur, curb = sbf, sbfb
                    for ji, j in enumerate(jorder):
                        P.pe(lambda e, j=j, qkT=qkT, cur=cur: e.matmul(banks[ob][32 * j:32 * j + 32, 0:128], lhsT=qkT[:, 32 * j:32 * j + 32], rhs=cur[:],
                                                                       start=True, stop=False, tile_position=(0, 32 * j), skip_group_check=True),
                             reads=[qkTb, curb], writes=[bankB[ob]])
                        tg, tgb = r128.next()
                        P.dve(lambda e, tg=tg, j=j, G_=G_: e.scalar_tensor_tensor(out=tg[:], in0=banks[2][:, j * 128:(j + 1) * 128], scalar=G_[:, j:j + 1], in1=BM,
                                                                                  op0=ALU.mult, op1=ALU.mult),
                              reads=[bankB[2], Gb_, B_cst], writes=[tgb])
                        P.dve(lambda e, S=S, tg=tg, j=j, G_=G_: e.scalar_tensor_tensor(out=S, in0=S, scalar=G_[:, j:j + 1], in1=tg[:], op0=ALU.mult, op1=ALU.add),
                              reads=[B_stS[d_], Gb_, tgb], writes=[B_stS[d_]])
                        if ji < 3:
                            cur, curb = rb128.next()
                            P.act(lambda e, S=S, cur=cur: e.copy(out=cur[:], in_=S), reads=[B_stS[d_]], writes=[curb])
                    for hh in range(2):
                        P.pe(lambda e, hh=hh, am=am, T=T: e.matmul(banks[ob][:, hh * 64:(hh + 1) * 64], lhsT=am[:, hh * 128:(hh + 1) * 128],
                                                                   rhs=v[:, T, hh * 64:(hh + 1) * 64], start=False, stop=(hh == 1), skip_group_check=True),
                             reads=[amb, B_v[T]], writes=[bankB[ob]])
                    is_out = (T % 2 == 1) if d_ == 0 else (T % 2 == 0)
                    if is_out:
                        so, sob = r128.next()
                        P.act(lambda e, so=so, S=S: e.copy(out=so[:], in_=S), reads=[B_stS[d_]], writes=[sob])
                        for hh in range(2):
                            P.dma(st_out[d_][l, T // 2, 2 * p + hh], so[hh * 64:(hh + 1) * 64, hh * 64:(hh + 1) * 64], reads=[sob])
                    if not done_first[T]:
                        done_first[T] = True
                        P.act(lambda e, T=T: e.copy(out=oacc[:, T, :], in_=banks[ob][:, 0:128]), reads=[bankB[ob]], writes=[B_oa[T]])
                    else:
                        ot, otb = r128.next()
                        P.dve(lambda e, ot=ot, T=T: e.tensor_tensor(out=ot[:], in0=banks[ob][:, 0:128], in1=oacc[:, T, :], op=ALU.add),
                              reads=[bankB[ob], B_oa[T]], writes=[otb])
                        finish_pair(ot[:], [otb], sg, B_sg, chunk, T, 1.0, 7)

        def out_phase(l, last):
            P.barrier()
            wo = arena_view(0, [128, 8, 1024], BF16)
            B_wo = Buf()
            for half in range(2):
                ws = wstage[:, 0:4096].rearrange("p (k n) -> p k n", k=8, n=512)
                P.dma(ws, wout_in[l][:, :, half * 512:(half + 1) * 512], writes=[B_wstage])
                for kc in range(8):
                    eng = "any"
                    P.on(eng, lambda e, kc=kc, half=half: e.tensor_copy(out=wo[:, kc, half * 512:(half + 1) * 512], in_=ws[:, kc, :]),
                         reads=[B_wstage], writes=[B_wo])
            src = x_in if l == 0 else xs_scr
            dst = y_out if last else xs_scr
            for T in range(NT):
                xt, xb_ = xring.next()
                P.dma(xt[:], src[T * 128:(T + 1) * 128, :], reads=([B_xs[T]] if l > 0 else []), writes=[xb_])
                for nb in range(2):
                    bk = 2 * (T % 2) + nb
                    for c in range(8):
                        P.pe(lambda e, c=c, nb=nb, bk=bk, T=T: e.matmul(banks[bk][:, 0:512], lhsT=mixT[:, c, tslice(T)], rhs=wo[:, c, nb * 512:(nb + 1) * 512],
                                                                        start=(c == 0), stop=(c == 7)),
                             reads=[B_mixT[c][T], B_wo], writes=[bankB[bk]])
                    tmp, tmpb = r512.next()
                    P.dve(lambda e, tmp=tmp, bk=bk, nb=nb: e.tensor_tensor(out=tmp[:], in0=banks[bk][:, 0:512], in1=gate_b[:, nb * 512:(nb + 1) * 512], op=ALU.mult),
                          reads=[bankB[bk], B_gate], writes=[tmpb])
                    P.any(lambda e, tmp=tmp, xt=xt, nb=nb: e.tensor_tensor(out=xt[:, nb * 512:(nb + 1) * 512], in0=tmp[:], in1=xt[:, nb * 512:(nb + 1) * 512], op=ALU.add),
                           reads=[tmpb, xb_], writes=[xb_])
                P.dma(dst[T * 128:(T + 1) * 128, :], xt[:], reads=[xb_], writes=([] if last else [B_xs[T]]))

        for l in range(L):
            setup_layer(l)
            norm_phase(l)
            RB = [(Buf(), [Buf(), Buf()]) for _ in range(2)]
            for p in range(2):
                if units_enabled is None or ("r%d" % p) in units_enabled:
                    ret_unit(l, p, RB)
            for p in range(2):
                if units_enabled is None or ("g%d" % p) in units_enabled:
                    hgrn_unit(l, p)
            DB = {"set": [([Buf() for _ in range(NT)], [Buf() for _ in range(NKT)], [Buf() for _ in range(NKT)], [Buf() for _ in range(NT)], Buf()) for _ in range(2)], "ck": Buf()}
            for h in range(4):
                if units_enabled is None or ("d%d" % h) in units_enabled:
                    diff_unit(l, h, DB)
            if dbg:
                P.barrier()
                P.dma(dbg_out[l], mixT[:], reads=[b for row in B_mixT for b in row])
            out_phase(l, last=(l == L - 1))

        with nc.Block() as block:
            run = P.build(sems, dsems, reorder=REORDER)
            block.sync(lambda e: run("sp", e))
            block.tensor(lambda e: run("pe", e))
            block.scalar(lambda e: run("act", e))
            block.vector(lambda e: run("dve", e))
            block.gpsimd(lambda e: run("pool", e))
    return nc


def _unit_perm():
    off = dict(rq=0, rk=256, rv=512, rg=768, dq=1024, dk=1536, dv=2048, dg=2560, hq=3072, hff=3328, hfb=3584, hi=3840, hg=4096)
    cols = []
    for p in range(2):
        for n in ("rq", "rk", "rv", "rg"):
            cols += list(range(off[n] + 128 * p, off[n] + 128 * p + 128))
    for p in range(2):
        for n in ("hff", "hfb", "hg", "hq", "hi"):
            cols += list(range(off[n] + 128 * p, off[n] + 128 * p + 128))
    for h in range(4):
        for n in ("dq", "dk", "dv", "dg"):
            cols += list(range(off[n] + 128 * h, off[n] + 128 * h + 128))
    return np.array(cols, dtype=np.int64)


def _constants():
    s = np.arange(128, dtype=np.float32)[:, None]
    t = np.arange(128, dtype=np.float32)[None, :]
    M1 = np.maximum(t - s, 0)
    L1 = (s <= t).astype(np.float32)
    M2 = np.maximum(s - t, 0)
    L2 = (s >= t).astype(np.float32)
    IOTA1 = np.broadcast_to(t + 1, (128, 128))
    IOTA2 = np.broadcast_to(128 - t, (128, 128))
    COLA = np.broadcast_to(127 - s, (128, 128))
    COLB = np.broadcast_to(s, (128, 128))
    same = (np.floor(s / 32) == np.floor(t / 32))
    TRIF = (same & (s <= t)).astype(np.float32)
    TRIB = (same & (s >= t)).astype(np.float32)
    BM = (np.floor(s / 64) == np.floor(t / 64)).astype(np.float32)
    cst = np.stack([M1, L1, M2, L2, IOTA1, IOTA2, COLA, COLB, TRIF, TRIB, BM], axis=1).astype(np.float32)
    ind = (np.floor(np.arange(128)[:, None] / 32) == np.arange(4)[None, :]).astype(np.float32)
    return np.ascontiguousarray(cst), np.ascontiguousarray(ind)


def _rope_tables(sample):
    ropec = np.ones((128, 16, 64), np.float32)
    ropes = np.zeros((128, 16, 64), np.float32)
    if sample:
        tt = np.arange(TOK)
        row = (tt // 64).astype(np.float32)
        col = (tt % 64).astype(np.float32)
        inv = (np.float32(10000.0) ** (-np.arange(16, dtype=np.float32) / np.float32(16))).astype(np.float32)
        ar = (row[:, None] * inv[None, :]).astype(np.float32)
        ac = (col[:, None] * inv[None, :]).astype(np.float32)
        c = np.concatenate([np.cos(ar), np.cos(ar), np.cos(ac), np.cos(ac)], axis=1).astype(np.float32)
        s_ = np.concatenate([-np.sin(ar), np.sin(ar), -np.sin(ac), np.sin(ac)], axis=1).astype(np.float32)
        ropec = np.ascontiguousarray(c.reshape(16, 128, 64).transpose(1, 0, 2))
        ropes = np.ascontiguousarray(s_.reshape(16, 128, 64).transpose(1, 0, 2))
    return ropec, ropes


_NC_CACHE = {}


def kernel(x_prompt, x_sample, c, c_ctx, cache_diff_k, cache_diff_v, state_ret_fwd, state_ret_bwd,
           state_hgrn_fwd, state_hgrn_bwd, norm_g, w_ada, b_ada, w_in, w_out, ret_decay_logit,
           diff_qn_g, diff_kn_g, diff_lambda, hgrn_lb_logit, _dbg=False, _units=None, _L=2):
    f32 = np.float32
    bf = ml_dtypes.bfloat16
    A = lambda a: np.ascontiguousarray(np.asarray(a, dtype=f32))
    x_prompt, x_sample, c, c_ctx = A(x_prompt), A(x_sample), A(c), A(c_ctx)
    perm = _unit_perm()
    w_in_p = A(w_in)[:, :, perm]
    win = np.ascontiguousarray(w_in_p.reshape(2, 8, 128, 4352).transpose(0, 2, 1, 3))
    wada = np.ascontiguousarray(A(w_ada).reshape(2, 8, 128, 3072).transpose(0, 2, 1, 3))
    wout = np.ascontiguousarray(A(w_out).reshape(2, 8, 128, 1024).transpose(0, 2, 1, 3))
    normg = np.ascontiguousarray(A(norm_g).reshape(2, 8, 128).transpose(0, 2, 1))
    cst, ind = _constants()
    shared = dict(
        normg=normg, wada=wada, bada=A(b_ada), win=win, wout=wout, rdl=A(ret_decay_logit).reshape(2, 8),
        qng=A(diff_qn_g), kng=A(diff_kn_g), dlam=A(diff_lambda).reshape(2, 256), hlb=A(hgrn_lb_logit).reshape(512),
        identb=np.eye(128, dtype=f32).astype(bf), identf=np.eye(128, dtype=f32), cst=cst, ind=ind,
    )
    ropec_s, ropes_s = _rope_tables(True)
    ropec_p, ropes_p = _rope_tables(False)
    z64 = np.zeros((2, 4, 64, 64), f32)
    zc = np.zeros((2, 512, 512), f32)
    in_maps = []
    for core in range(8):
        m = dict(shared)
        if core < 4:
            b = core
            m["x"] = x_sample[b]
            m["modv"] = np.ascontiguousarray(c[b].reshape(8, 128).T)
            m["ck"] = np.ascontiguousarray(A(cache_diff_k)[b].reshape(2, 512, 512))
            m["cv"] = np.ascontiguousarray(A(cache_diff_v)[b].reshape(2, 512, 512))
            m["srf"], m["srb"] = A(state_ret_fwd)[b], A(state_ret_bwd)[b]
            m["shf"], m["shb"] = A(state_hgrn_fwd)[b], A(state_hgrn_bwd)[b]
            m["ropec"], m["ropes"] = ropec_s, ropes_s
            m["qmask"] = np.zeros((8, 2048), f32).astype(bf)
            m["kmask"] = np.zeros((8, 2560), f32).astype(bf)
            m["keep"] = np.ones((128, 32), f32)
        else:
            j = core - 4
            m["x"] = np.ascontiguousarray(x_prompt[8 * j:8 * j + 8].reshape(2048, 1024))
            m["modv"] = np.ascontiguousarray(c_ctx.reshape(8, 128).T)
            m["ck"], m["cv"] = zc, zc
            m["srf"], m["srb"], m["shf"], m["shb"] = z64, z64, z64, z64
            m["ropec"], m["ropes"] = ropec_p, ropes_p
            seq = np.arange(2048) // 256
            qm = (seq[None, :] == np.arange(8)[:, None]).astype(f32)
            km = np.full((8, 2560), BIGNEG, f32)
            km[:, :2048] = np.where(seq[None, :] == np.arange(8)[:, None], 0.0, BIGNEG)
            m["qmask"] = qm.astype(bf)
            m["kmask"] = km.astype(bf)
            kf = np.array([0.0 if T % 2 == 0 else 1.0 for T in range(16)], f32)
            kb = np.array([0.0 if T % 2 == 1 else 1.0 for T in range(16)], f32)
            m["keep"] = np.ascontiguousarray(np.broadcast_to(np.concatenate([kf, kb])[None, :], (128, 32)))
        in_maps.append(m)

    key = (_L, _dbg, None if _units is None else tuple(sorted(_units)))
    if key not in _NC_CACHE:
        _NC_CACHE[key] = build_program(L=_L, dbg=_dbg, units_enabled=_units)
    nc = _NC_CACHE[key]
    res = run_bass_kernel_spmd(nc, in_maps, core_ids=list(range(8)))
    R = res.results

    y_sample = np.stack([R[b]["y"] for b in range(4)], axis=0)
    y_prompt = np.concatenate([R[4 + j]["y"].reshape(8, 256, 1024) for j in range(4)], axis=0)
    nk = np.concatenate([R[4 + j]["nk"].reshape(2, 8, 256, 4, 2, 64).transpose(1, 0, 2, 3, 4, 5) for j in range(4)], axis=0)
    nv = np.concatenate([R[4 + j]["nv"].reshape(2, 8, 256, 4, 128).transpose(1, 0, 2, 3, 4) for j in range(4)], axis=0)
    st = []
    for name in ("nsrf", "nsrb", "nshf", "nshb"):
        st.append(np.concatenate([R[4 + j][name].transpose(1, 0, 2, 3, 4) for j in range(4)], axis=0))
    outs = (y_prompt, y_sample, np.ascontiguousarray(nk), np.ascontiguousarray(nv), *[np.ascontiguousarray(s) for s in st])
    if _dbg:
        return outs, [R[i]["dbgmix"] for i in range(8)]
    return outs
```

```python
import math
import types
from contextlib import ExitStack

import numpy as np
import ml_dtypes

import concourse.bass as bass
import concourse.mybir as mybir
from concourse.bass_utils import run_bass_kernel_spmd

F32 = mybir.dt.float32
BF16 = mybir.dt.bfloat16
ALU = mybir.AluOpType
AF = mybir.ActivationFunctionType
AX = mybir.AxisListType

ENGS = ["pe", "act", "dve", "pool", "sp"]
NDSEM = 8
SAME_ENG_SYNC = True
import os as _os0
REORDER = _os0.environ.get('REORDER', '1') == '1'
PSUM_EXCL = _os0.environ.get('PSUM_EXCL', '1') == '1'
REORDER_ENGS = _os0.environ.get('REORDER_ENGS', 'pe,act,dve,pool,sp').split(',')

D_MODEL = 1024
NT = 16
TOK = 2048
NKT = 20
EPS = 1e-6
UNIT_W = [512, 512, 640, 640, 512, 512, 512, 512]
UNIT_OFF = [0, 512, 1024, 1664, 2304, 2816, 3328, 3840]
UNIT_CHUNK = [0, 1, 6, 7, 2, 3, 4, 5]
BIGNEG = -30000.0


class Buf:
    __slots__ = ("w", "r", "name", "excl")

    def __init__(self, name="", excl=False):
        self.w = None
        self.r = []
        self.name = name
        self.excl = excl


class Op:
    __slots__ = ("eng", "fn", "waits", "marked", "semval", "is_dma", "dsem", "dval", "cost", "lat", "idx", "prio",
                 "pos", "succs", "nrem", "ready", "fin", "is_bar", "per_eng", "per_dsem")


class _Probe:
    def __init__(self):
        self.rec = None

    def __getattr__(self, name):
        def f(*a, **k):
            self.rec = (name, a, k)
            return self
        return f


def _nfree(ap):
    n = 1
    for d in ap.shape[1:]:
        n *= int(d)
    return n


def _estimate(eng, fn, is_dma):
    pr = _Probe()
    try:
        fn(pr)
        name, a, k = pr.rec
    except Exception:
        name, a, k = "?", (), {}
    out = k.get("out", a[0] if a else None)
    try:
        if is_dma:
            nbytes = _nfree(out) * int(out.shape[0]) * mybir.dt.size(out.dtype)
            return 120.0, 2200.0 + nbytes / 120.0
        if eng == "pe":
            if name == "transpose":
                return 80.0, 80.0
            rhs = k.get("rhs", a[2] if len(a) > 2 else None)
            lhsT = k.get("lhsT", a[1] if len(a) > 1 else None)
            n = _nfree(rhs)
            c = (max(64, n) / 2.4 + 25.0) * 1.25
            if lhsT.dtype == F32:
                c *= 4.0
            return c, c
        n = _nfree(out)
        if eng == "act":
            c = 190.0 + n / 1.2 + (90.0 if k.get("accum_out") is not None else 0.0)
        elif eng == "dve":
            c = 130.0 + n / 0.7
        else:
            c = 700.0 + n / 0.4
        return c, c
    except Exception:
        return 300.0, 300.0


def _freeze(fn):
    if fn.__closure__ is None:
        return fn
    cells = []
    for c in fn.__closure__:
        try:
            cells.append(types.CellType(c.cell_contents))
        except ValueError:
            cells.append(c)
    return types.FunctionType(fn.__code__, fn.__globals__, fn.__name__, fn.__defaults__, tuple(cells))


LAT_X = 200.0
LAT_S = 50.0


class Prog:
    def __init__(self, nc):
        self.nc = nc
        self.all = []
        self.cur_bar = None
        self.since = []
        self.load = {e: 0.0 for e in ENGS}

    def _new(self, eng):
        op = Op()
        op.eng = eng
        op.fn = None
        op.marked = False
        op.semval = None
        op.is_dma = False
        op.dsem = None
        op.dval = None
        op.cost = 0.0
        op.lat = 0.0
        op.is_bar = False
        op.idx = len(self.all)
        op.waits = []
        self.all.append(op)
        return op

    def barrier(self):
        b = self._new("virt")
        b.is_bar = True
        b.waits = list(self.since)
        self.since = []
        self.cur_bar = b
        self.load = {e: 0.0 for e in ENGS}

    def emit(self, eng, fn, reads=(), writes=(), extra=(), is_dma=False):
        op = self._new(eng)
        op.fn = _freeze(fn)
        op.is_dma = is_dma
        op.cost, op.lat = _estimate(eng, op.fn, is_dma)
        self.load[eng] += op.cost
        waits = set()
        if PSUM_EXCL:
            for b in reads:
                if b.excl:
                    for r in b.r:
                        if r.eng != eng:
                            waits.add(r)
        for b in reads:
            if b.w is not None:
                waits.add(b.w)
        for b in writes:
            if b.w is not None:
                waits.add(b.w)
            for r in b.r:
                waits.add(r)
        for w in extra:
            if w is not None:
                waits.add(w)
        if self.cur_bar is not None:
            waits.add(self.cur_bar)
        waits.discard(op)
        op.waits = list(waits)
        for b in reads:
            b.r.append(op)
        for b in writes:
            b.w = op
            b.r = []
        self.since.append(op)
        return op

    def pe(self, fn, reads=(), writes=(), extra=()):
        return self.emit("pe", fn, reads, writes, extra)

    def act(self, fn, reads=(), writes=(), extra=()):
        return self.emit("act", fn, reads, writes, extra)

    def dve(self, fn, reads=(), writes=(), extra=()):
        return self.emit("dve", fn, reads, writes, extra)

    def pool(self, fn, reads=(), writes=(), extra=()):
        return self.emit("pool", fn, reads, writes, extra)

    def on(self, eng, fn, reads=(), writes=(), extra=()):
        if eng == "any":
            return self.any(fn, reads, writes, extra)
        return self.emit(eng, fn, reads, writes, extra)

    def any(self, fn, reads=(), writes=(), extra=()):
        f = _freeze(fn)
        best = None
        for e in ("dve", "pool"):
            c, _ = _estimate(e, f, False)
            tot = self.load[e] + c
            if best is None or tot < best[0]:
                best = (tot, e)
        return self.emit(best[1], fn, reads, writes, extra)

    def dma(self, out, in_, reads=(), writes=(), extra=()):
        return self.emit("sp", lambda e: e.dma_start(out=out, in_=in_), reads, writes, extra, is_dma=True)

    def schedule(self, reorder=True):
        import heapq
        ops = self.all
        for op in ops:
            op.succs = []
        for op in ops:
            for w in op.waits:
                w.succs.append(op)
        for op in reversed(ops):
            m = 0.0
            for s_ in op.succs:
                l_ = s_.prio + (0.0 if op.is_bar else (LAT_S if s_.eng == op.eng else LAT_X))
                if l_ > m:
                    m = l_
            op.prio = m + op.lat
        order = {e: [] for e in ENGS}
        if not reorder:
            for op in ops:
                if not op.is_bar:
                    order[op.eng].append(op)
            return order
        for op in ops:
            op.nrem = len(op.waits)
            op.ready = 0.0
            op.fin = None
        fixed = [e for e in ENGS if e not in REORDER_ENGS]
        lastop = {}
        for op in ops:
            if op.is_bar or op.eng not in fixed:
                continue
            p_ = lastop.get(op.eng)
            if p_ is not None and p_ not in op.waits:
                p_.succs.append(op)
                op.nrem += 1
            lastop[op.eng] = op
        future = {e: [] for e in ENGS}
        now = {e: [] for e in ENGS}
        free = {e: 0.0 for e in ENGS}

        def release(op):
            for s_ in op.succs:
                if op.is_bar:
                    t = op.fin
                elif s_.is_bar:
                    t = op.fin
                elif s_.eng == op.eng:
                    t = op.fin + (0.0 if op.eng == "pe" else LAT_S)
                else:
                    t = op.fin + LAT_X
                if t > s_.ready:
                    s_.ready = t
                s_.nrem -= 1
                if s_.nrem == 0:
                    if s_.is_bar:
                        s_.fin = s_.ready
                        release(s_)
                    else:
                        heapq.heappush(future[s_.eng], (s_.ready, s_.idx, s_))

        import sys
        sys.setrecursionlimit(100000)
        roots = [op for op in ops if op.nrem == 0]
        for op in roots:
            if op.is_bar:
                op.fin = 0.0
                release(op)
            else:
                heapq.heappush(future[op.eng], (0.0, op.idx, op))
        nleft = sum(1 for op in ops if not op.is_bar)
        while nleft > 0:
            best = None
            for e in ENGS:
                f = future[e]
                nw = now[e]
                while f and f[0][0] <= free[e]:
                    r_, i_, o_ = heapq.heappop(f)
                    heapq.heappush(nw, (-o_.prio, o_.idx, o_))
                if nw:
                    st = free[e]
                elif f:
                    st = f[0][0]
                else:
                    continue
                if best is None or st < best[0]:
                    best = (st, e)
            st, e = best
            if now[e]:
                _, _, op = heapq.heappop(now[e])
            else:
                _, _, op = heapq.heappop(future[e])
            if op.is_dma:
                free[e] = st + op.cost
                op.fin = st + op.lat
            else:
                free[e] = st + op.cost
                op.fin = st + op.cost
            order[e].append(op)
            nleft -= 1
            release(op)
        self.est_ns = max(free.values())
        return order

    def build(self, sems, dsems, reorder=True):
        order = self.schedule(reorder)
        if reorder:
            print('[sched] est_us=%.1f' % (self.est_ns / 1e3), {e: len(order[e]) for e in ENGS})
        for e in ENGS:
            for i, op in enumerate(order[e]):
                op.pos = i

        def skip_same(w_eng, eng):
            return w_eng == eng and (eng == "pe" or not SAME_ENG_SYNC)

        dcnt = [0] * NDSEM
        prev_on_sem = [None] * NDSEM
        dma_prev = {}
        nd = 0
        for op in order["sp"]:
            k = nd % NDSEM
            nd += 1
            op.dsem = k
            dcnt[k] += 16
            op.dval = dcnt[k]
            dma_prev[id(op)] = prev_on_sem[k]
            prev_on_sem[k] = op
        final_dvals = list(dcnt)
        for b in self.all:
            if b.is_bar:
                pe_ = {}
                pd_ = {}
                for w in b.waits:
                    if w.is_bar:
                        continue
                    if w.is_dma:
                        if pd_.get(w.dsem, 0) < w.dval:
                            pd_[w.dsem] = w.dval
                    else:
                        c = pe_.get(w.eng)
                        if c is None or c.pos < w.pos:
                            pe_[w.eng] = w
                b.per_eng = pe_
                b.per_dsem = pd_
        for op in self.all:
            if op.is_bar:
                for w in op.per_eng.values():
                    w.marked = True
                continue
            for w in op.waits:
                if w.is_bar or w.is_dma:
                    continue
                if not skip_same(w.eng, op.eng):
                    w.marked = True
        for e in ENGS:
            cnt = 0
            for op in order[e]:
                if not op.is_dma and op.marked:
                    cnt += 1
                    op.semval = cnt

        def run_engine(ename, eng):
            waited = {}

            def need(semkey, sem, val):
                if waited.get(semkey, 0) >= val:
                    return
                eng.wait_ge(sem, val)
                waited[semkey] = val

            for op in order[ename]:
                for w in op.waits:
                    if w.is_bar:
                        for we, wo in w.per_eng.items():
                            if not (we == ename and ename == "pe"):
                                need(("e", we), sems[we], wo.semval)
                        for k, v in w.per_dsem.items():
                            need(("d", k), dsems[k], v)
                    elif w.is_dma:
                        need(("d", w.dsem), dsems[w.dsem], w.dval)
                    elif not skip_same(w.eng, ename):
                        need(("e", w.eng), sems[w.eng], w.semval)
                if op.is_dma:
                    p = dma_prev[id(op)]
                    if p is not None:
                        need(("d", p.dsem), dsems[p.dsem], p.dval)
                ins = op.fn(eng)
                if op.is_dma:
                    ins.then_inc(dsems[op.dsem], 16)
                elif op.marked:
                    ins.then_inc(sems[ename], 1)
            if ename == "sp":
                for k in range(NDSEM):
                    if final_dvals[k] > 0:
                        need(("d", k), dsems[k], final_dvals[k])

        return run_engine


class Ring:
    def __init__(self, tiles):
        self.tiles = tiles
        self.bufs = [Buf() for _ in tiles]
        self.i = 0

    def next(self):
        k = self.i % len(self.tiles)
        self.i += 1
        return self.tiles[k], self.bufs[k]


def build_program(L=2, dbg=False, units_enabled=None):
    nc = bass.Bass("TRN2", target_bir_lowering=False)

    def din(name, shape, dt=F32):
        return nc.dram_tensor(name, list(shape), dt, kind="ExternalInput").ap()

    def dout(name, shape, dt=F32):
        return nc.dram_tensor(name, list(shape), dt, kind="ExternalOutput").ap()

    x_in = din("x", [TOK, D_MODEL])
    modv = din("modv", [128, 8])
    ck_in = din("ck", [2, 512, 512])
    cv_in = din("cv", [2, 512, 512])
    srf_in = din("srf", [2, 4, 64, 64])
    srb_in = din("srb", [2, 4, 64, 64])
    shf_in = din("shf", [2, 4, 64, 64])
    shb_in = din("shb", [2, 4, 64, 64])
    normg_in = din("normg", [2, 128, 8])
    wada_in = din("wada", [2, 128, 8, 3072])
    bada_in = din("bada", [2, 3072])
    win_in = din("win", [2, 128, 8, 4352])
    wout_in = din("wout", [2, 128, 8, 1024])
    rdl_in = din("rdl", [2, 8])
    qng_in = din("qng", [2, 64])
    kng_in = din("kng", [2, 64])
    dlam_in = din("dlam", [2, 256])
    hlb_in = din("hlb", [512])
    ropec_in = din("ropec", [128, 16, 64])
    ropes_in = din("ropes", [128, 16, 64])
    qmask_in = din("qmask", [8, 2048], BF16)
    kmask_in = din("kmask", [8, 2560], BF16)
    keep_in = din("keep", [128, 32])
    identb_in = din("identb", [128, 128], BF16)
    identf_in = din("identf", [128, 128])
    cst_in = din("cst", [128, 11, 128])
    ind_in = din("ind", [128, 4])

    y_out = dout("y", [TOK, D_MODEL])
    nk_out = dout("nk", [2, TOK, 512])
    nv_out = dout("nv", [2, TOK, 512])
    nsrf_out = dout("nsrf", [2, 8, 4, 64, 64])
    nsrb_out = dout("nsrb", [2, 8, 4, 64, 64])
    nshf_out = dout("nshf", [2, 8, 4, 64, 64])
    nshb_out = dout("nshb", [2, 8, 4, 64, 64])
    xs_scr = nc.dram_tensor("xs_scr", [TOK, D_MODEL], F32, kind="Internal").ap()
    dbg_out = dout("dbgmix", [2, 128, 8, TOK], BF16) if dbg else None

    es = ExitStack()
    with es:
        def sb(name, shape, dt):
            return es.enter_context(nc.sbuf_tensor("sb_" + name, list(shape), dt))

        hT = sb("hT", [128, 8, TOK], BF16)
        mixT = sb("mixT", [128, 8, TOK], BF16)
        wstage = sb("wstage", [128, 8 * 512], F32)
        wbf = sb("wbf", [128, 8 * 640], BF16)
        arena = sb("arena", [128, 61440], mybir.dt.uint8)
        cst = sb("cst", [128, 11, 128], F32)
        ind = sb("ind", [128, 4], F32)
        ropec = sb("ropec", [128, 16, 64], F32)
        ropes = sb("ropes", [128, 16, 64], F32)
        identb = sb("identb", [128, 128], BF16)
        identf = sb("identf", [128, 128], F32)
        keep = sb("keep", [128, 32], F32)
        gate_b = sb("gate_b", [128, 1024], F32)
        modt = sb("modt", [128, 8], F32)
        smod = sb("smod", [128, 8], F32)
        normg = sb("normg", [128, 8], F32)
        modT = sb("modT", [128, 2, 8], F32)
        modA = sb("modA", [128, 8], F32)
        small = sb("small", [128, 64], F32)
        rdl = sb("rdl", [128, 8], F32)
        lg = sb("lg", [128, 8], F32)
        g4 = sb("g4", [128, 256], F32)
        lball = sb("lball", [128, 2, 256], F32)
        lb2 = sb("lb2", [128, 256], F32)
        omlb2 = sb("omlb2", [128, 256], F32)
        cneg = sb("cneg", [128, 8], F32)
        zerosb = sb("zerosb", [128, 512], BF16)
        lgrow = sb("lgrow", [128, 4], F32)
        lgrow1 = sb("lgrow1", [128, 4], F32)
        dtm = sb("dtm", [128, 2, 128], F32)
        qd = sb("qd", [128, 2, 128], F32)
        kd = sb("kd", [128, 2, 128], F32)
        e12 = sb("e12", [128, 2, 128], F32)
        stS = sb("stS", [128, 2, 128], F32)

        n_f32_512 = 3
        r512 = Ring([sb("r512_%d" % i, [128, 512], F32) for i in range(n_f32_512)])
        r256 = Ring([sb("r256_%d" % i, [128, 256], F32) for i in range(8)])
        r128 = Ring([sb("r128_%d" % i, [128, 128], F32) for i in range(8)])
        rb256 = Ring([sb("rb256_%d" % i, [128, 256], BF16) for i in range(3)])
        rb128 = Ring([sb("rb128_%d" % i, [128, 128], BF16) for i in range(8)])
        rpt = Ring([sb("rpt_%d" % i, [128, 512], BF16) for i in range(3)])
        rs = Ring([sb("rs_%d" % i, [128, 8], F32) for i in range(16)])

        banks = [es.enter_context(nc.psum_tensor("bank%d" % i, [128, 512], F32)) for i in range(8)]
        banksb = [b.bitcast(BF16) for b in banks]
        bankB = [Buf("bank%d" % i, excl=True) for i in range(8)]

        sems = {e: es.enter_context(nc.semaphore("s_" + e)) for e in ENGS}
        dsems = [es.enter_context(nc.semaphore("d%d" % k)) for k in range(NDSEM)]

        P = Prog(nc)

        B_hT = [Buf() for _ in range(NT)]
        B_mixT = [[Buf() for _ in range(NT)] for _ in range(8)]
        B_wstage = Buf()
        B_wbf = Buf()
        B_cst = Buf()
        B_misc = Buf()
        B_gate = Buf()
        B_modAB = Buf()
        B_pair = Buf()
        B_stS = [Buf(), Buf()]

        def arena_view(off_bytes, shape, dt):
            n = 1
            for s in shape[1:]:
                n *= s
            esz = 2 if dt == BF16 else 4
            a = arena[:, off_bytes:off_bytes + n * esz].bitcast(dt)
            if len(shape) == 2:
                return a
            if len(shape) == 3:
                return a.rearrange("p (a b) -> p a b", a=shape[1], b=shape[2])
            if len(shape) == 4:
                return a.rearrange("p (a b c) -> p a b c", a=shape[1], b=shape[2], c=shape[3])
            raise ValueError

        xring = Ring([arena_view(36864 + i * 4096, [128, 1024], F32) for i in range(3)])
        xhring = Ring([arena_view(49152 + i * 2048, [128, 1024], BF16) for i in range(2)])
        B_xs = [Buf() for _ in range(NT)]

        M1, L1, M2, L2, IOTA1, IOTA2, COLA, COLB, TRIF, TRIB, BM = [cst[:, i, :] for i in range(11)]

        P.dma(cst[:], cst_in, writes=[B_cst])
        P.dma(ind[:], ind_in, writes=[B_cst])
        P.dma(ropec[:], ropec_in, writes=[B_cst])
        P.dma(ropes[:], ropes_in, writes=[B_cst])
        P.dma(identb[:], identb_in, writes=[B_cst])
        P.dma(identf[:], identf_in, writes=[B_cst])
        P.dma(keep[:], keep_in, writes=[B_cst])
        P.dma(modt[:], modv, writes=[B_cst])
        hlb, hlbb = r512.next()
        P.dma(hlb[:], hlb_in.partition_broadcast(128), writes=[hlbb])
        P.pool(lambda e: e.memset(cneg[:], -0.5), writes=[B_cst])
        P.pool(lambda e: e.memset(zerosb[:], 0.0), writes=[B_cst])
        t_, tb_ = rs.next()
        P.act(lambda e, t_=t_: e.activation(out=t_[:, 0:8], in_=modt[:], func=AF.Tanh, scale=0.5), reads=[B_cst], writes=[tb_])
        P.dve(lambda e, t_=t_: e.scalar_tensor_tensor(out=smod[:], in0=t_[:, 0:8], scalar=1.0, in1=modt[:], op0=ALU.add, op1=ALU.mult),
              reads=[tb_, B_cst], writes=[B_cst])
        P.dve(lambda e: e.tensor_scalar(out=smod[:], in0=smod[:], scalar1=0.5, scalar2=None, op0=ALU.mult), reads=[B_cst], writes=[B_cst])
        P.act(lambda e: e.activation(out=hlb[:], in_=hlb[:], func=AF.Exp), reads=[hlbb], writes=[hlbb])
        den_, denb_ = r256.next()
        P.dve(lambda e: e.tensor_tensor(out=den_[:], in0=hlb[:, 0:256], in1=hlb[:, 256:512], op=ALU.add), reads=[hlbb], writes=[denb_])
        P.dve(lambda e: e.reciprocal(out=den_[:], in_=den_[:]), reads=[denb_], writes=[denb_])
        P.dve(lambda e: e.tensor_tensor(out=hlb[:, 0:256], in0=hlb[:, 0:256], in1=den_[:], op=ALU.mult), reads=[hlbb, denb_], writes=[hlbb])
        P.dve(lambda e: e.tensor_tensor(out=hlb[:, 256:512], in0=hlb[:, 256:512], in1=den_[:], op=ALU.mult), reads=[hlbb, denb_], writes=[hlbb])
        P.dve(lambda e: e.tensor_tensor(out=lball[:, 0, :], in0=hlb[:, 0:256], in1=hlb[:, 0:256], op=ALU.subtract), reads=[hlbb], writes=[B_cst])
        P.dve(lambda e: e.tensor_tensor(out=lball[:, 1, :], in0=hlb[:, 0:256], in1=hlb[:, 256:512], op=ALU.add), reads=[hlbb], writes=[B_cst])
        P.dve(lambda e: e.tensor_tensor(out=lball[:, 1, :], in0=lball[:, 1, :], in1=hlb[:, 0:256], op=ALU.subtract), reads=[hlbb, B_cst], writes=[B_cst])

        def rstd_from_ss(ss_ap, ssb, n, mult, add):
            t1, b1 = rs.next()
            P.dve(lambda e: e.tensor_scalar(out=t1[:, 0:n], in0=ss_ap, scalar1=mult, scalar2=add, op0=ALU.mult, op1=ALU.add),
                  reads=[ssb], writes=[b1])
            t2, b2 = rs.next()
            P.pool(lambda e: e.tensor_tensor(out=t2[:, 0:n], in0=t1[:, 0:n], in1=cneg[:, 0:n], op=ALU.pow), reads=[b1, B_cst], writes=[b2])
            return t2, b2

        def rope(src, srcbufs, dst, dstbufs, T, G, eng_a, eng_b):
            W = G * 64
            t1, b1 = r256.next()
            t2, b2 = r256.next()
            cv_ = ropec[:, T, :]
            sv_ = ropes[:, T, :].rearrange("p (h j i) -> p h j i", h=2, j=2, i=16)
            src3 = src.rearrange("p (g d) -> p g d", g=G, d=64)
            src5 = src.rearrange("p (g h j i) -> p g h j i", g=G, h=2, j=2, i=16)
            t13 = t1[:, 0:W].rearrange("p (g d) -> p g d", g=G, d=64)
            t25 = t2[:, 0:W].rearrange("p (g h j i) -> p g h j i", g=G, h=2, j=2, i=16)
            P.on(eng_a, lambda e: e.tensor_tensor(out=t13, in0=src3, in1=cv_.unsqueeze(1).to_broadcast([128, G, 64]), op=ALU.mult),
                 reads=list(srcbufs) + [B_cst], writes=[b1])
            P.on(eng_b, lambda e: e.tensor_tensor(out=t25[:, :, :, 0, :], in0=src5[:, :, :, 1, :],
                                                  in1=sv_[:, :, 0, :].unsqueeze(1).to_broadcast([128, G, 2, 16]), op=ALU.mult),
                 reads=list(srcbufs) + [B_cst], writes=[b2])
            P.on(eng_b, lambda e: e.tensor_tensor(out=t25[:, :, :, 1, :], in0=src5[:, :, :, 0, :],
                                                  in1=sv_[:, :, 1, :].unsqueeze(1).to_broadcast([128, G, 2, 16]), op=ALU.mult),
                 reads=list(srcbufs) + [B_cst], writes=[b2])
            P.on(eng_a, lambda e: e.tensor_tensor(out=dst, in0=t1[:, 0:W], in1=t2[:, 0:W], op=ALU.add), reads=[b1, b2], writes=list(dstbufs))

        def tslice(T):
            return slice(T * 128, (T + 1) * 128)

        def setup_layer(l):
            stg = [wstage[:, 0:4096].rearrange("p (k n) -> p k n", k=8, n=512), arena_view(16384, [128, 8, 512], F32)]
            stgB = [B_wstage, Buf()]
            smb = arena_view(32768, [128, 8, 128], F32)
            B_smb = Buf()
            P.dve(lambda e: e.tensor_copy(out=smb, in_=smod[:].unsqueeze(2).to_broadcast([128, 8, 128])), reads=[B_cst], writes=[B_smb])
            P.dma(normg[:], normg_in[l], writes=[B_misc])
            P.dma(rdl[:], rdl_in[l].partition_broadcast(128), writes=[B_misc])
            dlam, dlamb = r256.next()
            P.dma(dlam[:], dlam_in[l].partition_broadcast(128), writes=[dlamb])
            P.dma(g4[:, 0:64], qng_in[l].partition_broadcast(128), writes=[B_misc])
            P.dma(g4[:, 64:128], qng_in[l].partition_broadcast(128), writes=[B_misc])
            P.dma(g4[:, 128:192], kng_in[l].partition_broadcast(128), writes=[B_misc])
            P.dma(g4[:, 192:256], kng_in[l].partition_broadcast(128), writes=[B_misc])
            for cb in range(6):
                st_, stb_ = stg[cb % 2], stgB[cb % 2]
                P.dma(st_, wada_in[l][:, :, cb * 512:(cb + 1) * 512], writes=[stb_])
                bt, btb = r512.next()
                P.dma(bt[:], bada_in[l][cb * 512:(cb + 1) * 512].partition_broadcast(128), writes=[btb])
                bk = cb % 4
                for kc in range(8):
                    P.pe(lambda e, kc=kc, st_=st_, bk=bk: e.matmul(banks[bk][:, 0:512], lhsT=smb[:, kc, :], rhs=st_[:, kc, :],
                                                                     start=(kc == 0), stop=(kc == 7)),
                         reads=[B_smb, stb_], writes=[bankB[bk]])
                if cb >= 4:
                    P.dve(lambda e, bk=bk, bt=bt, cb=cb: e.tensor_tensor(out=gate_b[:, (cb - 4) * 512:(cb - 3) * 512], in0=banks[bk][:, 0:512],
                                                                          in1=bt[:], op=ALU.add),
                          reads=[bankB[bk], btb], writes=[B_gate])
                else:
                    P.dve(lambda e, bk=bk, bt=bt: e.tensor_tensor(out=bt[:], in0=banks[bk][:, 0:512], in1=bt[:], op=ALU.add),
                          reads=[bankB[bk], btb], writes=[btb])
                    which = cb // 2
                    tb = 4 + (cb % 2)
                    for jj in range(4):
                        kc = (cb % 2) * 4 + jj
                        P.pe(lambda e, jj=jj, bt=bt, tb=tb: e.transpose(banks[tb][:, jj * 128:(jj + 1) * 128], bt[:, jj * 128:(jj + 1) * 128], identf[:]),
                             reads=[btb, B_cst], writes=[bankB[tb]])
                        P.act(lambda e, jj=jj, tb=tb, which=which, kc=kc: e.copy(out=modT[:, which, kc:kc + 1], in_=banks[tb][:, jj * 128:jj * 128 + 1]),
                              reads=[bankB[tb]], writes=[B_modAB])
            P.dve(lambda e: e.scalar_tensor_tensor(out=modA[:], in0=modT[:, 1, :], scalar=1.0, in1=normg[:], op0=ALU.add, op1=ALU.mult),
                  reads=[B_modAB, B_misc], writes=[B_modAB])
            pr, prb = r256.next()
            P.dve(lambda e: e.tensor_tensor(out=pr[:, 0:64], in0=dlam[:, 0:64], in1=dlam[:, 64:128], op=ALU.mult), reads=[dlamb], writes=[prb])
            P.dve(lambda e: e.tensor_tensor(out=pr[:, 64:128], in0=dlam[:, 128:192], in1=dlam[:, 192:256], op=ALU.mult), reads=[dlamb], writes=[prb])
            s12, s12b = rs.next()
            P.dve(lambda e: e.tensor_reduce(out=s12[:, 0:2], in_=pr[:, 0:128].rearrange("p (a d) -> p a d", a=2, d=64), axis=AX.X, op=ALU.add),
                  reads=[prb], writes=[s12b])
            P.act(lambda e: e.activation(out=s12[:, 0:2], in_=s12[:, 0:2], func=AF.Exp), reads=[s12b], writes=[s12b])
            lam_init = 0.8 - 0.6 * math.exp(-0.3 * l)
            P.dve(lambda e: e.tensor_tensor(out=small[:, 0:1], in0=s12[:, 1:2], in1=s12[:, 0:1], op=ALU.subtract), reads=[s12b], writes=[B_misc])
            P.dve(lambda e: e.tensor_scalar(out=small[:, 0:1], in0=small[:, 0:1], scalar1=-lam_init, scalar2=None, op0=ALU.add),
                  reads=[B_misc], writes=[B_misc])
            P.act(lambda e: e.activation(out=lg[:], in_=rdl[:], func=AF.Exp, scale=-1.0), reads=[B_misc], writes=[B_misc])
            P.act(lambda e: e.activation(out=lg[:], in_=lg[:], func=AF.Ln, bias=1.0), reads=[B_misc], writes=[B_misc])
            P.dve(lambda e: e.tensor_scalar(out=lg[:], in0=lg[:], scalar1=-1.0, scalar2=None, op0=ALU.mult), reads=[B_misc], writes=[B_misc])

        def norm_phase(l):
            src = x_in if l == 0 else xs_scr
            for T in range(NT):
                xt, xb_ = xring.next()
                P.dma(xt[:], src[T * 128:(T + 1) * 128, :], reads=([B_xs[T]] if l > 0 else []), writes=[xb_])
                xh, xhb = xhring.next()
                ss, ssb = rs.next()
                P.act(lambda e, xt=xt, xh=xh, ss=ss: e.activation(out=xh[:], in_=xt[:], func=AF.Square, accum_out=ss[:, 0:1]),
                      reads=[xb_], writes=[xhb, ssb])
                rstd, rb_ = rstd_from_ss(ss[:, 0:1], ssb, 1, 1.0 / D_MODEL, EPS)
                P.dve(lambda e, xt=xt, xh=xh, rstd=rstd: e.tensor_scalar(out=xh[:], in0=xt[:], scalar1=rstd[:, 0:1], scalar2=None, op0=ALU.mult),
                      reads=[xb_, rb_], writes=[xhb])
                bk = 6 + (T % 2)
                for kc in range(8):
                    P.pe(lambda e, kc=kc, xh=xh, bk=bk: e.transpose(banksb[bk][:, kc * 128:(kc + 1) * 128], xh[:, kc * 128:(kc + 1) * 128], identb[:]),
                         reads=[xhb, B_cst], writes=[bankB[bk]])
                for kc in range(8):
                    if kc % 2 == 0:
                        P.dve(lambda e, kc=kc, bk=bk, T=T: e.tensor_scalar(out=hT[:, kc, tslice(T)], in0=banksb[bk][:, kc * 128:(kc + 1) * 128],
                                                                            scalar1=modA[:, kc:kc + 1], scalar2=modT[:, 0, kc:kc + 1],
                                                                            op0=ALU.mult, op1=ALU.add),
                              reads=[bankB[bk], B_modAB], writes=[B_hT[T]])
                    else:
                        P.act(lambda e, kc=kc, bk=bk, T=T: e.activation(out=hT[:, kc, tslice(T)], in_=banksb[bk][:, kc * 128:(kc + 1) * 128],
                                                                         func=AF.Identity, scale=modA[:, kc:kc + 1], bias=modT[:, 0, kc:kc + 1]),
                              reads=[bankB[bk], B_modAB], writes=[B_hT[T]])

        def load_unit_weights(l, u):
            W = UNIT_W[u]
            wb = wbf[:, 0:8 * W].rearrange("p (k n) -> p k n", k=8, n=W)
            engs = (["dve", "act"] * 4) if u < 4 else (["dve"] * 8)
            for (a, b) in [(0, 512)] + ([(512, W)] if W > 512 else []):
                wd = b - a
                ws = wstage[:, 0:8 * wd].rearrange("p (k n) -> p k n", k=8, n=wd)
                P.dma(ws, win_in[l][:, :, UNIT_OFF[u] + a:UNIT_OFF[u] + b], writes=[B_wstage])
                for kc in range(8):
                    if engs[kc] == "act":
                        P.act(lambda e, kc=kc, ws=ws, a=a, b=b: e.copy(out=wb[:, kc, a:b], in_=ws[:, kc, :]), reads=[B_wstage], writes=[B_wbf])
                    else:
                        P.on(engs[kc], lambda e, kc=kc, ws=ws, a=a, b=b: e.tensor_copy(out=wb[:, kc, a:b], in_=ws[:, kc, :]), reads=[B_wstage], writes=[B_wbf])
            return wb

        def project(wb, T, bk, c0, c1):
            for kc in range(8):
                P.pe(lambda e, kc=kc: e.matmul(banks[bk][:, 0:c1 - c0], lhsT=hT[:, kc, tslice(T)], rhs=wb[:, kc, c0:c1],
                                               start=(kc == 0), stop=(kc == 7)),
                     reads=[B_hT[T], B_wbf], writes=[bankB[bk]])

        def mixed_out(mt, mtb, chunk, T, tbank, eng="act"):
            P.pe(lambda e: e.transpose(banksb[tbank][:, 0:128], mt, identb[:]), reads=[mtb, B_cst], writes=[bankB[tbank]])
            if eng == "act":
                P.act(lambda e: e.copy(out=mixT[:, chunk, tslice(T)], in_=banksb[tbank][:, 0:128]), reads=[bankB[tbank]], writes=[B_mixT[chunk][T]])
            else:
                P.dve(lambda e: e.tensor_copy(out=mixT[:, chunk, tslice(T)], in_=banksb[tbank][:, 0:128]), reads=[bankB[tbank]], writes=[B_mixT[chunk][T]])

        def diff_unit(l, h, DB):
            u = 4 + h
            chunk = UNIT_CHUNK[u]
            if h == 0:
                P.barrier()
            par = h % 2
            base = par * 27728
            QT = arena_view(base + 0, [128, 2, TOK], BF16)
            KT = arena_view(base + 8192, [128, 2, 2560], BF16)
            V = arena_view(base + 18432, [128, NKT, 130], BF16)
            sg = arena_view(base + 23632, [128, NT, 128], BF16)
            ckst = arena_view(55456, [128, 4, 128], F32)
            cvst = arena_view(57504, [128, 4, 128], F32)
            ckb = arena_view(59552, [128, 4, 128], BF16)
            B_QT, B_KT, B_V, B_sg, B_qm = DB["set"][par]
            B_ck = DB["ck"]
            wb = load_unit_weights(l, u)
            import os as _os
            DD0 = _os.environ.get('DIFFDBG', '')
            if 'nomask' not in DD0:
                for c in range(2):
                    P.dma(QT[64:72, c, :], qmask_in, writes=[B_qm])
                    P.dma(KT[64:72, c, :], kmask_in, writes=[B_qm])
            if 'noctx' not in DD0:
                P.dma(ckst, ck_in[l].rearrange("(t p) n -> p t n", p=128)[:, :, h * 128:(h + 1) * 128], writes=[B_ck])
                P.dma(cvst, cv_in[l].rearrange("(t p) n -> p t n", p=128)[:, :, h * 128:(h + 1) * 128], writes=[B_ck])
                P.pool(lambda e: e.memset(V[:, :, 128:130], 1.0), writes=B_V)
                P.dve(lambda e: e.tensor_copy(out=ckb, in_=ckst), reads=[B_ck], writes=[B_ck])
                for pt in range(4):
                    bk = 5
                    for c in range(2):
                        P.pe(lambda e, pt=pt, c=c, bk=bk: e.transpose(banksb[bk][0:64, c * 128:(c + 1) * 128], ckb[:, pt, c * 64:(c + 1) * 64], identb[:]),
                             reads=[B_ck, B_cst], writes=[bankB[bk]])
                    P.act(lambda e, pt=pt, bk=bk: e.copy(out=KT[0:64, :, 2048 + pt * 128:2048 + (pt + 1) * 128],
                                                         in_=banksb[bk][0:64, 0:256].rearrange("p (c t) -> p c t", c=2, t=128)),
                          reads=[bankB[bk]], writes=[B_KT[16 + pt]])
                    P.any(lambda e, pt=pt: e.tensor_copy(out=V[:, 16 + pt, 0:128], in_=cvst[:, pt, :]), reads=[B_ck], writes=[B_V[16 + pt]])

            if 'noA' in DD0:
                return
            for T in range(NT):
                zb = 6 + (T % 2)
                project(wb, T, zb, 0, 512)
                z = banks[zb]
                sq, sqb = r256.next()
                P.act(lambda e, z=z, sq=sq: e.activation(out=sq[:], in_=z[:, 0:256], func=AF.Square), reads=[bankB[zb]], writes=[sqb])
                ss, ssb = rs.next()
                P.dve(lambda e, sq=sq, ss=ss: e.tensor_reduce(out=ss[:, 0:4], in_=sq[:].rearrange("p (g d) -> p g d", g=4, d=64), axis=AX.X, op=ALU.add),
                      reads=[sqb], writes=[ssb])
                rstd, rb_ = rstd_from_ss(ss[:, 0:4], ssb, 4, 1.0 / 64, EPS)
                nq, nqb = r256.next()
                P.dve(lambda e, z=z, nq=nq, rstd=rstd: e.tensor_tensor(out=nq[:].rearrange("p (g d) -> p g d", g=4, d=64),
                                                                      in0=z[:, 0:256].rearrange("p (g d) -> p g d", g=4, d=64),
                                                                      in1=rstd[:, 0:4].unsqueeze(2).to_broadcast([128, 4, 64]), op=ALU.mult),
                      reads=[bankB[zb], rb_], writes=[nqb])
                P.any(lambda e, nq=nq: e.tensor_tensor(out=nq[:], in0=nq[:], in1=g4[:], op=ALU.mult), reads=[nqb, B_misc], writes=[nqb])
                if 'nonk' not in DD0:
                    P.dma(nk_out[l][T * 128:(T + 1) * 128, h * 128:(h + 1) * 128], nq[:, 128:256], reads=[nqb])
                rt, rtb = rb256.next()
                import os as _os
                rope(nq[:], [nqb], rt[:], [rtb], T, 4, "any", "any")
                if 'noT' not in DD0:
                    tb = 5
                    for g in range(4):
                        P.pe(lambda e, g=g, rt=rt, tb=tb: e.transpose(banksb[tb][0:64, g * 128:(g + 1) * 128], rt[:, g * 64:(g + 1) * 64], identb[:]),
                             reads=[rtb, B_cst], writes=[bankB[tb]])
                    if 'noTq' not in DD0:
                      P.act(lambda e, tb=tb, T=T: e.copy(out=QT[0:64, :, tslice(T)], in_=banksb[tb][0:64, 0:256].rearrange("p (c t) -> p c t", c=2, t=128)),
                          reads=[bankB[tb]], writes=[B_QT[T]])
                    if 'noTk' not in DD0:
                      P.act(lambda e, tb=tb, T=T: e.copy(out=KT[0:64, :, tslice(T)], in_=banksb[tb][0:64, 256:512].rearrange("p (c t) -> p c t", c=2, t=128)),
                          reads=[bankB[tb]], writes=[B_KT[T]])
                vst, vstb = r128.next()
                P.dve(lambda e, z=z, vst=vst: e.tensor_copy(out=vst[:], in_=z[:, 256:384]), reads=[bankB[zb]], writes=[vstb])
                if 'nonk' not in DD0:
                    P.dma(nv_out[l][T * 128:(T + 1) * 128, h * 128:(h + 1) * 128], vst[:], reads=[vstb])
                P.any(lambda e, vst=vst, T=T: e.tensor_copy(out=V[:, T, 0:128], in_=vst[:]), reads=[vstb], writes=[B_V[T]])
                th, thb = r128.next()
                P.act(lambda e, z=z, th=th: e.activation(out=th[:], in_=z[:, 384:512], func=AF.Tanh, scale=0.5), reads=[bankB[zb]], writes=[thb])
                P.dve(lambda e, z=z, th=th, T=T: e.scalar_tensor_tensor(out=sg[:, T, :], in0=th[:], scalar=1.0, in1=z[:, 384:512], op0=ALU.add, op1=ALU.mult),
                      reads=[thb, bankB[zb]], writes=[B_sg[T]])
            import os as _os
            DD = _os.environ.get('DIFFDBG', '')
            if 'noB' in DD:
                return
            KR = 64 if 'k64' in DD else 72
            lam_init = 0.8 - 0.6 * math.exp(-0.3 * l)
            c0 = 0.5 * (1.0 - lam_init)
            OB = [2, 3, 4]

            def acc(c, qi):
                a = c * 4 + qi
                return OB[a // 3], (a % 3) * 160

            for qb in range(4):
                for k in OB:
                    P.pe(lambda e, k=k: e.matmul(banks[k][:, 0:512], lhsT=zerosb[:, 0:128], rhs=zerosb[:, 0:512], start=True, stop=False, skip_group_check=True),
                         reads=[B_cst], writes=[bankB[k]])
                steps = [(c, kt) for c in range(2) for kt in range(NKT)]
                pts = {}

                def emit_st(i):
                    c, kt = steps[i]
                    sbk = i % 2
                    P.pe(lambda e, c=c, kt=kt, sbk=sbk: e.matmul(banks[sbk][:, 0:512], lhsT=KT[0:KR, c, kt * 128:(kt + 1) * 128],
                                                                 rhs=QT[0:KR, c, qb * 512:(qb + 1) * 512], start=True, stop=True),
                         reads=[B_KT[kt], B_qm] + B_QT[qb * 4:qb * 4 + 4], writes=[bankB[sbk]])
                    pt_, ptb = rpt.next()
                    P.act(lambda e, sbk=sbk, pt_=pt_: e.activation(out=pt_[:], in_=banks[sbk][:, 0:512], func=AF.Exp, scale=0.125),
                          reads=[bankB[sbk]], writes=[ptb])
                    pts[i] = (pt_, ptb)

                def emit_pv(i):
                    c, kt = steps[i]
                    pt_, ptb = pts.pop(i)
                    for qi in range(4):
                        bk, off = acc(c, qi)
                        P.pe(lambda e, qi=qi, bk=bk, off=off, pt_=pt_, kt=kt: e.matmul(banks[bk][:, off:off + 129], lhsT=pt_[:, qi * 128:(qi + 1) * 128],
                                                                                       rhs=V[:, kt, 0:129], start=False, stop=(kt == NKT - 1), skip_group_check=True),
                             reads=[ptb, B_V[kt]], writes=[bankB[bk]])

                emit_st(0)
                for i in range(len(steps)):
                    if i + 1 < len(steps):
                        emit_st(i + 1)
                    emit_pv(i)
                for qi in range(4):
                    T = qb * 4 + qi
                    b0, o0 = acc(0, qi)
                    b1, o1 = acc(1, qi)
                    r01, r01b = rs.next()
                    P.dve(lambda e, r01=r01: e.reciprocal(out=r01[:, 0:1], in_=banks[b0][:, o0 + 128:o0 + 129]), reads=[bankB[b0]], writes=[r01b])
                    P.dve(lambda e, r01=r01: e.reciprocal(out=r01[:, 1:2], in_=banks[b1][:, o1 + 128:o1 + 129]), reads=[bankB[b1]], writes=[r01b])
                    P.dve(lambda e, r01=r01: e.tensor_tensor(out=r01[:, 1:2], in0=r01[:, 1:2], in1=small[:, 0:1], op=ALU.mult), reads=[r01b, B_misc], writes=[r01b])
                    d, db = r128.next()
                    P.dve(lambda e, d=d, r01=r01: e.tensor_scalar(out=d[:], in0=banks[b0][:, o0:o0 + 128], scalar1=r01[:, 0:1], scalar2=None, op0=ALU.mult),
                          reads=[bankB[b0], r01b], writes=[db])
                    P.dve(lambda e, d=d, r01=r01: e.scalar_tensor_tensor(out=d[:], in0=banks[b1][:, o1:o1 + 128], scalar=r01[:, 1:2], in1=d[:],
                                                                        op0=ALU.mult, op1=ALU.add),
                          reads=[bankB[b1], r01b, db], writes=[db])
                    jk, jkb = r128.next()
                    ss, ssb = rs.next()
                    P.act(lambda e, d=d, jk=jk, ss=ss: e.activation(out=jk[:], in_=d[:], func=AF.Square, accum_out=ss[:, 0:1]), reads=[db], writes=[jkb, ssb])
                    rstd, rb_ = rstd_from_ss(ss[:, 0:1], ssb, 1, 1.0 / (128 * c0 * c0), EPS / (c0 * c0))
                    mt, mtb = rb128.next()
                    P.dve(lambda e, d=d, rstd=rstd, mt=mt, T=T: e.scalar_tensor_tensor(out=mt[:], in0=d[:], scalar=rstd[:, 0:1], in1=sg[:, T, :],
                                                                                      op0=ALU.mult, op1=ALU.mult),
                          reads=[db, rb_, B_sg[T]], writes=[mtb])
                    mixed_out(mt[:], mtb, chunk, T, 5, eng="dve")

        def ret_unit(l, p, RB):
            u = p
            chunk = UNIT_CHUNK[u]
            if p == 0:
                P.barrier()
            base = p * 28672
            qT = arena_view(base + 0, [128, TOK], BF16)
            kT = arena_view(base + 4096, [128, TOK], BF16)
            ktok = arena_view(base + 8192, [128, NT, 128], BF16)
            v = arena_view(base + 12288, [128, NT, 128], BF16)
            sg = arena_view(base + 16384, [128, NT, 128], BF16)
            Sbf = arena_view(base + 20480, [128, 2, NT, 128], BF16)
            if p == 0:
                dtm_, qd_, kd_, stS_, lgrow_ = dtm, qd, kd, stS, lgrow
            else:
                dtm_ = arena_view(57344, [128, 2, 128], F32)
                qd_ = arena_view(58368, [128, 2, 128], F32)
                kd_ = arena_view(59392, [128, 2, 128], F32)
                stS_ = arena_view(60416, [128, 2, 128], F32)
                lgrow_ = lgrow1
            B_pair_, B_stS_ = RB[p]
            B_q = [Buf() for _ in range(NT)]
            B_k = [Buf() for _ in range(NT)]
            B_kt = [Buf() for _ in range(NT)]
            B_v = [Buf() for _ in range(NT)]
            B_sg = [Buf() for _ in range(NT)]
            B_S = [[Buf() for _ in range(NT)] for _ in range(2)]
            wb = load_unit_weights(l, u)
            for d_ in range(2):
                for hh in range(2):
                    col = d_ * 4 + 2 * p + hh
                    P.dve(lambda e, d_=d_, hh=hh, col=col: e.tensor_copy(out=lgrow_[hh * 64:(hh + 1) * 64, d_:d_ + 1], in_=lg[hh * 64:(hh + 1) * 64, col:col + 1]),
                          reads=[B_misc], writes=[B_pair_])
            for hh in range(2):
                e12t, e12b = r256.next()
                cf = 2 * p + hh
                cb_ = 4 + 2 * p + hh
                P.act(lambda e, cf=cf: e.activation(out=e12t[:, 0:128], in_=M1, func=AF.Exp, scale=lg[:, cf:cf + 1]), reads=[e12b, B_cst, B_misc], writes=[e12b, B_pair_])
                P.act(lambda e, cb_=cb_: e.activation(out=e12t[:, 128:256], in_=M2, func=AF.Exp, scale=lg[:, cb_:cb_ + 1]), reads=[e12b, B_cst, B_misc], writes=[e12b, B_pair_])
                P.dve(lambda e: e.tensor_tensor(out=e12t[:, 0:128], in0=e12t[:, 0:128], in1=L1, op=ALU.mult), reads=[e12b, B_pair_, B_cst], writes=[e12b, B_pair_])
                P.dve(lambda e: e.tensor_tensor(out=e12t[:, 128:256], in0=e12t[:, 128:256], in1=L2, op=ALU.mult), reads=[e12b, B_pair_, B_cst], writes=[e12b, B_pair_])
                P.dve(lambda e, hh=hh: e.tensor_tensor(out=dtm_[:, hh, :], in0=e12t[:, 0:128], in1=e12t[:, 128:256], op=ALU.add), reads=[e12b, B_pair_], writes=[e12b, B_pair_])
                P.act(lambda e, hh=hh, cf=cf: e.activation(out=kd_[:, 0, hh * 64:(hh + 1) * 64], in_=COLA[:, 0:64], func=AF.Exp, scale=lg[:, cf:cf + 1]),
                      reads=[B_cst, B_misc], writes=[B_pair_])
                P.act(lambda e, hh=hh, cb_=cb_: e.activation(out=kd_[:, 1, hh * 64:(hh + 1) * 64], in_=COLB[:, 0:64], func=AF.Exp, scale=lg[:, cb_:cb_ + 1]),
                      reads=[B_cst, B_misc], writes=[B_pair_])
            P.dve(lambda e: e.tensor_scalar(out=kd_[:], in0=kd_[:], scalar1=0.125, scalar2=None, op0=ALU.mult), reads=[B_pair_], writes=[B_pair_])
            P.act(lambda e: e.activation(out=qd_[:, 0, :], in_=IOTA1, func=AF.Exp, scale=lgrow_[:, 0:1]), reads=[B_cst, B_pair_], writes=[B_pair_])
            P.act(lambda e: e.activation(out=qd_[:, 1, :], in_=IOTA2, func=AF.Exp, scale=lgrow_[:, 1:2]), reads=[B_cst, B_pair_], writes=[B_pair_])
            P.act(lambda e: e.activation(out=lgrow_[:, 2:4], in_=lgrow_[:, 0:2], func=AF.Exp, scale=128.0), reads=[B_pair_], writes=[B_pair_])
            for T in range(NT):
                zb = T % 4
                project(wb, T, zb, 0, 512)
                z = banks[zb]
                rq, rqb = rb128.next()
                t1, b1 = r256.next()
                t2, b2 = r256.next()
                cv_ = ropec[:, T, :]
                sv_ = ropes[:, T, :].rearrange("p (h j i) -> p h j i", h=2, j=2, i=16)
                src3 = z[:, 0:256].rearrange("p (g d) -> p g d", g=4, d=64)
                src5 = z[:, 0:256].rearrange("p (g h j i) -> p g h j i", g=4, h=2, j=2, i=16)
                t13 = t1[:].rearrange("p (g d) -> p g d", g=4, d=64)
                t25 = t2[:].rearrange("p (g h j i) -> p g h j i", g=4, h=2, j=2, i=16)
                P.dve(lambda e, t13=t13, src3=src3, cv_=cv_: e.tensor_tensor(out=t13, in0=src3, in1=cv_.unsqueeze(1).to_broadcast([128, 4, 64]), op=ALU.mult),
                      reads=[bankB[zb], B_cst], writes=[b1])
                P.dve(lambda e, t25=t25, src5=src5, sv_=sv_: e.tensor_tensor(out=t25[:, :, :, 0, :], in0=src5[:, :, :, 1, :],
                                                                             in1=sv_[:, :, 0, :].unsqueeze(1).to_broadcast([128, 4, 2, 16]), op=ALU.mult),
                      reads=[bankB[zb], B_cst], writes=[b2])
                P.dve(lambda e, t25=t25, src5=src5, sv_=sv_: e.tensor_tensor(out=t25[:, :, :, 1, :], in0=src5[:, :, :, 0, :],
                                                                             in1=sv_[:, :, 1, :].unsqueeze(1).to_broadcast([128, 4, 2, 16]), op=ALU.mult),
                      reads=[bankB[zb], B_cst], writes=[b2])
                P.any(lambda e, t1=t1, t2=t2, rq=rq: e.tensor_tensor(out=rq[:], in0=t1[:, 0:128], in1=t2[:, 0:128], op=ALU.add), reads=[b1, b2], writes=[rqb])
                P.any(lambda e, t1=t1, t2=t2, T=T: e.tensor_tensor(out=ktok[:, T, :], in0=t1[:, 128:256], in1=t2[:, 128:256], op=ALU.add),
                       reads=[b1, b2], writes=[B_kt[T]])
                tb = 4 + (T % 2)
                P.pe(lambda e, rq=rq, tb=tb: e.transpose(banksb[tb][:, 0:128], rq[:], identb[:]), reads=[rqb, B_cst], writes=[bankB[tb]])
                P.pe(lambda e, T=T, tb=tb: e.transpose(banksb[tb][:, 128:256], ktok[:, T, :], identb[:]), reads=[B_kt[T], B_cst], writes=[bankB[tb]])
                P.act(lambda e, T=T, tb=tb: e.copy(out=qT[:, tslice(T)], in_=banksb[tb][:, 0:128]), reads=[bankB[tb]], writes=[B_q[T]])
                P.act(lambda e, T=T, tb=tb: e.activation(out=kT[:, tslice(T)], in_=banksb[tb][:, 128:256], func=AF.Copy, scale=0.125),
                      reads=[bankB[tb]], writes=[B_k[T]])
                P.act(lambda e, z=z, T=T: e.copy(out=v[:, T, :], in_=z[:, 256:384]), reads=[bankB[zb]], writes=[B_v[T]])
                th, thb = r128.next()
                P.act(lambda e, z=z, th=th: e.activation(out=th[:], in_=z[:, 384:512], func=AF.Tanh, scale=0.5), reads=[bankB[zb]], writes=[thb])
                P.dve(lambda e, z=z, th=th, T=T: e.scalar_tensor_tensor(out=sg[:, T, :], in0=th[:], scalar=1.0, in1=z[:, 384:512], op0=ALU.add, op1=ALU.mult),
                      reads=[thb, bankB[zb]], writes=[B_sg[T]])
            st_in = [srf_in, srb_in]
            st_out = [nsrf_out, nsrb_out]
            for d_ in range(2):
                P.pool(lambda e, d_=d_: e.memset(stS_[:, d_, :], 0.0), writes=[B_stS_[d_]])
                for hh in range(2):
                    P.dma(stS_[hh * 64:(hh + 1) * 64, d_, hh * 64:(hh + 1) * 64], st_in[d_][l, 2 * p + hh], writes=[B_stS_[d_]])
            for step in range(NT):
                for d_ in range(2):
                    T = step if d_ == 0 else NT - 1 - step
                    S = stS_[:, d_, :]
                    kcol = d_ * 16 + T
                    P.dve(lambda e, S=S, kcol=kcol: e.tensor_scalar(out=S, in0=S, scalar1=keep[:, kcol:kcol + 1], scalar2=None, op0=ALU.mult),
                          reads=[B_stS_[d_], B_cst], writes=[B_stS_[d_]])
                    P.act(lambda e, S=S, d_=d_, T=T: e.copy(out=Sbf[:, d_, T, :], in_=S), reads=[B_stS_[d_]], writes=[B_S[d_][T]])
                    kt_, ktb_ = rb128.next()
                    P.any(lambda e, kt_=kt_, T=T, d_=d_: e.tensor_tensor(out=kt_[:], in0=ktok[:, T, :], in1=kd_[:, d_, :], op=ALU.mult),
                           reads=[B_kt[T], B_pair_], writes=[ktb_])
                    ub = d_
                    P.pe(lambda e, kt_=kt_, T=T, ub=ub: e.matmul(banks[ub][:, 0:128], lhsT=kt_[:], rhs=v[:, T, :], start=True, stop=True),
                         reads=[ktb_, B_v[T]], writes=[bankB[ub]])
                    tmp, tmpb = r128.next()
                    P.dve(lambda e, tmp=tmp, ub=ub: e.tensor_tensor(out=tmp[:], in0=banks[ub][:, 0:128], in1=BM, op=ALU.mult),
                          reads=[bankB[ub], B_cst], writes=[tmpb])
                    P.dve(lambda e, S=S, tmp=tmp, d_=d_: e.scalar_tensor_tensor(out=S, in0=S, scalar=lgrow_[:, 2 + d_:3 + d_], in1=tmp[:], op0=ALU.mult, op1=ALU.add),
                          reads=[B_stS_[d_], B_pair_, tmpb], writes=[B_stS_[d_]])
                    is_out = (T % 2 == 1) if d_ == 0 else (T % 2 == 0)
                    if is_out:
                        so, sob = r128.next()
                        P.act(lambda e, so=so, S=S: e.copy(out=so[:], in_=S), reads=[B_stS_[d_]], writes=[sob])
                        for hh in range(2):
                            P.dma(st_out[d_][l, T // 2, 2 * p + hh], so[hh * 64:(hh + 1) * 64, hh * 64:(hh + 1) * 64], reads=[sob])
            for T in range(NT):
                qf, qfb = rb128.next()
                qb_, qbb = rb128.next()
                P.dve(lambda e, qf=qf, T=T: e.tensor_tensor(out=qf[:], in0=qT[:, tslice(T)], in1=qd_[:, 0, :], op=ALU.mult), reads=[B_q[T], B_pair_], writes=[qfb])
                P.any(lambda e, qb_=qb_, T=T: e.tensor_tensor(out=qb_[:], in0=qT[:, tslice(T)], in1=qd_[:, 1, :], op=ALU.mult), reads=[B_q[T], B_pair_], writes=[qbb])
                ob = 6 + (T % 2)
                ab = 2 + (T % 2)
                P.pe(lambda e, qf=qf, T=T, ob=ob: e.matmul(banks[ob][:, 0:128], lhsT=qf[:], rhs=Sbf[:, 0, T, :], start=True, stop=False, skip_group_check=True),
                     reads=[qfb, B_S[0][T]], writes=[bankB[ob]])
                P.pe(lambda e, qb_=qb_, T=T, ob=ob: e.matmul(banks[ob][:, 0:128], lhsT=qb_[:], rhs=Sbf[:, 1, T, :], start=False, stop=False, skip_group_check=True),
                     reads=[qbb, B_S[1][T]], writes=[bankB[ob]])
                am, amb = rb256.next()
                for hh in range(2):
                    P.pe(lambda e, hh=hh, T=T: e.matmul(banks[2 + hh][:, 0:128], lhsT=kT[hh * 64:(hh + 1) * 64, tslice(T)],
                                                        rhs=qT[hh * 64:(hh + 1) * 64, tslice(T)], start=True, stop=True),
                         reads=[B_k[T], B_q[T]], writes=[bankB[2 + hh]])
                    P.dve(lambda e, am=am, hh=hh: e.tensor_tensor(out=am[:, hh * 128:(hh + 1) * 128], in0=banks[2 + hh][:, 0:128], in1=dtm_[:, hh, :], op=ALU.mult),
                          reads=[bankB[2 + hh], B_pair_], writes=[amb])
                for hh in range(2):
                    P.pe(lambda e, hh=hh, am=am, T=T, ob=ob: e.matmul(banks[ob][:, hh * 64:(hh + 1) * 64], lhsT=am[:, hh * 128:(hh + 1) * 128],
                                                                      rhs=v[:, T, hh * 64:(hh + 1) * 64], start=False, stop=(hh == 1), skip_group_check=True),
                         reads=[amb, B_v[T]], writes=[bankB[ob]])
                finish_pair(banks[ob][:, 0:128], [bankB[ob]], sg, B_sg, chunk, T, 0.5, 4 + (T % 2))

        def finish_pair(o_ap, obufs, sg, B_sg, chunk, T, c0, tbank, act_rstd=False):
            ss, ssb = rs.next()
            jk, jkb = r128.next()
            for hh in range(2):
                P.act(lambda e, hh=hh: e.activation(out=jk[:, hh * 64:(hh + 1) * 64], in_=o_ap[:, hh * 64:(hh + 1) * 64], func=AF.Square,
                                                    accum_out=ss[:, hh:hh + 1]),
                      reads=obufs, writes=[jkb, ssb])
            if act_rstd:
                rstd, rb_ = rs.next()
                P.act(lambda e: e.activation(out=rstd[:, 0:2], in_=ss[:, 0:2], func=AF.Ln, scale=1.0 / (64 * c0 * c0), bias=EPS / (c0 * c0)),
                      reads=[ssb], writes=[rb_])
                P.act(lambda e: e.activation(out=rstd[:, 0:2], in_=rstd[:, 0:2], func=AF.Exp, scale=-0.5), reads=[rb_], writes=[rb_])
            else:
                rstd, rb_ = rstd_from_ss(ss[:, 0:2], ssb, 2, 1.0 / (64 * c0 * c0), EPS / (c0 * c0))
            mt, mtb = rb128.next()
            for hh in range(2):
                P.dve(lambda e, hh=hh: e.scalar_tensor_tensor(out=mt[:, hh * 64:(hh + 1) * 64], in0=o_ap[:, hh * 64:(hh + 1) * 64], scalar=rstd[:, hh:hh + 1],
                                                              in1=sg[:, T, hh * 64:(hh + 1) * 64], op0=ALU.mult, op1=ALU.mult),
                      reads=list(obufs) + [rb_, B_sg[T]], writes=[mtb])
            mixed_out(mt[:], mtb, chunk, T, tbank)

        def hgrn_unit(l, p):
            u = 2 + p
            chunk = UNIT_CHUNK[u]
            P.barrier()
            q = arena_view(0, [128, NT, 128], BF16)
            kk = arena_view(4096, [128, NT, 256], BF16)
            lf = arena_view(12288, [128, NT, 256], F32)
            v = arena_view(28672, [128, NT, 128], BF16)
            sg = arena_view(32768, [128, NT, 128], BF16)
            oacc = arena_view(36864, [128, NT, 128], F32)
            vmall = arena_view(45056, [128, NT, 512], BF16)
            B_vm = [Buf() for _ in range(NT)]
            B_q = [Buf() for _ in range(NT)]
            B_kk = [Buf() for _ in range(NT)]
            B_lf = [Buf() for _ in range(NT)]
            B_v = [Buf() for _ in range(NT)]
            B_sg = [Buf() for _ in range(NT)]
            B_oa = [Buf() for _ in range(NT)]
            wb = load_unit_weights(l, u)
            for half in range(2):
                P.dve(lambda e, half=half: e.tensor_copy(out=lb2[:, half * 128:(half + 1) * 128], in_=lball[:, l, p * 128:(p + 1) * 128]),
                      reads=[B_cst], writes=[B_pair])
            P.dve(lambda e: e.tensor_scalar(out=omlb2[:], in0=lb2[:], scalar1=-1.0, scalar2=1.0, op0=ALU.mult, op1=ALU.add), reads=[B_pair], writes=[B_pair])
            for T in range(NT):
                zb = 2 * (T % 2)
                project(wb, T, zb, 0, 512)
                project(wb, T, zb + 1, 512, 640)
                z = banks[zb]
                z2 = banks[zb + 1]
                u_, ub_ = r512.next()
                P.act(lambda e, z=z, u_=u_: e.activation(out=u_[:, 0:384], in_=z[:, 0:384], func=AF.Exp, scale=-1.0), reads=[bankB[zb]], writes=[ub_])
                P.act(lambda e, u_=u_: e.activation(out=u_[:, 0:384], in_=u_[:, 0:384], func=AF.Ln, bias=1.0), reads=[ub_], writes=[ub_])
                P.act(lambda e, u_=u_: e.activation(out=u_[:, 0:384], in_=u_[:, 0:384], func=AF.Exp, scale=-1.0), reads=[ub_], writes=[ub_])
                P.dve(lambda e, u_=u_, z=z, T=T: e.tensor_tensor(out=sg[:, T, :], in0=u_[:, 256:384], in1=z[:, 256:384], op=ALU.mult),
                      reads=[ub_, bankB[zb]], writes=[B_sg[T]])
                f_, fb_ = r256.next()
                P.any(lambda e, u_=u_, f_=f_: e.tensor_tensor(out=f_[:], in0=u_[:, 0:256], in1=omlb2[:], op=ALU.mult), reads=[ub_, B_pair], writes=[fb_])
                P.any(lambda e, f_=f_: e.tensor_tensor(out=f_[:], in0=f_[:], in1=lb2[:], op=ALU.add), reads=[fb_, B_pair], writes=[fb_])
                P.act(lambda e, f_=f_, T=T: e.activation(out=lf[:, T, :], in_=f_[:], func=AF.Ln), reads=[fb_], writes=[B_lf[T]])
                P.act(lambda e, f_=f_, T=T: e.activation(out=kk[:, T, :], in_=f_[:], func=AF.Identity, scale=-1.0, bias=1.0),
                      reads=[fb_], writes=[B_kk[T]])
                P.act(lambda e, z=z, T=T: e.activation(out=q[:, T, :], in_=z[:, 384:512], func=AF.Copy, scale=0.125), reads=[bankB[zb]], writes=[B_q[T]])
                P.act(lambda e, z2=z2, T=T: e.copy(out=v[:, T, :], in_=z2[:, 0:128]), reads=[bankB[zb + 1]], writes=[B_v[T]])
            st_in = [shf_in, shb_in]
            st_out = [nshf_out, nshb_out]
            TRI = [TRIF, TRIB]
            for d_ in range(2):
                P.pool(lambda e, d_=d_: e.memset(stS[:, d_, :], 0.0), writes=[B_stS[d_]])
                for hh in range(2):
                    P.dma(stS[hh * 64:(hh + 1) * 64, d_, hh * 64:(hh + 1) * 64], st_in[d_][l, 2 * p + hh], writes=[B_stS[d_]])
            done_first = [False] * NT
            for step in range(NT):
                for d_ in range(2):
                    T = step if d_ == 0 else NT - 1 - step
                    S = stS[:, d_, :]
                    lfd = lf[:, T, d_ * 128:(d_ + 1) * 128]
                    kcol = d_ * 16 + T
                    P.dve(lambda e, S=S, kcol=kcol: e.tensor_scalar(out=S, in0=S, scalar1=keep[:, kcol:kcol + 1], scalar2=None, op0=ALU.mult),
                          reads=[B_stS[d_], B_cst], writes=[B_stS[d_]])
                    sbf, sbfb = rb128.next()
                    P.act(lambda e, S=S, sbf=sbf: e.copy(out=sbf[:], in_=S), reads=[B_stS[d_]], writes=[sbfb])
                    P.pe(lambda e, lfd=lfd, d_=d_: e.matmul(banks[0][:, 0:128], lhsT=TRI[d_], rhs=lfd, start=True, stop=True),
                         reads=[B_cst, B_lf[T]], writes=[bankB[0]])
                    P.pe(lambda e, lfd=lfd: e.matmul(banks[0][:, 128:132], lhsT=lfd, rhs=ind[:], start=True, stop=True),
                         reads=[B_cst, B_lf[T]], writes=[bankB[0]])
                    G_, Gb_ = rs.next()
                    P.act(lambda e, G_=G_: e.activation(out=G_[:, 0:4], in_=banks[0][:, 128:132], func=AF.Exp), reads=[bankB[0]], writes=[Gb_])
                    eq, eqb = r128.next()
                    ek, ekb = r128.next()
                    P.act(lambda e, eq=eq: e.activation(out=eq[:], in_=banks[0][:, 0:128], func=AF.Exp), reads=[bankB[0]], writes=[eqb])
                    P.act(lambda e, ek=ek: e.activation(out=ek[:], in_=banks[0][:, 0:128], func=AF.Exp, scale=-1.0), reads=[bankB[0]], writes=[ekb])
                    qt_, qtb = rb128.next()
                    kt_, ktb = rb128.next()
                    P.dve(lambda e, qt_=qt_, eq=eq, T=T: e.tensor_tensor(out=qt_[:], in0=q[:, T, :], in1=eq[:], op=ALU.mult), reads=[B_q[T], eqb], writes=[qtb])
                    P.any(lambda e, kt_=kt_, ek=ek, T=T, d_=d_: e.tensor_tensor(out=kt_[:], in0=kk[:, T, d_ * 128:(d_ + 1) * 128], in1=ek[:], op=ALU.mult),
                           reads=[B_kk[T], ekb], writes=[ktb])
                    P.pe(lambda e, qt_=qt_: e.transpose(banksb[1][:, 0:128], qt_[:], identb[:]), reads=[qtb, B_cst], writes=[bankB[1]])
                    P.pe(lambda e, kt_=kt_: e.transpose(banksb[1][:, 128:256], kt_[:], identb[:]), reads=[ktb, B_cst], writes=[bankB[1]])
                    qkT, qkTb = rb256.next()
                    P.act(lambda e, qkT=qkT: e.copy(out=qkT[:], in_=banksb[1][:, 0:256]), reads=[bankB[1]], writes=[qkTb])
                    vm, vmb = vmall[:, T, :], B_vm[T]
                    if not done_first[T]:
                        for j in range(4):
                            P.act(lambda e, j=j, vm=vm, T=T: e.activation(out=vm[:, j * 128:(j + 1) * 128], in_=v[:, T, :], func=AF.Identity,
                                                                          scale=ind[:, j:j + 1]),
                                  reads=[B_v[T], B_cst], writes=[vmb])
                    P.pe(lambda e, kt_=kt_, vm=vm: e.matmul(banks[2][:, 0:512], lhsT=kt_[:], rhs=vm, start=True, stop=True),
                         reads=[ktb, vmb], writes=[bankB[2]])
                    am, amb = rb256.next()
                    for hh in range(2):
                        abk = 3 + hh
                        P.pe(lambda e, hh=hh, qkT=qkT, abk=abk: e.matmul(banks[abk][:, 0:128], lhsT=qkT[hh * 64:(hh + 1) * 64, 128:256],
                                                                          rhs=qkT[hh * 64:(hh + 1) * 64, 0:128], start=True, stop=True),
                             reads=[qkTb], writes=[bankB[abk]])
                        P.dve(lambda e, am=am, d_=d_, hh=hh, abk=abk: e.tensor_tensor(out=am[:, hh * 128:(hh + 1) * 128], in0=banks[abk][:, 0:128], in1=TRI[d_], op=ALU.mult),
                              reads=[bankB[abk], B_cst], writes=[amb])
                    ob = 5 + d_
                    jorder = [0, 1, 2, 3] if d_ == 0 else [3, 2, 1, 0]
                    cur, curb = sbf, sbfb
                    for ji, j in enumerate(jorder):
                        P.pe(lambda e, j=j, qkT=qkT, cur=cur: e.matmul(banks[ob][32 * j:32 * j + 32, 0:128], lhsT=qkT[:, 32 * j:32 * j + 32], rhs=cur[:],
                                                                       start=True, stop=False, tile_position=(0, 32 * j), skip_group_check=True),
                             reads=[qkTb, curb], writes=[bankB[ob]])
                        tg, tgb = r128.next()
                        P.dve(lambda e, tg=tg, j=j, G_=G_: e.scalar_tensor_tensor(out=tg[:], in0=banks[2][:, j * 128:(j + 1) * 128], scalar=G_[:, j:j + 1], in1=BM,
                                                                                  op0=ALU.mult, op1=ALU.mult),
                              reads=[bankB[2], Gb_, B_cst], writes=[tgb])
                        P.dve(lambda e, S=S, tg=tg, j=j, G_=G_: e.scalar_tensor_tensor(out=S, in0=S, scalar=G_[:, j:j + 1], in1=tg[:], op0=ALU.mult, op1=ALU.add),
                              reads=[B_stS[d_], Gb_, tgb], writes=[B_stS[d_]])
                        if ji < 3:
                            cur, curb = rb128.next()
                            P.act(lambda e, S=S, cur=cur: e.copy(out=cur[:], in_=S), reads=[B_stS[d_]], writes=[curb])
                    for hh in range(2):
                        P.pe(lambda e, hh=hh, am=am, T=T: e.matmul(banks[ob][:, hh * 64:(hh + 1) * 64], lhsT=am[:, hh * 128:(hh + 1) * 128],
                                                                   rhs=v[:, T, hh * 64:(hh + 1) * 64], start=False, stop=(hh == 1), skip_group_check=True),
                             reads=[amb, B_v[T]], writes=[bankB[ob]])
                    is_out = (T % 2 == 1) if d_ == 0 else (T % 2 == 0)
                    if is_out:
                        so, sob = r128.next()
                        P.act(lambda e, so=so, S=S: e.copy(out=so[:], in_=S), reads=[B_stS[d_]], writes=[sob])
                        for hh in range(2):
                            P.dma(st_out[d_][l, T // 2, 2 * p + hh], so[hh * 64:(hh + 1) * 64, hh * 64:(hh + 1) * 64], reads=[sob])
                    if not done_first[T]:
                        done_first[T] = True
                        P.act(lambda e, T=T: e.copy(out=oacc[:, T, :], in_=banks[ob][:, 0:128]), reads=[bankB[ob]], writes=[B_oa[T]])
                    else:
                        ot, otb = r128.next()
                        P.dve(lambda e, ot=ot, T=T: e.tensor_tensor(out=ot[:], in0=banks[ob][:, 0:128], in1=oacc[:, T, :], op=ALU.add),
                              reads=[bankB[ob], B_oa[T]], writes=[otb])
                        finish_pair(ot[:], [otb], sg, B_sg, chunk, T, 1.0, 7, act_rstd=True)

        def out_phase(l, last):
            P.barrier()
            wo = arena_view(0, [128, 8, 1024], BF16)
            B_wo = Buf()
            for half in range(2):
                ws = wstage[:, 0:4096].rearrange("p (k n) -> p k n", k=8, n=512)
                P.dma(ws, wout_in[l][:, :, half * 512:(half + 1) * 512], writes=[B_wstage])
                for kc in range(8):
                    eng = "any"
                    P.on(eng, lambda e, kc=kc, half=half: e.tensor_copy(out=wo[:, kc, half * 512:(half + 1) * 512], in_=ws[:, kc, :]),
                         reads=[B_wstage], writes=[B_wo])
            src = x_in if l == 0 else xs_scr
            dst = y_out if last else xs_scr
            for T in range(NT):
                xt, xb_ = xring.next()
                P.dma(xt[:], src[T * 128:(T + 1) * 128, :], reads=([B_xs[T]] if l > 0 else []), writes=[xb_])
                for nb in range(2):
                    bk = 2 * (T % 2) + nb
                    for c in range(8):
                        P.pe(lambda e, c=c, nb=nb, bk=bk, T=T: e.matmul(banks[bk][:, 0:512], lhsT=mixT[:, c, tslice(T)], rhs=wo[:, c, nb * 512:(nb + 1) * 512],
                                                                        start=(c == 0), stop=(c == 7)),
                             reads=[B_mixT[c][T], B_wo], writes=[bankB[bk]])
                    tmp, tmpb = r512.next()
                    P.dve(lambda e, tmp=tmp, bk=bk, nb=nb: e.tensor_tensor(out=tmp[:], in0=banks[bk][:, 0:512], in1=gate_b[:, nb * 512:(nb + 1) * 512], op=ALU.mult),
                          reads=[bankB[bk], B_gate], writes=[tmpb])
                    P.any(lambda e, tmp=tmp, xt=xt, nb=nb: e.tensor_tensor(out=xt[:, nb * 512:(nb + 1) * 512], in0=tmp[:], in1=xt[:, nb * 512:(nb + 1) * 512], op=ALU.add),
                           reads=[tmpb, xb_], writes=[xb_])
                P.dma(dst[T * 128:(T + 1) * 128, :], xt[:], reads=[xb_], writes=([] if last else [B_xs[T]]))

        for l in range(L):
            setup_layer(l)
            norm_phase(l)
            RB = [(Buf(), [Buf(), Buf()]) for _ in range(2)]
            for p in range(2):
                if units_enabled is None or ("r%d" % p) in units_enabled:
                    ret_unit(l, p, RB)
            for p in range(2):
                if units_enabled is None or ("g%d" % p) in units_enabled:
                    hgrn_unit(l, p)
            DB = {"set": [([Buf() for _ in range(NT)], [Buf() for _ in range(NKT)], [Buf() for _ in range(NKT)], [Buf() for _ in range(NT)], Buf()) for _ in range(2)], "ck": Buf()}
            for h in range(4):
                if units_enabled is None or ("d%d" % h) in units_enabled:
                    diff_unit(l, h, DB)
            if dbg:
                P.barrier()
                P.dma(dbg_out[l], mixT[:], reads=[b for row in B_mixT for b in row])
            out_phase(l, last=(l == L - 1))

        with nc.Block() as block:
            run = P.build(sems, dsems, reorder=REORDER)
            block.sync(lambda e: run("sp", e))
            block.tensor(lambda e: run("pe", e))
            block.scalar(lambda e: run("act", e))
            block.vector(lambda e: run("dve", e))
            block.gpsimd(lambda e: run("pool", e))
    return nc


def _unit_perm():
    off = dict(rq=0, rk=256, rv=512, rg=768, dq=1024, dk=1536, dv=2048, dg=2560, hq=3072, hff=3328, hfb=3584, hi=3840, hg=4096)
    cols = []
    for p in range(2):
        for n in ("rq", "rk", "rv", "rg"):
            cols += list(range(off[n] + 128 * p, off[n] + 128 * p + 128))
    for p in range(2):
        for n in ("hff", "hfb", "hg", "hq", "hi"):
            cols += list(range(off[n] + 128 * p, off[n] + 128 * p + 128))
    for h in range(4):
        for n in ("dq", "dk", "dv", "dg"):
            cols += list(range(off[n] + 128 * h, off[n] + 128 * h + 128))
    return np.array(cols, dtype=np.int64)


def _constants():
    s = np.arange(128, dtype=np.float32)[:, None]
    t = np.arange(128, dtype=np.float32)[None, :]
    M1 = np.maximum(t - s, 0)
    L1 = (s <= t).astype(np.float32)
    M2 = np.maximum(s - t, 0)
    L2 = (s >= t).astype(np.float32)
    IOTA1 = np.broadcast_to(t + 1, (128, 128))
    IOTA2 = np.broadcast_to(128 - t, (128, 128))
    COLA = np.broadcast_to(127 - s, (128, 128))
    COLB = np.broadcast_to(s, (128, 128))
    same = (np.floor(s / 32) == np.floor(t / 32))
    TRIF = (same & (s <= t)).astype(np.float32)
    TRIB = (same & (s >= t)).astype(np.float32)
    BM = (np.floor(s / 64) == np.floor(t / 64)).astype(np.float32)
    cst = np.stack([M1, L1, M2, L2, IOTA1, IOTA2, COLA, COLB, TRIF, TRIB, BM], axis=1).astype(np.float32)
    ind = (np.floor(np.arange(128)[:, None] / 32) == np.arange(4)[None, :]).astype(np.float32)
    return np.ascontiguousarray(cst), np.ascontiguousarray(ind)


def _rope_tables(sample):
    ropec = np.ones((128, 16, 64), np.float32)
    ropes = np.zeros((128, 16, 64), np.float32)
    if sample:
        tt = np.arange(TOK)
        row = (tt // 64).astype(np.float32)
        col = (tt % 64).astype(np.float32)
        inv = (np.float32(10000.0) ** (-np.arange(16, dtype=np.float32) / np.float32(16))).astype(np.float32)
        ar = (row[:, None] * inv[None, :]).astype(np.float32)
        ac = (col[:, None] * inv[None, :]).astype(np.float32)
        c = np.concatenate([np.cos(ar), np.cos(ar), np.cos(ac), np.cos(ac)], axis=1).astype(np.float32)
        s_ = np.concatenate([-np.sin(ar), np.sin(ar), -np.sin(ac), np.sin(ac)], axis=1).astype(np.float32)
        ropec = np.ascontiguousarray(c.reshape(16, 128, 64).transpose(1, 0, 2))
        ropes = np.ascontiguousarray(s_.reshape(16, 128, 64).transpose(1, 0, 2))
    return ropec, ropes


_NC_CACHE = {}


def kernel(x_prompt, x_sample, c, c_ctx, cache_diff_k, cache_diff_v, state_ret_fwd, state_ret_bwd,
           state_hgrn_fwd, state_hgrn_bwd, norm_g, w_ada, b_ada, w_in, w_out, ret_decay_logit,
           diff_qn_g, diff_kn_g, diff_lambda, hgrn_lb_logit, _dbg=False, _units=None, _L=2):
    f32 = np.float32
    bf = ml_dtypes.bfloat16
    A = lambda a: np.ascontiguousarray(np.asarray(a, dtype=f32))
    x_prompt, x_sample, c, c_ctx = A(x_prompt), A(x_sample), A(c), A(c_ctx)
    perm = _unit_perm()
    w_in_p = A(w_in)[:, :, perm]
    win = np.ascontiguousarray(w_in_p.reshape(2, 8, 128, 4352).transpose(0, 2, 1, 3))
    wada = np.ascontiguousarray(A(w_ada).reshape(2, 8, 128, 3072).transpose(0, 2, 1, 3))
    wout = np.ascontiguousarray(A(w_out).reshape(2, 8, 128, 1024).transpose(0, 2, 1, 3))
    normg = np.ascontiguousarray(A(norm_g).reshape(2, 8, 128).transpose(0, 2, 1))
    cst, ind = _constants()
    shared = dict(
        normg=normg, wada=wada, bada=A(b_ada), win=win, wout=wout, rdl=A(ret_decay_logit).reshape(2, 8),
        qng=A(diff_qn_g), kng=A(diff_kn_g), dlam=A(diff_lambda).reshape(2, 256), hlb=A(hgrn_lb_logit).reshape(512),
        identb=np.eye(128, dtype=f32).astype(bf), identf=np.eye(128, dtype=f32), cst=cst, ind=ind,
    )
    ropec_s, ropes_s = _rope_tables(True)
    ropec_p, ropes_p = _rope_tables(False)
    z64 = np.zeros((2, 4, 64, 64), f32)
    zc = np.zeros((2, 512, 512), f32)
    in_maps = []
    for core in range(8):
        m = dict(shared)
        if core < 4:
            b = core
            m["x"] = x_sample[b]
            m["modv"] = np.ascontiguousarray(c[b].reshape(8, 128).T)
            m["ck"] = np.ascontiguousarray(A(cache_diff_k)[b].reshape(2, 512, 512))
            m["cv"] = np.ascontiguousarray(A(cache_diff_v)[b].reshape(2, 512, 512))
            m["srf"], m["srb"] = A(state_ret_fwd)[b], A(state_ret_bwd)[b]
            m["shf"], m["shb"] = A(state_hgrn_fwd)[b], A(state_hgrn_bwd)[b]
            m["ropec"], m["ropes"] = ropec_s, ropes_s
            m["qmask"] = np.zeros((8, 2048), f32).astype(bf)
            m["kmask"] = np.zeros((8, 2560), f32).astype(bf)
            m["keep"] = np.ones((128, 32), f32)
        else:
            j = core - 4
            m["x"] = np.ascontiguousarray(x_prompt[8 * j:8 * j + 8].reshape(2048, 1024))
            m["modv"] = np.ascontiguousarray(c_ctx.reshape(8, 128).T)
            m["ck"], m["cv"] = zc, zc
            m["srf"], m["srb"], m["shf"], m["shb"] = z64, z64, z64, z64
            m["ropec"], m["ropes"] = ropec_p, ropes_p
            seq = np.arange(2048) // 256
            qm = (seq[None, :] == np.arange(8)[:, None]).astype(f32)
            km = np.full((8, 2560), BIGNEG, f32)
            km[:, :2048] = np.where(seq[None, :] == np.arange(8)[:, None], 0.0, BIGNEG)
            m["qmask"] = qm.astype(bf)
            m["kmask"] = km.astype(bf)
            kf = np.array([0.0 if T % 2 == 0 else 1.0 for T in range(16)], f32)
            kb = np.array([0.0 if T % 2 == 1 else 1.0 for T in range(16)], f32)
            m["keep"] = np.ascontiguousarray(np.broadcast_to(np.concatenate([kf, kb])[None, :], (128, 32)))
        in_maps.append(m)

    key = (_L, _dbg, None if _units is None else tuple(sorted(_units)))
    if key not in _NC_CACHE:
        _NC_CACHE[key] = build_program(L=_L, dbg=_dbg, units_enabled=_units)
    nc = _NC_CACHE[key]
    res = run_bass_kernel_spmd(nc, in_maps, core_ids=list(range(8)))
    R = res.results

    y_sample = np.stack([R[b]["y"] for b in range(4)], axis=0)
    y_prompt = np.concatenate([R[4 + j]["y"].reshape(8, 256, 1024) for j in range(4)], axis=0)
    nk = np.concatenate([R[4 + j]["nk"].reshape(2, 8, 256, 4, 2, 64).transpose(1, 0, 2, 3, 4, 5) for j in range(4)], axis=0)
    nv = np.concatenate([R[4 + j]["nv"].reshape(2, 8, 256, 4, 128).transpose(1, 0, 2, 3, 4) for j in range(4)], axis=0)
    st = []
    for name in ("nsrf", "nsrb", "nshf", "nshb"):
        st.append(np.concatenate([R[4 + j][name].transpose(1, 0, 2, 3, 4) for j in range(4)], axis=0))
    outs = (y_prompt, y_sample, np.ascontiguousarray(nk), np.ascontiguousarray(nv), *[np.ascontiguousarray(s) for s in st])
    if _dbg:
        return outs, [R[i]["dbgmix"] for i in range(8)]
    return outs
```

```python
import math
import types
from contextlib import ExitStack

import numpy as np
import ml_dtypes

import concourse.bass as bass
import concourse.mybir as mybir
from concourse.bass_utils import run_bass_kernel_spmd

F32 = mybir.dt.float32
BF16 = mybir.dt.bfloat16
ALU = mybir.AluOpType
AF = mybir.ActivationFunctionType
AX = mybir.AxisListType

ENGS = ["pe", "act", "dve", "pool", "sp"]
NDSEM = 8
SAME_ENG_SYNC = True
import os as _os0
REORDER = _os0.environ.get('REORDER', '1') == '1'
PSUM_EXCL = _os0.environ.get('PSUM_EXCL', '1') == '1'
REORDER_ENGS = _os0.environ.get('REORDER_ENGS', 'pe,act,dve,pool,sp').split(',')

D_MODEL = 1024
NT = 16
TOK = 2048
NKT = 20
EPS = 1e-6
UNIT_W = [512, 512, 640, 640, 512, 512, 512, 512]
UNIT_OFF = [0, 512, 1024, 1664, 2304, 2816, 3328, 3840]
UNIT_CHUNK = [0, 1, 6, 7, 2, 3, 4, 5]
BIGNEG = -30000.0


class Buf:
    __slots__ = ("w", "r", "name", "excl")

    def __init__(self, name="", excl=False):
        self.w = None
        self.r = []
        self.name = name
        self.excl = excl


class Op:
    __slots__ = ("eng", "fn", "waits", "marked", "semval", "is_dma", "dsem", "dval", "cost", "lat", "idx", "prio",
                 "pos", "succs", "nrem", "ready", "fin", "is_bar", "per_eng", "per_dsem")


class _Probe:
    def __init__(self):
        self.rec = None

    def __getattr__(self, name):
        def f(*a, **k):
            self.rec = (name, a, k)
            return self
        return f


def _nfree(ap):
    n = 1
    for d in ap.shape[1:]:
        n *= int(d)
    return n


def _estimate(eng, fn, is_dma):
    pr = _Probe()
    try:
        fn(pr)
        name, a, k = pr.rec
    except Exception:
        name, a, k = "?", (), {}
    out = k.get("out", a[0] if a else None)
    try:
        if is_dma:
            nbytes = _nfree(out) * int(out.shape[0]) * mybir.dt.size(out.dtype)
            return 120.0, 2200.0 + nbytes / 120.0
        if eng == "pe":
            if name == "transpose":
                return 80.0, 80.0
            rhs = k.get("rhs", a[2] if len(a) > 2 else None)
            lhsT = k.get("lhsT", a[1] if len(a) > 1 else None)
            n = _nfree(rhs)
            c = (max(64, n) / 2.4 + 25.0) * 1.25
            if lhsT.dtype == F32:
                c *= 4.0
            return c, c
        n = _nfree(out)
        if eng == "act":
            c = 190.0 + n / 1.2 + (90.0 if k.get("accum_out") is not None else 0.0)
        elif eng == "dve":
            c = 130.0 + n / 0.7
        else:
            c = 1000.0 + n / 0.3
        return c, c
    except Exception:
        return 300.0, 300.0


def _freeze(fn):
    if fn.__closure__ is None:
        return fn
    cells = []
    for c in fn.__closure__:
        try:
            cells.append(types.CellType(c.cell_contents))
        except ValueError:
            cells.append(c)
    return types.FunctionType(fn.__code__, fn.__globals__, fn.__name__, fn.__defaults__, tuple(cells))


LAT_X = 200.0
LAT_S = 50.0


class Prog:
    def __init__(self, nc):
        self.nc = nc
        self.all = []
        self.cur_bar = None
        self.since = []
        self.load = {e: 0.0 for e in ENGS}

    def _new(self, eng):
        op = Op()
        op.eng = eng
        op.fn = None
        op.marked = False
        op.semval = None
        op.is_dma = False
        op.dsem = None
        op.dval = None
        op.cost = 0.0
        op.lat = 0.0
        op.is_bar = False
        op.idx = len(self.all)
        op.waits = []
        self.all.append(op)
        return op

    def barrier(self):
        b = self._new("virt")
        b.is_bar = True
        b.waits = list(self.since)
        self.since = []
        self.cur_bar = b
        self.load = {e: 0.0 for e in ENGS}

    def emit(self, eng, fn, reads=(), writes=(), extra=(), is_dma=False):
        op = self._new(eng)
        op.fn = _freeze(fn)
        op.is_dma = is_dma
        op.cost, op.lat = _estimate(eng, op.fn, is_dma)
        self.load[eng] += op.cost
        waits = set()
        if PSUM_EXCL:
            for b in reads:
                if b.excl:
                    for r in b.r:
                        if r.eng != eng:
                            waits.add(r)
        for b in reads:
            if b.w is not None:
                waits.add(b.w)
        for b in writes:
            if b.w is not None:
                waits.add(b.w)
            for r in b.r:
                waits.add(r)
        for w in extra:
            if w is not None:
                waits.add(w)
        if self.cur_bar is not None:
            waits.add(self.cur_bar)
        waits.discard(op)
        op.waits = list(waits)
        for b in reads:
            b.r.append(op)
        for b in writes:
            b.w = op
            b.r = []
        self.since.append(op)
        return op

    def pe(self, fn, reads=(), writes=(), extra=()):
        return self.emit("pe", fn, reads, writes, extra)

    def act(self, fn, reads=(), writes=(), extra=()):
        return self.emit("act", fn, reads, writes, extra)

    def dve(self, fn, reads=(), writes=(), extra=()):
        return self.emit("dve", fn, reads, writes, extra)

    def pool(self, fn, reads=(), writes=(), extra=()):
        return self.emit("pool", fn, reads, writes, extra)

    def on(self, eng, fn, reads=(), writes=(), extra=()):
        if eng == "any":
            return self.any(fn, reads, writes, extra)
        return self.emit(eng, fn, reads, writes, extra)

    def any(self, fn, reads=(), writes=(), extra=()):
        f = _freeze(fn)
        best = None
        for e in ("dve", "pool"):
            c, _ = _estimate(e, f, False)
            tot = self.load[e] + c
            if best is None or tot < best[0]:
                best = (tot, e)
        return self.emit(best[1], fn, reads, writes, extra)

    def dma(self, out, in_, reads=(), writes=(), extra=()):
        return self.emit("sp", lambda e: e.dma_start(out=out, in_=in_), reads, writes, extra, is_dma=True)

    def schedule(self, reorder=True):
        import heapq
        ops = self.all
        for op in ops:
            op.succs = []
        for op in ops:
            for w in op.waits:
                w.succs.append(op)
        for op in reversed(ops):
            m = 0.0
            for s_ in op.succs:
                l_ = s_.prio + (0.0 if op.is_bar else (LAT_S if s_.eng == op.eng else LAT_X))
                if l_ > m:
                    m = l_
            op.prio = m + op.lat
        order = {e: [] for e in ENGS}
        if not reorder:
            for op in ops:
                if not op.is_bar:
                    order[op.eng].append(op)
            return order
        for op in ops:
            op.nrem = len(op.waits)
            op.ready = 0.0
            op.fin = None
        fixed = [e for e in ENGS if e not in REORDER_ENGS]
        lastop = {}
        for op in ops:
            if op.is_bar or op.eng not in fixed:
                continue
            p_ = lastop.get(op.eng)
            if p_ is not None and p_ not in op.waits:
                p_.succs.append(op)
                op.nrem += 1
            lastop[op.eng] = op
        future = {e: [] for e in ENGS}
        now = {e: [] for e in ENGS}
        free = {e: 0.0 for e in ENGS}

        def release(op):
            for s_ in op.succs:
                if op.is_bar:
                    t = op.fin
                elif s_.is_bar:
                    t = op.fin
                elif s_.eng == op.eng:
                    t = op.fin + (0.0 if op.eng == "pe" else LAT_S)
                else:
                    t = op.fin + LAT_X
                if t > s_.ready:
                    s_.ready = t
                s_.nrem -= 1
                if s_.nrem == 0:
                    if s_.is_bar:
                        s_.fin = s_.ready
                        release(s_)
                    else:
                        heapq.heappush(future[s_.eng], (s_.ready, s_.idx, s_))

        import sys
        sys.setrecursionlimit(100000)
        roots = [op for op in ops if op.nrem == 0]
        for op in roots:
            if op.is_bar:
                op.fin = 0.0
                release(op)
            else:
                heapq.heappush(future[op.eng], (0.0, op.idx, op))
        nleft = sum(1 for op in ops if not op.is_bar)
        while nleft > 0:
            best = None
            for e in ENGS:
                f = future[e]
                nw = now[e]
                while f and f[0][0] <= free[e]:
                    r_, i_, o_ = heapq.heappop(f)
                    heapq.heappush(nw, (-o_.prio, o_.idx, o_))
                if nw:
                    st = free[e]
                elif f:
                    st = f[0][0]
                else:
                    continue
                if best is None or st < best[0]:
                    best = (st, e)
            st, e = best
            if now[e]:
                _, _, op = heapq.heappop(now[e])
            else:
                _, _, op = heapq.heappop(future[e])
            if op.is_dma:
                free[e] = st + op.cost
                op.fin = st + op.lat
            else:
                free[e] = st + op.cost
                op.fin = st + op.cost
            order[e].append(op)
            nleft -= 1
            release(op)
        self.est_ns = max(free.values())
        return order

    def build(self, sems, dsems, reorder=True):
        order = self.schedule(reorder)
        if reorder:
            print('[sched] est_us=%.1f' % (self.est_ns / 1e3), {e: len(order[e]) for e in ENGS})
        for e in ENGS:
            for i, op in enumerate(order[e]):
                op.pos = i

        def skip_same(w_eng, eng):
            return w_eng == eng and (eng == "pe" or not SAME_ENG_SYNC)

        dcnt = [0] * NDSEM
        prev_on_sem = [None] * NDSEM
        dma_prev = {}
        nd = 0
        for op in order["sp"]:
            k = nd % NDSEM
            nd += 1
            op.dsem = k
            dcnt[k] += 16
            op.dval = dcnt[k]
            dma_prev[id(op)] = prev_on_sem[k]
            prev_on_sem[k] = op
        final_dvals = list(dcnt)
        for b in self.all:
            if b.is_bar:
                pe_ = {}
                pd_ = {}
                for w in b.waits:
                    if w.is_bar:
                        continue
                    if w.is_dma:
                        if pd_.get(w.dsem, 0) < w.dval:
                            pd_[w.dsem] = w.dval
                    else:
                        c = pe_.get(w.eng)
                        if c is None or c.pos < w.pos:
                            pe_[w.eng] = w
                b.per_eng = pe_
                b.per_dsem = pd_
        for op in self.all:
            if op.is_bar:
                for w in op.per_eng.values():
                    w.marked = True
                continue
            for w in op.waits:
                if w.is_bar or w.is_dma:
                    continue
                if not skip_same(w.eng, op.eng):
                    w.marked = True
        for e in ENGS:
            cnt = 0
            for op in order[e]:
                if not op.is_dma and op.marked:
                    cnt += 1
                    op.semval = cnt

        def run_engine(ename, eng):
            waited = {}

            def need(semkey, sem, val):
                if waited.get(semkey, 0) >= val:
                    return
                eng.wait_ge(sem, val)
                waited[semkey] = val

            for op in order[ename]:
                for w in op.waits:
                    if w.is_bar:
                        for we, wo in w.per_eng.items():
                            if not (we == ename and ename == "pe"):
                                need(("e", we), sems[we], wo.semval)
                        for k, v in w.per_dsem.items():
                            need(("d", k), dsems[k], v)
                    elif w.is_dma:
                        need(("d", w.dsem), dsems[w.dsem], w.dval)
                    elif not skip_same(w.eng, ename):
                        need(("e", w.eng), sems[w.eng], w.semval)
                if op.is_dma:
                    p = dma_prev[id(op)]
                    if p is not None:
                        need(("d", p.dsem), dsems[p.dsem], p.dval)
                ins = op.fn(eng)
                if op.is_dma:
                    ins.then_inc(dsems[op.dsem], 16)
                elif op.marked:
                    ins.then_inc(sems[ename], 1)
            if ename == "sp":
                for k in range(NDSEM):
                    if final_dvals[k] > 0:
                        need(("d", k), dsems[k], final_dvals[k])

        return run_engine


class Ring:
    def __init__(self, tiles):
        self.tiles = tiles
        self.bufs = [Buf() for _ in tiles]
        self.i = 0

    def next(self):
        k = self.i % len(self.tiles)
        self.i += 1
        return self.tiles[k], self.bufs[k]


def build_program(L=2, dbg=False, units_enabled=None):
    nc = bass.Bass("TRN2", target_bir_lowering=False)

    def din(name, shape, dt=F32):
        return nc.dram_tensor(name, list(shape), dt, kind="ExternalInput").ap()

    def dout(name, shape, dt=F32):
        return nc.dram_tensor(name, list(shape), dt, kind="ExternalOutput").ap()

    x_in = din("x", [TOK, D_MODEL])
    modv = din("modv", [128, 8])
    ck_in = din("ck", [2, 512, 512])
    cv_in = din("cv", [2, 512, 512])
    srf_in = din("srf", [2, 4, 64, 64])
    srb_in = din("srb", [2, 4, 64, 64])
    shf_in = din("shf", [2, 4, 64, 64])
    shb_in = din("shb", [2, 4, 64, 64])
    normg_in = din("normg", [2, 128, 8])
    wada_in = din("wada", [2, 128, 8, 3072])
    bada_in = din("bada", [2, 3072])
    win_in = din("win", [2, 128, 8, 4352])
    wout_in = din("wout", [2, 128, 8, 1024])
    rdl_in = din("rdl", [2, 8])
    qng_in = din("qng", [2, 64])
    kng_in = din("kng", [2, 64])
    dlam_in = din("dlam", [2, 256])
    hlb_in = din("hlb", [512])
    ropec_in = din("ropec", [128, 16, 64])
    ropes_in = din("ropes", [128, 16, 64])
    qmask_in = din("qmask", [8, 2048], BF16)
    kmask_in = din("kmask", [8, 2560], BF16)
    keep_in = din("keep", [128, 32])
    identb_in = din("identb", [128, 128], BF16)
    identf_in = din("identf", [128, 128])
    cst_in = din("cst", [128, 11, 128])
    ind_in = din("ind", [128, 4])

    y_out = dout("y", [TOK, D_MODEL])
    nk_out = dout("nk", [2, TOK, 512])
    nv_out = dout("nv", [2, TOK, 512])
    nsrf_out = dout("nsrf", [2, 8, 4, 64, 64])
    nsrb_out = dout("nsrb", [2, 8, 4, 64, 64])
    nshf_out = dout("nshf", [2, 8, 4, 64, 64])
    nshb_out = dout("nshb", [2, 8, 4, 64, 64])
    xs_scr = nc.dram_tensor("xs_scr", [TOK, D_MODEL], F32, kind="Internal").ap()
    dbg_out = dout("dbgmix", [2, 128, 8, TOK], BF16) if dbg else None

    es = ExitStack()
    with es:
        def sb(name, shape, dt):
            return es.enter_context(nc.sbuf_tensor("sb_" + name, list(shape), dt))

        hT = sb("hT", [128, 8, TOK], BF16)
        mixT = sb("mixT", [128, 8, TOK], BF16)
        wstage = sb("wstage", [128, 8 * 512], F32)
        wbf = sb("wbf", [128, 8 * 640], BF16)
        arena = sb("arena", [128, 61440], mybir.dt.uint8)
        cst = sb("cst", [128, 11, 128], F32)
        ind = sb("ind", [128, 4], F32)
        ropec = sb("ropec", [128, 16, 64], F32)
        ropes = sb("ropes", [128, 16, 64], F32)
        identb = sb("identb", [128, 128], BF16)
        identf = sb("identf", [128, 128], F32)
        keep = sb("keep", [128, 32], F32)
        gate_b = sb("gate_b", [128, 1024], F32)
        modt = sb("modt", [128, 8], F32)
        smod = sb("smod", [128, 8], F32)
        normg = sb("normg", [128, 8], F32)
        modT = sb("modT", [128, 2, 8], F32)
        modA = sb("modA", [128, 8], F32)
        small = sb("small", [128, 64], F32)
        rdl = sb("rdl", [128, 8], F32)
        lg = sb("lg", [128, 8], F32)
        g4 = sb("g4", [128, 256], F32)
        lball = sb("lball", [128, 2, 256], F32)
        lb2 = sb("lb2", [128, 256], F32)
        omlb2 = sb("omlb2", [128, 256], F32)
        cneg = sb("cneg", [128, 8], F32)
        zerosb = sb("zerosb", [128, 512], BF16)
        lgrow = sb("lgrow", [128, 4], F32)
        lgrow1 = sb("lgrow1", [128, 4], F32)
        dtm = sb("dtm", [128, 2, 128], F32)
        qd = sb("qd", [128, 2, 128], F32)
        kd = sb("kd", [128, 2, 128], F32)
        e12 = sb("e12", [128, 2, 128], F32)
        stS = sb("stS", [128, 2, 128], F32)

        n_f32_512 = 3
        r512 = Ring([sb("r512_%d" % i, [128, 512], F32) for i in range(n_f32_512)])
        r256 = Ring([sb("r256_%d" % i, [128, 256], F32) for i in range(8)])
        r128 = Ring([sb("r128_%d" % i, [128, 128], F32) for i in range(8)])
        rb256 = Ring([sb("rb256_%d" % i, [128, 256], BF16) for i in range(3)])
        rb128 = Ring([sb("rb128_%d" % i, [128, 128], BF16) for i in range(8)])
        rpt = Ring([sb("rpt_%d" % i, [128, 512], BF16) for i in range(3)])
        rs = Ring([sb("rs_%d" % i, [128, 8], F32) for i in range(16)])

        banks = [es.enter_context(nc.psum_tensor("bank%d" % i, [128, 512], F32)) for i in range(8)]
        banksb = [b.bitcast(BF16) for b in banks]
        bankB = [Buf("bank%d" % i, excl=True) for i in range(8)]

        sems = {e: es.enter_context(nc.semaphore("s_" + e)) for e in ENGS}
        dsems = [es.enter_context(nc.semaphore("d%d" % k)) for k in range(NDSEM)]

        P = Prog(nc)

        B_hT = [Buf() for _ in range(NT)]
        B_mixT = [[Buf() for _ in range(NT)] for _ in range(8)]
        B_wstage = Buf()
        B_wbf = Buf()
        B_cst = Buf()
        B_misc = Buf()
        B_gate = Buf()
        B_modAB = Buf()
        B_pair = Buf()
        B_stS = [Buf(), Buf()]

        def arena_view(off_bytes, shape, dt):
            n = 1
            for s in shape[1:]:
                n *= s
            esz = 2 if dt == BF16 else 4
            a = arena[:, off_bytes:off_bytes + n * esz].bitcast(dt)
            if len(shape) == 2:
                return a
            if len(shape) == 3:
                return a.rearrange("p (a b) -> p a b", a=shape[1], b=shape[2])
            if len(shape) == 4:
                return a.rearrange("p (a b c) -> p a b c", a=shape[1], b=shape[2], c=shape[3])
            raise ValueError

        xring = Ring([arena_view(36864 + i * 4096, [128, 1024], F32) for i in range(3)])
        xhring = Ring([arena_view(49152 + i * 2048, [128, 1024], BF16) for i in range(2)])
        B_xs = [Buf() for _ in range(NT)]

        M1, L1, M2, L2, IOTA1, IOTA2, COLA, COLB, TRIF, TRIB, BM = [cst[:, i, :] for i in range(11)]

        P.dma(cst[:], cst_in, writes=[B_cst])
        P.dma(ind[:], ind_in, writes=[B_cst])
        P.dma(ropec[:], ropec_in, writes=[B_cst])
        P.dma(ropes[:], ropes_in, writes=[B_cst])
        P.dma(identb[:], identb_in, writes=[B_cst])
        P.dma(identf[:], identf_in, writes=[B_cst])
        P.dma(keep[:], keep_in, writes=[B_cst])
        P.dma(modt[:], modv, writes=[B_cst])
        hlb, hlbb = r512.next()
        P.dma(hlb[:], hlb_in.partition_broadcast(128), writes=[hlbb])
        P.pool(lambda e: e.memset(cneg[:], -0.5), writes=[B_cst])
        P.pool(lambda e: e.memset(zerosb[:], 0.0), writes=[B_cst])
        t_, tb_ = rs.next()
        P.act(lambda e, t_=t_: e.activation(out=t_[:, 0:8], in_=modt[:], func=AF.Tanh, scale=0.5), reads=[B_cst], writes=[tb_])
        P.dve(lambda e, t_=t_: e.scalar_tensor_tensor(out=smod[:], in0=t_[:, 0:8], scalar=1.0, in1=modt[:], op0=ALU.add, op1=ALU.mult),
              reads=[tb_, B_cst], writes=[B_cst])
        P.dve(lambda e: e.tensor_scalar(out=smod[:], in0=smod[:], scalar1=0.5, scalar2=None, op0=ALU.mult), reads=[B_cst], writes=[B_cst])
        P.act(lambda e: e.activation(out=hlb[:], in_=hlb[:], func=AF.Exp), reads=[hlbb], writes=[hlbb])
        den_, denb_ = r256.next()
        P.dve(lambda e: e.tensor_tensor(out=den_[:], in0=hlb[:, 0:256], in1=hlb[:, 256:512], op=ALU.add), reads=[hlbb], writes=[denb_])
        P.dve(lambda e: e.reciprocal(out=den_[:], in_=den_[:]), reads=[denb_], writes=[denb_])
        P.dve(lambda e: e.tensor_tensor(out=hlb[:, 0:256], in0=hlb[:, 0:256], in1=den_[:], op=ALU.mult), reads=[hlbb, denb_], writes=[hlbb])
        P.dve(lambda e: e.tensor_tensor(out=hlb[:, 256:512], in0=hlb[:, 256:512], in1=den_[:], op=ALU.mult), reads=[hlbb, denb_], writes=[hlbb])
        P.dve(lambda e: e.tensor_tensor(out=lball[:, 0, :], in0=hlb[:, 0:256], in1=hlb[:, 0:256], op=ALU.subtract), reads=[hlbb], writes=[B_cst])
        P.dve(lambda e: e.tensor_tensor(out=lball[:, 1, :], in0=hlb[:, 0:256], in1=hlb[:, 256:512], op=ALU.add), reads=[hlbb], writes=[B_cst])
        P.dve(lambda e: e.tensor_tensor(out=lball[:, 1, :], in0=lball[:, 1, :], in1=hlb[:, 0:256], op=ALU.subtract), reads=[hlbb, B_cst], writes=[B_cst])

        def rstd_from_ss(ss_ap, ssb, n, mult, add):
            t1, b1 = rs.next()
            P.dve(lambda e: e.tensor_scalar(out=t1[:, 0:n], in0=ss_ap, scalar1=mult, scalar2=add, op0=ALU.mult, op1=ALU.add),
                  reads=[ssb], writes=[b1])
            t2, b2 = rs.next()
            P.pool(lambda e: e.tensor_tensor(out=t2[:, 0:n], in0=t1[:, 0:n], in1=cneg[:, 0:n], op=ALU.pow), reads=[b1, B_cst], writes=[b2])
            return t2, b2

        def rope(src, srcbufs, dst, dstbufs, T, G, eng_a, eng_b):
            W = G * 64
            t1, b1 = r256.next()
            t2, b2 = r256.next()
            cv_ = ropec[:, T, :]
            sv_ = ropes[:, T, :].rearrange("p (h j i) -> p h j i", h=2, j=2, i=16)
            src3 = src.rearrange("p (g d) -> p g d", g=G, d=64)
            src5 = src.rearrange("p (g h j i) -> p g h j i", g=G, h=2, j=2, i=16)
            t13 = t1[:, 0:W].rearrange("p (g d) -> p g d", g=G, d=64)
            t25 = t2[:, 0:W].rearrange("p (g h j i) -> p g h j i", g=G, h=2, j=2, i=16)
            P.on(eng_a, lambda e: e.tensor_tensor(out=t13, in0=src3, in1=cv_.unsqueeze(1).to_broadcast([128, G, 64]), op=ALU.mult),
                 reads=list(srcbufs) + [B_cst], writes=[b1])
            P.on(eng_b, lambda e: e.tensor_tensor(out=t25[:, :, :, 0, :], in0=src5[:, :, :, 1, :],
                                                  in1=sv_[:, :, 0, :].unsqueeze(1).to_broadcast([128, G, 2, 16]), op=ALU.mult),
                 reads=list(srcbufs) + [B_cst], writes=[b2])
            P.on(eng_b, lambda e: e.tensor_tensor(out=t25[:, :, :, 1, :], in0=src5[:, :, :, 0, :],
                                                  in1=sv_[:, :, 1, :].unsqueeze(1).to_broadcast([128, G, 2, 16]), op=ALU.mult),
                 reads=list(srcbufs) + [B_cst], writes=[b2])
            P.on(eng_a, lambda e: e.tensor_tensor(out=dst, in0=t1[:, 0:W], in1=t2[:, 0:W], op=ALU.add), reads=[b1, b2], writes=list(dstbufs))

        def tslice(T):
            return slice(T * 128, (T + 1) * 128)

        def setup_layer(l):
            stg = [wstage[:, 0:4096].rearrange("p (k n) -> p k n", k=8, n=512), arena_view(16384, [128, 8, 512], F32)]
            stgB = [B_wstage, Buf()]
            smb = arena_view(32768, [128, 8, 128], F32)
            B_smb = Buf()
            P.dve(lambda e: e.tensor_copy(out=smb, in_=smod[:].unsqueeze(2).to_broadcast([128, 8, 128])), reads=[B_cst], writes=[B_smb])
            P.dma(normg[:], normg_in[l], writes=[B_misc])
            P.dma(rdl[:], rdl_in[l].partition_broadcast(128), writes=[B_misc])
            dlam, dlamb = r256.next()
            P.dma(dlam[:], dlam_in[l].partition_broadcast(128), writes=[dlamb])
            P.dma(g4[:, 0:64], qng_in[l].partition_broadcast(128), writes=[B_misc])
            P.dma(g4[:, 64:128], qng_in[l].partition_broadcast(128), writes=[B_misc])
            P.dma(g4[:, 128:192], kng_in[l].partition_broadcast(128), writes=[B_misc])
            P.dma(g4[:, 192:256], kng_in[l].partition_broadcast(128), writes=[B_misc])
            for cb in range(6):
                st_, stb_ = stg[cb % 2], stgB[cb % 2]
                P.dma(st_, wada_in[l][:, :, cb * 512:(cb + 1) * 512], writes=[stb_])
                bt, btb = r512.next()
                P.dma(bt[:], bada_in[l][cb * 512:(cb + 1) * 512].partition_broadcast(128), writes=[btb])
                bk = cb % 4
                for kc in range(8):
                    P.pe(lambda e, kc=kc, st_=st_, bk=bk: e.matmul(banks[bk][:, 0:512], lhsT=smb[:, kc, :], rhs=st_[:, kc, :],
                                                                     start=(kc == 0), stop=(kc == 7)),
                         reads=[B_smb, stb_], writes=[bankB[bk]])
                if cb >= 4:
                    P.dve(lambda e, bk=bk, bt=bt, cb=cb: e.tensor_tensor(out=gate_b[:, (cb - 4) * 512:(cb - 3) * 512], in0=banks[bk][:, 0:512],
                                                                          in1=bt[:], op=ALU.add),
                          reads=[bankB[bk], btb], writes=[B_gate])
                else:
                    P.dve(lambda e, bk=bk, bt=bt: e.tensor_tensor(out=bt[:], in0=banks[bk][:, 0:512], in1=bt[:], op=ALU.add),
                          reads=[bankB[bk], btb], writes=[btb])
                    which = cb // 2
                    tb = 4 + (cb % 2)
                    for jj in range(4):
                        kc = (cb % 2) * 4 + jj
                        P.pe(lambda e, jj=jj, bt=bt, tb=tb: e.transpose(banks[tb][:, jj * 128:(jj + 1) * 128], bt[:, jj * 128:(jj + 1) * 128], identf[:]),
                             reads=[btb, B_cst], writes=[bankB[tb]])
                        P.act(lambda e, jj=jj, tb=tb, which=which, kc=kc: e.copy(out=modT[:, which, kc:kc + 1], in_=banks[tb][:, jj * 128:jj * 128 + 1]),
                              reads=[bankB[tb]], writes=[B_modAB])
            P.dve(lambda e: e.scalar_tensor_tensor(out=modA[:], in0=modT[:, 1, :], scalar=1.0, in1=normg[:], op0=ALU.add, op1=ALU.mult),
                  reads=[B_modAB, B_misc], writes=[B_modAB])
            pr, prb = r256.next()
            P.dve(lambda e: e.tensor_tensor(out=pr[:, 0:64], in0=dlam[:, 0:64], in1=dlam[:, 64:128], op=ALU.mult), reads=[dlamb], writes=[prb])
            P.dve(lambda e: e.tensor_tensor(out=pr[:, 64:128], in0=dlam[:, 128:192], in1=dlam[:, 192:256], op=ALU.mult), reads=[dlamb], writes=[prb])
            s12, s12b = rs.next()
            P.dve(lambda e: e.tensor_reduce(out=s12[:, 0:2], in_=pr[:, 0:128].rearrange("p (a d) -> p a d", a=2, d=64), axis=AX.X, op=ALU.add),
                  reads=[prb], writes=[s12b])
            P.act(lambda e: e.activation(out=s12[:, 0:2], in_=s12[:, 0:2], func=AF.Exp), reads=[s12b], writes=[s12b])
            lam_init = 0.8 - 0.6 * math.exp(-0.3 * l)
            P.dve(lambda e: e.tensor_tensor(out=small[:, 0:1], in0=s12[:, 1:2], in1=s12[:, 0:1], op=ALU.subtract), reads=[s12b], writes=[B_misc])
            P.dve(lambda e: e.tensor_scalar(out=small[:, 0:1], in0=small[:, 0:1], scalar1=-lam_init, scalar2=None, op0=ALU.add),
                  reads=[B_misc], writes=[B_misc])
            P.act(lambda e: e.activation(out=lg[:], in_=rdl[:], func=AF.Exp, scale=-1.0), reads=[B_misc], writes=[B_misc])
            P.act(lambda e: e.activation(out=lg[:], in_=lg[:], func=AF.Ln, bias=1.0), reads=[B_misc], writes=[B_misc])
            P.dve(lambda e: e.tensor_scalar(out=lg[:], in0=lg[:], scalar1=-1.0, scalar2=None, op0=ALU.mult), reads=[B_misc], writes=[B_misc])

        def norm_phase(l):
            src = x_in if l == 0 else xs_scr
            for T in range(NT):
                xt, xb_ = xring.next()
                P.dma(xt[:], src[T * 128:(T + 1) * 128, :], reads=([B_xs[T]] if l > 0 else []), writes=[xb_])
                xh, xhb = xhring.next()
                ss, ssb = rs.next()
                P.act(lambda e, xt=xt, xh=xh, ss=ss: e.activation(out=xh[:], in_=xt[:], func=AF.Square, accum_out=ss[:, 0:1]),
                      reads=[xb_], writes=[xhb, ssb])
                rstd, rb_ = rstd_from_ss(ss[:, 0:1], ssb, 1, 1.0 / D_MODEL, EPS)
                P.dve(lambda e, xt=xt, xh=xh, rstd=rstd: e.tensor_scalar(out=xh[:], in0=xt[:], scalar1=rstd[:, 0:1], scalar2=None, op0=ALU.mult),
                      reads=[xb_, rb_], writes=[xhb])
                bk = 6 + (T % 2)
                for kc in range(8):
                    P.pe(lambda e, kc=kc, xh=xh, bk=bk: e.transpose(banksb[bk][:, kc * 128:(kc + 1) * 128], xh[:, kc * 128:(kc + 1) * 128], identb[:]),
                         reads=[xhb, B_cst], writes=[bankB[bk]])
                for kc in range(8):
                    if kc % 2 == 0:
                        P.dve(lambda e, kc=kc, bk=bk, T=T: e.tensor_scalar(out=hT[:, kc, tslice(T)], in0=banksb[bk][:, kc * 128:(kc + 1) * 128],
                                                                            scalar1=modA[:, kc:kc + 1], scalar2=modT[:, 0, kc:kc + 1],
                                                                            op0=ALU.mult, op1=ALU.add),
                              reads=[bankB[bk], B_modAB], writes=[B_hT[T]])
                    else:
                        P.act(lambda e, kc=kc, bk=bk, T=T: e.activation(out=hT[:, kc, tslice(T)], in_=banksb[bk][:, kc * 128:(kc + 1) * 128],
                                                                         func=AF.Identity, scale=modA[:, kc:kc + 1], bias=modT[:, 0, kc:kc + 1]),
                              reads=[bankB[bk], B_modAB], writes=[B_hT[T]])

        def load_unit_weights(l, u):
            W = UNIT_W[u]
            wb = wbf[:, 0:8 * W].rearrange("p (k n) -> p k n", k=8, n=W)
            engs = (["dve", "act"] * 4) if u < 4 else (["dve"] * 8)
            for (a, b) in [(0, 512)] + ([(512, W)] if W > 512 else []):
                wd = b - a
                ws = wstage[:, 0:8 * wd].rearrange("p (k n) -> p k n", k=8, n=wd)
                P.dma(ws, win_in[l][:, :, UNIT_OFF[u] + a:UNIT_OFF[u] + b], writes=[B_wstage])
                for kc in range(8):
                    if engs[kc] == "act":
                        P.act(lambda e, kc=kc, ws=ws, a=a, b=b: e.copy(out=wb[:, kc, a:b], in_=ws[:, kc, :]), reads=[B_wstage], writes=[B_wbf])
                    else:
                        P.on(engs[kc], lambda e, kc=kc, ws=ws, a=a, b=b: e.tensor_copy(out=wb[:, kc, a:b], in_=ws[:, kc, :]), reads=[B_wstage], writes=[B_wbf])
            return wb

        def project(wb, T, bk, c0, c1):
            for kc in range(8):
                P.pe(lambda e, kc=kc: e.matmul(banks[bk][:, 0:c1 - c0], lhsT=hT[:, kc, tslice(T)], rhs=wb[:, kc, c0:c1],
                                               start=(kc == 0), stop=(kc == 7)),
                     reads=[B_hT[T], B_wbf], writes=[bankB[bk]])

        def mixed_out(mt, mtb, chunk, T, tbank, eng="act"):
            P.pe(lambda e: e.transpose(banksb[tbank][:, 0:128], mt, identb[:]), reads=[mtb, B_cst], writes=[bankB[tbank]])
            if eng == "act":
                P.act(lambda e: e.copy(out=mixT[:, chunk, tslice(T)], in_=banksb[tbank][:, 0:128]), reads=[bankB[tbank]], writes=[B_mixT[chunk][T]])
            else:
                P.dve(lambda e: e.tensor_copy(out=mixT[:, chunk, tslice(T)], in_=banksb[tbank][:, 0:128]), reads=[bankB[tbank]], writes=[B_mixT[chunk][T]])

        def diff_unit(l, h, DB):
            u = 4 + h
            chunk = UNIT_CHUNK[u]
            if h == 0:
                P.barrier()
            par = h % 2
            base = par * 27728
            QT = arena_view(base + 0, [128, 2, TOK], BF16)
            KT = arena_view(base + 8192, [128, 2, 2560], BF16)
            V = arena_view(base + 18432, [128, NKT, 130], BF16)
            sg = arena_view(base + 23632, [128, NT, 128], BF16)
            ckst = arena_view(55456, [128, 4, 128], F32)
            cvst = arena_view(57504, [128, 4, 128], F32)
            ckb = arena_view(59552, [128, 4, 128], BF16)
            B_QT, B_KT, B_V, B_sg, B_qm = DB["set"][par]
            B_ck = DB["ck"]
            wb = load_unit_weights(l, u)
            import os as _os
            DD0 = _os.environ.get('DIFFDBG', '')
            if 'nomask' not in DD0:
                for c in range(2):
                    P.dma(QT[64:72, c, :], qmask_in, writes=[B_qm])
                    P.dma(KT[64:72, c, :], kmask_in, writes=[B_qm])
            if 'noctx' not in DD0:
                P.dma(ckst, ck_in[l].rearrange("(t p) n -> p t n", p=128)[:, :, h * 128:(h + 1) * 128], writes=[B_ck])
                P.dma(cvst, cv_in[l].rearrange("(t p) n -> p t n", p=128)[:, :, h * 128:(h + 1) * 128], writes=[B_ck])
                P.pool(lambda e: e.memset(V[:, :, 128:130], 1.0), writes=B_V)
                P.dve(lambda e: e.tensor_copy(out=ckb, in_=ckst), reads=[B_ck], writes=[B_ck])
                for pt in range(4):
                    bk = 5
                    for c in range(2):
                        P.pe(lambda e, pt=pt, c=c, bk=bk: e.transpose(banksb[bk][0:64, c * 128:(c + 1) * 128], ckb[:, pt, c * 64:(c + 1) * 64], identb[:]),
                             reads=[B_ck, B_cst], writes=[bankB[bk]])
                    P.act(lambda e, pt=pt, bk=bk: e.copy(out=KT[0:64, :, 2048 + pt * 128:2048 + (pt + 1) * 128],
                                                         in_=banksb[bk][0:64, 0:256].rearrange("p (c t) -> p c t", c=2, t=128)),
                          reads=[bankB[bk]], writes=[B_KT[16 + pt]])
                    P.any(lambda e, pt=pt: e.tensor_copy(out=V[:, 16 + pt, 0:128], in_=cvst[:, pt, :]), reads=[B_ck], writes=[B_V[16 + pt]])

            if 'noA' in DD0:
                return
            for T in range(NT):
                zb = 6 + (T % 2)
                project(wb, T, zb, 0, 512)
                z = banks[zb]
                sq, sqb = r256.next()
                P.act(lambda e, z=z, sq=sq: e.activation(out=sq[:], in_=z[:, 0:256], func=AF.Square), reads=[bankB[zb]], writes=[sqb])
                ss, ssb = rs.next()
                P.dve(lambda e, sq=sq, ss=ss: e.tensor_reduce(out=ss[:, 0:4], in_=sq[:].rearrange("p (g d) -> p g d", g=4, d=64), axis=AX.X, op=ALU.add),
                      reads=[sqb], writes=[ssb])
                rstd, rb_ = rstd_from_ss(ss[:, 0:4], ssb, 4, 1.0 / 64, EPS)
                nq, nqb = r256.next()
                P.dve(lambda e, z=z, nq=nq, rstd=rstd: e.tensor_tensor(out=nq[:].rearrange("p (g d) -> p g d", g=4, d=64),
                                                                      in0=z[:, 0:256].rearrange("p (g d) -> p g d", g=4, d=64),
                                                                      in1=rstd[:, 0:4].unsqueeze(2).to_broadcast([128, 4, 64]), op=ALU.mult),
                      reads=[bankB[zb], rb_], writes=[nqb])
                P.any(lambda e, nq=nq: e.tensor_tensor(out=nq[:], in0=nq[:], in1=g4[:], op=ALU.mult), reads=[nqb, B_misc], writes=[nqb])
                if 'nonk' not in DD0:
                    P.dma(nk_out[l][T * 128:(T + 1) * 128, h * 128:(h + 1) * 128], nq[:, 128:256], reads=[nqb])
                rt, rtb = rb256.next()
                import os as _os
                rope(nq[:], [nqb], rt[:], [rtb], T, 4, "any", "any")
                if 'noT' not in DD0:
                    tb = 5
                    for g in range(4):
                        P.pe(lambda e, g=g, rt=rt, tb=tb: e.transpose(banksb[tb][0:64, g * 128:(g + 1) * 128], rt[:, g * 64:(g + 1) * 64], identb[:]),
                             reads=[rtb, B_cst], writes=[bankB[tb]])
                    if 'noTq' not in DD0:
                      P.act(lambda e, tb=tb, T=T: e.copy(out=QT[0:64, :, tslice(T)], in_=banksb[tb][0:64, 0:256].rearrange("p (c t) -> p c t", c=2, t=128)),
                          reads=[bankB[tb]], writes=[B_QT[T]])
                    if 'noTk' not in DD0:
                      P.act(lambda e, tb=tb, T=T: e.copy(out=KT[0:64, :, tslice(T)], in_=banksb[tb][0:64, 256:512].rearrange("p (c t) -> p c t", c=2, t=128)),
                          reads=[bankB[tb]], writes=[B_KT[T]])
                vst, vstb = r128.next()
                P.dve(lambda e, z=z, vst=vst: e.tensor_copy(out=vst[:], in_=z[:, 256:384]), reads=[bankB[zb]], writes=[vstb])
                if 'nonk' not in DD0:
                    P.dma(nv_out[l][T * 128:(T + 1) * 128, h * 128:(h + 1) * 128], vst[:], reads=[vstb])
                P.any(lambda e, vst=vst, T=T: e.tensor_copy(out=V[:, T, 0:128], in_=vst[:]), reads=[vstb], writes=[B_V[T]])
                th, thb = r128.next()
                P.act(lambda e, z=z, th=th: e.activation(out=th[:], in_=z[:, 384:512], func=AF.Tanh, scale=0.5), reads=[bankB[zb]], writes=[thb])
                P.dve(lambda e, z=z, th=th, T=T: e.scalar_tensor_tensor(out=sg[:, T, :], in0=th[:], scalar=1.0, in1=z[:, 384:512], op0=ALU.add, op1=ALU.mult),
                      reads=[thb, bankB[zb]], writes=[B_sg[T]])
            import os as _os
            DD = _os.environ.get('DIFFDBG', '')
            if 'noB' in DD:
                return
            KR = 64 if 'k64' in DD else 72
            lam_init = 0.8 - 0.6 * math.exp(-0.3 * l)
            c0 = 0.5 * (1.0 - lam_init)
            OB = [2, 3, 4]

            def acc(c, qi):
                a = c * 4 + qi
                return OB[a // 3], (a % 3) * 160

            for qb in range(4):
                for k in OB:
                    P.pe(lambda e, k=k: e.matmul(banks[k][:, 0:512], lhsT=zerosb[:, 0:128], rhs=zerosb[:, 0:512], start=True, stop=False, skip_group_check=True),
                         reads=[B_cst], writes=[bankB[k]])
                steps = [(c, kt) for c in range(2) for kt in range(NKT)]
                pts = {}

                def emit_st(i):
                    c, kt = steps[i]
                    sbk = i % 2
                    P.pe(lambda e, c=c, kt=kt, sbk=sbk: e.matmul(banks[sbk][:, 0:512], lhsT=KT[0:KR, c, kt * 128:(kt + 1) * 128],
                                                                 rhs=QT[0:KR, c, qb * 512:(qb + 1) * 512], start=True, stop=True),
                         reads=[B_KT[kt], B_qm] + B_QT[qb * 4:qb * 4 + 4], writes=[bankB[sbk]])
                    pt_, ptb = rpt.next()
                    P.act(lambda e, sbk=sbk, pt_=pt_: e.activation(out=pt_[:], in_=banks[sbk][:, 0:512], func=AF.Exp, scale=0.125),
                          reads=[bankB[sbk]], writes=[ptb])
                    pts[i] = (pt_, ptb)

                def emit_pv(i):
                    c, kt = steps[i]
                    pt_, ptb = pts.pop(i)
                    for qi in range(4):
                        bk, off = acc(c, qi)
                        P.pe(lambda e, qi=qi, bk=bk, off=off, pt_=pt_, kt=kt: e.matmul(banks[bk][:, off:off + 129], lhsT=pt_[:, qi * 128:(qi + 1) * 128],
                                                                                       rhs=V[:, kt, 0:129], start=False, stop=(kt == NKT - 1), skip_group_check=True),
                             reads=[ptb, B_V[kt]], writes=[bankB[bk]])

                emit_st(0)
                for i in range(len(steps)):
                    if i + 1 < len(steps):
                        emit_st(i + 1)
                    emit_pv(i)
                for qi in range(4):
                    T = qb * 4 + qi
                    b0, o0 = acc(0, qi)
                    b1, o1 = acc(1, qi)
                    r01, r01b = rs.next()
                    P.dve(lambda e, r01=r01: e.reciprocal(out=r01[:, 0:1], in_=banks[b0][:, o0 + 128:o0 + 129]), reads=[bankB[b0]], writes=[r01b])
                    P.dve(lambda e, r01=r01: e.reciprocal(out=r01[:, 1:2], in_=banks[b1][:, o1 + 128:o1 + 129]), reads=[bankB[b1]], writes=[r01b])
                    P.dve(lambda e, r01=r01: e.tensor_tensor(out=r01[:, 1:2], in0=r01[:, 1:2], in1=small[:, 0:1], op=ALU.mult), reads=[r01b, B_misc], writes=[r01b])
                    d, db = r128.next()
                    P.dve(lambda e, d=d, r01=r01: e.tensor_scalar(out=d[:], in0=banks[b0][:, o0:o0 + 128], scalar1=r01[:, 0:1], scalar2=None, op0=ALU.mult),
                          reads=[bankB[b0], r01b], writes=[db])
                    P.dve(lambda e, d=d, r01=r01: e.scalar_tensor_tensor(out=d[:], in0=banks[b1][:, o1:o1 + 128], scalar=r01[:, 1:2], in1=d[:],
                                                                        op0=ALU.mult, op1=ALU.add),
                          reads=[bankB[b1], r01b, db], writes=[db])
                    jk, jkb = r128.next()
                    ss, ssb = rs.next()
                    P.act(lambda e, d=d, jk=jk, ss=ss: e.activation(out=jk[:], in_=d[:], func=AF.Square, accum_out=ss[:, 0:1]), reads=[db], writes=[jkb, ssb])
                    rstd, rb_ = rstd_from_ss(ss[:, 0:1], ssb, 1, 1.0 / (128 * c0 * c0), EPS / (c0 * c0))
                    mt, mtb = rb128.next()
                    P.dve(lambda e, d=d, rstd=rstd, mt=mt, T=T: e.scalar_tensor_tensor(out=mt[:], in0=d[:], scalar=rstd[:, 0:1], in1=sg[:, T, :],
                                                                                      op0=ALU.mult, op1=ALU.mult),
                          reads=[db, rb_, B_sg[T]], writes=[mtb])
                    mixed_out(mt[:], mtb, chunk, T, 5, eng="dve")

        def ret_unit(l, p, RB):
            u = p
            chunk = UNIT_CHUNK[u]
            if p == 0:
                P.barrier()
            base = p * 28672
            qT = arena_view(base + 0, [128, TOK], BF16)
            kT = arena_view(base + 4096, [128, TOK], BF16)
            ktok = arena_view(base + 8192, [128, NT, 128], BF16)
            v = arena_view(base + 12288, [128, NT, 128], BF16)
            sg = arena_view(base + 16384, [128, NT, 128], BF16)
            Sbf = arena_view(base + 20480, [128, 2, NT, 128], BF16)
            if p == 0:
                dtm_, qd_, kd_, stS_, lgrow_ = dtm, qd, kd, stS, lgrow
            else:
                dtm_ = arena_view(57344, [128, 2, 128], F32)
                qd_ = arena_view(58368, [128, 2, 128], F32)
                kd_ = arena_view(59392, [128, 2, 128], F32)
                stS_ = arena_view(60416, [128, 2, 128], F32)
                lgrow_ = lgrow1
            B_pair_, B_stS_ = RB[p]
            B_q = [Buf() for _ in range(NT)]
            B_k = [Buf() for _ in range(NT)]
            B_kt = [Buf() for _ in range(NT)]
            B_v = [Buf() for _ in range(NT)]
            B_sg = [Buf() for _ in range(NT)]
            B_S = [[Buf() for _ in range(NT)] for _ in range(2)]
            wb = load_unit_weights(l, u)
            for d_ in range(2):
                for hh in range(2):
                    col = d_ * 4 + 2 * p + hh
                    P.dve(lambda e, d_=d_, hh=hh, col=col: e.tensor_copy(out=lgrow_[hh * 64:(hh + 1) * 64, d_:d_ + 1], in_=lg[hh * 64:(hh + 1) * 64, col:col + 1]),
                          reads=[B_misc], writes=[B_pair_])
            for hh in range(2):
                e12t, e12b = r256.next()
                cf = 2 * p + hh
                cb_ = 4 + 2 * p + hh
                P.act(lambda e, cf=cf: e.activation(out=e12t[:, 0:128], in_=M1, func=AF.Exp, scale=lg[:, cf:cf + 1]), reads=[e12b, B_cst, B_misc], writes=[e12b, B_pair_])
                P.act(lambda e, cb_=cb_: e.activation(out=e12t[:, 128:256], in_=M2, func=AF.Exp, scale=lg[:, cb_:cb_ + 1]), reads=[e12b, B_cst, B_misc], writes=[e12b, B_pair_])
                P.dve(lambda e: e.tensor_tensor(out=e12t[:, 0:128], in0=e12t[:, 0:128], in1=L1, op=ALU.mult), reads=[e12b, B_pair_, B_cst], writes=[e12b, B_pair_])
                P.dve(lambda e: e.tensor_tensor(out=e12t[:, 128:256], in0=e12t[:, 128:256], in1=L2, op=ALU.mult), reads=[e12b, B_pair_, B_cst], writes=[e12b, B_pair_])
                P.dve(lambda e, hh=hh: e.tensor_tensor(out=dtm_[:, hh, :], in0=e12t[:, 0:128], in1=e12t[:, 128:256], op=ALU.add), reads=[e12b, B_pair_], writes=[e12b, B_pair_])
                P.act(lambda e, hh=hh, cf=cf: e.activation(out=kd_[:, 0, hh * 64:(hh + 1) * 64], in_=COLA[:, 0:64], func=AF.Exp, scale=lg[:, cf:cf + 1]),
                      reads=[B_cst, B_misc], writes=[B_pair_])
                P.act(lambda e, hh=hh, cb_=cb_: e.activation(out=kd_[:, 1, hh * 64:(hh + 1) * 64], in_=COLB[:, 0:64], func=AF.Exp, scale=lg[:, cb_:cb_ + 1]),
                      reads=[B_cst, B_misc], writes=[B_pair_])
            P.dve(lambda e: e.tensor_scalar(out=kd_[:], in0=kd_[:], scalar1=0.125, scalar2=None, op0=ALU.mult), reads=[B_pair_], writes=[B_pair_])
            P.act(lambda e: e.activation(out=qd_[:, 0, :], in_=IOTA1, func=AF.Exp, scale=lgrow_[:, 0:1]), reads=[B_cst, B_pair_], writes=[B_pair_])
            P.act(lambda e: e.activation(out=qd_[:, 1, :], in_=IOTA2, func=AF.Exp, scale=lgrow_[:, 1:2]), reads=[B_cst, B_pair_], writes=[B_pair_])
            P.act(lambda e: e.activation(out=lgrow_[:, 2:4], in_=lgrow_[:, 0:2], func=AF.Exp, scale=128.0), reads=[B_pair_], writes=[B_pair_])
            for T in range(NT):
                zb = T % 4
                project(wb, T, zb, 0, 512)
                z = banks[zb]
                rq, rqb = rb128.next()
                t1, b1 = r256.next()
                t2, b2 = r256.next()
                cv_ = ropec[:, T, :]
                sv_ = ropes[:, T, :].rearrange("p (h j i) -> p h j i", h=2, j=2, i=16)
                src3 = z[:, 0:256].rearrange("p (g d) -> p g d", g=4, d=64)
                src5 = z[:, 0:256].rearrange("p (g h j i) -> p g h j i", g=4, h=2, j=2, i=16)
                t13 = t1[:].rearrange("p (g d) -> p g d", g=4, d=64)
                t25 = t2[:].rearrange("p (g h j i) -> p g h j i", g=4, h=2, j=2, i=16)
                P.dve(lambda e, t13=t13, src3=src3, cv_=cv_: e.tensor_tensor(out=t13, in0=src3, in1=cv_.unsqueeze(1).to_broadcast([128, 4, 64]), op=ALU.mult),
                      reads=[bankB[zb], B_cst], writes=[b1])
                P.dve(lambda e, t25=t25, src5=src5, sv_=sv_: e.tensor_tensor(out=t25[:, :, :, 0, :], in0=src5[:, :, :, 1, :],
                                                                             in1=sv_[:, :, 0, :].unsqueeze(1).to_broadcast([128, 4, 2, 16]), op=ALU.mult),
                      reads=[bankB[zb], B_cst], writes=[b2])
                P.dve(lambda e, t25=t25, src5=src5, sv_=sv_: e.tensor_tensor(out=t25[:, :, :, 1, :], in0=src5[:, :, :, 0, :],
                                                                             in1=sv_[:, :, 1, :].unsqueeze(1).to_broadcast([128, 4, 2, 16]), op=ALU.mult),
                      reads=[bankB[zb], B_cst], writes=[b2])
                P.any(lambda e, t1=t1, t2=t2, rq=rq: e.tensor_tensor(out=rq[:], in0=t1[:, 0:128], in1=t2[:, 0:128], op=ALU.add), reads=[b1, b2], writes=[rqb])
                P.any(lambda e, t1=t1, t2=t2, T=T: e.tensor_tensor(out=ktok[:, T, :], in0=t1[:, 128:256], in1=t2[:, 128:256], op=ALU.add),
                       reads=[b1, b2], writes=[B_kt[T]])
                tb = 4 + (T % 2)
                P.pe(lambda e, rq=rq, tb=tb: e.transpose(banksb[tb][:, 0:128], rq[:], identb[:]), reads=[rqb, B_cst], writes=[bankB[tb]])
                P.pe(lambda e, T=T, tb=tb: e.transpose(banksb[tb][:, 128:256], ktok[:, T, :], identb[:]), reads=[B_kt[T], B_cst], writes=[bankB[tb]])
                P.act(lambda e, T=T, tb=tb: e.copy(out=qT[:, tslice(T)], in_=banksb[tb][:, 0:128]), reads=[bankB[tb]], writes=[B_q[T]])
                P.act(lambda e, T=T, tb=tb: e.activation(out=kT[:, tslice(T)], in_=banksb[tb][:, 128:256], func=AF.Copy, scale=0.125),
                      reads=[bankB[tb]], writes=[B_k[T]])
                P.act(lambda e, z=z, T=T: e.copy(out=v[:, T, :], in_=z[:, 256:384]), reads=[bankB[zb]], writes=[B_v[T]])
                th, thb = r128.next()
                P.act(lambda e, z=z, th=th: e.activation(out=th[:], in_=z[:, 384:512], func=AF.Tanh, scale=0.5), reads=[bankB[zb]], writes=[thb])
                P.dve(lambda e, z=z, th=th, T=T: e.scalar_tensor_tensor(out=sg[:, T, :], in0=th[:], scalar=1.0, in1=z[:, 384:512], op0=ALU.add, op1=ALU.mult),
                      reads=[thb, bankB[zb]], writes=[B_sg[T]])
            st_in = [srf_in, srb_in]
            st_out = [nsrf_out, nsrb_out]
            for d_ in range(2):
                P.pool(lambda e, d_=d_: e.memset(stS_[:, d_, :], 0.0), writes=[B_stS_[d_]])
                for hh in range(2):
                    P.dma(stS_[hh * 64:(hh + 1) * 64, d_, hh * 64:(hh + 1) * 64], st_in[d_][l, 2 * p + hh], writes=[B_stS_[d_]])
            for step in range(NT):
                for d_ in range(2):
                    T = step if d_ == 0 else NT - 1 - step
                    S = stS_[:, d_, :]
                    kcol = d_ * 16 + T
                    P.dve(lambda e, S=S, kcol=kcol: e.tensor_scalar(out=S, in0=S, scalar1=keep[:, kcol:kcol + 1], scalar2=None, op0=ALU.mult),
                          reads=[B_stS_[d_], B_cst], writes=[B_stS_[d_]])
                    P.act(lambda e, S=S, d_=d_, T=T: e.copy(out=Sbf[:, d_, T, :], in_=S), reads=[B_stS_[d_]], writes=[B_S[d_][T]])
                    kt_, ktb_ = rb128.next()
                    P.any(lambda e, kt_=kt_, T=T, d_=d_: e.tensor_tensor(out=kt_[:], in0=ktok[:, T, :], in1=kd_[:, d_, :], op=ALU.mult),
                           reads=[B_kt[T], B_pair_], writes=[ktb_])
                    ub = d_
                    P.pe(lambda e, kt_=kt_, T=T, ub=ub: e.matmul(banks[ub][:, 0:128], lhsT=kt_[:], rhs=v[:, T, :], start=True, stop=True),
                         reads=[ktb_, B_v[T]], writes=[bankB[ub]])
                    tmp, tmpb = r128.next()
                    P.dve(lambda e, tmp=tmp, ub=ub: e.tensor_tensor(out=tmp[:], in0=banks[ub][:, 0:128], in1=BM, op=ALU.mult),
                          reads=[bankB[ub], B_cst], writes=[tmpb])
                    P.dve(lambda e, S=S, tmp=tmp, d_=d_: e.scalar_tensor_tensor(out=S, in0=S, scalar=lgrow_[:, 2 + d_:3 + d_], in1=tmp[:], op0=ALU.mult, op1=ALU.add),
                          reads=[B_stS_[d_], B_pair_, tmpb], writes=[B_stS_[d_]])
                    is_out = (T % 2 == 1) if d_ == 0 else (T % 2 == 0)
                    if is_out:
                        so, sob = r128.next()
                        P.act(lambda e, so=so, S=S: e.copy(out=so[:], in_=S), reads=[B_stS_[d_]], writes=[sob])
                        for hh in range(2):
                            P.dma(st_out[d_][l, T // 2, 2 * p + hh], so[hh * 64:(hh + 1) * 64, hh * 64:(hh + 1) * 64], reads=[sob])
            for T in range(NT):
                qf, qfb = rb128.next()
                qb_, qbb = rb128.next()
                P.dve(lambda e, qf=qf, T=T: e.tensor_tensor(out=qf[:], in0=qT[:, tslice(T)], in1=qd_[:, 0, :], op=ALU.mult), reads=[B_q[T], B_pair_], writes=[qfb])
                P.any(lambda e, qb_=qb_, T=T: e.tensor_tensor(out=qb_[:], in0=qT[:, tslice(T)], in1=qd_[:, 1, :], op=ALU.mult), reads=[B_q[T], B_pair_], writes=[qbb])
                ob = 6 + (T % 2)
                ab = 2 + (T % 2)
                P.pe(lambda e, qf=qf, T=T, ob=ob: e.matmul(banks[ob][:, 0:128], lhsT=qf[:], rhs=Sbf[:, 0, T, :], start=True, stop=False, skip_group_check=True),
                     reads=[qfb, B_S[0][T]], writes=[bankB[ob]])
                P.pe(lambda e, qb_=qb_, T=T, ob=ob: e.matmul(banks[ob][:, 0:128], lhsT=qb_[:], rhs=Sbf[:, 1, T, :], start=False, stop=False, skip_group_check=True),
                     reads=[qbb, B_S[1][T]], writes=[bankB[ob]])
                am, amb = rb256.next()
                for hh in range(2):
                    P.pe(lambda e, hh=hh, T=T: e.matmul(banks[2 + hh][:, 0:128], lhsT=kT[hh * 64:(hh + 1) * 64, tslice(T)],
                                                        rhs=qT[hh * 64:(hh + 1) * 64, tslice(T)], start=True, stop=True),
                         reads=[B_k[T], B_q[T]], writes=[bankB[2 + hh]])
                    P.dve(lambda e, am=am, hh=hh: e.tensor_tensor(out=am[:, hh * 128:(hh + 1) * 128], in0=banks[2 + hh][:, 0:128], in1=dtm_[:, hh, :], op=ALU.mult),
                          reads=[bankB[2 + hh], B_pair_], writes=[amb])
                for hh in range(2):
                    P.pe(lambda e, hh=hh, am=am, T=T, ob=ob: e.matmul(banks[ob][:, hh * 64:(hh + 1) * 64], lhsT=am[:, hh * 128:(hh + 1) * 128],
                                                                      rhs=v[:, T, hh * 64:(hh + 1) * 64], start=False, stop=(hh == 1), skip_group_check=True),
                         reads=[amb, B_v[T]], writes=[bankB[ob]])
                finish_pair(banks[ob][:, 0:128], [bankB[ob]], sg, B_sg, chunk, T, 0.5, 4 + (T % 2))

        def finish_pair(o_ap, obufs, sg, B_sg, chunk, T, c0, tbank, act_rstd=False):
            ss, ssb = rs.next()
            jk, jkb = r128.next()
            for hh in range(2):
                P.act(lambda e, hh=hh: e.activation(out=jk[:, hh * 64:(hh + 1) * 64], in_=o_ap[:, hh * 64:(hh + 1) * 64], func=AF.Square,
                                                    accum_out=ss[:, hh:hh + 1]),
                      reads=obufs, writes=[jkb, ssb])
            if act_rstd:
                rstd, rb_ = rs.next()
                P.act(lambda e: e.activation(out=rstd[:, 0:2], in_=ss[:, 0:2], func=AF.Ln, scale=1.0 / (64 * c0 * c0), bias=EPS / (c0 * c0)),
                      reads=[ssb], writes=[rb_])
                P.act(lambda e: e.activation(out=rstd[:, 0:2], in_=rstd[:, 0:2], func=AF.Exp, scale=-0.5), reads=[rb_], writes=[rb_])
            else:
                rstd, rb_ = rstd_from_ss(ss[:, 0:2], ssb, 2, 1.0 / (64 * c0 * c0), EPS / (c0 * c0))
            mt, mtb = rb128.next()
            for hh in range(2):
                P.dve(lambda e, hh=hh: e.scalar_tensor_tensor(out=mt[:, hh * 64:(hh + 1) * 64], in0=o_ap[:, hh * 64:(hh + 1) * 64], scalar=rstd[:, hh:hh + 1],
                                                              in1=sg[:, T, hh * 64:(hh + 1) * 64], op0=ALU.mult, op1=ALU.mult),
                      reads=list(obufs) + [rb_, B_sg[T]], writes=[mtb])
            mixed_out(mt[:], mtb, chunk, T, tbank)

        def hgrn_unit(l, p):
            u = 2 + p
            chunk = UNIT_CHUNK[u]
            P.barrier()
            q = arena_view(0, [128, NT, 128], BF16)
            kk = arena_view(4096, [128, NT, 256], BF16)
            lf = arena_view(12288, [128, NT, 256], F32)
            v = arena_view(28672, [128, NT, 128], BF16)
            sg = arena_view(32768, [128, NT, 128], BF16)
            oacc = arena_view(36864, [128, NT, 128], F32)
            vmall = arena_view(45056, [128, NT, 512], BF16)
            B_vm = [Buf() for _ in range(NT)]
            B_q = [Buf() for _ in range(NT)]
            B_kk = [Buf() for _ in range(NT)]
            B_lf = [Buf() for _ in range(NT)]
            B_v = [Buf() for _ in range(NT)]
            B_sg = [Buf() for _ in range(NT)]
            B_oa = [Buf() for _ in range(NT)]
            wb = load_unit_weights(l, u)
            for half in range(2):
                P.dve(lambda e, half=half: e.tensor_copy(out=lb2[:, half * 128:(half + 1) * 128], in_=lball[:, l, p * 128:(p + 1) * 128]),
                      reads=[B_cst], writes=[B_pair])
            P.dve(lambda e: e.tensor_scalar(out=omlb2[:], in0=lb2[:], scalar1=-1.0, scalar2=1.0, op0=ALU.mult, op1=ALU.add), reads=[B_pair], writes=[B_pair])
            for T in range(NT):
                zb = 2 * (T % 2)
                project(wb, T, zb, 0, 512)
                project(wb, T, zb + 1, 512, 640)
                z = banks[zb]
                z2 = banks[zb + 1]
                u_, ub_ = r512.next()
                P.act(lambda e, z=z, u_=u_: e.activation(out=u_[:, 0:384], in_=z[:, 0:384], func=AF.Exp, scale=-1.0), reads=[bankB[zb]], writes=[ub_])
                P.act(lambda e, u_=u_: e.activation(out=u_[:, 0:384], in_=u_[:, 0:384], func=AF.Ln, bias=1.0), reads=[ub_], writes=[ub_])
                P.act(lambda e, u_=u_: e.activation(out=u_[:, 0:384], in_=u_[:, 0:384], func=AF.Exp, scale=-1.0), reads=[ub_], writes=[ub_])
                P.dve(lambda e, u_=u_, z=z, T=T: e.tensor_tensor(out=sg[:, T, :], in0=u_[:, 256:384], in1=z[:, 256:384], op=ALU.mult),
                      reads=[ub_, bankB[zb]], writes=[B_sg[T]])
                f_, fb_ = r256.next()
                P.any(lambda e, u_=u_, f_=f_: e.tensor_tensor(out=f_[:], in0=u_[:, 0:256], in1=omlb2[:], op=ALU.mult), reads=[ub_, B_pair], writes=[fb_])
                P.any(lambda e, f_=f_: e.tensor_tensor(out=f_[:], in0=f_[:], in1=lb2[:], op=ALU.add), reads=[fb_, B_pair], writes=[fb_])
                P.act(lambda e, f_=f_, T=T: e.activation(out=lf[:, T, :], in_=f_[:], func=AF.Ln), reads=[fb_], writes=[B_lf[T]])
                P.act(lambda e, f_=f_, T=T: e.activation(out=kk[:, T, :], in_=f_[:], func=AF.Identity, scale=-1.0, bias=1.0),
                      reads=[fb_], writes=[B_kk[T]])
                P.act(lambda e, z=z, T=T: e.activation(out=q[:, T, :], in_=z[:, 384:512], func=AF.Copy, scale=0.125), reads=[bankB[zb]], writes=[B_q[T]])
                P.act(lambda e, z2=z2, T=T: e.copy(out=v[:, T, :], in_=z2[:, 0:128]), reads=[bankB[zb + 1]], writes=[B_v[T]])
            st_in = [shf_in, shb_in]
            st_out = [nshf_out, nshb_out]
            TRI = [TRIF, TRIB]
            for d_ in range(2):
                P.pool(lambda e, d_=d_: e.memset(stS[:, d_, :], 0.0), writes=[B_stS[d_]])
                for hh in range(2):
                    P.dma(stS[hh * 64:(hh + 1) * 64, d_, hh * 64:(hh + 1) * 64], st_in[d_][l, 2 * p + hh], writes=[B_stS[d_]])
            done_first = [False] * NT
            for step in range(NT):
                for d_ in range(2):
                    T = step if d_ == 0 else NT - 1 - step
                    S = stS[:, d_, :]
                    lfd = lf[:, T, d_ * 128:(d_ + 1) * 128]
                    kcol = d_ * 16 + T
                    P.dve(lambda e, S=S, kcol=kcol: e.tensor_scalar(out=S, in0=S, scalar1=keep[:, kcol:kcol + 1], scalar2=None, op0=ALU.mult),
                          reads=[B_stS[d_], B_cst], writes=[B_stS[d_]])
                    sbf, sbfb = rb128.next()
                    P.act(lambda e, S=S, sbf=sbf: e.copy(out=sbf[:], in_=S), reads=[B_stS[d_]], writes=[sbfb])
                    P.pe(lambda e, lfd=lfd, d_=d_: e.matmul(banks[0][:, 0:128], lhsT=TRI[d_], rhs=lfd, start=True, stop=True),
                         reads=[B_cst, B_lf[T]], writes=[bankB[0]])
                    P.pe(lambda e, lfd=lfd: e.matmul(banks[0][:, 128:132], lhsT=lfd, rhs=ind[:], start=True, stop=True),
                         reads=[B_cst, B_lf[T]], writes=[bankB[0]])
                    G_, Gb_ = rs.next()
                    P.act(lambda e, G_=G_: e.activation(out=G_[:, 0:4], in_=banks[0][:, 128:132], func=AF.Exp), reads=[bankB[0]], writes=[Gb_])
                    eq, eqb = r128.next()
                    ek, ekb = r128.next()
                    P.act(lambda e, eq=eq: e.activation(out=eq[:], in_=banks[0][:, 0:128], func=AF.Exp), reads=[bankB[0]], writes=[eqb])
                    P.act(lambda e, ek=ek: e.activation(out=ek[:], in_=banks[0][:, 0:128], func=AF.Exp, scale=-1.0), reads=[bankB[0]], writes=[ekb])
                    qt_, qtb = rb128.next()
                    kt_, ktb = rb128.next()
                    P.dve(lambda e, qt_=qt_, eq=eq, T=T: e.tensor_tensor(out=qt_[:], in0=q[:, T, :], in1=eq[:], op=ALU.mult), reads=[B_q[T], eqb], writes=[qtb])
                    P.any(lambda e, kt_=kt_, ek=ek, T=T, d_=d_: e.tensor_tensor(out=kt_[:], in0=kk[:, T, d_ * 128:(d_ + 1) * 128], in1=ek[:], op=ALU.mult),
                           reads=[B_kk[T], ekb], writes=[ktb])
                    P.pe(lambda e, qt_=qt_: e.transpose(banksb[1][:, 0:128], qt_[:], identb[:]), reads=[qtb, B_cst], writes=[bankB[1]])
                    P.pe(lambda e, kt_=kt_: e.transpose(banksb[1][:, 128:256], kt_[:], identb[:]), reads=[ktb, B_cst], writes=[bankB[1]])
                    qkT, qkTb = rb256.next()
                    P.act(lambda e, qkT=qkT: e.copy(out=qkT[:], in_=banksb[1][:, 0:256]), reads=[bankB[1]], writes=[qkTb])
                    vm, vmb = vmall[:, T, :], B_vm[T]
                    if not done_first[T]:
                        for j in range(4):
                            P.act(lambda e, j=j, vm=vm, T=T: e.activation(out=vm[:, j * 128:(j + 1) * 128], in_=v[:, T, :], func=AF.Identity,
                                                                          scale=ind[:, j:j + 1]),
                                  reads=[B_v[T], B_cst], writes=[vmb])
                    P.pe(lambda e, kt_=kt_, vm=vm: e.matmul(banks[2][:, 0:512], lhsT=kt_[:], rhs=vm, start=True, stop=True),
                         reads=[ktb, vmb], writes=[bankB[2]])
                    am, amb = rb256.next()
                    for hh in range(2):
                        abk = 3 + hh
                        P.pe(lambda e, hh=hh, qkT=qkT, abk=abk: e.matmul(banks[abk][:, 0:128], lhsT=qkT[hh * 64:(hh + 1) * 64, 128:256],
                                                                          rhs=qkT[hh * 64:(hh + 1) * 64, 0:128], start=True, stop=True),
                             reads=[qkTb], writes=[bankB[abk]])
                        P.dve(lambda e, am=am, d_=d_, hh=hh, abk=abk: e.tensor_tensor(out=am[:, hh * 128:(hh + 1) * 128], in0=banks[abk][:, 0:128], in1=TRI[d_], op=ALU.mult),
                              reads=[bankB[abk], B_cst], writes=[amb])
                    ob = 5 + d_
                    jorder = [0, 1, 2, 3] if d_ == 0 else [3, 2, 1, 0]
                    cur, curb = sbf, sbfb
                    for ji, j in enumerate(jorder):
                        P.pe(lambda e, j=j, qkT=qkT, cur=cur: e.matmul(banks[ob][32 * j:32 * j + 32, 0:128], lhsT=qkT[:, 32 * j:32 * j + 32], rhs=cur[:],
                                                                       start=True, stop=False, tile_position=(0, 32 * j), skip_group_check=True),
                             reads=[qkTb, curb], writes=[bankB[ob]])
                        tg, tgb = r128.next()
                        P.dve(lambda e, tg=tg, j=j, G_=G_: e.scalar_tensor_tensor(out=tg[:], in0=banks[2][:, j * 128:(j + 1) * 128], scalar=G_[:, j:j + 1], in1=BM,
                                                                                  op0=ALU.mult, op1=ALU.mult),
                              reads=[bankB[2], Gb_, B_cst], writes=[tgb])
                        P.dve(lambda e, S=S, tg=tg, j=j, G_=G_: e.scalar_tensor_tensor(out=S, in0=S, scalar=G_[:, j:j + 1], in1=tg[:], op0=ALU.mult, op1=ALU.add),
                              reads=[B_stS[d_], Gb_, tgb], writes=[B_stS[d_]])
                        if ji < 3:
                            cur, curb = rb128.next()
                            P.act(lambda e, S=S, cur=cur: e.copy(out=cur[:], in_=S), reads=[B_stS[d_]], writes=[curb])
                    for hh in range(2):
                        P.pe(lambda e, hh=hh, am=am, T=T: e.matmul(banks[ob][:, hh * 64:(hh + 1) * 64], lhsT=am[:, hh * 128:(hh + 1) * 128],
                                                                   rhs=v[:, T, hh * 64:(hh + 1) * 64], start=False, stop=(hh == 1), skip_group_check=True),
                             reads=[amb, B_v[T]], writes=[bankB[ob]])
                    is_out = (T % 2 == 1) if d_ == 0 else (T % 2 == 0)
                    if is_out:
                        so, sob = r128.next()
                        P.act(lambda e, so=so, S=S: e.copy(out=so[:], in_=S), reads=[B_stS[d_]], writes=[sob])
                        for hh in range(2):
                            P.dma(st_out[d_][l, T // 2, 2 * p + hh], so[hh * 64:(hh + 1) * 64, hh * 64:(hh + 1) * 64], reads=[sob])
                    if not done_first[T]:
                        done_first[T] = True
                        P.act(lambda e, T=T: e.copy(out=oacc[:, T, :], in_=banks[ob][:, 0:128]), reads=[bankB[ob]], writes=[B_oa[T]])
                    else:
                        ot, otb = r128.next()
                        P.dve(lambda e, ot=ot, T=T: e.tensor_tensor(out=ot[:], in0=banks[ob][:, 0:128], in1=oacc[:, T, :], op=ALU.add),
                              reads=[bankB[ob], B_oa[T]], writes=[otb])
                        finish_pair(ot[:], [otb], sg, B_sg, chunk, T, 1.0, 7, act_rstd=True)

        def out_phase(l, last):
            P.barrier()
            wo = arena_view(0, [128, 8, 1024], BF16)
            B_wo = Buf()
            for half in range(2):
                ws = wstage[:, 0:4096].rearrange("p (k n) -> p k n", k=8, n=512)
                P.dma(ws, wout_in[l][:, :, half * 512:(half + 1) * 512], writes=[B_wstage])
                for kc in range(8):
                    eng = "any"
                    P.on(eng, lambda e, kc=kc, half=half: e.tensor_copy(out=wo[:, kc, half * 512:(half + 1) * 512], in_=ws[:, kc, :]),
                         reads=[B_wstage], writes=[B_wo])
            src = x_in if l == 0 else xs_scr
            dst = y_out if last else xs_scr
            for T in range(NT):
                xt, xb_ = xring.next()
                P.dma(xt[:], src[T * 128:(T + 1) * 128, :], reads=([B_xs[T]] if l > 0 else []), writes=[xb_])
                for nb in range(2):
                    bk = 2 * (T % 2) + nb
                    for c in range(8):
                        P.pe(lambda e, c=c, nb=nb, bk=bk, T=T: e.matmul(banks[bk][:, 0:512], lhsT=mixT[:, c, tslice(T)], rhs=wo[:, c, nb * 512:(nb + 1) * 512],
                                                                        start=(c == 0), stop=(c == 7)),
                             reads=[B_mixT[c][T], B_wo], writes=[bankB[bk]])
                    tmp, tmpb = r512.next()
                    P.dve(lambda e, tmp=tmp, bk=bk, nb=nb: e.tensor_tensor(out=tmp[:], in0=banks[bk][:, 0:512], in1=gate_b[:, nb * 512:(nb + 1) * 512], op=ALU.mult),
                          reads=[bankB[bk], B_gate], writes=[tmpb])
                    P.any(lambda e, tmp=tmp, xt=xt, nb=nb: e.tensor_tensor(out=xt[:, nb * 512:(nb + 1) * 512], in0=tmp[:], in1=xt[:, nb * 512:(nb + 1) * 512], op=ALU.add),
                           reads=[tmpb, xb_], writes=[xb_])
                P.dma(dst[T * 128:(T + 1) * 128, :], xt[:], reads=[xb_], writes=([] if last else [B_xs[T]]))

        for l in range(L):
            setup_layer(l)
            norm_phase(l)
            RB = [(Buf(), [Buf(), Buf()]) for _ in range(2)]
            for p in range(2):
                if units_enabled is None or ("r%d" % p) in units_enabled:
                    ret_unit(l, p, RB)
            for p in range(2):
                if units_enabled is None or ("g%d" % p) in units_enabled:
                    hgrn_unit(l, p)
            DB = {"set": [([Buf() for _ in range(NT)], [Buf() for _ in range(NKT)], [Buf() for _ in range(NKT)], [Buf() for _ in range(NT)], Buf()) for _ in range(2)], "ck": Buf()}
            for h in range(4):
                if units_enabled is None or ("d%d" % h) in units_enabled:
                    diff_unit(l, h, DB)
            if dbg:
                P.barrier()
                P.dma(dbg_out[l], mixT[:], reads=[b for row in B_mixT for b in row])
            out_phase(l, last=(l == L - 1))

        with nc.Block() as block:
            run = P.build(sems, dsems, reorder=REORDER)
            block.sync(lambda e: run("sp", e))
            block.tensor(lambda e: run("pe", e))
            block.scalar(lambda e: run("act", e))
            block.vector(lambda e: run("dve", e))
            block.gpsimd(lambda e: run("pool", e))
    return nc


def _unit_perm():
    off = dict(rq=0, rk=256, rv=512, rg=768, dq=1024, dk=1536, dv=2048, dg=2560, hq=3072, hff=3328, hfb=3584, hi=3840, hg=4096)
    cols = []
    for p in range(2):
        for n in ("rq", "rk", "rv", "rg"):
            cols += list(range(off[n] + 128 * p, off[n] + 128 * p + 128))
    for p in range(2):
        for n in ("hff", "hfb", "hg", "hq", "hi"):
            cols += list(range(off[n] + 128 * p, off[n] + 128 * p + 128))
    for h in range(4):
        for n in ("dq", "dk", "dv", "dg"):
            cols += list(range(off[n] + 128 * h, off[n] + 128 * h + 128))
    return np.array(cols, dtype=np.int64)


def _constants():
    s = np.arange(128, dtype=np.float32)[:, None]
    t = np.arange(128, dtype=np.float32)[None, :]
    M1 = np.maximum(t - s, 0)
    L1 = (s <= t).astype(np.float32)
    M2 = np.maximum(s - t, 0)
    L2 = (s >= t).astype(np.float32)
    IOTA1 = np.broadcast_to(t + 1, (128, 128))
    IOTA2 = np.broadcast_to(128 - t, (128, 128))
    COLA = np.broadcast_to(127 - s, (128, 128))
    COLB = np.broadcast_to(s, (128, 128))
    same = (np.floor(s / 32) == np.floor(t / 32))
    TRIF = (same & (s <= t)).astype(np.float32)
    TRIB = (same & (s >= t)).astype(np.float32)
    BM = (np.floor(s / 64) == np.floor(t / 64)).astype(np.float32)
    cst = np.stack([M1, L1, M2, L2, IOTA1, IOTA2, COLA, COLB, TRIF, TRIB, BM], axis=1).astype(np.float32)
    ind = (np.floor(np.arange(128)[:, None] / 32) == np.arange(4)[None, :]).astype(np.float32)
    return np.ascontiguousarray(cst), np.ascontiguousarray(ind)


def _rope_tables(sample):
    ropec = np.ones((128, 16, 64), np.float32)
    ropes = np.zeros((128, 16, 64), np.float32)
    if sample:
        tt = np.arange(TOK)
        row = (tt // 64).astype(np.float32)
        col = (tt % 64).astype(np.float32)
        inv = (np.float32(10000.0) ** (-np.arange(16, dtype=np.float32) / np.float32(16))).astype(np.float32)
        ar = (row[:, None] * inv[None, :]).astype(np.float32)
        ac = (col[:, None] * inv[None, :]).astype(np.float32)
        c = np.concatenate([np.cos(ar), np.cos(ar), np.cos(ac), np.cos(ac)], axis=1).astype(np.float32)
        s_ = np.concatenate([-np.sin(ar), np.sin(ar), -np.sin(ac), np.sin(ac)], axis=1).astype(np.float32)
        ropec = np.ascontiguousarray(c.reshape(16, 128, 64).transpose(1, 0, 2))
        ropes = np.ascontiguousarray(s_.reshape(16, 128, 64).transpose(1, 0, 2))
    return ropec, ropes


_NC_CACHE = {}


def kernel(x_prompt, x_sample, c, c_ctx, cache_diff_k, cache_diff_v, state_ret_fwd, state_ret_bwd,
           state_hgrn_fwd, state_hgrn_bwd, norm_g, w_ada, b_ada, w_in, w_out, ret_decay_logit,
           diff_qn_g, diff_kn_g, diff_lambda, hgrn_lb_logit, _dbg=False, _units=None, _L=2):
    f32 = np.float32
    bf = ml_dtypes.bfloat16
    A = lambda a: np.ascontiguousarray(np.asarray(a, dtype=f32))
    x_prompt, x_sample, c, c_ctx = A(x_prompt), A(x_sample), A(c), A(c_ctx)
    perm = _unit_perm()
    w_in_p = A(w_in)[:, :, perm]
    win = np.ascontiguousarray(w_in_p.reshape(2, 8, 128, 4352).transpose(0, 2, 1, 3))
    wada = np.ascontiguousarray(A(w_ada).reshape(2, 8, 128, 3072).transpose(0, 2, 1, 3))
    wout = np.ascontiguousarray(A(w_out).reshape(2, 8, 128, 1024).transpose(0, 2, 1, 3))
    normg = np.ascontiguousarray(A(norm_g).reshape(2, 8, 128).transpose(0, 2, 1))
    cst, ind = _constants()
    shared = dict(
        normg=normg, wada=wada, bada=A(b_ada), win=win, wout=wout, rdl=A(ret_decay_logit).reshape(2, 8),
        qng=A(diff_qn_g), kng=A(diff_kn_g), dlam=A(diff_lambda).reshape(2, 256), hlb=A(hgrn_lb_logit).reshape(512),
        identb=np.eye(128, dtype=f32).astype(bf), identf=np.eye(128, dtype=f32), cst=cst, ind=ind,
    )
    ropec_s, ropes_s = _rope_tables(True)
    ropec_p, ropes_p = _rope_tables(False)
    z64 = np.zeros((2, 4, 64, 64), f32)
    zc = np.zeros((2, 512, 512), f32)
    in_maps = []
    for core in range(8):
        m = dict(shared)
        if core < 4:
            b = core
            m["x"] = x_sample[b]
            m["modv"] = np.ascontiguousarray(c[b].reshape(8, 128).T)
            m["ck"] = np.ascontiguousarray(A(cache_diff_k)[b].reshape(2, 512, 512))
            m["cv"] = np.ascontiguousarray(A(cache_diff_v)[b].reshape(2, 512, 512))
            m["srf"], m["srb"] = A(state_ret_fwd)[b], A(state_ret_bwd)[b]
            m["shf"], m["shb"] = A(state_hgrn_fwd)[b], A(state_hgrn_bwd)[b]
            m["ropec"], m["ropes"] = ropec_s, ropes_s
            m["qmask"] = np.zeros((8, 2048), f32).astype(bf)
            m["kmask"] = np.zeros((8, 2560), f32).astype(bf)
            m["keep"] = np.ones((128, 32), f32)
        else:
            j = core - 4
            m["x"] = np.ascontiguousarray(x_prompt[8 * j:8 * j + 8].reshape(2048, 1024))
            m["modv"] = np.ascontiguousarray(c_ctx.reshape(8, 128).T)
            m["ck"], m["cv"] = zc, zc
            m["srf"], m["srb"], m["shf"], m["shb"] = z64, z64, z64, z64
            m["ropec"], m["ropes"] = ropec_p, ropes_p
            seq = np.arange(2048) // 256
            qm = (seq[None, :] == np.arange(8)[:, None]).astype(f32)
            km = np.full((8, 2560), BIGNEG, f32)
            km[:, :2048] = np.where(seq[None, :] == np.arange(8)[:, None], 0.0, BIGNEG)
            m["qmask"] = qm.astype(bf)
            m["kmask"] = km.astype(bf)
            kf = np.array([0.0 if T % 2 == 0 else 1.0 for T in range(16)], f32)
            kb = np.array([0.0 if T % 2 == 1 else 1.0 for T in range(16)], f32)
            m["keep"] = np.ascontiguousarray(np.broadcast_to(np.concatenate([kf, kb])[None, :], (128, 32)))
        in_maps.append(m)

    key = (_L, _dbg, None if _units is None else tuple(sorted(_units)))
    if key not in _NC_CACHE:
        _NC_CACHE[key] = build_program(L=_L, dbg=_dbg, units_enabled=_units)
    nc = _NC_CACHE[key]
    res = run_bass_kernel_spmd(nc, in_maps, core_ids=list(range(8)))
    R = res.results

    y_sample = np.stack([R[b]["y"] for b in range(4)], axis=0)
    y_prompt = np.concatenate([R[4 + j]["y"].reshape(8, 256, 1024) for j in range(4)], axis=0)
    nk = np.concatenate([R[4 + j]["nk"].reshape(2, 8, 256, 4, 2, 64).transpose(1, 0, 2, 3, 4, 5) for j in range(4)], axis=0)
    nv = np.concatenate([R[4 + j]["nv"].reshape(2, 8, 256, 4, 128).transpose(1, 0, 2, 3, 4) for j in range(4)], axis=0)
    st = []
    for name in ("nsrf", "nsrb", "nshf", "nshb"):
        st.append(np.concatenate([R[4 + j][name].transpose(1, 0, 2, 3, 4) for j in range(4)], axis=0))
    outs = (y_prompt, y_sample, np.ascontiguousarray(nk), np.ascontiguousarray(nv), *[np.ascontiguousarray(s) for s in st])
    if _dbg:
        return outs, [R[i]["dbgmix"] for i in range(8)]
    return outs
```
